# Optimizing a Trainium2 kernel written in Bass

```python
import math
import jax
import jax.numpy as jnp
from jax import lax
import numpy as np

D_MODEL = 1024
BATCH = 8
SEQ = 2048
DEPTH = 2
DEC_BATCH = 32
DEC_SEQ = 4
PAST_LEN = 16384
PAGE_SIZE = 128

N_MIXERS = 2
N_MLA = (DEPTH + 1) // 2
N_DN = DEPTH // 2

MLA_HEADS = 8
QK_NOPE = 128
QK_ROPE = 64
V_HEAD = 128
KV_LORA = 256
Q_LORA = 384
MLA_ROW = KV_LORA + QK_ROPE
MLA_SCALE = (QK_NOPE + QK_ROPE) ** -0.5
ROPE_THETA = 10000.0
Q_BLOCK = 128

DN_HEADS = 8
DN_DK = 128
DN_DV = 128
CONV_W = 4
DN_CHUNK = 64
DN_HK = DN_HEADS * DN_DK
DN_QKV = DN_HEADS * (2 * DN_DK + DN_DV)
DN_Z = DN_HEADS * DN_DV

D_FF = ((-(-8 * D_MODEL // 3) + 255) // 256) * 256

RMS_EPS = 1e-6
L2_EPS = 1e-6

kernel_name = "hybrid_mla_gated_deltanet_step"


def rmsnorm(x, w):
    x32 = x.astype(jnp.float32)
    y = x32 * lax.rsqrt(jnp.mean(x32 * x32, axis=-1, keepdims=True) + RMS_EPS)
    return (y * w.astype(jnp.float32)).astype(x.dtype)


def l2norm(x):
    return x * lax.rsqrt(jnp.sum(x * x, axis=-1, keepdims=True) + L2_EPS)


def rope(x, pos):
    half = x.shape[-1] // 2
    freq = ROPE_THETA ** (-jnp.arange(half, dtype=jnp.float32) / half)
    ang = pos.astype(jnp.float32)[:, None] * freq[None, :]
    bshape = (ang.shape[0],) + (1,) * (x.ndim - 3) + (half,)
    cos = jnp.cos(ang).reshape(bshape)
    sin = jnp.sin(ang).reshape(bshape)
    x32 = x.astype(jnp.float32)
    x1, x2 = x32[..., :half], x32[..., half:]
    return jnp.concatenate([x1 * cos - x2 * sin, x2 * cos + x1 * sin], axis=-1).astype(x.dtype)


def mla_project(u, pos, w_in, g_q, g_kv, w_uq, w_uk):
    b, t, _ = u.shape
    a = u @ w_in
    c_q = rmsnorm(a[..., :Q_LORA], g_q)
    c_kv = rmsnorm(a[..., Q_LORA:Q_LORA + KV_LORA], g_kv)
    k_r = rope(a[..., Q_LORA + KV_LORA:], pos)
    q = (c_q @ w_uq).reshape(b, t, MLA_HEADS, QK_NOPE + QK_ROPE)
    q_lat = jnp.einsum('bthn,hrn->bthr', q[..., :QK_NOPE], w_uk)
    q_rope = rope(q[..., QK_NOPE:], pos)
    rows = jnp.concatenate([c_kv, k_r], axis=-1)
    return q_lat, q_rope, rows


def mla_core(q_lat, q_rope, rows, q_pos, k_pos):
    c = rows[..., :KV_LORA]
    kr = rows[..., KV_LORA:]
    s = (jnp.einsum('bqhr,bkr->bhqk', q_lat, c)
         + jnp.einsum('bqhp,bkp->bhqk', q_rope, kr)).astype(jnp.float32) * MLA_SCALE
    s = jnp.where(k_pos[None, :] <= q_pos[:, None], s, -jnp.inf)
    p = jax.nn.softmax(s, axis=-1).astype(c.dtype)
    return jnp.einsum('bhqk,bkr->bqhr', p, c)


def mla_prompt_attention(q_lat, q_rope, rows, pos):
    b, t, h, r = q_lat.shape
    nb = t // Q_BLOCK

    def blk(x):
        return jnp.moveaxis(x.reshape(b, nb, Q_BLOCK, *x.shape[2:]), 1, 0)

    def one(args):
        ql, qr, qp = args
        return mla_core(ql, qr, rows, qp, pos)

    o = lax.map(one, (blk(q_lat), blk(q_rope), pos.reshape(nb, Q_BLOCK)))
    return jnp.moveaxis(o, 0, 1).reshape(b, t, h, r)


def mla_out(o_lat, w_uv, w_o):
    b, t, _, _ = o_lat.shape
    o = jnp.einsum('bthr,hrv->bthv', o_lat, w_uv).reshape(b, t, MLA_HEADS * V_HEAD)
    return o @ w_o


def gdn_chunked(q, k, v, g, beta, s0):
    b, t, h, _ = q.shape
    dv = v.shape[-1]
    n = t // DN_CHUNK
    c = DN_CHUNK

    def blk(x):
        return jnp.moveaxis(x.reshape(b, n, c, h, *x.shape[3:]), 3, 1)

    q, k, v, g, beta = blk(q), blk(k), blk(v), blk(g), blk(beta)
    gc = jnp.cumsum(g, axis=-1)
    idx = jnp.arange(c)
    causal = idx[:, None] >= idx[None, :]
    strict = idx[:, None] > idx[None, :]
    decay = jnp.exp(jnp.where(causal, gc[..., :, None] - gc[..., None, :], -jnp.inf))
    kb = k * beta[..., None]
    vb = v * beta[..., None]
    lower = jnp.where(strict, jnp.einsum('bhncd,bhnsd->bhncs', kb, k) * decay, 0.0)
    a_mat = jnp.eye(c, dtype=jnp.float32) + lower
    rhs = jnp.concatenate([vb, kb * jnp.exp(gc)[..., None]], axis=-1)
    sol = lax.linalg.triangular_solve(a_mat, rhs, left_side=True, lower=True, unit_diagonal=True)
    u_c, w_c = sol[..., :dv], sol[..., dv:]
    qk = jnp.where(causal, jnp.einsum('bhncd,bhnsd->bhncs', q, k) * decay, 0.0)
    q_dec = q * jnp.exp(gc)[..., None]
    k_dec = k * jnp.exp(gc[..., -1:] - gc)[..., None]
    g_last = jnp.exp(gc[..., -1])

    def step(s, xs):
        u_i, w_i, qk_i, qd_i, kd_i, gl_i = xs
        v_new = u_i - jnp.einsum('bhcd,bhde->bhce', w_i, s)
        o_i = jnp.einsum('bhcd,bhde->bhce', qd_i, s) + jnp.einsum('bhcs,bhse->bhce', qk_i, v_new)
        s = s * gl_i[..., None, None] + jnp.einsum('bhcd,bhce->bhde', kd_i, v_new)
        return s, o_i

    xs = tuple(jnp.moveaxis(x, 2, 0) for x in (u_c, w_c, qk, q_dec, k_dec, g_last))
    s, o = lax.scan(step, s0, xs)
    o = jnp.transpose(o, (1, 0, 3, 2, 4)).reshape(b, t, h, dv)
    return o, s


def gdn_recurrent(q, k, v, g, beta, s0):
    def step(s, xs):
        q_t, k_t, v_t, g_t, b_t = xs
        s = s * jnp.exp(g_t)[..., None, None]
        kv = jnp.einsum('bhd,bhde->bhe', k_t, s)
        delta = (v_t - kv) * b_t[..., None]
        s = s + jnp.einsum('bhd,bhe->bhde', k_t, delta)
        return s, jnp.einsum('bhd,bhde->bhe', q_t, s)

    xs = tuple(jnp.moveaxis(x, 1, 0) for x in (q, k, v, g, beta))
    s, o = lax.scan(step, s0, xs)
    return jnp.moveaxis(o, 0, 1), s


def dn_mixer(u, s0, conv0, w_in, conv_w, a_log, dt_bias, g_out, w_o, chunked):
    b, t, _ = u.shape
    proj = u @ w_in
    qkv_raw = proj[..., :DN_QKV]
    z = proj[..., DN_QKV:DN_QKV + DN_Z].reshape(b, t, DN_HEADS, DN_DV)
    b_raw = proj[..., DN_QKV + DN_Z:DN_QKV + DN_Z + DN_HEADS]
    a_raw = proj[..., DN_QKV + DN_Z + DN_HEADS:]
    xc = jnp.concatenate([conv0.astype(qkv_raw.dtype), qkv_raw], axis=1)
    new_conv = xc[:, xc.shape[1] - (CONV_W - 1):]
    qkv = lax.conv_general_dilated(xc, conv_w.astype(xc.dtype)[:, None, :], (1,), 'VALID',
                                   dimension_numbers=('NWC', 'WIO', 'NWC'),
                                   feature_group_count=DN_QKV)
    qkv = jax.nn.silu(qkv).astype(jnp.float32)
    q = l2norm(qkv[..., :DN_HK].reshape(b, t, DN_HEADS, DN_DK)) * (DN_DK ** -0.5)
    k = l2norm(qkv[..., DN_HK:2 * DN_HK].reshape(b, t, DN_HEADS, DN_DK))
    v = qkv[..., 2 * DN_HK:].reshape(b, t, DN_HEADS, DN_DV)
    beta = jax.nn.sigmoid(b_raw.astype(jnp.float32))
    g = -jnp.exp(a_log.astype(jnp.float32)) * jax.nn.softplus(
        a_raw.astype(jnp.float32) + dt_bias.astype(jnp.float32))
    core = gdn_chunked if chunked else gdn_recurrent
    o, s = core(q, k, v, g, beta, s0.astype(jnp.float32))
    o = rmsnorm(o, g_out).astype(u.dtype) * jax.nn.silu(z)
    return o.reshape(b, t, DN_HEADS * DN_DV) @ w_o, s, new_conv


def swiglu(u, w_in, w_out):
    gu = u @ w_in
    return (jax.nn.silu(gu[..., :D_FF]) * gu[..., D_FF:]) @ w_out


def setup_inputs(seed: int = 0) -> dict:
    key = jax.random.key(seed)
    ks = jax.random.split(key, 24)
    f32 = jnp.float32

    def nrm(i, shape, scale=1.0):
        return jax.random.normal(ks[i], shape, f32) * scale

    n_pages = PAST_LEN // PAGE_SIZE
    n_used = DEC_BATCH * n_pages
    n_pool = n_used + n_used // 4
    page_table = jax.random.permutation(ks[0], n_pool)[:n_used].reshape(DEC_BATCH, n_pages).astype(jnp.int32)
    dt = jnp.exp(jax.random.uniform(ks[1], (N_DN, DN_HEADS), f32, math.log(1e-3), math.log(1e-1)))
    return {
        "x_prompt": nrm(2, (BATCH, SEQ, D_MODEL)),
        "x_sample": nrm(3, (DEC_BATCH, DEC_SEQ, D_MODEL)),
        "cache_mla": nrm(4, (N_MLA, n_pool, PAGE_SIZE, MLA_ROW)),
        "state_dn": nrm(5, (N_DN, DEC_BATCH, DN_HEADS, DN_DK, DN_DV), 0.1),
        "state_dn_conv": nrm(6, (N_DN, DEC_BATCH, CONV_W - 1, DN_QKV)),
        "page_table": page_table,
        "norm_w": 1.0 + nrm(7, (DEPTH, 4, D_MODEL), 0.02),
        "mla_w_in": nrm(8, (N_MLA, D_MODEL, Q_LORA + MLA_ROW), D_MODEL ** -0.5),
        "mla_g_q": 1.0 + nrm(9, (N_MLA, Q_LORA), 0.02),
        "mla_g_kv": 1.0 + nrm(10, (N_MLA, KV_LORA), 0.02),
        "mla_w_uq": nrm(11, (N_MLA, Q_LORA, MLA_HEADS * (QK_NOPE + QK_ROPE)), Q_LORA ** -0.5),
        "mla_w_uk": nrm(12, (N_MLA, MLA_HEADS, KV_LORA, QK_NOPE), KV_LORA ** -0.5),
        "mla_w_uv": nrm(13, (N_MLA, MLA_HEADS, KV_LORA, V_HEAD), KV_LORA ** -0.5),
        "mla_w_o": nrm(14, (N_MLA, MLA_HEADS * V_HEAD, D_MODEL), (MLA_HEADS * V_HEAD) ** -0.5),
        "dn_w_in": nrm(15, (N_DN, D_MODEL, DN_QKV + DN_Z + 2 * DN_HEADS), D_MODEL ** -0.5),
        "dn_conv_w": nrm(16, (N_DN, CONV_W, DN_QKV), CONV_W ** -0.5),
        "dn_a_log": jnp.log(jax.random.uniform(ks[17], (N_DN, DN_HEADS), f32, 1.0, 16.0)),
        "dn_dt_bias": dt + jnp.log(-jnp.expm1(-dt)),
        "dn_g_out": 1.0 + nrm(18, (N_DN, DN_DV), 0.02),
        "dn_w_o": nrm(19, (N_DN, DN_HEADS * DN_DV, D_MODEL), (DN_HEADS * DN_DV) ** -0.5),
        "ffn_w_in": nrm(20, (DEPTH, D_MODEL, 2 * D_FF), D_MODEL ** -0.5),
        "ffn_w_out": nrm(21, (DEPTH, D_FF, D_MODEL), D_FF ** -0.5),
    }


def reference(x_prompt, x_sample, cache_mla, state_dn, state_dn_conv, page_table, norm_w,
              mla_w_in, mla_g_q, mla_g_kv, mla_w_uq, mla_w_uk, mla_w_uv, mla_w_o,
              dn_w_in, dn_conv_w, dn_a_log, dn_dt_bias, dn_g_out, dn_w_o,
              ffn_w_in, ffn_w_out):
    bp, tp, _ = x_prompt.shape
    bs, ts, _ = x_sample.shape
    past = page_table.shape[1] * PAGE_SIZE
    pos_p = jnp.arange(tp)
    pos_s = past + jnp.arange(ts)
    pos_all = jnp.arange(past + ts)
    hp, hs = x_prompt, x_sample
    rows_p_l, rows_s_l, sp_l, ss_l, cp_l, cs_l = [], [], [], [], [], []
    for layer in range(DEPTH):
        j = layer // N_MIXERS
        nw = norm_w[layer]
        up = rmsnorm(hp, nw[0])
        us = rmsnorm(hs, nw[0])
        if layer % N_MIXERS == 0:
            ql, qr, rows_p = mla_project(up, pos_p, mla_w_in[j], mla_g_q[j], mla_g_kv[j], mla_w_uq[j], mla_w_uk[j])
            o_p = mla_prompt_attention(ql, qr, rows_p, pos_p)
            ql, qr, rows_s = mla_project(us, pos_s, mla_w_in[j], mla_g_q[j], mla_g_kv[j], mla_w_uq[j], mla_w_uk[j])
            past_rows = cache_mla[j, page_table].reshape(bs, past, MLA_ROW)
            all_rows = jnp.concatenate([past_rows.astype(rows_s.dtype), rows_s], axis=1)
            o_s = mla_core(ql, qr, all_rows, pos_s, pos_all)
            mix_p = mla_out(o_p, mla_w_uv[j], mla_w_o[j])
            mix_s = mla_out(o_s, mla_w_uv[j], mla_w_o[j])
            rows_p_l.append(rows_p)
            rows_s_l.append(rows_s)
        else:
            s0_p = jnp.zeros((bp, DN_HEADS, DN_DK, DN_DV), jnp.float32)
            c0_p = jnp.zeros((bp, CONV_W - 1, DN_QKV), up.dtype)
            mix_p, s_p, c_p = dn_mixer(up, s0_p, c0_p, dn_w_in[j], dn_conv_w[j], dn_a_log[j],
                                       dn_dt_bias[j], dn_g_out[j], dn_w_o[j], True)
            mix_s, s_s, c_s = dn_mixer(us, state_dn[j], state_dn_conv[j], dn_w_in[j], dn_conv_w[j],
                                       dn_a_log[j], dn_dt_bias[j], dn_g_out[j], dn_w_o[j], False)
            sp_l.append(s_p.astype(state_dn.dtype))
            ss_l.append(s_s.astype(state_dn.dtype))
            cp_l.append(c_p.astype(state_dn_conv.dtype))
            cs_l.append(c_s.astype(state_dn_conv.dtype))
        hp = hp + rmsnorm(mix_p, nw[1])
        hs = hs + rmsnorm(mix_s, nw[1])
        hp = hp + rmsnorm(swiglu(rmsnorm(hp, nw[2]), ffn_w_in[layer], ffn_w_out[layer]), nw[3])
        hs = hs + rmsnorm(swiglu(rmsnorm(hs, nw[2]), ffn_w_in[layer], ffn_w_out[layer]), nw[3])
    mla_rows_prompt = jnp.stack(rows_p_l)
    mla_rows_sample = jnp.stack(rows_s_l)
    dn_state_prompt = jnp.stack(sp_l)
    dn_state_sample = jnp.stack(ss_l)
    dn_conv_prompt = jnp.stack(cp_l)
    dn_conv_sample = jnp.stack(cs_l)
    return (hp, hs, mla_rows_prompt, mla_rows_sample, dn_state_prompt, dn_state_sample, dn_conv_prompt, dn_conv_sample)
```

```python
import contextlib
import math
import numpy as np
import concourse.bass as bass
import concourse.mybir as mybir
from concourse.bass_utils import run_bass_kernel_spmd

F32 = mybir.dt.float32
BF16 = mybir.dt.bfloat16
I32 = mybir.dt.int32
AF = mybir.ActivationFunctionType
ALU = mybir.AluOpType

D = 1024
KC = 8
NH = 8
QLORA, KVLORA, ROPE = 384, 256, 64
ROW = KVLORA + ROPE
DFF = 2816
FC = DFF // 128
NEG = -30000.0
MLA_SCALE = (128 + 64) ** -0.5


class Reg:
    __slots__ = ("name", "w", "rd", "excl")

    def __init__(self, name="", excl=False):
        self.name = name
        self.w = None
        self.rd = {}
        self.excl = excl


class Trk:
    def __init__(self, nc, es):
        self.nc = nc
        self.es = es
        self.eng = {"pe": nc.tensor, "act": nc.scalar, "dve": nc.vector, "pool": nc.gpsimd, "sp": nc.sync}
        self.semh = {}
        self.cnt = {}
        self.seen = {k: {} for k in self.eng}
        for k in self.eng:
            self.semh[k] = es.enter_context(nc.semaphore("s_" + k))
            self.cnt[k] = 0
        self.same_sync = {"pool": True, "act": True, "dve": True}
        self.nchan = 0

    def chan(self, name=None):
        self.nchan += 1
        k = "c%d" % self.nchan
        self.semh[k] = self.es.enter_context(self.nc.semaphore("d_%d" % self.nchan))
        self.cnt[k] = 0
        return k

    def _waits(self, e, reads, writes, skip=None):
        need = {}
        for r in reads:
            if r.w is not None and need.get(r.w[0], 0) < r.w[1]:
                need[r.w[0]] = r.w[1]
            if r.excl:
                for k, c in r.rd.items():
                    if k != e and need.get(k, 0) < c:
                        need[k] = c
        for w in writes:
            if w.w is not None and need.get(w.w[0], 0) < w.w[1]:
                need[w.w[0]] = w.w[1]
            for k, c in w.rd.items():
                if need.get(k, 0) < c:
                    need[k] = c
        for k, c in need.items():
            if k == skip:
                continue
            if k == e and not self.same_sync.get(e):
                continue
            if self.seen[e].get(k, 0) >= c:
                continue
            self.eng[e].wait_ge(self.semh[k], c)
            self.seen[e][k] = c

    def op(self, e, fn, reads=(), writes=()):
        self._waits(e, reads, writes)
        ins = fn(self.eng[e])
        self.cnt[e] += 1
        ins.then_inc(self.semh[e], 1)
        c = self.cnt[e]
        for r in reads:
            r.rd[e] = c
        for w in writes:
            w.w = (e, c)
            w.rd = {}
        return ins

    def dma(self, q, ch, fn, reads=(), writes=(), cont=False):
        if not cont and self.cnt[ch] > self.seen[q].get(ch, 0):
            self.eng[q].wait_ge(self.semh[ch], self.cnt[ch])
            self.seen[q][ch] = self.cnt[ch]
        self._waits(q, reads, writes, skip=ch)
        ins = fn(self.eng[q])
        self.cnt[ch] += 16
        ins.then_inc(self.semh[ch], 16)
        c = self.cnt[ch]
        for r in reads:
            r.rd[ch] = c
        for w in writes:
            w.w = (ch, c)
            w.rd = {}
        return ins

    def barrier(self):
        for e in self.eng:
            for k, c in self.cnt.items():
                if c == 0:
                    continue
                if self.seen[e].get(k, 0) >= c:
                    continue
                self.eng[e].wait_ge(self.semh[k], c)
                self.seen[e][k] = c

    def final(self):
        e = "sp"
        for k, c in self.cnt.items():
            if k == e or c == 0:
                continue
            if self.seen[e].get(k, 0) >= c:
                continue
            self.eng[e].wait_ge(self.semh[k], c)
            self.seen[e][k] = c


def host_consts(SEQ, PAST):
    T = SEQ + 16
    c = np.zeros((128, 1536), np.float32)
    idx = np.arange(128)
    c[:, 0:128] = np.eye(128, dtype=np.float32)
    c[:, 128:256] = (idx[:, None] <= idx[None, :]).astype(np.float32)
    c[:, 256:384] = (idx[:, None] > idx[None, :]).astype(np.float32)
    c[:, 384:512] = np.where(idx[None, :] >= idx[:, None], NEG, 0.0)
    c[:, 512:640] = np.where(idx[None, :] < idx[:, None], NEG, 0.0)
    rot = np.zeros((128, 64), np.float32)
    for m in range(32):
        rot[m + 32, m] = -1.0
        rot[m, m + 32] = 1.0
    c[:, 640:704] = rot
    c[:, 704:832] = 1.0
    dm = np.zeros((128, 32), np.float32)
    for j in range(4):
        for q in range(32):
            dm[j, q] = 1.0 if j <= (q % 4) else 0.0
    c[:, 832:864] = dm
    for j in range(4):
        c[:, 1024 + j * 128:1024 + (j + 1) * 128] = np.eye(128, dtype=np.float32)
    half = 32
    freq = (10000.0 ** (-np.arange(half, dtype=np.float32) / half)).astype(np.float32)
    pos = np.concatenate([np.arange(SEQ), np.tile(PAST + np.arange(4), 4)]).astype(np.float32)
    ang = pos[None, :] * freq[:, None]
    rope = np.zeros((64, 2, T), np.float32)
    rope[0:32, 0] = np.cos(ang)
    rope[32:64, 0] = np.cos(ang)
    rope[0:32, 1] = np.sin(ang)
    rope[32:64, 1] = np.sin(ang)
    return c, rope


def host_gmask():
    idx = np.arange(128)
    c_, s_ = idx[:, None], idx[None, :]
    g = np.zeros((128, 128 + 5 * 256), np.float32)
    g[:, 0:128] = (c_ // 4 == s_ // 4)
    for li, b in enumerate([4, 8, 16, 32, 64]):
        m = ((c_ // (2 * b)) == (s_ // (2 * b))) & ((c_ // b) > (s_ // b))
        g[:, 128 + li * 256:128 + li * 256 + 128] = m
        g[:, 128 + li * 256 + 128:128 + li * 256 + 256] = m.T
    return g


INV_F32 = False
import os as _os
GCUT = int(_os.environ.get('GCUT', '99'))
GSK = int(_os.environ.get('GSK', '0'))
GOP = int(_os.environ.get('GOP', '0'))


def build(SEQ=2048, NPG=128, NPOOL=5120, stages=("mla", "ffn0", "gdn", "ffn1")):
    T = SEQ + 16
    HALF = SEQ // 2
    PAST = NPG * 128
    nc = bass.Bass("TRN2", target_bir_lowering=False)
    es = contextlib.ExitStack()
    tk = Trk(nc, es)

    def din(name, shape, dt=F32):
        return nc.dram_tensor(name, list(shape), dt, kind="ExternalInput").ap()

    def dout(name, shape, dt=F32):
        return nc.dram_tensor(name, list(shape), dt, kind="ExternalOutput").ap()

    xT = din("xT", [D, T])
    cache = din("cache", [NPOOL * 16, 8 * ROW])
    ptab = din("ptab", [128, 4], I32)
    st_in = din("st_in", [4, NH, 128, 128])
    cv_in = din("cv_in", [3072, 12])
    normw = din("normw", [128, 64])
    consts_d = din("consts", [128, 1536])
    gmask_d = din("gmask", [128, 1408])
    rope_d = din("rope", [64, 2, T])
    m_win = din("m_win", [D, 704])
    m_gq = din("m_gq", [128, 3])
    m_gkv = din("m_gkv", [128, 2])
    m_wuq = din("m_wuq", [QLORA, NH * 192])
    m_wuk = din("m_wuk", [NH, KVLORA, 128])
    m_wukT = din("m_wukT", [NH, 128, KVLORA])
    m_wuv = din("m_wuv", [NH, KVLORA, 128])
    m_wo = din("m_wo", [D, D])
    d_win = din("d_win", [D, 4112])
    d_cw = din("d_cw", [128, 24, 4])
    d_alog = din("d_alog", [1, NH])
    d_dtb = din("d_dtb", [1, NH])
    d_gout = din("d_gout", [128, 1])
    d_wo = din("d_wo", [D, D])
    f_win = din("f_win", [2, D, 2 * DFF])
    f_wout = din("f_wout", [2, DFF, D])

    yT = dout("yT", [D, T])
    rowsT = dout("rowsT", [ROW, T])
    st_p = dout("st_p", [NH, 128, 128])
    st_s = dout("st_s", [4, NH, 128, 128])
    cv_p = dout("cv_p", [3072, 3])
    cv_s = dout("cv_s", [3072, 12])

    uid = [0]

    def sb(stack, name, shape, dt):
        uid[0] += 1
        return stack.enter_context(nc.sbuf_tensor("%s_%d" % (name, uid[0]), list(shape), dt))

    h = sb(es, "h", [128, KC, T], F32)
    h_r = [[Reg("h%d_%d" % (k, j)) for j in range(8)] for k in range(KC)]
    cf = sb(es, "cf", [128, 1536], F32)
    cb = sb(es, "cb", [128, 1536], BF16)
    nw = sb(es, "nw", [128, 64], F32)
    epst = sb(es, "epst", [128, 2], F32)
    c_r = Reg("consts")
    I_f, U_f, MS_f, NEGL_f, NEGQ_f, ROT_f, ONE_f = (cf[:, 0:128], cf[:, 128:256], cf[:, 256:384],
                                                    cf[:, 384:512], cf[:, 512:640], cf[:, 640:704],
                                                    cf[:, 704:832])
    I_b, U_b, ONE_b, DM_b = cb[:, 0:128], cb[:, 128:256], cb[:, 704:832], cb[:, 832:864]
    I4_b = cb[:, 1024:1536]
    psb = [es.enter_context(nc.psum_tensor("ps%d" % i, [128, 512], F32)) for i in range(8)]
    ps_r = [Reg("ps%d" % i, excl=True) for i in range(8)]
    bank_i = [0]

    reserved = set()

    def bank():
        for _ in range(16):
            i = bank_i[0]
            bank_i[0] = (i + 1) % 8
            if i not in reserved:
                return psb[i], ps_r[i]
        raise RuntimeError("no free psum bank")

    def reserve(n):
        out = []
        for _ in range(n):
            for _ in range(16):
                i = bank_i[0]
                bank_i[0] = (i + 1) % 8
                if i not in reserved:
                    break
            reserved.add(i)
            out.append((psb[i], ps_r[i], i))
        return out

    def release(lst):
        for (_, _, i) in lst:
            reserved.discard(i)

    ch_in = tk.chan()
    tk.dma("sp", ch_in, lambda e: e.dma_start(out=cf[:], in_=consts_d), writes=[c_r])
    ch_cb = tk.chan()
    cb_r = Reg("cb")
    tk.dma("pool", ch_cb, lambda e: e.dma_start(out=cb[:], in_=consts_d), writes=[cb_r])
    tk.dma("sp", ch_in, lambda e: e.dma_start(out=nw[:], in_=normw), writes=[c_r])
    tk.op("dve", lambda e: e.memset(epst[:, 0:1], 1e-6), writes=[c_r])
    tk.op("dve", lambda e: e.memset(epst[:, 1:2], 1.0), reads=[cb_r], writes=[c_r])
    ch_x = tk.chan()
    for k in range(KC):
        tk.dma("sp", ch_x, lambda e, k=k: e.dma_start(out=h[:, k, :], in_=xT[k * 128:(k + 1) * 128, :]),
               writes=h_r[k], cont=(k > 0))
    for k in range(KC):
        for r_ in h_r[k]:
            r_.w = (ch_x, tk.cnt[ch_x])

    def hregs(ts, n):
        j0, j1 = ts // 512, (ts + n - 1) // 512
        return [h_r[k][j] for k in range(KC) for j in range(j0, j1 + 1)]

    def hreg_k(k, ts, n):
        j0, j1 = ts // 512, (ts + n - 1) // 512
        return [h_r[k][j] for j in range(j0, j1 + 1)]

    def tiles(t0, t1):
        out = []
        t = t0
        while t < min(t1, SEQ):
            n = min(512, min(t1, SEQ) - t)
            out.append((t, n))
            t += n
        if t1 > SEQ:
            out.append((SEQ, t1 - SEQ))
        return out

    def ftiles(t0, t1):
        th = t1 - t0
        nt = (th + 511) // 512
        base, rem = divmod(th, nt)
        out, t = [], t0
        for i in range(nt):
            n = base + (1 if i < rem else 0)
            out.append((t, n))
            t += n
        return out

    def nwcol(layer, i, k):
        j = (layer * 4 + i) * 8 + k
        return nw[:, j:j + 1]

    def rstd_from_ss(ps_ap, ps_reg, npart, n, scale, tmp, tmp_r, out, out_r):
        tk.op("act", lambda e: e.activation(out=tmp[:npart, :n], in_=ps_ap, func=AF.Ln,
                                            bias=epst[:npart, 0:1], scale=scale),
              reads=[ps_reg, c_r], writes=[tmp_r])
        tk.op("act", lambda e: e.activation(out=out[:npart, :n], in_=tmp[:npart, :n], func=AF.Exp, scale=-0.5),
              reads=[tmp_r], writes=[out_r])

    class Scratch:
        def __init__(self, stack, name, shape, dt, nbuf, nreg=None, chan=False):
            self.t = [sb(stack, "%s%d" % (name, i), shape, dt) for i in range(nbuf)]
            if nreg is None:
                self.r = [Reg("%s%d" % (name, i)) for i in range(nbuf)]
            else:
                self.r = [[Reg("%s%d_%d" % (name, i, j)) for j in range(nreg)] for i in range(nbuf)]
            self.ch = [tk.chan() for _ in range(nbuf)] if chan else None
            self.i = 0
            self.last = 0

        def get(self):
            i = self.i
            self.last = i
            self.i = (i + 1) % len(self.t)
            return self.t[i], self.r[i]

        def chan(self):
            return self.ch[self.last]

    def rmsnorm_pre(layer, wi, ts, n, dst, dst_r, dst_off, sq_s, t1_s, t2_s):
        sq, sq_r = sq_s.get()
        tk.op("act", lambda e: e.activation(out=sq[:, :, :n], in_=h[:, :, ts:ts + n], func=AF.Square),
              reads=hregs(ts, n), writes=[sq_r])
        pb, pr = bank()
        for k in range(KC):
            tk.op("pe", lambda e, k=k: e.matmul(pb[:, :n], ONE_b, sq[:, k, :n], start=(k == 0), stop=(k == KC - 1)),
                  reads=[sq_r, c_r], writes=[pr])
        t1, t1r = t1_s.get()
        t2, t2r = t2_s.get()
        rstd_from_ss(pb[:, :n], pr, 128, n, 1.0 / D, t1, t1r, t2, t2r)
        for k in range(KC):
            tk.op("dve", lambda e, k=k: e.scalar_tensor_tensor(
                out=dst[:, k, dst_off:dst_off + n], in0=h[:, k, ts:ts + n], scalar=nwcol(layer, wi, k),
                in1=t2[:, :n], op0=ALU.mult, op1=ALU.mult),
                reads=hreg_k(k, ts, n) + [t2r, c_r], writes=[dst_r])

    def postnorm_add(layer, wi, y, y_r, yoff, ts, n, sq_s, t1_s, t2_s):
        sq, sq_r = sq_s.get()
        tk.op("act", lambda e: e.activation(out=sq[:, :, :n], in_=y[:, :, yoff:yoff + n], func=AF.Square),
              reads=y_r, writes=[sq_r])
        pb, pr = bank()
        for k in range(KC):
            tk.op("pe", lambda e, k=k: e.matmul(pb[:, :n], ONE_b, sq[:, k, :n], start=(k == 0), stop=(k == KC - 1)),
                  reads=[sq_r, c_r], writes=[pr])
        t1, t1r = t1_s.get()
        t2, t2r = t2_s.get()
        rstd_from_ss(pb[:, :n], pr, 128, n, 1.0 / D, t1, t1r, t2, t2r)
        for k in range(KC):
            tk.op("dve", lambda e, k=k: e.scalar_tensor_tensor(
                out=y[:, k, yoff:yoff + n], in0=y[:, k, yoff:yoff + n], scalar=nwcol(layer, wi, k),
                in1=t2[:, :n], op0=ALU.mult, op1=ALU.mult),
                reads=[t2r, c_r, y_r[k]], writes=[y_r[k]])
            tk.op("pool", lambda e, k=k: e.tensor_tensor(out=h[:, k, ts:ts + n], in0=h[:, k, ts:ts + n],
                                                         in1=y[:, k, yoff:yoff + n], op=ALU.add),
                  reads=[y_r[k]], writes=hreg_k(k, ts, n))

    ev_tog = [0]

    def evac(out_ap, in_ap, reads, writes):
        ev_tog[0] ^= 1
        if ev_tog[0]:
            tk.op("act", lambda e: e.copy(out=out_ap, in_=in_ap), reads=reads, writes=writes)
        else:
            tk.op("dve", lambda e: e.tensor_copy(out=out_ap, in_=in_ap), reads=reads, writes=writes)

    wch = [tk.chan() for _ in range(4)]

    def ffn_phase(layer, t0, t1):
        TH = t1 - t0
        tl = ftiles(t0, t1)
        ph = contextlib.ExitStack()
        act = sb(ph, "f_act", [128, FC, TH], BF16)
        act_r = [Reg("act%d" % c) for c in range(FC)]
        w_in_v = f_win[layer].rearrange("(k p) o -> p k o", p=128)
        w_out_v = f_wout[layer].rearrange("(k p) o -> p k o", p=128)
        pa = contextlib.ExitStack()
        u = sb(pa, "f_u", [128, KC, TH], BF16)
        u_r = Reg("f_u")
        sq_s = Scratch(pa, "f_sq", [128, KC, 512], BF16, 1)
        t1_s = Scratch(pa, "f_t1", [128, 512], F32, 2)
        t2_s = Scratch(pa, "f_t2", [128, 512], F32, 2)
        sg_s = Scratch(pa, "f_sg", [128, 512], BF16, 3)
        slab = [sb(pa, "f_ws%d" % i, [128, KC, 2, 512], BF16) for i in range(2)]
        slab_r = [Reg("f_ws%d" % i) for i in range(2)]
        groups = [(g * 4, min(4, FC - g * 4)) for g in range((FC + 3) // 4)]

        def load_slab(gi):
            c0, ncg = groups[gi]
            s = gi % 2
            for two in range(2):
                col = two * DFF + c0 * 128
                tk.dma("pool", wch[s], lambda e, two=two, col=col: e.dma_start(
                    out=slab[s][:, :, two, :ncg * 128], in_=w_in_v[:, :, col:col + ncg * 128]),
                    writes=[slab_r[s]], cont=(two > 0))

        load_slab(0)
        for (ts, n) in tl:
            rmsnorm_pre(layer, 2, ts, n, u, u_r, ts - t0, sq_s, t1_s, t2_s)
        for gi, (c0, ncg) in enumerate(groups):
            if gi + 1 < len(groups):
                load_slab(gi + 1)
            s = gi % 2
            for cc in range(ncg):
                c = c0 + cc
                for (ts, n) in tl:
                    o = ts - t0
                    pg, pgr = bank()
                    for k in range(KC):
                        tk.op("pe", lambda e, k=k: e.matmul(pg[:, :n], slab[s][:, k, 0, cc * 128:(cc + 1) * 128],
                                                            u[:, k, o:o + n], start=(k == 0), stop=(k == KC - 1)),
                              reads=[slab_r[s], u_r], writes=[pgr])
                    pu, pur = bank()
                    for k in range(KC):
                        tk.op("pe", lambda e, k=k: e.matmul(pu[:, :n], slab[s][:, k, 1, cc * 128:(cc + 1) * 128],
                                                            u[:, k, o:o + n], start=(k == 0), stop=(k == KC - 1)),
                              reads=[slab_r[s], u_r], writes=[pur])
                    sg, sgr = sg_s.get()
                    tk.op("act", lambda e: e.activation(out=sg[:, :n], in_=pg[:, :n], func=AF.Silu),
                          reads=[pgr], writes=[sgr])
                    tk.op("dve", lambda e: e.tensor_tensor(out=act[:, c, o:o + n], in0=sg[:, :n], in1=pu[:, :n],
                                                           op=ALU.mult),
                          reads=[sgr, pur], writes=[act_r[c]])
        tk.barrier()
        pa.close()
        pbk = contextlib.ExitStack()
        y = sb(pbk, "f_y", [128, KC, TH], F32)
        y_r = [[Reg("f_y%d_%d" % (i, k)) for k in range(KC)] for i in range(len(tl))]
        sq_s = Scratch(pbk, "f_sq2", [128, KC, 512], BF16, 1)
        t1_s = Scratch(pbk, "f_t1b", [128, 512], F32, 2)
        t2_s = Scratch(pbk, "f_t2b", [128, 512], F32, 2)
        oslab = [sb(pbk, "f_wo%d" % i, [128, FC, 256], BF16) for i in range(2)]
        oslab_r = [Reg("f_wo%d" % i) for i in range(2)]

        def load_oslab(gi):
            s = gi % 2
            for kk in range(0, FC, 11):
                tk.dma("pool", wch[2 + s], lambda e, kk=kk: e.dma_start(
                    out=oslab[s][:, kk:kk + 11, :], in_=w_out_v[:, kk:kk + 11, gi * 256:(gi + 1) * 256]),
                    writes=[oslab_r[s]], cont=(kk > 0))

        load_oslab(0)
        for gi in range(4):
            if gi + 1 < 4:
                load_oslab(gi + 1)
            s = gi % 2
            for oc in range(2):
                ochunk = gi * 2 + oc
                for ti, (ts, n) in enumerate(tl):
                    o = ts - t0
                    pb_, pbr = bank()
                    for c in range(FC):
                        tk.op("pe", lambda e, c=c: e.matmul(pb_[:, :n], oslab[s][:, c, oc * 128:(oc + 1) * 128],
                                                            act[:, c, o:o + n], start=(c == 0), stop=(c == FC - 1)),
                              reads=[oslab_r[s], act_r[c]], writes=[pbr])
                    evac(y[:, ochunk, o:o + n], pb_[:, :n], [pbr], [y_r[ti][ochunk]])
        for ti, (ts, n) in enumerate(tl):
            postnorm_add(layer, 3, y, y_r[ti], ts - t0, ts, n, sq_s, t1_s, t2_s)
        tk.barrier()
        pbk.close()
        ph.close()

    l0s = contextlib.ExitStack()
    ckv_b = sb(l0s, "ckv_b", [128, 2, T], BF16)
    kr_b = sb(l0s, "kr_b", [64, T], BF16)
    ckv_r = [Reg("ckv%d" % j) for j in range(8)]
    och = [tk.chan() for _ in range(4)]
    och_i = [0]

    def next_och():
        och_i[0] = (och_i[0] + 1) % len(och)
        return och[och_i[0]]

    def kvregs(ts, n):
        return [ckv_r[j] for j in range(ts // 512, (ts + n - 1) // 512 + 1)]

    def mla_phase(t0, t1):
        TH = t1 - t0
        tl = tiles(t0, t1)
        has_s = t1 > SEQ
        ph = contextlib.ExitStack()
        cq = sb(ph, "m_cq", [128, 3, TH], BF16)
        cq_r = Reg("m_cq")
        ropet = sb(ph, "m_rope", [64, 2, TH], F32)
        rope_r = Reg("m_rope")
        tk.dma("sp", ch_in, lambda e: e.dma_start(out=ropet[:], in_=rope_d[:, :, t0:t1]), writes=[rope_r])
        oT = sb(ph, "m_oT", [128, NH, TH], BF16)
        oT_r = [Reg("m_oT%d" % hh) for hh in range(NH)]
        p1 = contextlib.ExitStack()
        win = sb(p1, "m_win", [128, KC, 704], BF16)
        win_r = Reg("m_win")
        gq = sb(p1, "m_gq", [128, 3], F32)
        gkv = sb(p1, "m_gkv", [128, 2], F32)
        tk.dma("pool", wch[0], lambda e: e.dma_start(out=win[:], in_=m_win.rearrange("(k p) o -> p k o", p=128)),
               writes=[win_r])
        gv_r = Reg("m_gv")
        tk.dma("sp", ch_in, lambda e: e.dma_start(out=gq[:], in_=m_gq), writes=[gv_r])
        tk.dma("sp", ch_in, lambda e: e.dma_start(out=gkv[:], in_=m_gkv), writes=[gv_r])
        u_s = Scratch(p1, "m_u", [128, KC, 512], BF16, 2)
        sq_s = Scratch(p1, "m_sq", [128, KC, 512], BF16, 1)
        t1_s = Scratch(p1, "m_t1", [128, 512], F32, 2)
        t2_s = Scratch(p1, "m_t2", [128, 512], F32, 2)
        a_s = Scratch(p1, "m_a", [128, 6, 512], F32, 2)
        rowo_s = Scratch(p1, "m_rowo", [128, 3, 512], F32, 2, chan=True)
        for (ts, n) in tl:
            o = ts - t0
            u, u_r = u_s.get()
            rmsnorm_pre(0, 0, ts, n, u, u_r, 0, sq_s, t1_s, t2_s)
            a, a_r = a_s.get()
            for oc in range(6):
                m = 128 if oc < 5 else 64
                pb_, pbr = bank()
                for k in range(KC):
                    tk.op("pe", lambda e, k=k: e.matmul(pb_[:m, :n], win[:, k, oc * 128:oc * 128 + m], u[:, k, :n],
                                                        start=(k == 0), stop=(k == KC - 1)),
                          reads=[win_r, u_r], writes=[pbr])
                evac(a[:m, oc, :n], pb_[:m, :n], [pbr], [a_r])
            sq, sq_r = sq_s.get()
            tk.op("act", lambda e: e.activation(out=sq[:, 0:5, :n], in_=a[:, 0:5, :n], func=AF.Square),
                  reads=[a_r], writes=[sq_r])
            pq, pqr = bank()
            for k in range(3):
                tk.op("pe", lambda e, k=k: e.matmul(pq[:, :n], ONE_b, sq[:, k, :n], start=(k == 0), stop=(k == 2)),
                      reads=[sq_r, c_r], writes=[pqr])
            pk, pkr = bank()
            for k in range(2):
                tk.op("pe", lambda e, k=k: e.matmul(pk[:, :n], ONE_b, sq[:, 3 + k, :n], start=(k == 0), stop=(k == 1)),
                      reads=[sq_r, c_r], writes=[pkr])
            ta, tar = t1_s.get()
            rq, rqr = t2_s.get()
            rstd_from_ss(pq[:, :n], pqr, 128, n, 1.0 / QLORA, ta, tar, rq, rqr)
            tb, tbr = t1_s.get()
            rk, rkr = t2_s.get()
            rstd_from_ss(pk[:, :n], pkr, 128, n, 1.0 / KVLORA, tb, tbr, rk, rkr)
            for k in range(3):
                tk.op("dve", lambda e, k=k: e.scalar_tensor_tensor(
                    out=cq[:, k, o:o + n], in0=a[:, k, :n], scalar=gq[:, k:k + 1], in1=rq[:, :n],
                    op0=ALU.mult, op1=ALU.mult), reads=[a_r, rqr, gv_r], writes=[cq_r])
            rowo, rowo_r = rowo_s.get()
            for k in range(2):
                tk.op("dve", lambda e, k=k: e.scalar_tensor_tensor(
                    out=rowo[:, k, :n], in0=a[:, 3 + k, :n], scalar=gkv[:, k:k + 1], in1=rk[:, :n],
                    op0=ALU.mult, op1=ALU.mult), reads=[a_r, rkr, gv_r], writes=[rowo_r])
            tk.op("act", lambda e: e.copy(out=ckv_b[:, :, ts:ts + n], in_=rowo[:, 0:2, :n]),
                  reads=[rowo_r], writes=kvregs(ts, n))
            pr_, prr = bank()
            tk.op("pe", lambda e: e.matmul(pr_[:64, :n], ROT_f[:64, :], a[:64, 5, :n], start=True, stop=True),
                  reads=[a_r, c_r], writes=[prr])
            tk.op("dve", lambda e: e.tensor_tensor(out=rowo[:64, 2, :n], in0=a[:64, 5, :n],
                                                   in1=ropet[:, 0, o:o + n], op=ALU.mult),
                  reads=[a_r, rope_r], writes=[rowo_r])
            tk.op("dve", lambda e: e.tensor_tensor(out=a[:64, 5, :n], in0=pr_[:64, :n],
                                                   in1=ropet[:, 1, o:o + n], op=ALU.mult),
                  reads=[prr, rope_r], writes=[a_r])
            tk.op("dve", lambda e: e.tensor_tensor(out=rowo[:64, 2, :n], in0=rowo[:64, 2, :n],
                                                   in1=a[:64, 5, :n], op=ALU.add),
                  reads=[a_r], writes=[rowo_r])
            tk.op("act", lambda e: e.copy(out=kr_b[:, ts:ts + n], in_=rowo[:64, 2, :n]),
                  reads=[rowo_r], writes=kvregs(ts, n))
            oc_ = rowo_s.chan()
            tk.dma("sp", oc_, lambda e: e.dma_start(
                out=rowsT[0:256, ts:ts + n].rearrange("(k p) t -> p k t", p=128), in_=rowo[:, 0:2, :n]),
                reads=[rowo_r])
            tk.dma("sp", oc_, lambda e: e.dma_start(out=rowsT[256:320, ts:ts + n], in_=rowo[:64, 2, :n]),
                   reads=[rowo_r], cont=True)
        tk.barrier()
        p1.close()
        p2 = contextlib.ExitStack()
        wuq = sb(p2, "m_wuq", [128, 3, NH * 192], BF16)
        wuk = sb(p2, "m_wuk", [128, NH, 2, 128], BF16)
        wuv = sb(p2, "m_wuv", [128, NH, 2, 128], BF16)
        wukT = sb(p2, "m_wukT", [128, NH, 256], BF16)
        w2_r = Reg("m_w2")
        tk.dma("pool", wch[1], lambda e: e.dma_start(out=wuq[:], in_=m_wuq.rearrange("(k p) o -> p k o", p=128)),
               writes=[w2_r])
        tk.dma("pool", wch[1], lambda e: e.dma_start(out=wuk[:], in_=m_wuk.rearrange("h (k p) n -> p h k n", p=128)),
               writes=[w2_r], cont=True)
        tk.dma("pool", wch[1], lambda e: e.dma_start(out=wuv[:], in_=m_wuv.rearrange("h (k p) n -> p h k n", p=128)),
               writes=[w2_r], cont=True)
        tk.dma("pool", wch[1], lambda e: e.dma_start(out=wukT[:], in_=m_wukT.rearrange("h p r -> p h r")),
               writes=[w2_r], cont=True)
        NKB = t1 // 128 if not has_s else SEQ // 128
        if has_s:
            Qs = sb(p2, "m_Qs", [128, 3, 4, 32], BF16)
            Qs_r = Reg("m_Qs")
        p2a = contextlib.ExitStack()
        qn_s = Scratch(p2a, "m_qn", [128, TH], BF16, 2)
        qr_s = Scratch(p2a, "m_qr", [64, TH], BF16, 2)
        qx_s = Scratch(p2a, "m_qx", [64, 2, 512], F32, 2)
        kn_s = Scratch(p2a, "m_kn", [128, SEQ], BF16, 2)
        vp_s = Scratch(p2a, "m_vp", [128, SEQ // 128, 132], BF16, 2)
        pT_s = Scratch(p2a, "m_pT", [128, 512], BF16, 3)
        on_s = Scratch(p2a, "m_on", [128, 4, 128], BF16, 2)
        rs_s = Scratch(p2a, "m_rs", [128, 4], F32, 2)
        ptl = [x for x in tl if x[0] < SEQ]
        for hh in range(NH):
            qn, qn_r = qn_s.get()
            qr, qr_r = qr_s.get()
            for (ts, n) in tl:
                o = ts - t0
                pb_, pbr = bank()
                for k in range(3):
                    tk.op("pe", lambda e, k=k: e.matmul(pb_[:, :n], wuq[:, k, hh * 192:hh * 192 + 128], cq[:, k, o:o + n],
                                                        start=(k == 0), stop=(k == 2)),
                          reads=[w2_r, cq_r], writes=[pbr])
                evac(qn[:, o:o + n], pb_[:, :n], [pbr], [qn_r])
                p2_, p2r = bank()
                for k in range(3):
                    tk.op("pe", lambda e, k=k: e.matmul(p2_[:64, :n], wuq[:, k, hh * 192 + 128:hh * 192 + 192],
                                                        cq[:, k, o:o + n], start=(k == 0), stop=(k == 2)),
                          reads=[w2_r, cq_r], writes=[p2r])
                qx, qx_r = qx_s.get()
                tk.op("act", lambda e: e.copy(out=qx[:, 0, :n], in_=p2_[:64, :n]), reads=[p2r], writes=[qx_r])
                p3_, p3r = bank()
                tk.op("pe", lambda e: e.matmul(p3_[:64, :n], ROT_f[:64, :], qx[:, 0, :n], start=True, stop=True),
                      reads=[qx_r, c_r], writes=[p3r])
                tk.op("dve", lambda e: e.tensor_tensor(out=qx[:, 1, :n], in0=p3_[:64, :n], in1=ropet[:, 1, o:o + n],
                                                       op=ALU.mult), reads=[p3r, rope_r], writes=[qx_r])
                tk.op("dve", lambda e: e.tensor_tensor(out=qx[:, 0, :n], in0=qx[:, 0, :n], in1=ropet[:, 0, o:o + n],
                                                       op=ALU.mult), reads=[rope_r], writes=[qx_r])
                tk.op("dve", lambda e: e.tensor_tensor(out=qr[:, o:o + n], in0=qx[:, 0, :n], in1=qx[:, 1, :n],
                                                       op=ALU.add), reads=[qx_r], writes=[qr_r])
            kn, kn_r = kn_s.get()
            vp, vp_r = vp_s.get()
            nk = NKB * 128
            for ks in range(0, nk, 512):
                n = min(512, nk - ks)
                pb_, pbr = bank()
                for k in range(2):
                    tk.op("pe", lambda e, k=k: e.matmul(pb_[:, :n], wuk[:, hh, k, :], ckv_b[:, k, ks:ks + n],
                                                        start=(k == 0), stop=(k == 1)),
                          reads=[w2_r] + kvregs(ks, n), writes=[pbr])
                evac(kn[:, ks:ks + n], pb_[:, :n], [pbr], [kn_r])
            for kb4 in range(0, NKB, 4):
                nb = min(4, NKB - kb4)
                pb_, pbr = bank()
                for j in range(nb):
                    kb = kb4 + j
                    for k in range(2):
                        tk.op("pe", lambda e, k=k, j=j, kb=kb: e.matmul(
                            pb_[:, j * 128:(j + 1) * 128], ckv_b[:, k, kb * 128:(kb + 1) * 128], wuv[:, hh, k, :],
                            start=(k == 0), stop=(k == 1)),
                            reads=[w2_r] + kvregs(kb * 128, 128), writes=[pbr])
                evac(vp[:, kb4:kb4 + nb, 0:128], pb_[:, :nb * 128].rearrange("p (j v) -> p j v", v=128), [pbr], [vp_r])
            tk.op("dve", lambda e: e.memset(vp[:, :, 128:129], 1.0), writes=[vp_r])
            for (ts, n) in ptl:
                o = ts - t0
                nsub = n // 128
                kb_hi = (ts + n) // 128
                accs_l = reserve(nsub)
                accs = [(a_, r_) for (a_, r_, _) in accs_l]
                pend = None

                def emit_pv(pv):
                    kb, d, j0, pT, pT_r = pv
                    for si in range(max(d, 0), nsub):
                        ab, abr = accs[si]
                        last_kb = ts // 128 + si
                        c0 = si * 128 - j0
                        tk.op("pe", lambda e, ab=ab, c0=c0, kb=kb, last_kb=last_kb: e.matmul(
                            ab[:, 0:129], pT[:, c0:c0 + 128], vp[:, kb, 0:129], start=(kb == 0), stop=(kb == last_kb)),
                            reads=[pT_r, vp_r], writes=[abr])

                for kb in range(kb_hi):
                    d = kb - ts // 128
                    j0 = max(d, 0) * 128
                    ncol = n - j0
                    psc, pscr = bank()
                    tk.op("pe", lambda e: e.matmul(psc[:, :ncol], kn[:, kb * 128:(kb + 1) * 128],
                                                   qn[:, o + j0:o + n], start=True, stop=False),
                          reads=[kn_r, qn_r], writes=[pscr])
                    tk.op("pe", lambda e: e.matmul(psc[:, :ncol], kr_b[:, kb * 128:(kb + 1) * 128],
                                                   qr[:, o + j0:o + n], start=False, stop=True),
                          reads=kvregs(kb * 128, 128) + [qr_r], writes=[pscr])
                    if pend is not None:
                        emit_pv(pend)
                    pT, pT_r = pT_s.get()
                    tk.op("act", lambda e: e.activation(out=pT[:, :ncol], in_=psc[:, :ncol], func=AF.Exp,
                                                        scale=MLA_SCALE), reads=[pscr], writes=[pT_r])
                    if d >= 0:
                        tk.op("dve", lambda e: e.tensor_tensor(out=pT[:, 0:128], in0=pT[:, 0:128], in1=U_b,
                                                               op=ALU.mult), reads=[c_r], writes=[pT_r])
                    pend = (kb, d, j0, pT, pT_r)
                emit_pv(pend)
                rs, rs_r = rs_s.get()
                on, on_r = on_s.get()
                for si in range(nsub):
                    ab, abr = accs[si]
                    tk.op("dve", lambda e, ab=ab, si=si: e.reciprocal(out=rs[:, si:si + 1], in_=ab[:, 128:129]),
                          reads=[abr], writes=[rs_r])
                    tk.op("act", lambda e, ab=ab, si=si: e.activation(out=on[:, si, :], in_=ab[:, 0:128], func=AF.Copy,
                                                                      scale=rs[:, si:si + 1]),
                          reads=[abr, rs_r], writes=[on_r])
                ptb, ptr = bank()
                for si in range(nsub):
                    tk.op("pe", lambda e, si=si: e.matmul(ptb[:, si * 128:(si + 1) * 128], on[:, si, :], I_b,
                                                          start=True, stop=True),
                          reads=[on_r, c_r], writes=[ptr])
                evac(oT[:, hh, o:o + n], ptb[:, :n], [ptr], [oT_r[hh]])
                release(accs_l)
            if has_s:
                so = SEQ - t0
                pb_, pbr = bank()
                for k in range(2):
                    tk.op("pe", lambda e, k=k: e.matmul(pb_[:, k * 16:(k + 1) * 16], wukT[:, hh, k * 128:(k + 1) * 128],
                                                        qn[:, so:so + 16], start=True, stop=True),
                          reads=[w2_r, qn_r], writes=[pbr])
                evac(Qs[:, 0:2, :, hh * 4:(hh + 1) * 4],
                     pb_[:, 0:32].rearrange("p (k s t) -> p k s t", k=2, s=4), [pbr], [Qs_r])
                tk.op("act", lambda e: e.copy(out=Qs[:64, 2, :, hh * 4:(hh + 1) * 4],
                                              in_=qr[:, so:so + 16].rearrange("p (s t) -> p s t", s=4)),
                      reads=[qr_r], writes=[Qs_r])
        tk.barrier()
        p2a.close()
        if has_s:
            R = 8
            pt_sb = sb(p2, "m_pt", [128, 4], I32)
            idx_sb = sb(p2, "m_idx", [128, 16, 4], I32)
            pt_r = Reg("m_pt")
            tk.dma("sp", ch_in, lambda e: e.dma_start(out=pt_sb[:], in_=ptab), writes=[pt_r])
            for gi in range(16):
                tk.op("dve", lambda e, gi=gi: e.tensor_scalar(out=idx_sb[:, gi, :], in0=pt_sb[:, :], scalar1=16.0,
                                                               scalar2=float(gi), op0=ALU.mult, op1=ALU.add),
                      reads=[pt_r], writes=[pt_r])
            NCB = 3
            cbuf = [sb(p2, "m_cb%d" % i, [128, R, ROW], F32) for i in range(NCB)]
            cbuf_r = [Reg("m_cb%d" % i) for i in range(NCB)]
            cch = [tk.chan() for _ in range(NCB)]
            ctok_s = Scratch(p2, "m_ctok", [128, R, 324], BF16, 3)
            for t_ in ctok_s.t:
                tk.op("dve", lambda e, t_=t_: e.memset(t_[:, :, 320:321], 1.0), writes=ctok_s.r)
            pd_s = Scratch(p2, "m_pd", [128, 32], BF16, 2)
            cnew = sb(p2, "m_cnew", [4, 324], BF16)
            cnew_r = Reg("m_cnew")
            oln = sb(p2, "m_oln", [32, 256], BF16)
            oln_r = Reg("m_oln")
            olT = sb(p2, "m_olT", [128, 2, 32], BF16)
            olT_r = Reg("m_olT")
            rsd = sb(p2, "m_rsd", [32, 1], F32)
            ngath = 128 // R
            so = SEQ - t0

            def gather(s, gi):
                i = (s * ngath + gi) % NCB
                tk.dma("pool", cch[i], lambda e: e.indirect_dma_start(
                    out=cbuf[i][:].rearrange("p r d -> p (r d)"), out_offset=None,
                    in_=cache,
                    in_offset=bass.IndirectOffsetOnAxis(ap=idx_sb[:, gi, s:s + 1], axis=0)),
                    reads=[pt_r], writes=[cbuf_r[i]])

            gather(0, 0)
            gather(0, 1)
            cT4_s = Scratch(p2, "m_cT4", [128, 3, 512], BF16, 3)
            pd4_s = Scratch(p2, "m_pd4", [128, 128], BF16, 3)
            batches = [(s_, gi, r0) for s_ in range(4) for gi in range(ngath) for r0 in range(0, R, 4)]
            nbt = len(batches)
            bst = [dict() for _ in range(nbt)]
            grp = {}
            accs_d = {}

            def stA(b):
                s, gi, r0 = batches[b]
                g = s * ngath + gi
                if r0 == 0:
                    nxt = g + 2
                    if nxt < 4 * ngath:
                        gather(nxt // ngath, nxt % ngath)
                    i = g % NCB
                    ctok, ctok_r = ctok_s.get()
                    tk.op("dve", lambda e: e.tensor_copy(out=ctok[:, :, 0:320], in_=cbuf[i][:, :, :]),
                          reads=[cbuf_r[i]], writes=[ctok_r])
                    grp[g] = (ctok, ctok_r)
                ctok, ctok_r = grp[g]
                cT4, cT4_r = cT4_s.get()
                for k in range(3):
                    m = 128 if k < 2 else 64
                    ptp, ptpr = bank()
                    for j in range(4):
                        tk.op("pe", lambda e, k=k, m=m, j=j: e.matmul(
                            ptp[:m, j * 128:(j + 1) * 128], ctok[:, r0 + j, k * 128:k * 128 + m], I_b,
                            start=True, stop=True), reads=[ctok_r, c_r], writes=[ptpr])
                    if k == 1:
                        tk.op("dve", lambda e, k=k, m=m, ptp=ptp: e.tensor_copy(out=cT4[:m, k, :], in_=ptp[:m, :]),
                              reads=[ptpr], writes=[cT4_r])
                    else:
                        tk.op("act", lambda e, k=k, m=m, ptp=ptp: e.copy(out=cT4[:m, k, :], in_=ptp[:m, :]),
                              reads=[ptpr], writes=[cT4_r])
                bst[b]["cT4"] = (cT4, cT4_r)

            def stB(b):
                s, gi, r0 = batches[b]
                cT4, cT4_r = bst[b]["cT4"]
                psc, pscr = bank()
                for j in range(4):
                    for k in range(3):
                        m = 128 if k < 2 else 64
                        tk.op("pe", lambda e, k=k, m=m, j=j: e.matmul(
                            psc[:, j * 32:(j + 1) * 32], cT4[:m, k, j * 128:(j + 1) * 128], Qs[:m, k, s, :],
                            start=(k == 0), stop=(k == 2)), reads=[cT4_r, Qs_r], writes=[pscr])
                pd4, pd4_r = pd4_s.get()
                tk.op("act", lambda e: e.activation(out=pd4[:, :], in_=psc[:, 0:128], func=AF.Exp, scale=MLA_SCALE),
                      reads=[pscr], writes=[pd4_r])
                bst[b]["pd4"] = (pd4, pd4_r)

            def stC(b):
                s, gi, r0 = batches[b]
                g = s * ngath + gi
                ctok, ctok_r = grp[g]
                pd4, pd4_r = bst[b]["pd4"]
                if s not in accs_d:
                    accs_d[s] = reserve(1)
                (acc, acc_r, _), = acc_l = accs_d[s]
                for j in range(4):
                    first = (gi == 0 and r0 == 0 and j == 0)
                    tk.op("pe", lambda e, j=j, first=first: e.matmul(
                        acc[:32, 0:321], pd4[:, j * 32:(j + 1) * 32], ctok[:, r0 + j, 0:321],
                        start=first, stop=False), reads=[pd4_r, ctok_r], writes=[acc_r])
                if gi == ngath - 1 and r0 == R - 4:
                    tail(s, acc, acc_r, acc_l)

            def tail(s, acc, acc_r, acc_l):
                    tcol = SEQ + s * 4
                    ptp, ptpr = bank()
                    for k in range(2):
                        tk.op("pe", lambda e, k=k: e.matmul(ptp[:4, k * 128:(k + 1) * 128], ckv_b[:, k, tcol:tcol + 4], I_b,
                                                            start=True, stop=True),
                              reads=kvregs(tcol, 4) + [c_r], writes=[ptpr])
                    tk.op("dve", lambda e: e.memset(cnew[:, :], 0.0), writes=[cnew_r])
                    tk.op("act", lambda e: e.copy(out=cnew[:4, 0:256], in_=ptp[:4, 0:256]), reads=[ptpr], writes=[cnew_r])
                    tk.op("dve", lambda e: e.memset(cnew[:4, 320:321], 1.0), writes=[cnew_r])
                    psc, pscr = bank()
                    for k in range(3):
                        m = 128 if k < 2 else 64
                        src = ckv_b[:, k, tcol:tcol + 4] if k < 2 else kr_b[:, tcol:tcol + 4]
                        tk.op("pe", lambda e, k=k, m=m, src=src: e.matmul(psc[:4, 0:32], src, Qs[:m, k, s, :],
                                                                          start=(k == 0), stop=(k == 2)),
                              reads=kvregs(tcol, 4) + [Qs_r], writes=[pscr])
                    pd, pd_r = pd_s.get()
                    tk.op("act", lambda e: e.activation(out=pd[:4, :], in_=psc[:4, 0:32], func=AF.Exp, scale=MLA_SCALE),
                          reads=[pscr], writes=[pd_r])
                    tk.op("dve", lambda e: e.tensor_tensor(out=pd[:4, :], in0=pd[:4, :], in1=DM_b[:4, :], op=ALU.mult),
                          reads=[c_r], writes=[pd_r])
                    tk.op("pe", lambda e: e.matmul(acc[:32, 0:321], pd[:4, :], cnew[:4, 0:321], start=False, stop=True),
                          reads=[pd_r, cnew_r], writes=[acc_r])
                    tk.op("dve", lambda e: e.reciprocal(out=rsd[:, :], in_=acc[:32, 320:321]), reads=[acc_r], writes=[oln_r])
                    tk.op("act", lambda e: e.activation(out=oln[:, :], in_=acc[:32, 0:256], func=AF.Copy, scale=rsd[:, 0:1]),
                          reads=[acc_r, oln_r], writes=[oln_r])
                    release(acc_l)
                    ptp, ptpr = bank()
                    for k in range(2):
                        tk.op("pe", lambda e, k=k: e.matmul(ptp[:, k * 32:(k + 1) * 32], oln[:, k * 128:(k + 1) * 128],
                                                            I_b[:32, :32], start=True, stop=True),
                              reads=[oln_r, c_r], writes=[ptpr])
                    evac(olT[:, :, :], ptp[:, 0:64].rearrange("p (k q) -> p k q", k=2), [ptpr], [olT_r])
                    pov, povr = bank()
                    for hh in range(NH):
                        for k in range(2):
                            tk.op("pe", lambda e, k=k, hh=hh: e.matmul(pov[:, hh * 4:(hh + 1) * 4], wuv[:, hh, k, :],
                                                                       olT[:, k, hh * 4:(hh + 1) * 4],
                                                                       start=(k == 0), stop=(k == 1)),
                                  reads=[w2_r, olT_r], writes=[povr])
                    tk.op("act", lambda e: e.copy(out=oT[:, :, so + s * 4:so + s * 4 + 4],
                                                  in_=pov[:, 0:32].rearrange("p (h t) -> p h t", h=NH)),
                          reads=[povr], writes=oT_r)
            for b in range(nbt + 2):
                if b < nbt:
                    stA(b)
                if 0 <= b - 1 < nbt:
                    stB(b - 1)
                if 0 <= b - 2 < nbt:
                    stC(b - 2)
        tk.barrier()
        p2.close()
        p4 = contextlib.ExitStack()
        wo = sb(p4, "m_wo", [128, KC, D], BF16)
        wo_r = Reg("m_wo")
        for k0 in range(0, KC, 4):
            tk.dma("pool", wch[2], lambda e, k0=k0: e.dma_start(
                out=wo[:, k0:k0 + 4, :], in_=m_wo.rearrange("(k p) o -> p k o", p=128)[:, k0:k0 + 4, :]),
                writes=[wo_r], cont=(k0 > 0))
        y_s = Scratch(p4, "m_y", [128, KC, 512], F32, 2, nreg=KC)
        sq_s = Scratch(p4, "m_sq4", [128, KC, 512], BF16, 1)
        t1_s = Scratch(p4, "m_t14", [128, 512], F32, 2)
        t2_s = Scratch(p4, "m_t24", [128, 512], F32, 2)
        for (ts, n) in ftiles(t0, t1):
            o = ts - t0
            y, y_r = y_s.get()
            for oc in range(KC):
                pb_, pbr = bank()
                for k in range(NH):
                    tk.op("pe", lambda e, k=k: e.matmul(pb_[:, :n], wo[:, k, oc * 128:(oc + 1) * 128], oT[:, k, o:o + n],
                                                        start=(k == 0), stop=(k == NH - 1)),
                          reads=[wo_r, oT_r[k]], writes=[pbr])
                evac(y[:, oc, :n], pb_[:, :n], [pbr], [y_r[oc]])
            postnorm_add(0, 1, y, y_r, 0, ts, n, sq_s, t1_s, t2_s)
        tk.barrier()
        p4.close()
        ph.close()


    Sst_r = [Reg("Sst%d" % i) for i in range(NH)]
    craw_r = [Reg("craw%d" % i) for i in range(24)]
    gst = {}

    def gdn_init():
        gst["Sst"] = sb(es, "Sst", [128, NH, 128], F32)
        gst["craw"] = sb(es, "craw", [128, 24, 3], F32)
        tk.op("dve", lambda e: e.memset(gst["Sst"][:], 0.0), writes=Sst_r)
        tk.op("dve", lambda e: e.memset(gst["craw"][:], 0.0), writes=craw_r)
        XD_ = F32 if INV_F32 else BF16
        gst["gm"] = sb(es, "gm", [128, 1408], XD_)
        gst["gm_r"] = Reg("gm")
        gst["i4x"] = sb(es, "i4x", [128, 512], XD_)
        if INV_F32:
            tk.dma("sp", ch_in, lambda e: e.dma_start(out=gst["gm"][:], in_=gmask_d), writes=[gst["gm_r"]])
            tk.dma("sp", ch_in, lambda e: e.dma_start(out=gst["i4x"][:], in_=consts_d[:, 1024:1536]), writes=[gst["gm_r"]])
        else:
            tk.dma("pool", ch_cb, lambda e: e.dma_start(out=gst["gm"][:], in_=gmask_d), writes=[gst["gm_r"]])
            tk.dma("pool", ch_cb, lambda e: e.dma_start(out=gst["i4x"][:], in_=consts_d[:, 1024:1536]),
                   writes=[gst["gm_r"]], cont=True)

    def v3(ap2, C, nb):
        return ap2.rearrange("p (j c) -> p j c", c=128)[:, :nb, :C]

    def gdn_phase(t0, t1):
        Sst, craw = gst["Sst"], gst["craw"]
        gm, gm_r, i4x = gst["gm"], gst["gm_r"], gst["i4x"]
        XD = F32 if INV_F32 else BF16
        I_x = I_f if INV_F32 else I_b
        TH = t1 - t0
        tl = tiles(t0, t1)
        has_s = t1 > SEQ
        THp = min(t1, SEQ) - t0
        NB = THp // 128
        ph = contextlib.ExitStack()
        u = sb(ph, "g_u", [128, KC, TH], BF16)
        u_r = Reg("g_u")
        oT = sb(ph, "g_oT", [128, NH, TH], BF16)
        oT_r = [Reg("g_oT%d" % i) for i in range(NH)]
        p0 = contextlib.ExitStack()
        sq_s = Scratch(p0, "g_sq", [128, KC, 512], BF16, 1)
        t1_s = Scratch(p0, "g_t1", [128, 512], F32, 2)
        t2_s = Scratch(p0, "g_t2", [128, 512], F32, 2)
        for (ts, n) in ftiles(t0, t1):
            rmsnorm_pre(1, 0, ts, n, u, u_r, ts - t0, sq_s, t1_s, t2_s)
        tk.barrier()
        p0.close()
        p1 = contextlib.ExitStack()
        t1_s = Scratch(p1, "g_t1b", [128, 512], F32, 2)
        t2_s = Scratch(p1, "g_t2b", [128, 512], F32, 2)
        cw = sb(p1, "g_cw", [128, 24, 4], F32)
        gout = sb(p1, "g_gout", [128, 1], F32)
        abc = sb(p1, "g_abc", [128, 2, NH], F32)
        wba = sb(p1, "g_wba", [128, KC, 16], BF16)
        gp_r = Reg("g_par")
        tk.dma("sp", ch_in, lambda e: e.dma_start(out=cw[:], in_=d_cw), writes=[gp_r])
        tk.dma("sp", ch_in, lambda e: e.dma_start(out=gout[:], in_=d_gout), writes=[gp_r])
        tk.dma("sp", ch_in, lambda e: e.dma_start(out=abc[:, 0, :], in_=d_alog[0].partition_broadcast(128)), writes=[gp_r])
        tk.dma("sp", ch_in, lambda e: e.dma_start(out=abc[:, 1, :], in_=d_dtb[0].partition_broadcast(128)), writes=[gp_r])
        tk.op("act", lambda e: e.activation(out=abc[:, 0, :], in_=abc[:, 0, :], func=AF.Exp), reads=[gp_r], writes=[gp_r])
        wba_r = Reg("g_wba")
        tk.dma("pool", wch[2], lambda e: e.dma_start(
            out=wba[:], in_=d_win.rearrange("(k p) o -> p k o", p=128)[:, :, 4096:4112]), writes=[wba_r])
        if has_s:
            cvin = sb(p1, "g_cvin", [128, 24, 4, 3], F32)
            cvs = sb(p1, "g_cvs", [128, 24, 4, 3], F32)
            cvin_r = Reg("g_cvin")
            cvs_r = Reg("g_cvs")
            tk.dma("sp", ch_in, lambda e: e.dma_start(out=cvin[:], in_=cv_in.rearrange("(c p) (s j) -> p c s j", p=128, j=3)),
                   writes=[cvin_r])
        NBS = NB + (1 if has_s else 0)
        gtok = sb(p1, "g_gtok", [128, NBS, 8], F32)
        btok = sb(p1, "g_btok", [128, NBS, 8], F32)
        nbtok = sb(p1, "g_nbtok", [128, NBS, 8], F32)
        gt_r = Reg("g_gt")
        xt_ = sb(p1, "g_xt", [128, NBS, 8], F32)
        pba, pbar = bank()
        for b in range(NB):
            for k in range(KC):
                tk.op("pe", lambda e, k=k, b=b: e.matmul(pba[:, b * 16:(b + 1) * 16], u[:, k, b * 128:(b + 1) * 128],
                                                         wba[:, k, :], start=(k == 0), stop=(k == KC - 1)),
                      reads=[u_r, wba_r], writes=[pbar])
        if has_s:
            pbs, pbsr = bank()
            for s_ in range(4):
                for k in range(KC):
                    tk.op("pe", lambda e, k=k, s_=s_: e.matmul(pbs[:4, s_ * 16:(s_ + 1) * 16],
                                                               u[:, k, THp + 4 * s_:THp + 4 * s_ + 4], wba[:, k, :],
                                                               start=(k == 0), stop=(k == KC - 1)),
                          reads=[u_r, wba_r], writes=[pbsr])
        def ba_post(P_, src3, dstsl, preg):
            gt, bt, nbt, xt = dstsl
            nblk = src3.shape[1]
            tk.op("act", lambda e: e.activation(out=bt, in_=src3[:, :, 0:8], func=AF.Sigmoid), reads=[preg], writes=[gt_r])
            tk.op("dve", lambda e: e.tensor_scalar(out=nbt, in0=bt, scalar1=-1.0, scalar2=None, op0=ALU.mult),
                  reads=[gt_r], writes=[gt_r])
            for b in range(nblk):
                tk.op("dve", lambda e, b=b: e.tensor_tensor(out=xt[:, b, :], in0=src3[:, b, 8:16], in1=abc[:P_, 1, :],
                                                            op=ALU.add), reads=[preg, gp_r], writes=[gt_r])
            tk.op("act", lambda e: e.activation(out=xt, in_=xt, func=AF.Exp), reads=[gt_r], writes=[gt_r])
            tk.op("act", lambda e: e.activation(out=xt, in_=xt, func=AF.Ln, bias=epst[:P_, 1:2]), reads=[gt_r, c_r],
                  writes=[gt_r])
            for b in range(nblk):
                tk.op("dve", lambda e, b=b: e.scalar_tensor_tensor(out=gt[:, b, :], in0=xt[:, b, :], scalar=-1.0,
                                                                   in1=abc[:P_, 0, :], op0=ALU.mult, op1=ALU.mult),
                      reads=[gt_r, gp_r], writes=[gt_r])

        ba_post(128, pba[:, 0:NB * 16].rearrange("p (b x) -> p b x", x=16),
                (gtok[:, 0:NB, :], btok[:, 0:NB, :], nbtok[:, 0:NB, :], xt_[:, 0:NB, :]), pbar)
        if has_s:
            gts = sb(p1, "g_gts", [4, 4, 8], F32)
            bts = sb(p1, "g_bts", [4, 4, 8], F32)
            nbts = sb(p1, "g_nbts", [4, 4, 8], F32)
            xts = sb(p1, "g_xts", [4, 4, 8], F32)
            ba_post(4, pbs[:4, 0:64].rearrange("p (b x) -> p b x", x=16), (gts[:], bts[:], nbts[:], xts[:]), pbsr)
        wsl = [sb(p1, "g_wsl%d" % i, [128, KC, 4, 128], BF16) for i in range(2)]
        wsl_r = [Reg("g_wsl%d" % i) for i in range(2)]
        d_win_v = d_win.rearrange("(k p) o -> p k o", p=128)

        def load_wsl(hh):
            s_ = hh % 2
            for w_ in range(4):
                col = w_ * 1024 + hh * 128
                tk.dma("pool", wch[s_], lambda e, w_=w_, col=col: e.dma_start(
                    out=wsl[s_][:, :, w_, :], in_=d_win_v[:, :, col:col + 128]), writes=[wsl_r[s_]], cont=(w_ > 0))

        raw = sb(p1, "g_raw", [128, 3 + THp], F32)
        raw_r = Reg("g_raw")
        raws = sb(p1, "g_raws", [128, 4, 7], F32)
        raws_r = Reg("g_raws")
        acc = sb(p1, "g_acc", [128, TH], F32)
        acc_r = Reg("g_acc")
        sqb_s = Scratch(p1, "g_sqb", [128, 512], BF16, 2)
        qn_s = Scratch(p1, "g_qn", [128, TH], BF16, 2)
        kn_s = Scratch(p1, "g_kn", [128, TH], BF16, 2)
        vv_s = Scratch(p1, "g_vv", [128, TH], BF16, 2)
        zs_s = Scratch(p1, "g_zs", [128, TH], BF16, 2)
        if has_s:
            Ssm_s = Scratch(p1, "g_Ssm", [128, 4, 128], F32, 2, nreg=4, chan=True)
            ssl_ch = [tk.chan() for _ in range(2)]

        class BB:
            pass

        bbs = []
        for i in range(1):
            b_ = BB()
            b_.R = sb(p1, "b_R%d" % i, [128, 4, 128], F32)
            b_.gB = sb(p1, "b_gB%d" % i, [128, 4, 128], F32)
            b_.sc = sb(p1, "b_sc%d" % i, [128, 4, 4], F32)
            b_.gl = sb(p1, "b_gl%d" % i, [128, 4, 4], F32)
            for nm in ("E1", "E2", "egbc", "qkt", "kdec", "kbg", "vb", "qd", "utok", "wtok", "nWk"):
                setattr(b_, nm, sb(p1, "b_%s%d" % (nm, i), [128, 4, 128], BF16))
            for nm in ("Lp", "LT", "X", "XT", "W1", "W2", "Ao", "AoT"):
                setattr(b_, nm, sb(p1, "b_%s%d" % (nm, i), [128, 4, 128], XD))
            b_.r = {nm: Reg("b_%s%d" % (nm, i)) for nm in
                    ("R", "gB", "E1", "E2", "eg", "sc", "utok", "wtok", "nWk", "Lp", "LT", "X", "XT", "W1", "W2", "Ao", "AoT",
                     "qkt", "kdec", "kbg", "vb", "qd")}
            bbs.append(b_)
        bb_i = [0]
        Sb_s = Scratch(p1, "g_Sb", [128, 128], BF16, 2)
        vn_s = Scratch(p1, "g_vn", [128, 128], BF16, 2)
        on_s = Scratch(p1, "g_on", [128, 128], BF16, 2)
        jk_s = Scratch(p1, "g_jk", [128, 128], BF16, 2)
        ss_s = Scratch(p1, "g_ss", [128, 4], F32, 2)
        og_s = Scratch(p1, "g_og", [128, 128], BF16, 2)

        def gdn_batch(hh, C, nb, o0, gmat, bcols, nbcols, S_f, S_regs, qn, qn_r, kn, kn_r, vv, vv_r, zs, zs_r,
                      carry):
            b_ = bbs[0]
            r = b_.r
            NC_ = nb * C
            gcols = [gmat[:, j:j + 1] for j in range(nb)]

            def flat(t, P_=C):
                return t[:].rearrange("p j c -> p (j c)")[:P_, 0:NC_]

            def cv(t, P_=C):
                return flat(t, P_).rearrange("p (j c) -> p j c", c=C)

            def cvp(ps, P_=C):
                return ps[:P_, 0:NC_].rearrange("p (j c) -> p j c", c=C)

            for j in range(nb):
                tk.op("dve", lambda e, j=j: e.tensor_scalar(out=cv(b_.R)[:, j, :], in0=MS_f[:C, :C], scalar1=gcols[j],
                                                            scalar2=None, op0=ALU.mult), reads=[c_r, gt_r], writes=[r["R"]])
                tk.op("dve", lambda e, j=j: e.tensor_scalar(out=cv(b_.gB)[:, j, :], in0=U_f[:C, :C], scalar1=gcols[j],
                                                            scalar2=None, op0=ALU.mult), reads=[c_r, gt_r], writes=[r["gB"]])
            k1, k1r = bank()
            k2, k2r = bank()
            k3, k3r = bank()
            k4, k4r = bank()
            tk.op("pe", lambda e: e.matmul(k1[:C, 0:NC_], U_f[:C, :C], flat(b_.R), start=True, stop=True),
                  reads=[c_r, r["R"]], writes=[k1r])
            tk.op("pe", lambda e: e.matmul(k2[:C, 0:NC_], MS_f[:C, :C], flat(b_.gB), start=True, stop=True),
                  reads=[c_r, r["gB"]], writes=[k2r])
            tk.op("pe", lambda e: e.matmul(k3[:, 0:NC_], ONE_f[:C, :], flat(b_.gB), start=True, stop=True),
                  reads=[c_r, r["gB"]], writes=[k3r])
            tk.op("pe", lambda e: e.matmul(k4[:C, 0:nb], U_f[:C, :C], gmat, start=True, stop=True),
                  reads=[c_r, gt_r], writes=[k4r])
            tk.op("pe", lambda e: e.matmul(k4[:, 16:16 + nb], ONE_f[:C, :], gmat, start=True, stop=True),
                  reads=[c_r, gt_r], writes=[k4r])
            yield
            tk.op("act", lambda e: e.activation(out=cv(b_.E1), in_=cvp(k1), func=AF.Exp), reads=[k1r], writes=[r["E1"]])
            tk.op("act", lambda e: e.activation(out=cv(b_.E2), in_=cvp(k2), func=AF.Exp), reads=[k2r], writes=[r["E2"]])
            tk.op("act", lambda e: e.activation(out=cv(b_.egbc, 128), in_=cvp(k3, 128), func=AF.Exp),
                  reads=[k3r], writes=[r["eg"]])
            tk.op("act", lambda e: e.activation(out=b_.sc[:C, :nb, 0:1], in_=k4[:C, 0:nb].unsqueeze(2), func=AF.Exp),
                  reads=[k4r], writes=[r["sc"]])
            tk.op("act", lambda e: e.activation(out=b_.gl[:, :nb, 0:1], in_=k4[:, 16:16 + nb].unsqueeze(2), func=AF.Exp),
                  reads=[k4r], writes=[r["sc"]])
            tk.op("dve", lambda e: e.tensor_copy(out=b_.sc[:C, :nb, 1:2], in_=cv(b_.E2)[:, :, C - 1:C]),
                  reads=[r["E2"]], writes=[r["sc"]])
            tk.op("pool", lambda e: e.tensor_tensor(out=cv(b_.E1), in0=cv(b_.E1),
                                                    in1=cb[:C, 256:256 + C].unsqueeze(1).to_broadcast([C, nb, C]),
                                                    op=ALU.mult), reads=[c_r], writes=[r["E1"]])
            tk.op("pool", lambda e: e.tensor_tensor(out=cv(b_.E2), in0=cv(b_.E2),
                                                    in1=cb[:C, 128:128 + C].unsqueeze(1).to_broadcast([C, nb, C]),
                                                    op=ALU.mult), reads=[c_r, r["sc"]], writes=[r["E2"]])
            for j in range(nb):
                tk.op("dve", lambda e, j=j: e.tensor_tensor(out=b_.sc[:C, j, 2:3], in0=b_.sc[:C, j, 0:1], in1=bcols[j],
                                                            op=ALU.mult), reads=[gt_r], writes=[r["sc"]])
            yield
            k5, k5r = bank()
            k6, k6r = bank()
            k7, k7r = bank()
            k8, k8r = bank()
            for j in range(nb):
                cs = slice(j * 128, j * 128 + C)
                ts_ = slice(o0 + j * C, o0 + (j + 1) * C)
                tk.op("pe", lambda e, cs=cs, ts_=ts_: e.matmul(k5[:C, cs], kn[:, ts_], kn[:, ts_], start=True, stop=True),
                      reads=[kn_r], writes=[k5r])
                tk.op("pe", lambda e, cs=cs, ts_=ts_: e.matmul(k6[:C, cs], kn[:, ts_], qn[:, ts_], start=True, stop=True),
                      reads=[kn_r, qn_r], writes=[k6r])
                tk.op("pe", lambda e, j=j, ts_=ts_: e.matmul(k7[:C, j * 128:(j + 1) * 128], kn[:, ts_], I_b,
                                                             start=True, stop=True), reads=[kn_r, c_r], writes=[k7r])
                tk.op("pe", lambda e, j=j, ts_=ts_: e.matmul(k8[:C, j * 128:(j + 1) * 128], vv[:, ts_], I_b,
                                                             start=True, stop=True), reads=[vv_r, c_r], writes=[k8r])
            yield
            for j in range(nb):
                cs = slice(j * 128, j * 128 + C)
                tk.op("dve", lambda e, j=j, cs=cs: e.scalar_tensor_tensor(
                    out=b_.Lp[:C, j, :C], in0=k5[:C, cs], scalar=bcols[j], in1=cv(b_.E1)[:, j, :],
                    op0=ALU.mult, op1=ALU.mult), reads=[k5r, r["E1"], gt_r], writes=[r["Lp"]])
                tk.op("act", lambda e, j=j: e.activation(out=b_.kdec[:C, j, :], in_=k7[:C, j * 128:(j + 1) * 128],
                                                         func=AF.Copy, scale=b_.sc[:C, j, 1:2]),
                      reads=[k7r, r["sc"]], writes=[r["kdec"]])
                tk.op("dve", lambda e, j=j: e.tensor_scalar(out=b_.kbg[:C, j, :], in0=k7[:C, j * 128:(j + 1) * 128],
                                                            scalar1=b_.sc[:C, j, 2:3], scalar2=None, op0=ALU.mult),
                      reads=[k7r, r["sc"]], writes=[r["kbg"]])
                tk.op("act", lambda e, j=j: e.activation(out=b_.vb[:C, j, :], in_=k8[:C, j * 128:(j + 1) * 128],
                                                         func=AF.Copy, scale=bcols[j]),
                      reads=[k8r, gt_r], writes=[r["vb"]])
            tk.op("dve", lambda e: e.tensor_tensor(out=b_.qkt[:C, :nb, :C], in0=v3(k6[:C, :], C, nb),
                                                   in1=cv(b_.E2), op=ALU.mult),
                  reads=[k6r, r["E2"]], writes=[r["qkt"]])
            tk.op("pool", lambda e: e.tensor_tensor(
                out=b_.qd[:, :nb, :C], in0=qn[:, o0:o0 + nb * C].rearrange("p (j c) -> p j c", c=C),
                in1=cv(b_.egbc, 128), op=ALU.mult), reads=[qn_r, r["eg"]], writes=[r["qd"]])
            yield
            def bc(off):
                return gm[:C, off:off + C].unsqueeze(1).to_broadcast([C, nb, C])

            def mm4(lhs, rhs, lr, rr, PO=C, wl=C, wr=C):
                kx, kxr = bank()
                for j in range(nb):
                    tk.op("pe", lambda e, j=j: e.matmul(kx[:PO, j * 128:j * 128 + wr], lhs[:C, j, :wl],
                                                        rhs[:C, j, :wr], start=True, stop=True),
                          reads=[lr, rr], writes=[kxr])
                return kx, kxr

            kt, ktr = bank()
            for j in range(nb):
                cs = slice(j * 128, j * 128 + C)
                tk.op("pe", lambda e, j=j, cs=cs: e.matmul(kt[:C, cs], b_.Lp[:C, j, :C], I_x[:C, :C], start=True, stop=True),
                      reads=[r["Lp"], c_r], writes=[ktr])
            tk.op("act", lambda e: e.copy(out=b_.LT[:C, :nb, :C], in_=v3(kt[:C, :], C, nb)), reads=[ktr], writes=[r["LT"]])
            tk.op("dve", lambda e: e.scalar_tensor_tensor(out=b_.Ao[:C, :nb, :C], in0=b_.Lp[:C, :nb, :C], scalar=-1.0,
                                                          in1=bc(0), op0=ALU.mult, op1=ALU.mult),
                  reads=[r["Lp"], gm_r], writes=[r["Ao"]])
            tk.op("dve", lambda e: e.scalar_tensor_tensor(out=b_.AoT[:C, :nb, :C], in0=b_.LT[:C, :nb, :C], scalar=-1.0,
                                                          in1=bc(0), op0=ALU.mult, op1=ALU.mult),
                  reads=[r["LT"], gm_r], writes=[r["AoT"]])
            tk.op("pool", lambda e: e.tensor_tensor(out=b_.X[:C, :nb, :C], in0=b_.Ao[:C, :nb, :C],
                                                    in1=v3(i4x[:C, :], C, nb), op=ALU.add),
                  reads=[r["Ao"], gm_r], writes=[r["X"]])
            tk.op("pool", lambda e: e.tensor_tensor(out=b_.XT[:C, :nb, :C], in0=b_.AoT[:C, :nb, :C],
                                                    in1=v3(i4x[:C, :], C, nb), op=ALU.add),
                  reads=[r["AoT"], gm_r], writes=[r["XT"]])
            kx, kxr = mm4(b_.AoT, b_.Ao, r["AoT"], r["Ao"])
            tk.op("act", lambda e: e.copy(out=b_.W1[:C, :nb, :C], in_=v3(kx[:C, :], C, nb)), reads=[kxr], writes=[r["W1"]])
            yield
            kxa, kxar = mm4(b_.XT, b_.W1, r["XT"], r["W1"])
            kxb, kxbr = mm4(b_.W1, b_.XT, r["W1"], r["XT"])
            tk.op("dve", lambda e: e.tensor_tensor(out=b_.X[:C, :nb, :C], in0=v3(kxa[:C, :], C, nb),
                                                   in1=b_.X[:C, :nb, :C], op=ALU.add),
                  reads=[kxar, r["X"]], writes=[r["X"]])
            tk.op("dve", lambda e: e.tensor_tensor(out=b_.XT[:C, :nb, :C], in0=v3(kxb[:C, :], C, nb),
                                                   in1=b_.XT[:C, :nb, :C], op=ALU.add),
                  reads=[kxbr, r["XT"]], writes=[r["XT"]])
            yield
            levels = [b for b in (4, 8, 16, 32, 64) if 2 * b <= C]

            def emit_masks(li):
                off = 128 + li * 256
                tk.op("pool", lambda e: e.tensor_tensor(out=b_.Ao[:C, :nb, :C], in0=b_.Lp[:C, :nb, :C],
                                                        in1=bc(off), op=ALU.mult),
                      reads=[r["Lp"], gm_r], writes=[r["Ao"]])
                tk.op("pool", lambda e: e.tensor_tensor(out=b_.AoT[:C, :nb, :C], in0=b_.LT[:C, :nb, :C],
                                                        in1=bc(off + 128), op=ALU.mult),
                      reads=[r["LT"], gm_r], writes=[r["AoT"]])

            if levels:
                emit_masks(0)
            for li, b in enumerate(levels):
                last = (li == len(levels) - 1)
                kx2, kx2r = mm4(b_.Ao, b_.XT, r["Ao"], r["XT"])
                tk.op("act", lambda e, kx2=kx2: e.copy(out=b_.W2[:C, :nb, :C], in_=v3(kx2[:C, :], C, nb)),
                      reads=[kx2r], writes=[r["W2"]])
                if not last:
                    kx1, kx1r = mm4(b_.AoT, b_.X, r["AoT"], r["X"])
                    tk.op("dve", lambda e, kx1=kx1: e.tensor_copy(out=b_.W1[:C, :nb, :C], in_=v3(kx1[:C, :], C, nb)),
                          reads=[kx1r], writes=[r["W1"]])
                    emit_masks(li + 1)
                yield
                kx3, kx3r = mm4(b_.X, b_.W2, r["X"], r["W2"])
                if not last:
                    kx4, kx4r = mm4(b_.XT, b_.W1, r["XT"], r["W1"])
                tk.op("dve", lambda e, kx3=kx3: e.tensor_tensor(
                    out=b_.XT[:C, :nb, :C], in0=b_.XT[:C, :nb, :C], in1=v3(kx3[:C, :], C, nb), op=ALU.subtract),
                    reads=[kx3r, r["XT"]], writes=[r["XT"]])
                if not last:
                    tk.op("dve", lambda e, kx4=kx4: e.tensor_tensor(
                        out=b_.X[:C, :nb, :C], in0=b_.X[:C, :nb, :C], in1=v3(kx4[:C, :], C, nb), op=ALU.subtract),
                        reads=[kx4r, r["X"]], writes=[r["X"]])
                yield
            TTc, rTT = b_.XT, r["XT"]
            ku, kur = bank()
            kw, kwr = bank()
            for j in range(nb):
                tk.op("pe", lambda e, j=j: e.matmul(ku[:C, j * 128:(j + 1) * 128], TTc[:C, j, :C], b_.vb[:C, j, :],
                                                    start=True, stop=True), reads=[rTT, r["vb"]], writes=[kur])
                tk.op("pe", lambda e, j=j: e.matmul(kw[:C, j * 128:(j + 1) * 128], TTc[:C, j, :C], b_.kbg[:C, j, :],
                                                    start=True, stop=True), reads=[rTT, r["kbg"]], writes=[kwr])
            tk.op("act", lambda e: e.copy(out=b_.utok[:C, :nb, :], in_=ku[:C, 0:nb * 128].rearrange("p (j c) -> p j c", c=128)),
                  reads=[kur], writes=[r["utok"]])
            tk.op("dve", lambda e: e.tensor_copy(out=b_.wtok[:C, :nb, :], in_=kw[:C, 0:nb * 128].rearrange("p (j c) -> p j c", c=128)),
                  reads=[kwr], writes=[r["wtok"]])
            yield
            kk, kkr = bank()
            kq, kqr = bank()
            for j in range(nb):
                tk.op("pe", lambda e, j=j: e.matmul(kk[:, j * 128:(j + 1) * 128], b_.wtok[:C, j, :], b_.kdec[:C, j, :],
                                                    start=True, stop=True), reads=[r["wtok"], r["kdec"]], writes=[kkr])
                tk.op("pe", lambda e, j=j: e.matmul(kq[:, j * 128:j * 128 + C], b_.wtok[:C, j, :], b_.qkt[:C, j, :C],
                                                    start=True, stop=True), reads=[r["wtok"], r["qkt"]], writes=[kqr])
            tk.op("act", lambda e: e.activation(out=b_.nWk[:, :nb, :], in_=kk[:, 0:nb * 128].rearrange("p (j c) -> p j c", c=128),
                                                func=AF.Copy, scale=-1.0), reads=[kkr], writes=[r["nWk"]])
            tk.op("dve", lambda e: e.tensor_tensor(out=b_.qd[:, :nb, :C], in0=b_.qd[:, :nb, :C], in1=v3(kq[:, :], C, nb),
                                                   op=ALU.subtract), reads=[kqr, r["qd"]], writes=[r["qd"]])
            yield
            Sb, Sb_r = None, None
            for j in range(nb):
                Sf, Sreg = S_f[j], S_regs[j]
                if Sb is None or not carry:
                    Sb, Sb_r = Sb_s.get()
                    tk.op("dve", lambda e, Sb=Sb, Sf=Sf: e.tensor_copy(out=Sb[:, :], in_=Sf), reads=[Sreg], writes=[Sb_r])
                cs = slice(o0 + j * C, o0 + (j + 1) * C)
                ks_, ksr = bank()
                tk.op("pe", lambda e, j=j: e.matmul(ks_[:, 0:128], b_.kdec[:C, j, :], b_.utok[:C, j, :], start=True, stop=False),
                      reads=[r["kdec"], r["utok"]], writes=[ksr])
                tk.op("pe", lambda e, j=j, Sb=Sb: e.matmul(ks_[:, 0:128], b_.nWk[:, j, :], Sb[:, :], start=False, stop=True),
                      reads=[r["nWk"], Sb_r], writes=[ksr])
                ko, kor = bank()
                tk.op("pe", lambda e, j=j: e.matmul(ko[:C, 0:128], b_.qkt[:C, j, :C], b_.utok[:C, j, :], start=True, stop=False),
                      reads=[r["qkt"], r["utok"]], writes=[kor])
                tk.op("pe", lambda e, j=j, Sb=Sb: e.matmul(ko[:C, 0:128], b_.qd[:, j, :C], Sb[:, :], start=False, stop=True),
                      reads=[r["qd"], Sb_r], writes=[kor])
                tk.op("dve", lambda e, j=j, Sf=Sf: e.scalar_tensor_tensor(out=Sf, in0=Sf, scalar=b_.gl[:, j, 0:1],
                                                                          in1=ks_[:, 0:128], op0=ALU.mult, op1=ALU.add),
                      reads=[ksr, r["sc"], Sreg], writes=[Sreg])
                if carry and j + 1 < nb:
                    Sb, Sb_r = Sb_s.get()
                    tk.op("dve", lambda e, Sb=Sb, Sf=Sf: e.tensor_copy(out=Sb[:, :], in_=Sf), reads=[Sreg], writes=[Sb_r])
                jk, jk_r = jk_s.get()
                ss, ss_r = ss_s.get()
                tk.op("act", lambda e, jk=jk, ss=ss: e.activation(out=jk[:C, :], in_=ko[:C, 0:128], func=AF.Square,
                                                                  accum_out=ss[:C, 0:1]), reads=[kor], writes=[jk_r, ss_r])
                tk.op("act", lambda e, ss=ss: e.activation(out=ss[:C, 1:2], in_=ss[:C, 0:1], func=AF.Ln,
                                                           bias=epst[:C, 0:1], scale=1.0 / 128), reads=[c_r], writes=[ss_r])
                tk.op("act", lambda e, ss=ss: e.activation(out=ss[:C, 2:3], in_=ss[:C, 1:2], func=AF.Exp, scale=-0.5),
                      writes=[ss_r])
                on, on_r = on_s.get()
                tk.op("act", lambda e, on=on, ss=ss: e.activation(out=on[:C, :], in_=ko[:C, 0:128], func=AF.Copy,
                                                                  scale=ss[:C, 2:3]), reads=[kor, ss_r], writes=[on_r])
                kp, kpr = bank()
                tk.op("pe", lambda e, on=on: e.matmul(kp[:, 0:C], on[:C, :], I_b[:C, :C], start=True, stop=True),
                      reads=[on_r, c_r], writes=[kpr])
                og, og_r = og_s.get()
                tk.op("act", lambda e, og=og: e.activation(out=og[:, 0:C], in_=kp[:, 0:C], func=AF.Copy, scale=gout[:, 0:1]),
                      reads=[kpr, gp_r], writes=[og_r])
                tk.op("pool", lambda e, og=og, cs=cs: e.tensor_tensor(out=oT[:, hh, cs], in0=og[:, 0:C], in1=zs[:, cs],
                                                                     op=ALU.mult), reads=[og_r, zs_r], writes=[oT_r[hh]])
                yield

        heads = {}

        def bulk(hh):
            if hh + 1 < NH:
                load_wsl(hh + 1)
            ws_, ws_r = wsl[hh % 2], wsl_r[hh % 2]
            outs = {}
            for w_ in range(3):
                cidx = w_ * 8 + hh
                tk.op("dve", lambda e: e.tensor_copy(out=raw[:, 0:3], in_=craw[:, cidx, :]),
                      reads=[craw_r[cidx]], writes=[raw_r])
                for (ts, n) in tl:
                    o = ts - t0
                    pb_, pbr = bank()
                    for k in range(KC):
                        tk.op("pe", lambda e, k=k: e.matmul(pb_[:, :n], ws_[:, k, w_, :], u[:, k, o:o + n],
                                                            start=(k == 0), stop=(k == KC - 1)),
                              reads=[ws_r, u_r], writes=[pbr])
                    if ts < SEQ:
                        evac(raw[:, 3 + o:3 + o + n], pb_[:, :n], [pbr], [raw_r])
                    else:
                        tk.op("act", lambda e: e.copy(out=raws[:, :, 3:7], in_=pb_[:, 0:16].rearrange("p (s t) -> p s t", s=4)),
                              reads=[pbr], writes=[raws_r])
                        tk.op("dve", lambda e: e.tensor_copy(out=raws[:, :, 0:3], in_=cvin[:, cidx, :, :]),
                              reads=[cvin_r], writes=[raws_r])
                        tk.op("dve", lambda e: e.tensor_copy(out=cvs[:, cidx, :, :], in_=raws[:, :, 4:7]),
                              reads=[raws_r], writes=[cvs_r])
                    yield
                tk.op("dve", lambda e: e.tensor_copy(out=craw[:, cidx, :], in_=raw[:, THp:THp + 3]),
                      reads=[raw_r], writes=[craw_r[cidx]])
                tk.op("dve", lambda e: e.tensor_scalar(out=acc[:, 0:THp], in0=raw[:, 0:THp], scalar1=cw[:, cidx, 0:1],
                                                       scalar2=None, op0=ALU.mult), reads=[raw_r, gp_r], writes=[acc_r])
                for j in range(1, 4):
                    tk.op("dve", lambda e, j=j: e.scalar_tensor_tensor(
                        out=acc[:, 0:THp], in0=raw[:, j:j + THp], scalar=cw[:, cidx, j:j + 1], in1=acc[:, 0:THp],
                        op0=ALU.mult, op1=ALU.add), reads=[raw_r, gp_r], writes=[acc_r])
                if has_s:
                    accs = acc[:, THp:THp + 16].rearrange("p (s t) -> p s t", s=4)
                    tk.op("dve", lambda e: e.tensor_scalar(out=accs, in0=raws[:, :, 0:4], scalar1=cw[:, cidx, 0:1],
                                                           scalar2=None, op0=ALU.mult), reads=[raws_r, gp_r], writes=[acc_r])
                    for j in range(1, 4):
                        tk.op("dve", lambda e, j=j: e.scalar_tensor_tensor(
                            out=accs, in0=raws[:, :, j:j + 4], scalar=cw[:, cidx, j:j + 1], in1=accs,
                            op0=ALU.mult, op1=ALU.add), reads=[raws_r, gp_r], writes=[acc_r])
                yield
                if w_ == 2:
                    vv, vv_r = vv_s.get()
                    tk.op("act", lambda e: e.activation(out=vv[:, :], in_=acc[:, :], func=AF.Silu), reads=[acc_r], writes=[vv_r])
                    outs[2] = (vv, vv_r)
                else:
                    tk.op("act", lambda e: e.activation(out=acc[:, :], in_=acc[:, :], func=AF.Silu), reads=[acc_r], writes=[acc_r])
                    dst, dst_r = (qn_s if w_ == 0 else kn_s).get()
                    outs[w_] = (dst, dst_r)
                    for (ts, n) in tl:
                        o = ts - t0
                        sqb, sqb_r = sqb_s.get()
                        tk.op("act", lambda e: e.activation(out=sqb[:, :n], in_=acc[:, o:o + n], func=AF.Square),
                              reads=[acc_r], writes=[sqb_r])
                        pb_, pbr = bank()
                        tk.op("pe", lambda e: e.matmul(pb_[:, :n], ONE_b, sqb[:, :n], start=True, stop=True),
                              reads=[sqb_r, c_r], writes=[pbr])
                        ta, tar = t1_s.get()
                        tb, tbr = t2_s.get()
                        rstd_from_ss(pb_[:, :n], pbr, 128, n, 1.0, ta, tar, tb, tbr)
                        if w_ == 0:
                            tk.op("dve", lambda e: e.scalar_tensor_tensor(
                                out=dst[:, o:o + n], in0=acc[:, o:o + n], scalar=128.0 ** -0.5, in1=tb[:, :n],
                                op0=ALU.mult, op1=ALU.mult), reads=[acc_r, tbr], writes=[dst_r])
                        else:
                            tk.op("dve", lambda e: e.tensor_tensor(out=dst[:, o:o + n], in0=acc[:, o:o + n], in1=tb[:, :n],
                                                                   op=ALU.mult), reads=[acc_r, tbr], writes=[dst_r])
                        yield
            zs, zs_r = zs_s.get()
            for (ts, n) in tl:
                o = ts - t0
                pb_, pbr = bank()
                for k in range(KC):
                    tk.op("pe", lambda e, k=k: e.matmul(pb_[:, :n], ws_[:, k, 3, :], u[:, k, o:o + n],
                                                        start=(k == 0), stop=(k == KC - 1)),
                          reads=[ws_r, u_r], writes=[pbr])
                tk.op("act", lambda e: e.activation(out=zs[:, o:o + n], in_=pb_[:, :n], func=AF.Silu),
                      reads=[pbr], writes=[zs_r])
                yield
            heads[hh] = (outs[0], outs[1], outs[2], (zs, zs_r))
            yield

        def chain(hh):
            (qn, qn_r), (kn, kn_r), (vv, vv_r), (zs, zs_r) = heads[hh]
            for b0 in range(0, NB if GSK != 2 else 0, 4):
                nb = min(4, NB - b0)
                yield from gdn_batch(hh, 128, nb, b0 * 128,
                          gtok[:, b0:b0 + nb, hh],
                          [btok[:, b0 + j, hh:hh + 1] for j in range(nb)],
                          [nbtok[:, b0 + j, hh:hh + 1] for j in range(nb)],
                                     [Sst[:, hh, :]] * nb, [Sst_r[hh]] * nb, qn, qn_r, kn, kn_r, vv, vv_r, zs, zs_r, True)
            if has_s and GSK != 1:
                Ssm, Ssm_r = Ssm_s.get()
                ch_ = Ssm_s.chan()
                tk.dma("sp", ch_, lambda e: e.dma_start(out=Ssm[:], in_=st_in[:, hh, :, :].rearrange("s d e -> d s e")),
                       writes=Ssm_r)
                yield from gdn_batch(hh, 4, 4, THp,
                          gts[:4, 0:4, hh],
                          [bts[:4, j, hh:hh + 1] for j in range(4)],
                          [nbts[:4, j, hh:hh + 1] for j in range(4)],
                          [Ssm[:, j, :] for j in range(4)], [Ssm_r[j] for j in range(4)],
                          qn, qn_r, kn, kn_r, vv, vv_r, zs, zs_r, False)
                tk.dma("sp", ch_, lambda e: e.dma_start(out=st_s[:, hh, :, :].rearrange("s d e -> d s e"), in_=Ssm[:]),
                       reads=Ssm_r)
            yield

        def run_il(gens):
            gens = list(gens)
            while gens:
                for g_ in list(gens):
                    try:
                        next(g_)
                    except StopIteration:
                        gens.remove(g_)

        load_wsl(0)
        run_il([bulk(0)])
        for hh in range(NH):
            gl_ = [chain(hh)]
            if hh + 1 < NH:
                gl_.append(bulk(hh + 1))
            run_il(gl_)
        if has_s:
            och_f = tk.chan()
            for hh in range(NH):
                tk.dma("sp", och_f, lambda e, hh=hh: e.dma_start(out=st_p[hh], in_=Sst[:, hh, :]), reads=[Sst_r[hh]],
                       cont=(hh > 0))
            tk.dma("sp", och_f, lambda e: e.dma_start(out=cv_p.rearrange("(c p) j -> p c j", p=128), in_=craw[:]),
                   reads=craw_r, cont=True)
            tk.dma("sp", och_f, lambda e: e.dma_start(out=cv_s.rearrange("(c p) (s j) -> p c s j", p=128, j=3), in_=cvs[:]),
                   reads=[cvs_r], cont=True)
        tk.barrier()
        p1.close()
        p4 = contextlib.ExitStack()
        wo = sb(p4, "g_wo", [128, KC, D], BF16)
        wo_r = Reg("g_wo")
        for k0 in range(0, KC, 4):
            tk.dma("pool", wch[2], lambda e, k0=k0: e.dma_start(
                out=wo[:, k0:k0 + 4, :], in_=d_wo.rearrange("(k p) o -> p k o", p=128)[:, k0:k0 + 4, :]),
                writes=[wo_r], cont=(k0 > 0))
        y_s = Scratch(p4, "g_y", [128, KC, 512], F32, 2, nreg=KC)
        sq_s = Scratch(p4, "g_sq4", [128, KC, 512], BF16, 1)
        t1_s = Scratch(p4, "g_t14", [128, 512], F32, 2)
        t2_s = Scratch(p4, "g_t24", [128, 512], F32, 2)
        for (ts, n) in ftiles(t0, t1):
            o = ts - t0
            y, y_r = y_s.get()
            for oc in range(KC):
                pb_, pbr = bank()
                for k in range(NH):
                    tk.op("pe", lambda e, k=k: e.matmul(pb_[:, :n], wo[:, k, oc * 128:(oc + 1) * 128], oT[:, k, o:o + n],
                                                        start=(k == 0), stop=(k == NH - 1)),
                          reads=[wo_r, oT_r[k]], writes=[pbr])
                evac(y[:, oc, :n], pb_[:, :n], [pbr], [y_r[oc]])
            postnorm_add(1, 1, y, y_r, 0, ts, n, sq_s, t1_s, t2_s)
        tk.barrier()
        p4.close()
        ph.close()

    halves = [(0, HALF), (HALF, T)]
    if "mla" in stages:
        for (t0, t1) in halves:
            mla_phase(t0, t1)
            if "ffn0" in stages:
                ffn_phase(0, t0, t1)
    elif "ffn0" in stages:
        for (t0, t1) in halves:
            ffn_phase(0, t0, t1)
    tk.barrier()
    l0s.close()
    if "gdn" in stages:
        gdn_init()
        for (t0, t1) in halves:
            gdn_phase(t0, t1)
            if "ffn1" in stages:
                ffn_phase(1, t0, t1)
    elif "ffn1" in stages:
        for (t0, t1) in halves:
            ffn_phase(1, t0, t1)

    for k in range(KC):
        tk.dma("sp", next_och(), lambda e, k=k: e.dma_start(out=yT[k * 128:(k + 1) * 128, :], in_=h[:, k, :]),
               reads=h_r[k])
    tk.final()
    es.close()
    return nc


_NC_CACHE = {}


def prep_core_inputs(c, inp, SEQ, NPG):
    PAST = NPG * 128
    T = SEQ + 16
    x_p = inp["x_prompt"][c]
    x_s = inp["x_sample"][4 * c:4 * c + 4].reshape(16, D)
    xT = np.ascontiguousarray(np.concatenate([x_p, x_s], axis=0).T)
    consts, rope = host_consts(SEQ, PAST)
    cm = inp["cache_mla"][0]
    d = {
        "xT": xT,
        "cache": cm.reshape(cm.shape[0] * 16, 8 * ROW),
        "ptab": np.ascontiguousarray(inp["page_table"][4 * c:4 * c + 4].T.astype(np.int32)),
        "st_in": np.ascontiguousarray(inp["state_dn"][0, 4 * c:4 * c + 4]),
        "cv_in": np.ascontiguousarray(inp["state_dn_conv"][0, 4 * c:4 * c + 4].transpose(2, 0, 1)).reshape(3072, 12),
        "normw": np.ascontiguousarray(inp["norm_w"].reshape(2, 4, 8, 128).transpose(3, 0, 1, 2).reshape(128, 64)),
        "consts": consts,
        "gmask": host_gmask(),
        "rope": rope,
        "m_win": inp["mla_w_in"][0],
        "m_gq": np.ascontiguousarray(inp["mla_g_q"][0].reshape(3, 128).T),
        "m_gkv": np.ascontiguousarray(inp["mla_g_kv"][0].reshape(2, 128).T),
        "m_wuq": inp["mla_w_uq"][0],
        "m_wuk": inp["mla_w_uk"][0],
        "m_wukT": np.ascontiguousarray(inp["mla_w_uk"][0].transpose(0, 2, 1)),
        "m_wuv": inp["mla_w_uv"][0],
        "m_wo": inp["mla_w_o"][0],
        "d_win": inp["dn_w_in"][0],
        "d_cw": np.ascontiguousarray(inp["dn_conv_w"][0].T.reshape(24, 128, 4).transpose(1, 0, 2)),
        "d_alog": inp["dn_a_log"],
        "d_dtb": inp["dn_dt_bias"],
        "d_gout": np.ascontiguousarray(inp["dn_g_out"][0].reshape(128, 1)),
        "d_wo": inp["dn_w_o"][0],
        "f_win": inp["ffn_w_in"],
        "f_wout": inp["ffn_w_out"],
    }
    return {k: np.ascontiguousarray(np.asarray(v)) for k, v in d.items()}


def kernel(**inputs):
    inp = {k: np.asarray(v) for k, v in inputs.items()}
    B, SEQ, _ = inp["x_prompt"].shape
    NPG = inp["page_table"].shape[1]
    NPOOL = inp["cache_mla"].shape[1]
    key = (SEQ, NPG, NPOOL)
    nc = build(SEQ, NPG, NPOOL)
    in_maps = [prep_core_inputs(c, inp, SEQ, NPG) for c in range(8)]
    res = run_bass_kernel_spmd(nc, in_maps, core_ids=list(range(8))).results
    y_p = np.stack([res[c]["yT"][:, :SEQ].T for c in range(8)])
    y_s = np.concatenate([res[c]["yT"][:, SEQ:].T.reshape(4, 4, D) for c in range(8)])
    r_p = np.stack([res[c]["rowsT"][:, :SEQ].T for c in range(8)])[None]
    r_s = np.concatenate([res[c]["rowsT"][:, SEQ:].T.reshape(4, 4, ROW) for c in range(8)])[None]
    s_p = np.stack([res[c]["st_p"] for c in range(8)])[None]
    s_s = np.concatenate([res[c]["st_s"] for c in range(8)])[None]
    c_p = np.stack([res[c]["cv_p"].T for c in range(8)])[None]
    c_s = np.concatenate([res[c]["cv_s"].reshape(3072, 4, 3).transpose(1, 2, 0) for c in range(8)])[None]
    f = lambda a: np.ascontiguousarray(a.astype(np.float32))
    return (f(y_p), f(y_s), f(r_p), f(r_s), f(s_p), f(s_s), f(c_p), f(c_s))
```

```python
import contextlib
import math
import numpy as np
import concourse.bass as bass
import concourse.mybir as mybir
from concourse.bass_utils import run_bass_kernel_spmd

F32 = mybir.dt.float32
BF16 = mybir.dt.bfloat16
I32 = mybir.dt.int32
AF = mybir.ActivationFunctionType
ALU = mybir.AluOpType

D = 1024
KC = 8
NH = 8
QLORA, KVLORA, ROPE = 384, 256, 64
ROW = KVLORA + ROPE
DFF = 2816
FC = DFF // 128
NEG = -30000.0
MLA_SCALE = (128 + 64) ** -0.5


class Reg:
    __slots__ = ("name", "w", "rd", "excl")

    def __init__(self, name="", excl=False):
        self.name = name
        self.w = None
        self.rd = {}
        self.excl = excl


class Trk:
    def __init__(self, nc, es):
        self.nc = nc
        self.es = es
        self.eng = {"pe": nc.tensor, "act": nc.scalar, "dve": nc.vector, "pool": nc.gpsimd, "sp": nc.sync}
        self.semh = {}
        self.cnt = {}
        self.seen = {k: {} for k in self.eng}
        for k in self.eng:
            self.semh[k] = es.enter_context(nc.semaphore("s_" + k))
            self.cnt[k] = 0
        self.same_sync = {"pool": True, "act": True, "dve": True}
        self.nchan = 0

    def chan(self, name=None):
        self.nchan += 1
        k = "c%d" % self.nchan
        self.semh[k] = self.es.enter_context(self.nc.semaphore("d_%d" % self.nchan))
        self.cnt[k] = 0
        return k

    def _waits(self, e, reads, writes, skip=None):
        need = {}
        for r in reads:
            if r.w is not None and need.get(r.w[0], 0) < r.w[1]:
                need[r.w[0]] = r.w[1]
            if r.excl:
                for k, c in r.rd.items():
                    if k != e and need.get(k, 0) < c:
                        need[k] = c
        for w in writes:
            if w.w is not None and need.get(w.w[0], 0) < w.w[1]:
                need[w.w[0]] = w.w[1]
            for k, c in w.rd.items():
                if need.get(k, 0) < c:
                    need[k] = c
        for k, c in need.items():
            if k == skip:
                continue
            if k == e and not self.same_sync.get(e):
                continue
            if self.seen[e].get(k, 0) >= c:
                continue
            self.eng[e].wait_ge(self.semh[k], c)
            self.seen[e][k] = c

    def op(self, e, fn, reads=(), writes=()):
        self._waits(e, reads, writes)
        ins = fn(self.eng[e])
        self.cnt[e] += 1
        ins.then_inc(self.semh[e], 1)
        c = self.cnt[e]
        for r in reads:
            r.rd[e] = c
        for w in writes:
            w.w = (e, c)
            w.rd = {}
        return ins

    def dma(self, q, ch, fn, reads=(), writes=(), cont=False):
        if not cont and self.cnt[ch] > self.seen[q].get(ch, 0):
            self.eng[q].wait_ge(self.semh[ch], self.cnt[ch])
            self.seen[q][ch] = self.cnt[ch]
        self._waits(q, reads, writes, skip=ch)
        ins = fn(self.eng[q])
        self.cnt[ch] += 16
        ins.then_inc(self.semh[ch], 16)
        c = self.cnt[ch]
        for r in reads:
            r.rd[ch] = c
        for w in writes:
            w.w = (ch, c)
            w.rd = {}
        return ins

    def barrier(self):
        for e in self.eng:
            for k, c in self.cnt.items():
                if c == 0:
                    continue
                if self.seen[e].get(k, 0) >= c:
                    continue
                self.eng[e].wait_ge(self.semh[k], c)
                self.seen[e][k] = c

    def final(self):
        e = "sp"
        for k, c in self.cnt.items():
            if k == e or c == 0:
                continue
            if self.seen[e].get(k, 0) >= c:
                continue
            self.eng[e].wait_ge(self.semh[k], c)
            self.seen[e][k] = c


def host_consts(SEQ, PAST):
    T = SEQ + 16
    c = np.zeros((128, 1536), np.float32)
    idx = np.arange(128)
    c[:, 0:128] = np.eye(128, dtype=np.float32)
    c[:, 128:256] = (idx[:, None] <= idx[None, :]).astype(np.float32)
    c[:, 256:384] = (idx[:, None] > idx[None, :]).astype(np.float32)
    c[:, 384:512] = np.where(idx[None, :] >= idx[:, None], NEG, 0.0)
    c[:, 512:640] = np.where(idx[None, :] < idx[:, None], NEG, 0.0)
    rot = np.zeros((128, 64), np.float32)
    for m in range(32):
        rot[m + 32, m] = -1.0
        rot[m, m + 32] = 1.0
    c[:, 640:704] = rot
    c[:, 704:832] = 1.0
    dm = np.zeros((128, 32), np.float32)
    for j in range(4):
        for q in range(32):
            dm[j, q] = 1.0 if j <= (q % 4) else 0.0
    c[:, 832:864] = dm
    for j in range(4):
        c[:, 1024 + j * 128:1024 + (j + 1) * 128] = np.eye(128, dtype=np.float32)
    half = 32
    freq = (10000.0 ** (-np.arange(half, dtype=np.float32) / half)).astype(np.float32)
    pos = np.concatenate([np.arange(SEQ), np.tile(PAST + np.arange(4), 4)]).astype(np.float32)
    ang = pos[None, :] * freq[:, None]
    rope = np.zeros((64, 2, T), np.float32)
    rope[0:32, 0] = np.cos(ang)
    rope[32:64, 0] = np.cos(ang)
    rope[0:32, 1] = np.sin(ang)
    rope[32:64, 1] = np.sin(ang)
    return c, rope


def host_gmask():
    idx = np.arange(128)
    c_, s_ = idx[:, None], idx[None, :]
    g = np.zeros((128, 1408), np.float32)
    g[:, 0:128] = (c_ // 4 == s_ // 4) & (c_ > s_)
    for li, b in enumerate([4, 8, 16, 32, 64]):
        m = ((c_ // (2 * b)) == (s_ // (2 * b))) & ((c_ // b) > (s_ // b))
        g[:, 128 + li * 128:128 + (li + 1) * 128] = np.eye(128) - m
    g[:, 768:896] = (c_ // 4 == s_ // 4) & (c_ < s_)
    return g


INV_F32 = False
import os as _os
GCUT = int(_os.environ.get('GCUT', '99'))
GSK = int(_os.environ.get('GSK', '0'))
GOP = int(_os.environ.get('GOP', '0'))


def build(SEQ=2048, NPG=128, NPOOL=5120, stages=("mla", "ffn0", "gdn", "ffn1")):
    T = SEQ + 16
    HALF = SEQ // 2
    PAST = NPG * 128
    nc = bass.Bass("TRN2", target_bir_lowering=False)
    es = contextlib.ExitStack()
    tk = Trk(nc, es)

    def din(name, shape, dt=F32):
        return nc.dram_tensor(name, list(shape), dt, kind="ExternalInput").ap()

    def dout(name, shape, dt=F32):
        return nc.dram_tensor(name, list(shape), dt, kind="ExternalOutput").ap()

    xT = din("xT", [D, T])
    cache = din("cache", [NPOOL * 16, 8 * ROW])
    ptab = din("ptab", [128, 4], I32)
    st_in = din("st_in", [4, NH, 128, 128])
    cv_in = din("cv_in", [3072, 12])
    normw = din("normw", [128, 64])
    consts_d = din("consts", [128, 1536])
    gmask_d = din("gmask", [128, 1408])
    rope_d = din("rope", [64, 2, T])
    m_win = din("m_win", [D, 704])
    m_gq = din("m_gq", [128, 3])
    m_gkv = din("m_gkv", [128, 2])
    m_wuq = din("m_wuq", [QLORA, NH * 192])
    m_wuk = din("m_wuk", [NH, KVLORA, 128])
    m_wukT = din("m_wukT", [NH, 128, KVLORA])
    m_wuv = din("m_wuv", [NH, KVLORA, 128])
    m_wo = din("m_wo", [D, D])
    d_win = din("d_win", [D, 4112])
    d_cw = din("d_cw", [128, 24, 4])
    d_alog = din("d_alog", [1, NH])
    d_dtb = din("d_dtb", [1, NH])
    d_gout = din("d_gout", [128, 1])
    d_wo = din("d_wo", [D, D])
    f_win = din("f_win", [2, D, 2 * DFF])
    f_wout = din("f_wout", [2, DFF, D])

    yT = dout("yT", [D, T])
    rowsT = dout("rowsT", [ROW, T])
    st_p = dout("st_p", [NH, 128, 128])
    st_s = dout("st_s", [4, NH, 128, 128])
    cv_p = dout("cv_p", [3072, 3])
    cv_s = dout("cv_s", [3072, 12])

    uid = [0]

    def sb(stack, name, shape, dt):
        uid[0] += 1
        return stack.enter_context(nc.sbuf_tensor("%s_%d" % (name, uid[0]), list(shape), dt))

    h = sb(es, "h", [128, KC, T], F32)
    h_r = [[Reg("h%d_%d" % (k, j)) for j in range(8)] for k in range(KC)]
    cf = sb(es, "cf", [128, 1536], F32)
    cb = sb(es, "cb", [128, 1536], BF16)
    nw = sb(es, "nw", [128, 64], F32)
    epst = sb(es, "epst", [128, 2], F32)
    c_r = Reg("consts")
    I_f, U_f, MS_f, NEGL_f, NEGQ_f, ROT_f, ONE_f = (cf[:, 0:128], cf[:, 128:256], cf[:, 256:384],
                                                    cf[:, 384:512], cf[:, 512:640], cf[:, 640:704],
                                                    cf[:, 704:832])
    I_b, U_b, ONE_b, DM_b = cb[:, 0:128], cb[:, 128:256], cb[:, 704:832], cb[:, 832:864]
    I4_b = cb[:, 1024:1536]
    psb = [es.enter_context(nc.psum_tensor("ps%d" % i, [128, 512], F32)) for i in range(8)]
    ps_r = [Reg("ps%d" % i, excl=True) for i in range(8)]
    bank_i = [0]

    reserved = set()

    def bank():
        for _ in range(16):
            i = bank_i[0]
            bank_i[0] = (i + 1) % 8
            if i not in reserved:
                return psb[i], ps_r[i]
        raise RuntimeError("no free psum bank")

    def reserve(n):
        out = []
        for _ in range(n):
            for _ in range(16):
                i = bank_i[0]
                bank_i[0] = (i + 1) % 8
                if i not in reserved:
                    break
            reserved.add(i)
            out.append((psb[i], ps_r[i], i))
        return out

    def release(lst):
        for (_, _, i) in lst:
            reserved.discard(i)

    ch_in = tk.chan()
    tk.dma("sp", ch_in, lambda e: e.dma_start(out=cf[:], in_=consts_d), writes=[c_r])
    ch_cb = tk.chan()
    cb_r = Reg("cb")
    tk.dma("pool", ch_cb, lambda e: e.dma_start(out=cb[:], in_=consts_d), writes=[cb_r])
    tk.dma("sp", ch_in, lambda e: e.dma_start(out=nw[:], in_=normw), writes=[c_r])
    tk.op("dve", lambda e: e.memset(epst[:, 0:1], 1e-6), writes=[c_r])
    tk.op("dve", lambda e: e.memset(epst[:, 1:2], 1.0), reads=[cb_r], writes=[c_r])
    ch_x = tk.chan()
    for k in range(KC):
        tk.dma("sp", ch_x, lambda e, k=k: e.dma_start(out=h[:, k, :], in_=xT[k * 128:(k + 1) * 128, :]),
               writes=h_r[k], cont=(k > 0))
    for k in range(KC):
        for r_ in h_r[k]:
            r_.w = (ch_x, tk.cnt[ch_x])

    def hregs(ts, n):
        j0, j1 = ts // 512, (ts + n - 1) // 512
        return [h_r[k][j] for k in range(KC) for j in range(j0, j1 + 1)]

    def hreg_k(k, ts, n):
        j0, j1 = ts // 512, (ts + n - 1) // 512
        return [h_r[k][j] for j in range(j0, j1 + 1)]

    def tiles(t0, t1):
        out = []
        t = t0
        while t < min(t1, SEQ):
            n = min(512, min(t1, SEQ) - t)
            out.append((t, n))
            t += n
        if t1 > SEQ:
            out.append((SEQ, t1 - SEQ))
        return out

    def ftiles(t0, t1):
        th = t1 - t0
        nt = (th + 511) // 512
        base, rem = divmod(th, nt)
        out, t = [], t0
        for i in range(nt):
            n = base + (1 if i < rem else 0)
            out.append((t, n))
            t += n
        return out

    def nwcol(layer, i, k):
        j = (layer * 4 + i) * 8 + k
        return nw[:, j:j + 1]

    def rstd_from_ss(ps_ap, ps_reg, npart, n, scale, tmp, tmp_r, out, out_r):
        tk.op("act", lambda e: e.activation(out=tmp[:npart, :n], in_=ps_ap, func=AF.Ln,
                                            bias=epst[:npart, 0:1], scale=scale),
              reads=[ps_reg, c_r], writes=[tmp_r])
        tk.op("act", lambda e: e.activation(out=out[:npart, :n], in_=tmp[:npart, :n], func=AF.Exp, scale=-0.5),
              reads=[tmp_r], writes=[out_r])

    class Scratch:
        def __init__(self, stack, name, shape, dt, nbuf, nreg=None, chan=False):
            self.t = [sb(stack, "%s%d" % (name, i), shape, dt) for i in range(nbuf)]
            if nreg is None:
                self.r = [Reg("%s%d" % (name, i)) for i in range(nbuf)]
            else:
                self.r = [[Reg("%s%d_%d" % (name, i, j)) for j in range(nreg)] for i in range(nbuf)]
            self.ch = [tk.chan() for _ in range(nbuf)] if chan else None
            self.i = 0
            self.last = 0

        def get(self):
            i = self.i
            self.last = i
            self.i = (i + 1) % len(self.t)
            return self.t[i], self.r[i]

        def chan(self):
            return self.ch[self.last]

    def rmsnorm_pre(layer, wi, ts, n, dst, dst_r, dst_off, sq_s, t1_s, t2_s):
        sq, sq_r = sq_s.get()
        tk.op("act", lambda e: e.activation(out=sq[:, :, :n], in_=h[:, :, ts:ts + n], func=AF.Square),
              reads=hregs(ts, n), writes=[sq_r])
        pb, pr = bank()
        for k in range(KC):
            tk.op("pe", lambda e, k=k: e.matmul(pb[:, :n], ONE_b, sq[:, k, :n], start=(k == 0), stop=(k == KC - 1)),
                  reads=[sq_r, c_r], writes=[pr])
        t1, t1r = t1_s.get()
        t2, t2r = t2_s.get()
        rstd_from_ss(pb[:, :n], pr, 128, n, 1.0 / D, t1, t1r, t2, t2r)
        for k in range(KC):
            tk.op("dve", lambda e, k=k: e.scalar_tensor_tensor(
                out=dst[:, k, dst_off:dst_off + n], in0=h[:, k, ts:ts + n], scalar=nwcol(layer, wi, k),
                in1=t2[:, :n], op0=ALU.mult, op1=ALU.mult),
                reads=hreg_k(k, ts, n) + [t2r, c_r], writes=[dst_r])

    def postnorm_add(layer, wi, y, y_r, yoff, ts, n, sq_s, t1_s, t2_s):
        sq, sq_r = sq_s.get()
        tk.op("act", lambda e: e.activation(out=sq[:, :, :n], in_=y[:, :, yoff:yoff + n], func=AF.Square),
              reads=y_r, writes=[sq_r])
        pb, pr = bank()
        for k in range(KC):
            tk.op("pe", lambda e, k=k: e.matmul(pb[:, :n], ONE_b, sq[:, k, :n], start=(k == 0), stop=(k == KC - 1)),
                  reads=[sq_r, c_r], writes=[pr])
        t1, t1r = t1_s.get()
        t2, t2r = t2_s.get()
        rstd_from_ss(pb[:, :n], pr, 128, n, 1.0 / D, t1, t1r, t2, t2r)
        for k in range(KC):
            tk.op("dve", lambda e, k=k: e.scalar_tensor_tensor(
                out=y[:, k, yoff:yoff + n], in0=y[:, k, yoff:yoff + n], scalar=nwcol(layer, wi, k),
                in1=t2[:, :n], op0=ALU.mult, op1=ALU.mult),
                reads=[t2r, c_r, y_r[k]], writes=[y_r[k]])
            tk.op("pool", lambda e, k=k: e.tensor_tensor(out=h[:, k, ts:ts + n], in0=h[:, k, ts:ts + n],
                                                         in1=y[:, k, yoff:yoff + n], op=ALU.add),
                  reads=[y_r[k]], writes=hreg_k(k, ts, n))

    ev_tog = [0]

    def evac(out_ap, in_ap, reads, writes):
        ev_tog[0] ^= 1
        if ev_tog[0]:
            tk.op("act", lambda e: e.copy(out=out_ap, in_=in_ap), reads=reads, writes=writes)
        else:
            tk.op("dve", lambda e: e.tensor_copy(out=out_ap, in_=in_ap), reads=reads, writes=writes)

    wch = [tk.chan() for _ in range(4)]

    def ffn_phase(layer, t0, t1):
        TH = t1 - t0
        tl = ftiles(t0, t1)
        ph = contextlib.ExitStack()
        act = sb(ph, "f_act", [128, FC, TH], BF16)
        act_r = [Reg("act%d" % c) for c in range(FC)]
        w_in_v = f_win[layer].rearrange("(k p) o -> p k o", p=128)
        w_out_v = f_wout[layer].rearrange("(k p) o -> p k o", p=128)
        pa = contextlib.ExitStack()
        u = sb(pa, "f_u", [128, KC, TH], BF16)
        u_r = Reg("f_u")
        sq_s = Scratch(pa, "f_sq", [128, KC, 512], BF16, 1)
        t1_s = Scratch(pa, "f_t1", [128, 512], F32, 2)
        t2_s = Scratch(pa, "f_t2", [128, 512], F32, 2)
        sg_s = Scratch(pa, "f_sg", [128, 512], BF16, 3)
        slab = [sb(pa, "f_ws%d" % i, [128, KC, 2, 512], BF16) for i in range(2)]
        slab_r = [Reg("f_ws%d" % i) for i in range(2)]
        groups = [(g * 4, min(4, FC - g * 4)) for g in range((FC + 3) // 4)]

        def load_slab(gi):
            c0, ncg = groups[gi]
            s = gi % 2
            for two in range(2):
                col = two * DFF + c0 * 128
                tk.dma("pool", wch[s], lambda e, two=two, col=col: e.dma_start(
                    out=slab[s][:, :, two, :ncg * 128], in_=w_in_v[:, :, col:col + ncg * 128]),
                    writes=[slab_r[s]], cont=(two > 0))

        load_slab(0)
        for (ts, n) in tl:
            rmsnorm_pre(layer, 2, ts, n, u, u_r, ts - t0, sq_s, t1_s, t2_s)
        for gi, (c0, ncg) in enumerate(groups):
            if gi + 1 < len(groups):
                load_slab(gi + 1)
            s = gi % 2
            for cc in range(ncg):
                c = c0 + cc
                for (ts, n) in tl:
                    o = ts - t0
                    pg, pgr = bank()
                    for k in range(KC):
                        tk.op("pe", lambda e, k=k: e.matmul(pg[:, :n], slab[s][:, k, 0, cc * 128:(cc + 1) * 128],
                                                            u[:, k, o:o + n], start=(k == 0), stop=(k == KC - 1)),
                              reads=[slab_r[s], u_r], writes=[pgr])
                    pu, pur = bank()
                    for k in range(KC):
                        tk.op("pe", lambda e, k=k: e.matmul(pu[:, :n], slab[s][:, k, 1, cc * 128:(cc + 1) * 128],
                                                            u[:, k, o:o + n], start=(k == 0), stop=(k == KC - 1)),
                              reads=[slab_r[s], u_r], writes=[pur])
                    sg, sgr = sg_s.get()
                    tk.op("act", lambda e: e.activation(out=sg[:, :n], in_=pg[:, :n], func=AF.Silu),
                          reads=[pgr], writes=[sgr])
                    tk.op("dve", lambda e: e.tensor_tensor(out=act[:, c, o:o + n], in0=sg[:, :n], in1=pu[:, :n],
                                                           op=ALU.mult),
                          reads=[sgr, pur], writes=[act_r[c]])
        tk.barrier()
        pa.close()
        pbk = contextlib.ExitStack()
        y = sb(pbk, "f_y", [128, KC, TH], F32)
        y_r = [[Reg("f_y%d_%d" % (i, k)) for k in range(KC)] for i in range(len(tl))]
        sq_s = Scratch(pbk, "f_sq2", [128, KC, 512], BF16, 1)
        t1_s = Scratch(pbk, "f_t1b", [128, 512], F32, 2)
        t2_s = Scratch(pbk, "f_t2b", [128, 512], F32, 2)
        oslab = [sb(pbk, "f_wo%d" % i, [128, FC, 256], BF16) for i in range(2)]
        oslab_r = [Reg("f_wo%d" % i) for i in range(2)]

        def load_oslab(gi):
            s = gi % 2
            for kk in range(0, FC, 11):
                tk.dma("pool", wch[2 + s], lambda e, kk=kk: e.dma_start(
                    out=oslab[s][:, kk:kk + 11, :], in_=w_out_v[:, kk:kk + 11, gi * 256:(gi + 1) * 256]),
                    writes=[oslab_r[s]], cont=(kk > 0))

        load_oslab(0)
        for gi in range(4):
            if gi + 1 < 4:
                load_oslab(gi + 1)
            s = gi % 2
            for oc in range(2):
                ochunk = gi * 2 + oc
                for ti, (ts, n) in enumerate(tl):
                    o = ts - t0
                    pb_, pbr = bank()
                    for c in range(FC):
                        tk.op("pe", lambda e, c=c: e.matmul(pb_[:, :n], oslab[s][:, c, oc * 128:(oc + 1) * 128],
                                                            act[:, c, o:o + n], start=(c == 0), stop=(c == FC - 1)),
                              reads=[oslab_r[s], act_r[c]], writes=[pbr])
                    evac(y[:, ochunk, o:o + n], pb_[:, :n], [pbr], [y_r[ti][ochunk]])
        for ti, (ts, n) in enumerate(tl):
            postnorm_add(layer, 3, y, y_r[ti], ts - t0, ts, n, sq_s, t1_s, t2_s)
        tk.barrier()
        pbk.close()
        ph.close()

    l0s = contextlib.ExitStack()
    ckv_b = sb(l0s, "ckv_b", [128, 2, T], BF16)
    kr_b = sb(l0s, "kr_b", [64, T], BF16)
    ckv_r = [Reg("ckv%d" % j) for j in range(8)]
    och = [tk.chan() for _ in range(4)]
    och_i = [0]

    def next_och():
        och_i[0] = (och_i[0] + 1) % len(och)
        return och[och_i[0]]

    def kvregs(ts, n):
        return [ckv_r[j] for j in range(ts // 512, (ts + n - 1) // 512 + 1)]

    def mla_phase(t0, t1):
        TH = t1 - t0
        tl = tiles(t0, t1)
        has_s = t1 > SEQ
        ph = contextlib.ExitStack()
        cq = sb(ph, "m_cq", [128, 3, TH], BF16)
        cq_r = Reg("m_cq")
        ropet = sb(ph, "m_rope", [64, 2, TH], F32)
        rope_r = Reg("m_rope")
        tk.dma("sp", ch_in, lambda e: e.dma_start(out=ropet[:], in_=rope_d[:, :, t0:t1]), writes=[rope_r])
        oT = sb(ph, "m_oT", [128, NH, TH], BF16)
        oT_r = [Reg("m_oT%d" % hh) for hh in range(NH)]
        p1 = contextlib.ExitStack()
        win = sb(p1, "m_win", [128, KC, 704], BF16)
        win_r = Reg("m_win")
        gq = sb(p1, "m_gq", [128, 3], F32)
        gkv = sb(p1, "m_gkv", [128, 2], F32)
        tk.dma("pool", wch[0], lambda e: e.dma_start(out=win[:], in_=m_win.rearrange("(k p) o -> p k o", p=128)),
               writes=[win_r])
        gv_r = Reg("m_gv")
        tk.dma("sp", ch_in, lambda e: e.dma_start(out=gq[:], in_=m_gq), writes=[gv_r])
        tk.dma("sp", ch_in, lambda e: e.dma_start(out=gkv[:], in_=m_gkv), writes=[gv_r])
        u_s = Scratch(p1, "m_u", [128, KC, 512], BF16, 2)
        sq_s = Scratch(p1, "m_sq", [128, KC, 512], BF16, 1)
        t1_s = Scratch(p1, "m_t1", [128, 512], F32, 2)
        t2_s = Scratch(p1, "m_t2", [128, 512], F32, 2)
        a_s = Scratch(p1, "m_a", [128, 6, 512], F32, 2)
        rowo_s = Scratch(p1, "m_rowo", [128, 3, 512], F32, 2, chan=True)
        for (ts, n) in tl:
            o = ts - t0
            u, u_r = u_s.get()
            rmsnorm_pre(0, 0, ts, n, u, u_r, 0, sq_s, t1_s, t2_s)
            a, a_r = a_s.get()
            for oc in range(6):
                m = 128 if oc < 5 else 64
                pb_, pbr = bank()
                for k in range(KC):
                    tk.op("pe", lambda e, k=k: e.matmul(pb_[:m, :n], win[:, k, oc * 128:oc * 128 + m], u[:, k, :n],
                                                        start=(k == 0), stop=(k == KC - 1)),
                          reads=[win_r, u_r], writes=[pbr])
                evac(a[:m, oc, :n], pb_[:m, :n], [pbr], [a_r])
            sq, sq_r = sq_s.get()
            tk.op("act", lambda e: e.activation(out=sq[:, 0:5, :n], in_=a[:, 0:5, :n], func=AF.Square),
                  reads=[a_r], writes=[sq_r])
            pq, pqr = bank()
            for k in range(3):
                tk.op("pe", lambda e, k=k: e.matmul(pq[:, :n], ONE_b, sq[:, k, :n], start=(k == 0), stop=(k == 2)),
                      reads=[sq_r, c_r], writes=[pqr])
            pk, pkr = bank()
            for k in range(2):
                tk.op("pe", lambda e, k=k: e.matmul(pk[:, :n], ONE_b, sq[:, 3 + k, :n], start=(k == 0), stop=(k == 1)),
                      reads=[sq_r, c_r], writes=[pkr])
            ta, tar = t1_s.get()
            rq, rqr = t2_s.get()
            rstd_from_ss(pq[:, :n], pqr, 128, n, 1.0 / QLORA, ta, tar, rq, rqr)
            tb, tbr = t1_s.get()
            rk, rkr = t2_s.get()
            rstd_from_ss(pk[:, :n], pkr, 128, n, 1.0 / KVLORA, tb, tbr, rk, rkr)
            for k in range(3):
                tk.op("dve", lambda e, k=k: e.scalar_tensor_tensor(
                    out=cq[:, k, o:o + n], in0=a[:, k, :n], scalar=gq[:, k:k + 1], in1=rq[:, :n],
                    op0=ALU.mult, op1=ALU.mult), reads=[a_r, rqr, gv_r], writes=[cq_r])
            rowo, rowo_r = rowo_s.get()
            for k in range(2):
                tk.op("dve", lambda e, k=k: e.scalar_tensor_tensor(
                    out=rowo[:, k, :n], in0=a[:, 3 + k, :n], scalar=gkv[:, k:k + 1], in1=rk[:, :n],
                    op0=ALU.mult, op1=ALU.mult), reads=[a_r, rkr, gv_r], writes=[rowo_r])
            tk.op("act", lambda e: e.copy(out=ckv_b[:, :, ts:ts + n], in_=rowo[:, 0:2, :n]),
                  reads=[rowo_r], writes=kvregs(ts, n))
            pr_, prr = bank()
            tk.op("pe", lambda e: e.matmul(pr_[:64, :n], ROT_f[:64, :], a[:64, 5, :n], start=True, stop=True),
                  reads=[a_r, c_r], writes=[prr])
            tk.op("dve", lambda e: e.tensor_tensor(out=rowo[:64, 2, :n], in0=a[:64, 5, :n],
                                                   in1=ropet[:, 0, o:o + n], op=ALU.mult),
                  reads=[a_r, rope_r], writes=[rowo_r])
            tk.op("dve", lambda e: e.tensor_tensor(out=a[:64, 5, :n], in0=pr_[:64, :n],
                                                   in1=ropet[:, 1, o:o + n], op=ALU.mult),
                  reads=[prr, rope_r], writes=[a_r])
            tk.op("dve", lambda e: e.tensor_tensor(out=rowo[:64, 2, :n], in0=rowo[:64, 2, :n],
                                                   in1=a[:64, 5, :n], op=ALU.add),
                  reads=[a_r], writes=[rowo_r])
            tk.op("act", lambda e: e.copy(out=kr_b[:, ts:ts + n], in_=rowo[:64, 2, :n]),
                  reads=[rowo_r], writes=kvregs(ts, n))
            oc_ = rowo_s.chan()
            tk.dma("sp", oc_, lambda e: e.dma_start(
                out=rowsT[0:256, ts:ts + n].rearrange("(k p) t -> p k t", p=128), in_=rowo[:, 0:2, :n]),
                reads=[rowo_r])
            tk.dma("sp", oc_, lambda e: e.dma_start(out=rowsT[256:320, ts:ts + n], in_=rowo[:64, 2, :n]),
                   reads=[rowo_r], cont=True)
        tk.barrier()
        p1.close()
        p2 = contextlib.ExitStack()
        wuq = sb(p2, "m_wuq", [128, 3, NH * 192], BF16)
        wuk = sb(p2, "m_wuk", [128, NH, 2, 128], BF16)
        wuv = sb(p2, "m_wuv", [128, NH, 2, 128], BF16)
        wukT = sb(p2, "m_wukT", [128, NH, 256], BF16)
        w2_r = Reg("m_w2")
        tk.dma("pool", wch[1], lambda e: e.dma_start(out=wuq[:], in_=m_wuq.rearrange("(k p) o -> p k o", p=128)),
               writes=[w2_r])
        tk.dma("pool", wch[1], lambda e: e.dma_start(out=wuk[:], in_=m_wuk.rearrange("h (k p) n -> p h k n", p=128)),
               writes=[w2_r], cont=True)
        tk.dma("pool", wch[1], lambda e: e.dma_start(out=wuv[:], in_=m_wuv.rearrange("h (k p) n -> p h k n", p=128)),
               writes=[w2_r], cont=True)
        tk.dma("pool", wch[1], lambda e: e.dma_start(out=wukT[:], in_=m_wukT.rearrange("h p r -> p h r")),
               writes=[w2_r], cont=True)
        NKB = t1 // 128 if not has_s else SEQ // 128
        if has_s:
            Qs = sb(p2, "m_Qs", [128, 3, 4, 32], BF16)
            Qs_r = Reg("m_Qs")
        p2a = contextlib.ExitStack()
        qn_s = Scratch(p2a, "m_qn", [128, TH], BF16, 2)
        qr_s = Scratch(p2a, "m_qr", [64, TH], BF16, 2)
        qx_s = Scratch(p2a, "m_qx", [64, 2, 512], F32, 2)
        kn_s = Scratch(p2a, "m_kn", [128, SEQ], BF16, 2)
        vp_s = Scratch(p2a, "m_vp", [128, SEQ // 128, 132], BF16, 2)
        pT_s = Scratch(p2a, "m_pT", [128, 512], BF16, 3)
        on_s = Scratch(p2a, "m_on", [128, 4, 128], BF16, 2)
        rs_s = Scratch(p2a, "m_rs", [128, 4], F32, 2)
        ptl = [x for x in tl if x[0] < SEQ]
        for hh in range(NH):
            qn, qn_r = qn_s.get()
            qr, qr_r = qr_s.get()
            for (ts, n) in tl:
                o = ts - t0
                pb_, pbr = bank()
                for k in range(3):
                    tk.op("pe", lambda e, k=k: e.matmul(pb_[:, :n], wuq[:, k, hh * 192:hh * 192 + 128], cq[:, k, o:o + n],
                                                        start=(k == 0), stop=(k == 2)),
                          reads=[w2_r, cq_r], writes=[pbr])
                evac(qn[:, o:o + n], pb_[:, :n], [pbr], [qn_r])
                p2_, p2r = bank()
                for k in range(3):
                    tk.op("pe", lambda e, k=k: e.matmul(p2_[:64, :n], wuq[:, k, hh * 192 + 128:hh * 192 + 192],
                                                        cq[:, k, o:o + n], start=(k == 0), stop=(k == 2)),
                          reads=[w2_r, cq_r], writes=[p2r])
                qx, qx_r = qx_s.get()
                tk.op("act", lambda e: e.copy(out=qx[:, 0, :n], in_=p2_[:64, :n]), reads=[p2r], writes=[qx_r])
                p3_, p3r = bank()
                tk.op("pe", lambda e: e.matmul(p3_[:64, :n], ROT_f[:64, :], qx[:, 0, :n], start=True, stop=True),
                      reads=[qx_r, c_r], writes=[p3r])
                tk.op("dve", lambda e: e.tensor_tensor(out=qx[:, 1, :n], in0=p3_[:64, :n], in1=ropet[:, 1, o:o + n],
                                                       op=ALU.mult), reads=[p3r, rope_r], writes=[qx_r])
                tk.op("dve", lambda e: e.tensor_tensor(out=qx[:, 0, :n], in0=qx[:, 0, :n], in1=ropet[:, 0, o:o + n],
                                                       op=ALU.mult), reads=[rope_r], writes=[qx_r])
                tk.op("dve", lambda e: e.tensor_tensor(out=qr[:, o:o + n], in0=qx[:, 0, :n], in1=qx[:, 1, :n],
                                                       op=ALU.add), reads=[qx_r], writes=[qr_r])
            kn, kn_r = kn_s.get()
            vp, vp_r = vp_s.get()
            nk = NKB * 128
            for ks in range(0, nk, 512):
                n = min(512, nk - ks)
                pb_, pbr = bank()
                for k in range(2):
                    tk.op("pe", lambda e, k=k: e.matmul(pb_[:, :n], wuk[:, hh, k, :], ckv_b[:, k, ks:ks + n],
                                                        start=(k == 0), stop=(k == 1)),
                          reads=[w2_r] + kvregs(ks, n), writes=[pbr])
                evac(kn[:, ks:ks + n], pb_[:, :n], [pbr], [kn_r])
            for kb4 in range(0, NKB, 4):
                nb = min(4, NKB - kb4)
                pb_, pbr = bank()
                for j in range(nb):
                    kb = kb4 + j
                    for k in range(2):
                        tk.op("pe", lambda e, k=k, j=j, kb=kb: e.matmul(
                            pb_[:, j * 128:(j + 1) * 128], ckv_b[:, k, kb * 128:(kb + 1) * 128], wuv[:, hh, k, :],
                            start=(k == 0), stop=(k == 1)),
                            reads=[w2_r] + kvregs(kb * 128, 128), writes=[pbr])
                evac(vp[:, kb4:kb4 + nb, 0:128], pb_[:, :nb * 128].rearrange("p (j v) -> p j v", v=128), [pbr], [vp_r])
            tk.op("dve", lambda e: e.memset(vp[:, :, 128:129], 1.0), writes=[vp_r])
            for (ts, n) in ptl:
                o = ts - t0
                nsub = n // 128
                kb_hi = (ts + n) // 128
                accs_l = reserve(nsub)
                accs = [(a_, r_) for (a_, r_, _) in accs_l]
                pend = None

                def emit_pv(pv):
                    kb, d, j0, pT, pT_r = pv
                    for si in range(max(d, 0), nsub):
                        ab, abr = accs[si]
                        last_kb = ts // 128 + si
                        c0 = si * 128 - j0
                        tk.op("pe", lambda e, ab=ab, c0=c0, kb=kb, last_kb=last_kb: e.matmul(
                            ab[:, 0:129], pT[:, c0:c0 + 128], vp[:, kb, 0:129], start=(kb == 0), stop=(kb == last_kb)),
                            reads=[pT_r, vp_r], writes=[abr])

                for kb in range(kb_hi):
                    d = kb - ts // 128
                    j0 = max(d, 0) * 128
                    ncol = n - j0
                    psc, pscr = bank()
                    tk.op("pe", lambda e: e.matmul(psc[:, :ncol], kn[:, kb * 128:(kb + 1) * 128],
                                                   qn[:, o + j0:o + n], start=True, stop=False),
                          reads=[kn_r, qn_r], writes=[pscr])
                    tk.op("pe", lambda e: e.matmul(psc[:, :ncol], kr_b[:, kb * 128:(kb + 1) * 128],
                                                   qr[:, o + j0:o + n], start=False, stop=True),
                          reads=kvregs(kb * 128, 128) + [qr_r], writes=[pscr])
                    if pend is not None:
                        emit_pv(pend)
                    pT, pT_r = pT_s.get()
                    tk.op("act", lambda e: e.activation(out=pT[:, :ncol], in_=psc[:, :ncol], func=AF.Exp,
                                                        scale=MLA_SCALE), reads=[pscr], writes=[pT_r])
                    if d >= 0:
                        tk.op("dve", lambda e: e.tensor_tensor(out=pT[:, 0:128], in0=pT[:, 0:128], in1=U_b,
                                                               op=ALU.mult), reads=[c_r], writes=[pT_r])
                    pend = (kb, d, j0, pT, pT_r)
                emit_pv(pend)
                rs, rs_r = rs_s.get()
                on, on_r = on_s.get()
                for si in range(nsub):
                    ab, abr = accs[si]
                    tk.op("dve", lambda e, ab=ab, si=si: e.reciprocal(out=rs[:, si:si + 1], in_=ab[:, 128:129]),
                          reads=[abr], writes=[rs_r])
                    tk.op("act", lambda e, ab=ab, si=si: e.activation(out=on[:, si, :], in_=ab[:, 0:128], func=AF.Copy,
                                                                      scale=rs[:, si:si + 1]),
                          reads=[abr, rs_r], writes=[on_r])
                ptb, ptr = bank()
                for si in range(nsub):
                    tk.op("pe", lambda e, si=si: e.matmul(ptb[:, si * 128:(si + 1) * 128], on[:, si, :], I_b,
                                                          start=True, stop=True),
                          reads=[on_r, c_r], writes=[ptr])
                evac(oT[:, hh, o:o + n], ptb[:, :n], [ptr], [oT_r[hh]])
                release(accs_l)
            if has_s:
                so = SEQ - t0
                pb_, pbr = bank()
                for k in range(2):
                    tk.op("pe", lambda e, k=k: e.matmul(pb_[:, k * 16:(k + 1) * 16], wukT[:, hh, k * 128:(k + 1) * 128],
                                                        qn[:, so:so + 16], start=True, stop=True),
                          reads=[w2_r, qn_r], writes=[pbr])
                evac(Qs[:, 0:2, :, hh * 4:(hh + 1) * 4],
                     pb_[:, 0:32].rearrange("p (k s t) -> p k s t", k=2, s=4), [pbr], [Qs_r])
                tk.op("act", lambda e: e.copy(out=Qs[:64, 2, :, hh * 4:(hh + 1) * 4],
                                              in_=qr[:, so:so + 16].rearrange("p (s t) -> p s t", s=4)),
                      reads=[qr_r], writes=[Qs_r])
        tk.barrier()
        p2a.close()
        if has_s:
            R = 8
            pt_sb = sb(p2, "m_pt", [128, 4], I32)
            idx_sb = sb(p2, "m_idx", [128, 16, 4], I32)
            pt_r = Reg("m_pt")
            tk.dma("sp", ch_in, lambda e: e.dma_start(out=pt_sb[:], in_=ptab), writes=[pt_r])
            for gi in range(16):
                tk.op("dve", lambda e, gi=gi: e.tensor_scalar(out=idx_sb[:, gi, :], in0=pt_sb[:, :], scalar1=16.0,
                                                               scalar2=float(gi), op0=ALU.mult, op1=ALU.add),
                      reads=[pt_r], writes=[pt_r])
            NCB = 3
            cbuf = [sb(p2, "m_cb%d" % i, [128, R, ROW], F32) for i in range(NCB)]
            cbuf_r = [Reg("m_cb%d" % i) for i in range(NCB)]
            cch = [tk.chan() for _ in range(NCB)]
            ctok_s = Scratch(p2, "m_ctok", [128, R, 324], BF16, 3)
            for t_ in ctok_s.t:
                tk.op("dve", lambda e, t_=t_: e.memset(t_[:, :, 320:321], 1.0), writes=ctok_s.r)
            pd_s = Scratch(p2, "m_pd", [128, 32], BF16, 2)
            cnew = sb(p2, "m_cnew", [4, 324], BF16)
            cnew_r = Reg("m_cnew")
            oln = sb(p2, "m_oln", [32, 256], BF16)
            oln_r = Reg("m_oln")
            olT = sb(p2, "m_olT", [128, 2, 32], BF16)
            olT_r = Reg("m_olT")
            rsd = sb(p2, "m_rsd", [32, 1], F32)
            ngath = 128 // R
            so = SEQ - t0

            def gather(s, gi):
                i = (s * ngath + gi) % NCB
                tk.dma("pool", cch[i], lambda e: e.indirect_dma_start(
                    out=cbuf[i][:].rearrange("p r d -> p (r d)"), out_offset=None,
                    in_=cache,
                    in_offset=bass.IndirectOffsetOnAxis(ap=idx_sb[:, gi, s:s + 1], axis=0)),
                    reads=[pt_r], writes=[cbuf_r[i]])

            gather(0, 0)
            gather(0, 1)
            cT4_s = Scratch(p2, "m_cT4", [128, 3, 512], BF16, 3)
            pd4_s = Scratch(p2, "m_pd4", [128, 128], BF16, 3)
            batches = [(s_, gi, r0) for s_ in range(4) for gi in range(ngath) for r0 in range(0, R, 4)]
            nbt = len(batches)
            bst = [dict() for _ in range(nbt)]
            grp = {}
            accs_d = {}

            def stA(b):
                s, gi, r0 = batches[b]
                g = s * ngath + gi
                if r0 == 0:
                    nxt = g + 2
                    if nxt < 4 * ngath:
                        gather(nxt // ngath, nxt % ngath)
                    i = g % NCB
                    ctok, ctok_r = ctok_s.get()
                    tk.op("dve", lambda e: e.tensor_copy(out=ctok[:, :, 0:320], in_=cbuf[i][:, :, :]),
                          reads=[cbuf_r[i]], writes=[ctok_r])
                    grp[g] = (ctok, ctok_r)
                ctok, ctok_r = grp[g]
                cT4, cT4_r = cT4_s.get()
                for k in range(3):
                    m = 128 if k < 2 else 64
                    ptp, ptpr = bank()
                    for j in range(4):
                        tk.op("pe", lambda e, k=k, m=m, j=j: e.matmul(
                            ptp[:m, j * 128:(j + 1) * 128], ctok[:, r0 + j, k * 128:k * 128 + m], I_b,
                            start=True, stop=True), reads=[ctok_r, c_r], writes=[ptpr])
                    if k == 1:
                        tk.op("dve", lambda e, k=k, m=m, ptp=ptp: e.tensor_copy(out=cT4[:m, k, :], in_=ptp[:m, :]),
                              reads=[ptpr], writes=[cT4_r])
                    else:
                        tk.op("act", lambda e, k=k, m=m, ptp=ptp: e.copy(out=cT4[:m, k, :], in_=ptp[:m, :]),
                              reads=[ptpr], writes=[cT4_r])
                bst[b]["cT4"] = (cT4, cT4_r)

            def stB(b):
                s, gi, r0 = batches[b]
                cT4, cT4_r = bst[b]["cT4"]
                psc, pscr = bank()
                for j in range(4):
                    for k in range(3):
                        m = 128 if k < 2 else 64
                        tk.op("pe", lambda e, k=k, m=m, j=j: e.matmul(
                            psc[:, j * 32:(j + 1) * 32], cT4[:m, k, j * 128:(j + 1) * 128], Qs[:m, k, s, :],
                            start=(k == 0), stop=(k == 2)), reads=[cT4_r, Qs_r], writes=[pscr])
                pd4, pd4_r = pd4_s.get()
                tk.op("act", lambda e: e.activation(out=pd4[:, :], in_=psc[:, 0:128], func=AF.Exp, scale=MLA_SCALE),
                      reads=[pscr], writes=[pd4_r])
                bst[b]["pd4"] = (pd4, pd4_r)

            def stC(b):
                s, gi, r0 = batches[b]
                g = s * ngath + gi
                ctok, ctok_r = grp[g]
                pd4, pd4_r = bst[b]["pd4"]
                if s not in accs_d:
                    accs_d[s] = reserve(1)
                (acc, acc_r, _), = acc_l = accs_d[s]
                for j in range(4):
                    first = (gi == 0 and r0 == 0 and j == 0)
                    tk.op("pe", lambda e, j=j, first=first: e.matmul(
                        acc[:32, 0:321], pd4[:, j * 32:(j + 1) * 32], ctok[:, r0 + j, 0:321],
                        start=first, stop=False), reads=[pd4_r, ctok_r], writes=[acc_r])
                if gi == ngath - 1 and r0 == R - 4:
                    tail(s, acc, acc_r, acc_l)

            def tail(s, acc, acc_r, acc_l):
                    tcol = SEQ + s * 4
                    ptp, ptpr = bank()
                    for k in range(2):
                        tk.op("pe", lambda e, k=k: e.matmul(ptp[:4, k * 128:(k + 1) * 128], ckv_b[:, k, tcol:tcol + 4], I_b,
                                                            start=True, stop=True),
                              reads=kvregs(tcol, 4) + [c_r], writes=[ptpr])
                    tk.op("dve", lambda e: e.memset(cnew[:, :], 0.0), writes=[cnew_r])
                    tk.op("act", lambda e: e.copy(out=cnew[:4, 0:256], in_=ptp[:4, 0:256]), reads=[ptpr], writes=[cnew_r])
                    tk.op("dve", lambda e: e.memset(cnew[:4, 320:321], 1.0), writes=[cnew_r])
                    psc, pscr = bank()
                    for k in range(3):
                        m = 128 if k < 2 else 64
                        src = ckv_b[:, k, tcol:tcol + 4] if k < 2 else kr_b[:, tcol:tcol + 4]
                        tk.op("pe", lambda e, k=k, m=m, src=src: e.matmul(psc[:4, 0:32], src, Qs[:m, k, s, :],
                                                                          start=(k == 0), stop=(k == 2)),
                              reads=kvregs(tcol, 4) + [Qs_r], writes=[pscr])
                    pd, pd_r = pd_s.get()
                    tk.op("act", lambda e: e.activation(out=pd[:4, :], in_=psc[:4, 0:32], func=AF.Exp, scale=MLA_SCALE),
                          reads=[pscr], writes=[pd_r])
                    tk.op("dve", lambda e: e.tensor_tensor(out=pd[:4, :], in0=pd[:4, :], in1=DM_b[:4, :], op=ALU.mult),
                          reads=[c_r], writes=[pd_r])
                    tk.op("pe", lambda e: e.matmul(acc[:32, 0:321], pd[:4, :], cnew[:4, 0:321], start=False, stop=True),
                          reads=[pd_r, cnew_r], writes=[acc_r])
                    tk.op("dve", lambda e: e.reciprocal(out=rsd[:, :], in_=acc[:32, 320:321]), reads=[acc_r], writes=[oln_r])
                    tk.op("act", lambda e: e.activation(out=oln[:, :], in_=acc[:32, 0:256], func=AF.Copy, scale=rsd[:, 0:1]),
                          reads=[acc_r, oln_r], writes=[oln_r])
                    release(acc_l)
                    ptp, ptpr = bank()
                    for k in range(2):
                        tk.op("pe", lambda e, k=k: e.matmul(ptp[:, k * 32:(k + 1) * 32], oln[:, k * 128:(k + 1) * 128],
                                                            I_b[:32, :32], start=True, stop=True),
                              reads=[oln_r, c_r], writes=[ptpr])
                    evac(olT[:, :, :], ptp[:, 0:64].rearrange("p (k q) -> p k q", k=2), [ptpr], [olT_r])
                    pov, povr = bank()
                    for hh in range(NH):
                        for k in range(2):
                            tk.op("pe", lambda e, k=k, hh=hh: e.matmul(pov[:, hh * 4:(hh + 1) * 4], wuv[:, hh, k, :],
                                                                       olT[:, k, hh * 4:(hh + 1) * 4],
                                                                       start=(k == 0), stop=(k == 1)),
                                  reads=[w2_r, olT_r], writes=[povr])
                    tk.op("act", lambda e: e.copy(out=oT[:, :, so + s * 4:so + s * 4 + 4],
                                                  in_=pov[:, 0:32].rearrange("p (h t) -> p h t", h=NH)),
                          reads=[povr], writes=oT_r)
            for b in range(nbt + 2):
                if b < nbt:
                    stA(b)
                if 0 <= b - 1 < nbt:
                    stB(b - 1)
                if 0 <= b - 2 < nbt:
                    stC(b - 2)
        tk.barrier()
        p2.close()
        p4 = contextlib.ExitStack()
        wo = sb(p4, "m_wo", [128, KC, D], BF16)
        wo_r = Reg("m_wo")
        for k0 in range(0, KC, 4):
            tk.dma("pool", wch[2], lambda e, k0=k0: e.dma_start(
                out=wo[:, k0:k0 + 4, :], in_=m_wo.rearrange("(k p) o -> p k o", p=128)[:, k0:k0 + 4, :]),
                writes=[wo_r], cont=(k0 > 0))
        y_s = Scratch(p4, "m_y", [128, KC, 512], F32, 2, nreg=KC)
        sq_s = Scratch(p4, "m_sq4", [128, KC, 512], BF16, 1)
        t1_s = Scratch(p4, "m_t14", [128, 512], F32, 2)
        t2_s = Scratch(p4, "m_t24", [128, 512], F32, 2)
        for (ts, n) in ftiles(t0, t1):
            o = ts - t0
            y, y_r = y_s.get()
            for oc in range(KC):
                pb_, pbr = bank()
                for k in range(NH):
                    tk.op("pe", lambda e, k=k: e.matmul(pb_[:, :n], wo[:, k, oc * 128:(oc + 1) * 128], oT[:, k, o:o + n],
                                                        start=(k == 0), stop=(k == NH - 1)),
                          reads=[wo_r, oT_r[k]], writes=[pbr])
                evac(y[:, oc, :n], pb_[:, :n], [pbr], [y_r[oc]])
            postnorm_add(0, 1, y, y_r, 0, ts, n, sq_s, t1_s, t2_s)
        tk.barrier()
        p4.close()
        ph.close()


    Sst_r = [Reg("Sst%d" % i) for i in range(NH)]
    craw_r = [Reg("craw%d" % i) for i in range(24)]
    gst = {}

    def gdn_init():
        gst["Sst"] = sb(es, "Sst", [128, NH, 128], F32)
        gst["craw"] = sb(es, "craw", [128, 24, 3], F32)
        tk.op("dve", lambda e: e.memset(gst["Sst"][:], 0.0), writes=Sst_r)
        tk.op("dve", lambda e: e.memset(gst["craw"][:], 0.0), writes=craw_r)
        XD_ = F32 if INV_F32 else BF16
        gst["gm"] = sb(es, "gm", [128, 1408], XD_)
        gst["gm_r"] = Reg("gm")
        gst["i4x"] = sb(es, "i4x", [128, 512], XD_)
        if INV_F32:
            tk.dma("sp", ch_in, lambda e: e.dma_start(out=gst["gm"][:], in_=gmask_d), writes=[gst["gm_r"]])
            tk.dma("sp", ch_in, lambda e: e.dma_start(out=gst["i4x"][:], in_=consts_d[:, 1024:1536]), writes=[gst["gm_r"]])
        else:
            tk.dma("pool", ch_cb, lambda e: e.dma_start(out=gst["gm"][:], in_=gmask_d), writes=[gst["gm_r"]])
            tk.dma("pool", ch_cb, lambda e: e.dma_start(out=gst["i4x"][:], in_=consts_d[:, 1024:1536]),
                   writes=[gst["gm_r"]], cont=True)

    def v3(ap2, C, nb):
        return ap2.rearrange("p (j c) -> p j c", c=128)[:, :nb, :C]

    def gdn_phase(t0, t1):
        Sst, craw = gst["Sst"], gst["craw"]
        gm, gm_r, i4x = gst["gm"], gst["gm_r"], gst["i4x"]
        XD = F32 if INV_F32 else BF16
        I_x = I_f if INV_F32 else I_b
        TH = t1 - t0
        tl = tiles(t0, t1)
        has_s = t1 > SEQ
        THp = min(t1, SEQ) - t0
        NB = THp // 128
        ph = contextlib.ExitStack()
        u = sb(ph, "g_u", [128, KC, TH], BF16)
        u_r = Reg("g_u")
        oT = sb(ph, "g_oT", [128, NH, TH], BF16)
        oT_r = [Reg("g_oT%d" % i) for i in range(NH)]
        p0 = contextlib.ExitStack()
        sq_s = Scratch(p0, "g_sq", [128, KC, 512], BF16, 1)
        t1_s = Scratch(p0, "g_t1", [128, 512], F32, 2)
        t2_s = Scratch(p0, "g_t2", [128, 512], F32, 2)
        for (ts, n) in ftiles(t0, t1):
            rmsnorm_pre(1, 0, ts, n, u, u_r, ts - t0, sq_s, t1_s, t2_s)
        tk.barrier()
        p0.close()
        p1 = contextlib.ExitStack()
        t1_s = Scratch(p1, "g_t1b", [128, 512], F32, 1)
        t2_s = Scratch(p1, "g_t2b", [128, 512], F32, 1)
        cw = sb(p1, "g_cw", [128, 24, 4], F32)
        gout = sb(p1, "g_gout", [128, 1], F32)
        abc = sb(p1, "g_abc", [128, 2, NH], F32)
        wba = sb(p1, "g_wba", [128, KC, 16], BF16)
        gp_r = Reg("g_par")
        tk.dma("sp", ch_in, lambda e: e.dma_start(out=cw[:], in_=d_cw), writes=[gp_r])
        tk.dma("sp", ch_in, lambda e: e.dma_start(out=gout[:], in_=d_gout), writes=[gp_r])
        tk.dma("sp", ch_in, lambda e: e.dma_start(out=abc[:, 0, :], in_=d_alog[0].partition_broadcast(128)), writes=[gp_r])
        tk.dma("sp", ch_in, lambda e: e.dma_start(out=abc[:, 1, :], in_=d_dtb[0].partition_broadcast(128)), writes=[gp_r])
        tk.op("act", lambda e: e.activation(out=abc[:, 0, :], in_=abc[:, 0, :], func=AF.Exp), reads=[gp_r], writes=[gp_r])
        wba_r = Reg("g_wba")
        tk.dma("pool", wch[2], lambda e: e.dma_start(
            out=wba[:], in_=d_win.rearrange("(k p) o -> p k o", p=128)[:, :, 4096:4112]), writes=[wba_r])
        if has_s:
            cvin = sb(p1, "g_cvin", [128, 24, 4, 3], F32)
            cvs = sb(p1, "g_cvs", [128, 24, 4, 3], F32)
            cvin_r = Reg("g_cvin")
            cvs_r = Reg("g_cvs")
            tk.dma("sp", ch_in, lambda e: e.dma_start(out=cvin[:], in_=cv_in.rearrange("(c p) (s j) -> p c s j", p=128, j=3)),
                   writes=[cvin_r])
        NBS = NB + (1 if has_s else 0)
        gtok = sb(p1, "g_gtok", [128, NBS, 8], F32)
        btok = sb(p1, "g_btok", [128, NBS, 8], F32)
        nbtok = sb(p1, "g_nbtok", [128, NBS, 8], F32)
        gt_r = Reg("g_gt")
        xt_ = sb(p1, "g_xt", [128, NBS, 8], F32)
        pba, pbar = bank()
        for b in range(NB):
            for k in range(KC):
                tk.op("pe", lambda e, k=k, b=b: e.matmul(pba[:, b * 16:(b + 1) * 16], u[:, k, b * 128:(b + 1) * 128],
                                                         wba[:, k, :], start=(k == 0), stop=(k == KC - 1)),
                      reads=[u_r, wba_r], writes=[pbar])
        if has_s:
            pbs, pbsr = bank()
            for s_ in range(4):
                for k in range(KC):
                    tk.op("pe", lambda e, k=k, s_=s_: e.matmul(pbs[:4, s_ * 16:(s_ + 1) * 16],
                                                               u[:, k, THp + 4 * s_:THp + 4 * s_ + 4], wba[:, k, :],
                                                               start=(k == 0), stop=(k == KC - 1)),
                          reads=[u_r, wba_r], writes=[pbsr])
        def ba_post(P_, src3, dstsl, preg):
            gt, bt, nbt, xt = dstsl
            nblk = src3.shape[1]
            tk.op("act", lambda e: e.activation(out=bt, in_=src3[:, :, 0:8], func=AF.Sigmoid), reads=[preg], writes=[gt_r])
            tk.op("dve", lambda e: e.tensor_scalar(out=nbt, in0=bt, scalar1=-1.0, scalar2=None, op0=ALU.mult),
                  reads=[gt_r], writes=[gt_r])
            for b in range(nblk):
                tk.op("dve", lambda e, b=b: e.tensor_tensor(out=xt[:, b, :], in0=src3[:, b, 8:16], in1=abc[:P_, 1, :],
                                                            op=ALU.add), reads=[preg, gp_r], writes=[gt_r])
            tk.op("act", lambda e: e.activation(out=xt, in_=xt, func=AF.Exp), reads=[gt_r], writes=[gt_r])
            tk.op("act", lambda e: e.activation(out=xt, in_=xt, func=AF.Ln, bias=epst[:P_, 1:2]), reads=[gt_r, c_r],
                  writes=[gt_r])
            for b in range(nblk):
                tk.op("dve", lambda e, b=b: e.scalar_tensor_tensor(out=gt[:, b, :], in0=xt[:, b, :], scalar=-1.0,
                                                                   in1=abc[:P_, 0, :], op0=ALU.mult, op1=ALU.mult),
                      reads=[gt_r, gp_r], writes=[gt_r])

        ba_post(128, pba[:, 0:NB * 16].rearrange("p (b x) -> p b x", x=16),
                (gtok[:, 0:NB, :], btok[:, 0:NB, :], nbtok[:, 0:NB, :], xt_[:, 0:NB, :]), pbar)
        if has_s:
            gts = sb(p1, "g_gts", [4, 4, 8], F32)
            bts = sb(p1, "g_bts", [4, 4, 8], F32)
            nbts = sb(p1, "g_nbts", [4, 4, 8], F32)
            xts = sb(p1, "g_xts", [4, 4, 8], F32)
            ba_post(4, pbs[:4, 0:64].rearrange("p (b x) -> p b x", x=16), (gts[:], bts[:], nbts[:], xts[:]), pbsr)
        wsl = [sb(p1, "g_wsl%d" % i, [128, KC, 4, 128], BF16) for i in range(2)]
        wsl_r = [Reg("g_wsl%d" % i) for i in range(2)]
        d_win_v = d_win.rearrange("(k p) o -> p k o", p=128)

        def load_wsl(hh):
            s_ = hh % 2
            for w_ in range(4):
                col = w_ * 1024 + hh * 128
                tk.dma("pool", wch[s_], lambda e, w_=w_, col=col: e.dma_start(
                    out=wsl[s_][:, :, w_, :], in_=d_win_v[:, :, col:col + 128]), writes=[wsl_r[s_]], cont=(w_ > 0))

        raw = sb(p1, "g_raw", [128, 3 + THp], F32)
        raw_r = Reg("g_raw")
        raws = sb(p1, "g_raws", [128, 4, 7], F32)
        raws_r = Reg("g_raws")
        acc = sb(p1, "g_acc", [128, TH], F32)
        acc_r = Reg("g_acc")
        sqb_s = Scratch(p1, "g_sqb", [128, 512], BF16, 2)
        qn_s = Scratch(p1, "g_qn", [128, TH], BF16, 2)
        kn_s = Scratch(p1, "g_kn", [128, TH], BF16, 2)
        vv_s = Scratch(p1, "g_vv", [128, TH], BF16, 2)
        zs_s = Scratch(p1, "g_zs", [128, TH], BF16, 2)
        if has_s:
            Ssm_s = Scratch(p1, "g_Ssm", [128, 4, 128], F32, 2, nreg=4, chan=True)
            ssl_ch = [tk.chan() for _ in range(2)]

        class BB:
            pass

        def make_bb(tag, wide):
            b_ = BB()
            CW = 128 if wide else 4
            F1 = sb(p1, "b_F1" + tag, [128, 4, CW], F32)
            F2 = sb(p1, "b_F2" + tag, [128, 4, CW], F32)
            b_.R, b_.gB = F1, F2
            F1b, F2b = F1[:].bitcast(BF16), F2[:].bitcast(BF16)
            b_.LT, b_.X = F1b[:, :, 0:CW], F1b[:, :, CW:2 * CW]
            b_.W1, b_.W2 = F2b[:, :, 0:CW], F2b[:, :, CW:2 * CW]
            b_.sc = sb(p1, "b_sc" + tag, [128, 4, 4], F32)
            b_.gl = sb(p1, "b_gl" + tag, [128, 4, 4], F32)
            b_.E1 = sb(p1, "b_E1" + tag, [128, 4, CW], BF16)
            b_.E2 = sb(p1, "b_E2" + tag, [128, 4, CW], BF16)
            b_.Ao, b_.AoT = b_.E1, b_.E2
            b_.Lp = sb(p1, "b_Lp" + tag, [128, 4, CW], BF16)
            b_.XT = sb(p1, "b_XT" + tag, [128, 4, CW], BF16)
            b_.qkt = sb(p1, "b_qkt" + tag, [128, 4, CW], BF16)
            b_.qd = sb(p1, "b_qd" + tag, [128, 4, CW], BF16)
            AOFF = _os.environ.get("AOFF", "")
            if wide:
                b_.egbc = sb(p1, "b_eg" + tag, [128, 4, 128], BF16)
                b_.utok = b_.egbc if "u" not in AOFF else sb(p1, "b_ut" + tag, [128, 4, 128], BF16)
                b_.wtok = b_.Lp if "w" not in AOFF else sb(p1, "b_wt" + tag, [128, 4, 128], BF16)
                b_.nWk = F1b[:, :, 0:128] if "n" not in AOFF else sb(p1, "b_nw" + tag, [128, 4, 128], BF16)
                if "a" in AOFF:
                    b_.Ao = sb(p1, "b_Ao" + tag, [128, 4, 128], BF16)
                    b_.AoT = sb(p1, "b_AoT" + tag, [128, 4, 128], BF16)
                if "x" in AOFF:
                    b_.LT = sb(p1, "b_LT" + tag, [128, 4, 128], BF16)
                    b_.X = sb(p1, "b_X" + tag, [128, 4, 128], BF16)
                    b_.W1 = sb(p1, "b_W1" + tag, [128, 4, 128], BF16)
                    b_.W2 = sb(p1, "b_W2" + tag, [128, 4, 128], BF16)
            else:
                b_.egbc = sb(p1, "b_eg" + tag, [128, 4, 4], BF16)
                b_.utok = sb(p1, "b_ut" + tag, [128, 4, 128], BF16)
                b_.wtok = sb(p1, "b_wt" + tag, [128, 4, 128], BF16)
                b_.nWk = sb(p1, "b_nw" + tag, [128, 4, 128], BF16)
            for nm in ("kdec", "kbg", "vb"):
                setattr(b_, nm, sb(p1, "b_%s%s" % (nm, tag), [128, 4, 128], BF16))
            rg = {nm: Reg("b_%s%s" % (nm, tag)) for nm in ("F1", "F2", "E1", "E2", "eg", "Lp", "XT", "qkt", "qd", "sc",
                                                          "kdec", "kbg", "vb", "ut", "wt", "nw")}
            b_.r = {"R": rg["F1"], "LT": rg["F1"], "X": rg["F1"], "gB": rg["F2"], "W1": rg["F2"], "W2": rg["F2"],
                    "E1": rg["E1"], "Ao": rg["E1"], "E2": rg["E2"], "AoT": rg["E2"], "Lp": rg["Lp"], "XT": rg["XT"],
                    "qkt": rg["qkt"], "qd": rg["qd"], "sc": rg["sc"], "kdec": rg["kdec"], "kbg": rg["kbg"],
                    "vb": rg["vb"]}
            if wide:
                b_.r.update({"eg": rg["eg"], "utok": rg["eg"], "wtok": rg["Lp"], "nWk": rg["F1"]})
            else:
                b_.r.update({"eg": rg["eg"], "utok": rg["ut"], "wtok": rg["wt"], "nWk": rg["nw"]})
            return b_

        bbs = {"p0": make_bb("p0", True), "p1": make_bb("p1", True)}
        if has_s:
            bbs["s"] = make_bb("s", False)
        bb_i = [0]
        Sb_s = Scratch(p1, "g_Sb", [128, 128], BF16, 2)
        vn_s = Scratch(p1, "g_vn", [128, 128], BF16, 2)
        on_s = Scratch(p1, "g_on", [128, 128], BF16, 2)
        jk_s = Scratch(p1, "g_jk", [128, 128], BF16, 2)
        ss_s = Scratch(p1, "g_ss", [128, 4], F32, 2)
        og_s = Scratch(p1, "g_og", [128, 128], BF16, 2)

        def gdn_pre(b_, hh, C, nb, o0, gmat, bcols, qn, qn_r, kn, kn_r, vv, vv_r):
            r = b_.r
            NC_ = nb * C
            gcols = [gmat[:, j:j + 1] for j in range(nb)]

            def flat(t, P_=C):
                return t[:].rearrange("p j c -> p (j c)")[:P_, 0:NC_]

            def cv(t, P_=C):
                return flat(t, P_).rearrange("p (j c) -> p j c", c=C)

            def cvp(ps, P_=C):
                return ps[:P_, 0:NC_].rearrange("p (j c) -> p j c", c=C)

            for j in range(nb):
                tk.op("dve", lambda e, j=j: e.tensor_scalar(out=cv(b_.R)[:, j, :], in0=MS_f[:C, :C], scalar1=gcols[j],
                                                            scalar2=None, op0=ALU.mult), reads=[c_r, gt_r], writes=[r["R"]])
                tk.op("dve", lambda e, j=j: e.tensor_scalar(out=cv(b_.gB)[:, j, :], in0=U_f[:C, :C], scalar1=gcols[j],
                                                            scalar2=None, op0=ALU.mult), reads=[c_r, gt_r], writes=[r["gB"]])
            k1, k1r = bank()
            k2, k2r = bank()
            k3, k3r = bank()
            k4, k4r = bank()
            tk.op("pe", lambda e: e.matmul(k1[:C, 0:NC_], U_f[:C, :C], flat(b_.R), start=True, stop=True),
                  reads=[c_r, r["R"]], writes=[k1r])
            tk.op("pe", lambda e: e.matmul(k2[:C, 0:NC_], MS_f[:C, :C], flat(b_.gB), start=True, stop=True),
                  reads=[c_r, r["gB"]], writes=[k2r])
            tk.op("pe", lambda e: e.matmul(k3[:, 0:NC_], ONE_f[:C, :], flat(b_.gB), start=True, stop=True),
                  reads=[c_r, r["gB"]], writes=[k3r])
            tk.op("pe", lambda e: e.matmul(k4[:C, 0:nb], U_f[:C, :C], gmat, start=True, stop=True),
                  reads=[c_r, gt_r], writes=[k4r])
            tk.op("pe", lambda e: e.matmul(k4[:, 16:16 + nb], ONE_f[:C, :], gmat, start=True, stop=True),
                  reads=[c_r, gt_r], writes=[k4r])
            tk.op("act", lambda e: e.activation(out=cv(b_.E1), in_=cvp(k1), func=AF.Exp), reads=[k1r], writes=[r["E1"]])
            tk.op("act", lambda e: e.activation(out=cv(b_.E2), in_=cvp(k2), func=AF.Exp), reads=[k2r], writes=[r["E2"]])
            tk.op("act", lambda e: e.activation(out=cv(b_.egbc, 128), in_=cvp(k3, 128), func=AF.Exp),
                  reads=[k3r], writes=[r["eg"]])
            tk.op("act", lambda e: e.activation(out=b_.sc[:C, :nb, 0:1], in_=k4[:C, 0:nb].unsqueeze(2), func=AF.Exp),
                  reads=[k4r], writes=[r["sc"]])
            tk.op("act", lambda e: e.activation(out=b_.gl[:, :nb, 0:1], in_=k4[:, 16:16 + nb].unsqueeze(2), func=AF.Exp),
                  reads=[k4r], writes=[r["sc"]])
            tk.op("dve", lambda e: e.tensor_copy(out=b_.sc[:C, :nb, 1:2], in_=cv(b_.E2)[:, :, C - 1:C]),
                  reads=[r["E2"]], writes=[r["sc"]])
            tk.op("pool", lambda e: e.tensor_tensor(out=cv(b_.E1), in0=cv(b_.E1),
                                                    in1=cb[:C, 256:256 + C].unsqueeze(1).to_broadcast([C, nb, C]),
                                                    op=ALU.mult), reads=[c_r], writes=[r["E1"]])
            tk.op("pool", lambda e: e.tensor_tensor(out=cv(b_.E2), in0=cv(b_.E2),
                                                    in1=cb[:C, 128:128 + C].unsqueeze(1).to_broadcast([C, nb, C]),
                                                    op=ALU.mult), reads=[c_r, r["sc"]], writes=[r["E2"]])
            for j in range(nb):
                tk.op("dve", lambda e, j=j: e.tensor_tensor(out=b_.sc[:C, j, 2:3], in0=b_.sc[:C, j, 0:1], in1=bcols[j],
                                                            op=ALU.mult), reads=[gt_r], writes=[r["sc"]])
            yield
            k5, k5r = bank()
            k6, k6r = bank()
            k7, k7r = bank()
            k8, k8r = bank()
            for j in range(nb):
                cs = slice(j * 128, j * 128 + C)
                ts_ = slice(o0 + j * C, o0 + (j + 1) * C)
                tk.op("pe", lambda e, cs=cs, ts_=ts_: e.matmul(k5[:C, cs], kn[:, ts_], kn[:, ts_], start=True, stop=True),
                      reads=[kn_r], writes=[k5r])
                tk.op("pe", lambda e, cs=cs, ts_=ts_: e.matmul(k6[:C, cs], kn[:, ts_], qn[:, ts_], start=True, stop=True),
                      reads=[kn_r, qn_r], writes=[k6r])
                tk.op("pe", lambda e, j=j, ts_=ts_: e.matmul(k7[:C, j * 128:(j + 1) * 128], kn[:, ts_], I_b,
                                                             start=True, stop=True), reads=[kn_r, c_r], writes=[k7r])
                tk.op("pe", lambda e, j=j, ts_=ts_: e.matmul(k8[:C, j * 128:(j + 1) * 128], vv[:, ts_], I_b,
                                                             start=True, stop=True), reads=[vv_r, c_r], writes=[k8r])
            for j in range(nb):
                cs = slice(j * 128, j * 128 + C)
                tk.op("dve", lambda e, j=j, cs=cs: e.scalar_tensor_tensor(
                    out=b_.Lp[:C, j, :C], in0=k5[:C, cs], scalar=bcols[j], in1=cv(b_.E1)[:, j, :],
                    op0=ALU.mult, op1=ALU.mult), reads=[k5r, r["E1"], gt_r], writes=[r["Lp"]])
                tk.op("act", lambda e, j=j: e.activation(out=b_.kdec[:C, j, :], in_=k7[:C, j * 128:(j + 1) * 128],
                                                         func=AF.Copy, scale=b_.sc[:C, j, 1:2]),
                      reads=[k7r, r["sc"]], writes=[r["kdec"]])
                tk.op("dve", lambda e, j=j: e.tensor_scalar(out=b_.kbg[:C, j, :], in0=k7[:C, j * 128:(j + 1) * 128],
                                                            scalar1=b_.sc[:C, j, 2:3], scalar2=None, op0=ALU.mult),
                      reads=[k7r, r["sc"]], writes=[r["kbg"]])
                tk.op("act", lambda e, j=j: e.activation(out=b_.vb[:C, j, :], in_=k8[:C, j * 128:(j + 1) * 128],
                                                         func=AF.Copy, scale=bcols[j]),
                      reads=[k8r, gt_r], writes=[r["vb"]])
            tk.op("dve", lambda e: e.tensor_tensor(out=b_.qkt[:C, :nb, :C], in0=v3(k6[:C, :], C, nb),
                                                   in1=cv(b_.E2), op=ALU.mult),
                  reads=[k6r, r["E2"]], writes=[r["qkt"]])
            tk.op("pool", lambda e: e.tensor_tensor(
                out=b_.qd[:, :nb, :C], in0=qn[:, o0:o0 + nb * C].rearrange("p (j c) -> p j c", c=C),
                in1=cv(b_.egbc, 128), op=ALU.mult), reads=[qn_r, r["eg"]], writes=[r["qd"]])
            yield
            def bc(off):
                return gm[:C, off:off + C].unsqueeze(1).to_broadcast([C, nb, C])

            def mm4(lhs, rhs, lr, rr, PO=C, wl=C, wr=C):
                kx, kxr = bank()
                for j in range(nb):
                    tk.op("pe", lambda e, j=j: e.matmul(kx[:PO, j * 128:j * 128 + wr], lhs[:C, j, :wl],
                                                        rhs[:C, j, :wr], start=True, stop=True),
                          reads=[lr, rr], writes=[kxr])
                return kx, kxr

            kt, ktr = bank()
            for j in range(nb):
                cs = slice(j * 128, j * 128 + C)
                tk.op("pe", lambda e, j=j, cs=cs: e.matmul(kt[:C, cs], b_.Lp[:C, j, :C], I_x[:C, :C], start=True, stop=True),
                      reads=[r["Lp"], c_r], writes=[ktr])
            tk.op("act", lambda e: e.copy(out=b_.LT[:C, :nb, :C], in_=v3(kt[:C, :], C, nb)), reads=[ktr], writes=[r["LT"]])
            tk.op("dve", lambda e: e.scalar_tensor_tensor(out=b_.Ao[:C, :nb, :C], in0=b_.Lp[:C, :nb, :C], scalar=-1.0,
                                                          in1=bc(0), op0=ALU.mult, op1=ALU.mult),
                  reads=[r["Lp"], gm_r], writes=[r["Ao"]])
            tk.op("dve", lambda e: e.scalar_tensor_tensor(out=b_.AoT[:C, :nb, :C], in0=b_.LT[:C, :nb, :C], scalar=-1.0,
                                                          in1=bc(768), op0=ALU.mult, op1=ALU.mult),
                  reads=[r["LT"], gm_r], writes=[r["AoT"]])
            tk.op("dve", lambda e: e.tensor_tensor(out=b_.X[:C, :nb, :C], in0=b_.Ao[:C, :nb, :C],
                                                   in1=v3(i4x[:C, :], C, nb), op=ALU.add),
                  reads=[r["Ao"], gm_r], writes=[r["X"]])
            tk.op("dve", lambda e: e.tensor_tensor(out=b_.XT[:C, :nb, :C], in0=b_.AoT[:C, :nb, :C],
                                                   in1=v3(i4x[:C, :], C, nb), op=ALU.add),
                  reads=[r["AoT"], gm_r], writes=[r["XT"]])
            kx, kxr = mm4(b_.AoT, b_.Ao, r["AoT"], r["Ao"])
            tk.op("act", lambda e: e.copy(out=b_.W1[:C, :nb, :C], in_=v3(kx[:C, :], C, nb)), reads=[kxr], writes=[r["W1"]])
            yield
            kxa, kxar = mm4(b_.XT, b_.W1, r["XT"], r["W1"])
            kxb, kxbr = mm4(b_.W1, b_.XT, r["W1"], r["XT"])
            tk.op("dve", lambda e: e.tensor_tensor(out=b_.X[:C, :nb, :C], in0=v3(kxa[:C, :], C, nb),
                                                   in1=b_.X[:C, :nb, :C], op=ALU.add),
                  reads=[kxar, r["X"]], writes=[r["X"]])
            tk.op("dve", lambda e: e.tensor_tensor(out=b_.XT[:C, :nb, :C], in0=v3(kxb[:C, :], C, nb),
                                                   in1=b_.XT[:C, :nb, :C], op=ALU.add),
                  reads=[kxbr, r["XT"]], writes=[r["XT"]])
            yield
            levels = [b for b in (4, 8, 16, 32, 64) if 2 * b <= C]
            for li, b in enumerate(levels):
                last = (li == len(levels) - 1)
                off = 128 + li * 128
                kx, kxr = bank()
                for j in range(nb):
                    cs = slice(j * 128, j * 128 + C)
                    tk.op("pe", lambda e, j=j, cs=cs: e.matmul(kx[:C, cs], b_.LT[:C, j, :C], b_.X[:C, j, :C],
                                                               start=True, stop=False), reads=[r["LT"], r["X"]], writes=[kxr])
                    tk.op("pe", lambda e, j=j, cs=cs: e.matmul(kx[:C, cs], I_b[:C, :C], I_b[:C, :C],
                                                               start=False, stop=True), reads=[c_r], writes=[kxr])
                tk.op("dve", lambda e, kx=kx, off=off: e.tensor_tensor(out=b_.W1[:C, :nb, :C], in0=v3(kx[:C, :], C, nb),
                                                                       in1=bc(off), op=ALU.mult),
                      reads=[kxr, gm_r], writes=[r["W1"]])
                yield
                kb_, kbr = mm4(b_.W1, b_.XT, r["W1"], r["XT"])
                if not last:
                    ka_, kar = mm4(b_.XT, b_.W1, r["XT"], r["W1"])
                    tk.op("act", lambda e, ka_=ka_: e.copy(out=b_.X[:C, :nb, :C], in_=v3(ka_[:C, :], C, nb)),
                          reads=[kar], writes=[r["X"]])
                tk.op("dve", lambda e, kb_=kb_: e.tensor_copy(out=b_.XT[:C, :nb, :C], in_=v3(kb_[:C, :], C, nb)),
                      reads=[kbr], writes=[r["XT"]])
                yield
            TTc, rTT = b_.XT, r["XT"]
            ku, kur = bank()
            kw, kwr = bank()
            for j in range(nb):
                tk.op("pe", lambda e, j=j: e.matmul(ku[:C, j * 128:(j + 1) * 128], TTc[:C, j, :C], b_.vb[:C, j, :],
                                                    start=True, stop=True), reads=[rTT, r["vb"]], writes=[kur])
                tk.op("pe", lambda e, j=j: e.matmul(kw[:C, j * 128:(j + 1) * 128], TTc[:C, j, :C], b_.kbg[:C, j, :],
                                                    start=True, stop=True), reads=[rTT, r["kbg"]], writes=[kwr])
            tk.op("act", lambda e: e.copy(out=b_.utok[:C, :nb, :], in_=ku[:C, 0:nb * 128].rearrange("p (j c) -> p j c", c=128)),
                  reads=[kur], writes=[r["utok"]])
            tk.op("dve", lambda e: e.tensor_copy(out=b_.wtok[:C, :nb, :], in_=kw[:C, 0:nb * 128].rearrange("p (j c) -> p j c", c=128)),
                  reads=[kwr], writes=[r["wtok"]])
            yield
            kk, kkr = bank()
            kq, kqr = bank()
            for j in range(nb):
                tk.op("pe", lambda e, j=j: e.matmul(kk[:, j * 128:(j + 1) * 128], b_.wtok[:C, j, :], b_.kdec[:C, j, :],
                                                    start=True, stop=True), reads=[r["wtok"], r["kdec"]], writes=[kkr])
                tk.op("pe", lambda e, j=j: e.matmul(kq[:, j * 128:j * 128 + C], b_.wtok[:C, j, :], b_.qkt[:C, j, :C],
                                                    start=True, stop=True), reads=[r["wtok"], r["qkt"]], writes=[kqr])
            tk.op("act", lambda e: e.activation(out=b_.nWk[:, :nb, :], in_=kk[:, 0:nb * 128].rearrange("p (j c) -> p j c", c=128),
                                                func=AF.Copy, scale=-1.0), reads=[kkr], writes=[r["nWk"]])
            tk.op("dve", lambda e: e.tensor_tensor(out=b_.qd[:, :nb, :C], in0=b_.qd[:, :nb, :C], in1=v3(kq[:, :], C, nb),
                                                   op=ALU.subtract), reads=[kqr, r["qd"]], writes=[r["qd"]])
            yield
        def gdn_scan(b_, hh, C, nb, o0, S_f, S_regs, zs, zs_r, carry):
            r = b_.r
            Sb, Sb_r = None, None
            for j in range(nb):
                Sf, Sreg = S_f[j], S_regs[j]
                if Sb is None or not carry:
                    Sb, Sb_r = Sb_s.get()
                    tk.op("dve", lambda e, Sb=Sb, Sf=Sf: e.tensor_copy(out=Sb[:, :], in_=Sf), reads=[Sreg], writes=[Sb_r])
                cs = slice(o0 + j * C, o0 + (j + 1) * C)
                ks_, ksr = bank()
                tk.op("pe", lambda e, j=j: e.matmul(ks_[:, 0:128], b_.kdec[:C, j, :], b_.utok[:C, j, :], start=True, stop=False),
                      reads=[r["kdec"], r["utok"]], writes=[ksr])
                tk.op("pe", lambda e, j=j, Sb=Sb: e.matmul(ks_[:, 0:128], b_.nWk[:, j, :], Sb[:, :], start=False, stop=True),
                      reads=[r["nWk"], Sb_r], writes=[ksr])
                ko, kor = bank()
                tk.op("pe", lambda e, j=j: e.matmul(ko[:C, 0:128], b_.qkt[:C, j, :C], b_.utok[:C, j, :], start=True, stop=False),
                      reads=[r["qkt"], r["utok"]], writes=[kor])
                tk.op("pe", lambda e, j=j, Sb=Sb: e.matmul(ko[:C, 0:128], b_.qd[:, j, :C], Sb[:, :], start=False, stop=True),
                      reads=[r["qd"], Sb_r], writes=[kor])
                tk.op("dve", lambda e, j=j, Sf=Sf: e.scalar_tensor_tensor(out=Sf, in0=Sf, scalar=b_.gl[:, j, 0:1],
                                                                          in1=ks_[:, 0:128], op0=ALU.mult, op1=ALU.add),
                      reads=[ksr, r["sc"], Sreg], writes=[Sreg])
                if carry and j + 1 < nb:
                    Sb, Sb_r = Sb_s.get()
                    tk.op("dve", lambda e, Sb=Sb, Sf=Sf: e.tensor_copy(out=Sb[:, :], in_=Sf), reads=[Sreg], writes=[Sb_r])
                jk, jk_r = jk_s.get()
                ss, ss_r = ss_s.get()
                tk.op("act", lambda e, jk=jk, ss=ss: e.activation(out=jk[:C, :], in_=ko[:C, 0:128], func=AF.Square,
                                                                  accum_out=ss[:C, 0:1]), reads=[kor], writes=[jk_r, ss_r])
                tk.op("act", lambda e, ss=ss: e.activation(out=ss[:C, 1:2], in_=ss[:C, 0:1], func=AF.Ln,
                                                           bias=epst[:C, 0:1], scale=1.0 / 128), reads=[c_r], writes=[ss_r])
                tk.op("act", lambda e, ss=ss: e.activation(out=ss[:C, 2:3], in_=ss[:C, 1:2], func=AF.Exp, scale=-0.5),
                      writes=[ss_r])
                on, on_r = on_s.get()
                tk.op("act", lambda e, on=on, ss=ss: e.activation(out=on[:C, :], in_=ko[:C, 0:128], func=AF.Copy,
                                                                  scale=ss[:C, 2:3]), reads=[kor, ss_r], writes=[on_r])
                kp, kpr = bank()
                tk.op("pe", lambda e, on=on: e.matmul(kp[:, 0:C], on[:C, :], I_b[:C, :C], start=True, stop=True),
                      reads=[on_r, c_r], writes=[kpr])
                og, og_r = og_s.get()
                tk.op("act", lambda e, og=og: e.activation(out=og[:, 0:C], in_=kp[:, 0:C], func=AF.Copy, scale=gout[:, 0:1]),
                      reads=[kpr, gp_r], writes=[og_r])
                tk.op("pool", lambda e, og=og, cs=cs: e.tensor_tensor(out=oT[:, hh, cs], in0=og[:, 0:C], in1=zs[:, cs],
                                                                     op=ALU.mult), reads=[og_r, zs_r], writes=[oT_r[hh]])
                yield

        heads = {}

        def bulk(hh):
            if hh + 1 < NH:
                load_wsl(hh + 1)
            ws_, ws_r = wsl[hh % 2], wsl_r[hh % 2]
            outs = {}
            for w_ in range(3):
                cidx = w_ * 8 + hh
                tk.op("dve", lambda e: e.tensor_copy(out=raw[:, 0:3], in_=craw[:, cidx, :]),
                      reads=[craw_r[cidx]], writes=[raw_r])
                for (ts, n) in tl:
                    o = ts - t0
                    pb_, pbr = bank()
                    for k in range(KC):
                        tk.op("pe", lambda e, k=k: e.matmul(pb_[:, :n], ws_[:, k, w_, :], u[:, k, o:o + n],
                                                            start=(k == 0), stop=(k == KC - 1)),
                              reads=[ws_r, u_r], writes=[pbr])
                    if ts < SEQ:
                        evac(raw[:, 3 + o:3 + o + n], pb_[:, :n], [pbr], [raw_r])
                    else:
                        tk.op("act", lambda e: e.copy(out=raws[:, :, 3:7], in_=pb_[:, 0:16].rearrange("p (s t) -> p s t", s=4)),
                              reads=[pbr], writes=[raws_r])
                        tk.op("dve", lambda e: e.tensor_copy(out=raws[:, :, 0:3], in_=cvin[:, cidx, :, :]),
                              reads=[cvin_r], writes=[raws_r])
                        tk.op("dve", lambda e: e.tensor_copy(out=cvs[:, cidx, :, :], in_=raws[:, :, 4:7]),
                              reads=[raws_r], writes=[cvs_r])
                    yield
                tk.op("dve", lambda e: e.tensor_copy(out=craw[:, cidx, :], in_=raw[:, THp:THp + 3]),
                      reads=[raw_r], writes=[craw_r[cidx]])
                tk.op("dve", lambda e: e.tensor_scalar(out=acc[:, 0:THp], in0=raw[:, 0:THp], scalar1=cw[:, cidx, 0:1],
                                                       scalar2=None, op0=ALU.mult), reads=[raw_r, gp_r], writes=[acc_r])
                for j in range(1, 4):
                    tk.op("dve", lambda e, j=j: e.scalar_tensor_tensor(
                        out=acc[:, 0:THp], in0=raw[:, j:j + THp], scalar=cw[:, cidx, j:j + 1], in1=acc[:, 0:THp],
                        op0=ALU.mult, op1=ALU.add), reads=[raw_r, gp_r], writes=[acc_r])
                if has_s:
                    accs = acc[:, THp:THp + 16].rearrange("p (s t) -> p s t", s=4)
                    tk.op("dve", lambda e: e.tensor_scalar(out=accs, in0=raws[:, :, 0:4], scalar1=cw[:, cidx, 0:1],
                                                           scalar2=None, op0=ALU.mult), reads=[raws_r, gp_r], writes=[acc_r])
                    for j in range(1, 4):
                        tk.op("dve", lambda e, j=j: e.scalar_tensor_tensor(
                            out=accs, in0=raws[:, :, j:j + 4], scalar=cw[:, cidx, j:j + 1], in1=accs,
                            op0=ALU.mult, op1=ALU.add), reads=[raws_r, gp_r], writes=[acc_r])
                yield
                if w_ == 2:
                    vv, vv_r = vv_s.get()
                    tk.op("act", lambda e: e.activation(out=vv[:, :], in_=acc[:, :], func=AF.Silu), reads=[acc_r], writes=[vv_r])
                    outs[2] = (vv, vv_r)
                else:
                    tk.op("act", lambda e: e.activation(out=acc[:, :], in_=acc[:, :], func=AF.Silu), reads=[acc_r], writes=[acc_r])
                    dst, dst_r = (qn_s if w_ == 0 else kn_s).get()
                    outs[w_] = (dst, dst_r)
                    for (ts, n) in tl:
                        o = ts - t0
                        sqb, sqb_r = sqb_s.get()
                        tk.op("act", lambda e: e.activation(out=sqb[:, :n], in_=acc[:, o:o + n], func=AF.Square),
                              reads=[acc_r], writes=[sqb_r])
                        pb_, pbr = bank()
                        tk.op("pe", lambda e: e.matmul(pb_[:, :n], ONE_b, sqb[:, :n], start=True, stop=True),
                              reads=[sqb_r, c_r], writes=[pbr])
                        ta, tar = t1_s.get()
                        tb, tbr = t2_s.get()
                        rstd_from_ss(pb_[:, :n], pbr, 128, n, 1.0, ta, tar, tb, tbr)
                        if w_ == 0:
                            tk.op("dve", lambda e: e.scalar_tensor_tensor(
                                out=dst[:, o:o + n], in0=acc[:, o:o + n], scalar=128.0 ** -0.5, in1=tb[:, :n],
                                op0=ALU.mult, op1=ALU.mult), reads=[acc_r, tbr], writes=[dst_r])
                        else:
                            tk.op("dve", lambda e: e.tensor_tensor(out=dst[:, o:o + n], in0=acc[:, o:o + n], in1=tb[:, :n],
                                                                   op=ALU.mult), reads=[acc_r, tbr], writes=[dst_r])
                        yield
            zs, zs_r = zs_s.get()
            for (ts, n) in tl:
                o = ts - t0
                pb_, pbr = bank()
                for k in range(KC):
                    tk.op("pe", lambda e, k=k: e.matmul(pb_[:, :n], ws_[:, k, 3, :], u[:, k, o:o + n],
                                                        start=(k == 0), stop=(k == KC - 1)),
                          reads=[ws_r, u_r], writes=[pbr])
                tk.op("act", lambda e: e.activation(out=zs[:, o:o + n], in_=pb_[:, :n], func=AF.Silu),
                      reads=[pbr], writes=[zs_r])
                yield
            heads[hh] = (outs[0], outs[1], outs[2], (zs, zs_r))
            yield

        def seq_gens(*gs):
            for g_ in gs:
                yield from g_

        def head_pres(hh):
            (qn, qn_r), (kn, kn_r), (vv, vv_r), (zs, zs_r) = heads[hh]
            pres, scans_p, scan_s = [], [], None
            for bi, b0 in enumerate(range(0, NB, 4)):
                nb = min(4, NB - b0)
                b_ = bbs["p%d" % (bi % 2)]
                pres.append(gdn_pre(b_, hh, 128, nb, b0 * 128, gtok[:, b0:b0 + nb, hh],
                                    [btok[:, b0 + j, hh:hh + 1] for j in range(nb)], qn, qn_r, kn, kn_r, vv, vv_r))
                scans_p.append(gdn_scan(b_, hh, 128, nb, b0 * 128, [Sst[:, hh, :]] * nb, [Sst_r[hh]] * nb, zs, zs_r, True))
            if has_s:
                Ssm, Ssm_r = Ssm_s.get()
                ch_ = Ssm_s.chan()
                tk.dma("sp", ch_, lambda e: e.dma_start(out=Ssm[:], in_=st_in[:, hh, :, :].rearrange("s d e -> d s e")),
                       writes=Ssm_r)
                pres.append(gdn_pre(bbs["s"], hh, 4, 4, THp, gts[:4, 0:4, hh],
                                    [bts[:4, j, hh:hh + 1] for j in range(4)], qn, qn_r, kn, kn_r, vv, vv_r))

                def sample_scan(Ssm=Ssm, Ssm_r=Ssm_r, ch_=ch_):
                    yield from gdn_scan(bbs["s"], hh, 4, 4, THp, [Ssm[:, j, :] for j in range(4)],
                                        [Ssm_r[j] for j in range(4)], zs, zs_r, False)
                    tk.dma("sp", ch_, lambda e: e.dma_start(out=st_s[:, hh, :, :].rearrange("s d e -> d s e"), in_=Ssm[:]),
                           reads=Ssm_r)
                    yield
                scan_s = sample_scan()
            assert len(scans_p) <= 2
            return pres, scans_p, scan_s

        def run_il(gens):
            gens = list(gens)
            while gens:
                for g_ in list(gens):
                    try:
                        next(g_)
                    except StopIteration:
                        gens.remove(g_)

        load_wsl(0)
        run_il([bulk(0)])
        for hh in range(NH):
            pres, scans_p, scan_s = head_pres(hh)
            gl_ = list(pres)
            if hh + 1 < NH:
                gl_.append(bulk(hh + 1))
            run_il(gl_)
            sl_ = [seq_gens(*scans_p)]
            if scan_s is not None:
                sl_.append(scan_s)
            run_il(sl_)
        if has_s:
            och_f = tk.chan()
            for hh in range(NH):
                tk.dma("sp", och_f, lambda e, hh=hh: e.dma_start(out=st_p[hh], in_=Sst[:, hh, :]), reads=[Sst_r[hh]],
                       cont=(hh > 0))
            tk.dma("sp", och_f, lambda e: e.dma_start(out=cv_p.rearrange("(c p) j -> p c j", p=128), in_=craw[:]),
                   reads=craw_r, cont=True)
            tk.dma("sp", och_f, lambda e: e.dma_start(out=cv_s.rearrange("(c p) (s j) -> p c s j", p=128, j=3), in_=cvs[:]),
                   reads=[cvs_r], cont=True)
        tk.barrier()
        p1.close()
        p4 = contextlib.ExitStack()
        wo = sb(p4, "g_wo", [128, KC, D], BF16)
        wo_r = Reg("g_wo")
        for k0 in range(0, KC, 4):
            tk.dma("pool", wch[2], lambda e, k0=k0: e.dma_start(
                out=wo[:, k0:k0 + 4, :], in_=d_wo.rearrange("(k p) o -> p k o", p=128)[:, k0:k0 + 4, :]),
                writes=[wo_r], cont=(k0 > 0))
        y_s = Scratch(p4, "g_y", [128, KC, 512], F32, 2, nreg=KC)
        sq_s = Scratch(p4, "g_sq4", [128, KC, 512], BF16, 1)
        t1_s = Scratch(p4, "g_t14", [128, 512], F32, 2)
        t2_s = Scratch(p4, "g_t24", [128, 512], F32, 2)
        for (ts, n) in ftiles(t0, t1):
            o = ts - t0
            y, y_r = y_s.get()
            for oc in range(KC):
                pb_, pbr = bank()
                for k in range(NH):
                    tk.op("pe", lambda e, k=k: e.matmul(pb_[:, :n], wo[:, k, oc * 128:(oc + 1) * 128], oT[:, k, o:o + n],
                                                        start=(k == 0), stop=(k == NH - 1)),
                          reads=[wo_r, oT_r[k]], writes=[pbr])
                evac(y[:, oc, :n], pb_[:, :n], [pbr], [y_r[oc]])
            postnorm_add(1, 1, y, y_r, 0, ts, n, sq_s, t1_s, t2_s)
        tk.barrier()
        p4.close()
        ph.close()

    halves = [(0, HALF), (HALF, T)]
    if "mla" in stages:
        for (t0, t1) in halves:
            mla_phase(t0, t1)
            if "ffn0" in stages:
                ffn_phase(0, t0, t1)
    elif "ffn0" in stages:
        for (t0, t1) in halves:
            ffn_phase(0, t0, t1)
    tk.barrier()
    l0s.close()
    if "gdn" in stages:
        gdn_init()
        for (t0, t1) in halves:
            gdn_phase(t0, t1)
            if "ffn1" in stages:
                ffn_phase(1, t0, t1)
    elif "ffn1" in stages:
        for (t0, t1) in halves:
            ffn_phase(1, t0, t1)

    for k in range(KC):
        tk.dma("sp", next_och(), lambda e, k=k: e.dma_start(out=yT[k * 128:(k + 1) * 128, :], in_=h[:, k, :]),
               reads=h_r[k])
    tk.final()
    es.close()
    return nc


_NC_CACHE = {}


def prep_core_inputs(c, inp, SEQ, NPG):
    PAST = NPG * 128
    T = SEQ + 16
    x_p = inp["x_prompt"][c]
    x_s = inp["x_sample"][4 * c:4 * c + 4].reshape(16, D)
    xT = np.ascontiguousarray(np.concatenate([x_p, x_s], axis=0).T)
    consts, rope = host_consts(SEQ, PAST)
    cm = inp["cache_mla"][0]
    d = {
        "xT": xT,
        "cache": cm.reshape(cm.shape[0] * 16, 8 * ROW),
        "ptab": np.ascontiguousarray(inp["page_table"][4 * c:4 * c + 4].T.astype(np.int32)),
        "st_in": np.ascontiguousarray(inp["state_dn"][0, 4 * c:4 * c + 4]),
        "cv_in": np.ascontiguousarray(inp["state_dn_conv"][0, 4 * c:4 * c + 4].transpose(2, 0, 1)).reshape(3072, 12),
        "normw": np.ascontiguousarray(inp["norm_w"].reshape(2, 4, 8, 128).transpose(3, 0, 1, 2).reshape(128, 64)),
        "consts": consts,
        "gmask": host_gmask(),
        "rope": rope,
        "m_win": inp["mla_w_in"][0],
        "m_gq": np.ascontiguousarray(inp["mla_g_q"][0].reshape(3, 128).T),
        "m_gkv": np.ascontiguousarray(inp["mla_g_kv"][0].reshape(2, 128).T),
        "m_wuq": inp["mla_w_uq"][0],
        "m_wuk": inp["mla_w_uk"][0],
        "m_wukT": np.ascontiguousarray(inp["mla_w_uk"][0].transpose(0, 2, 1)),
        "m_wuv": inp["mla_w_uv"][0],
        "m_wo": inp["mla_w_o"][0],
        "d_win": inp["dn_w_in"][0],
        "d_cw": np.ascontiguousarray(inp["dn_conv_w"][0].T.reshape(24, 128, 4).transpose(1, 0, 2)),
        "d_alog": inp["dn_a_log"],
        "d_dtb": inp["dn_dt_bias"],
        "d_gout": np.ascontiguousarray(inp["dn_g_out"][0].reshape(128, 1)),
        "d_wo": inp["dn_w_o"][0],
        "f_win": inp["ffn_w_in"],
        "f_wout": inp["ffn_w_out"],
    }
    return {k: np.ascontiguousarray(np.asarray(v)) for k, v in d.items()}


def kernel(**inputs):
    inp = {k: np.asarray(v) for k, v in inputs.items()}
    B, SEQ, _ = inp["x_prompt"].shape
    NPG = inp["page_table"].shape[1]
    NPOOL = inp["cache_mla"].shape[1]
    key = (SEQ, NPG, NPOOL)
    nc = build(SEQ, NPG, NPOOL)
    in_maps = [prep_core_inputs(c, inp, SEQ, NPG) for c in range(8)]
    res = run_bass_kernel_spmd(nc, in_maps, core_ids=list(range(8))).results
    y_p = np.stack([res[c]["yT"][:, :SEQ].T for c in range(8)])
    y_s = np.concatenate([res[c]["yT"][:, SEQ:].T.reshape(4, 4, D) for c in range(8)])
    r_p = np.stack([res[c]["rowsT"][:, :SEQ].T for c in range(8)])[None]
    r_s = np.concatenate([res[c]["rowsT"][:, SEQ:].T.reshape(4, 4, ROW) for c in range(8)])[None]
    s_p = np.stack([res[c]["st_p"] for c in range(8)])[None]
    s_s = np.concatenate([res[c]["st_s"] for c in range(8)])[None]
    c_p = np.stack([res[c]["cv_p"].T for c in range(8)])[None]
    c_s = np.concatenate([res[c]["cv_s"].reshape(3072, 4, 3).transpose(1, 2, 0) for c in range(8)])[None]
    f = lambda a: np.ascontiguousarray(a.astype(np.float32))
    return (f(y_p), f(y_s), f(r_p), f(r_s), f(s_p), f(s_s), f(c_p), f(c_s))
```

```python
import contextlib
import math
import numpy as np
import concourse.bass as bass
import concourse.mybir as mybir
from concourse.bass_utils import run_bass_kernel_spmd

F32 = mybir.dt.float32
BF16 = mybir.dt.bfloat16
I32 = mybir.dt.int32
AF = mybir.ActivationFunctionType
ALU = mybir.AluOpType

D = 1024
KC = 8
NH = 8
QLORA, KVLORA, ROPE = 384, 256, 64
ROW = KVLORA + ROPE
DFF = 2816
FC = DFF // 128
NEG = -30000.0
MLA_SCALE = (128 + 64) ** -0.5


class Reg:
    __slots__ = ("name", "w", "rd", "excl")

    def __init__(self, name="", excl=False):
        self.name = name
        self.w = None
        self.rd = {}
        self.excl = excl


class Trk:
    def __init__(self, nc, es):
        self.nc = nc
        self.es = es
        self.eng = {"pe": nc.tensor, "act": nc.scalar, "dve": nc.vector, "pool": nc.gpsimd, "sp": nc.sync}
        self.semh = {}
        self.cnt = {}
        self.seen = {k: {} for k in self.eng}
        for k in self.eng:
            self.semh[k] = es.enter_context(nc.semaphore("s_" + k))
            self.cnt[k] = 0
        self.same_sync = {"pool": True, "act": True, "dve": True}
        self.nchan = 0

    def chan(self, name=None):
        self.nchan += 1
        k = "c%d" % self.nchan
        self.semh[k] = self.es.enter_context(self.nc.semaphore("d_%d" % self.nchan))
        self.cnt[k] = 0
        return k

    def _waits(self, e, reads, writes, skip=None):
        need = {}
        for r in reads:
            if r.w is not None and need.get(r.w[0], 0) < r.w[1]:
                need[r.w[0]] = r.w[1]
            if r.excl:
                for k, c in r.rd.items():
                    if k != e and need.get(k, 0) < c:
                        need[k] = c
        for w in writes:
            if w.w is not None and need.get(w.w[0], 0) < w.w[1]:
                need[w.w[0]] = w.w[1]
            for k, c in w.rd.items():
                if need.get(k, 0) < c:
                    need[k] = c
        for k, c in need.items():
            if k == skip:
                continue
            if k == e and not self.same_sync.get(e):
                continue
            if self.seen[e].get(k, 0) >= c:
                continue
            self.eng[e].wait_ge(self.semh[k], c)
            self.seen[e][k] = c

    def op(self, e, fn, reads=(), writes=()):
        self._waits(e, reads, writes)
        ins = fn(self.eng[e])
        self.cnt[e] += 1
        ins.then_inc(self.semh[e], 1)
        c = self.cnt[e]
        for r in reads:
            r.rd[e] = c
        for w in writes:
            w.w = (e, c)
            w.rd = {}
        return ins

    def dma(self, q, ch, fn, reads=(), writes=(), cont=False):
        if not cont and self.cnt[ch] > self.seen[q].get(ch, 0):
            self.eng[q].wait_ge(self.semh[ch], self.cnt[ch])
            self.seen[q][ch] = self.cnt[ch]
        self._waits(q, reads, writes, skip=ch)
        ins = fn(self.eng[q])
        self.cnt[ch] += 16
        ins.then_inc(self.semh[ch], 16)
        c = self.cnt[ch]
        for r in reads:
            r.rd[ch] = c
        for w in writes:
            w.w = (ch, c)
            w.rd = {}
        return ins

    def barrier(self):
        for e in self.eng:
            for k, c in self.cnt.items():
                if c == 0:
                    continue
                if self.seen[e].get(k, 0) >= c:
                    continue
                self.eng[e].wait_ge(self.semh[k], c)
                self.seen[e][k] = c

    def final(self):
        e = "sp"
        for k, c in self.cnt.items():
            if k == e or c == 0:
                continue
            if self.seen[e].get(k, 0) >= c:
                continue
            self.eng[e].wait_ge(self.semh[k], c)
            self.seen[e][k] = c


def host_consts(SEQ, PAST):
    T = SEQ + 16
    c = np.zeros((128, 1536), np.float32)
    idx = np.arange(128)
    c[:, 0:128] = np.eye(128, dtype=np.float32)
    c[:, 128:256] = (idx[:, None] <= idx[None, :]).astype(np.float32)
    c[:, 256:384] = (idx[:, None] > idx[None, :]).astype(np.float32)
    c[:, 384:512] = np.where(idx[None, :] >= idx[:, None], NEG, 0.0)
    c[:, 512:640] = np.where(idx[None, :] < idx[:, None], NEG, 0.0)
    rot = np.zeros((128, 64), np.float32)
    for m in range(32):
        rot[m + 32, m] = -1.0
        rot[m, m + 32] = 1.0
    c[:, 640:704] = rot
    c[:, 704:832] = 1.0
    dm = np.zeros((128, 32), np.float32)
    for j in range(4):
        for q in range(32):
            dm[j, q] = 1.0 if j <= (q % 4) else 0.0
    c[:, 832:864] = dm
    for j in range(4):
        c[:, 1024 + j * 128:1024 + (j + 1) * 128] = np.eye(128, dtype=np.float32)
    half = 32
    freq = (10000.0 ** (-np.arange(half, dtype=np.float32) / half)).astype(np.float32)
    pos = np.concatenate([np.arange(SEQ), np.tile(PAST + np.arange(4), 4)]).astype(np.float32)
    ang = pos[None, :] * freq[:, None]
    rope = np.zeros((64, 2, T), np.float32)
    rope[0:32, 0] = np.cos(ang)
    rope[32:64, 0] = np.cos(ang)
    rope[0:32, 1] = np.sin(ang)
    rope[32:64, 1] = np.sin(ang)
    return c, rope


def host_gmask():
    idx = np.arange(128)
    c_, s_ = idx[:, None], idx[None, :]
    g = np.zeros((128, 1408), np.float32)
    g[:, 0:128] = (c_ // 4 == s_ // 4) & (c_ > s_)
    for li, b in enumerate([4, 8, 16, 32, 64]):
        m = ((c_ // (2 * b)) == (s_ // (2 * b))) & ((c_ // b) > (s_ // b))
        g[:, 128 + li * 128:128 + (li + 1) * 128] = np.eye(128) - m
    g[:, 768:896] = (c_ // 4 == s_ // 4) & (c_ < s_)
    return g


INV_F32 = False
import os as _os
GCUT = int(_os.environ.get('GCUT', '99'))
GSK = int(_os.environ.get('GSK', '0'))
GOP = int(_os.environ.get('GOP', '0'))


def build(SEQ=2048, NPG=128, NPOOL=5120, stages=("mla", "ffn0", "gdn", "ffn1")):
    T = SEQ + 16
    HALF = SEQ // 2
    PAST = NPG * 128
    nc = bass.Bass("TRN2", target_bir_lowering=False)
    es = contextlib.ExitStack()
    tk = Trk(nc, es)

    def din(name, shape, dt=F32):
        return nc.dram_tensor(name, list(shape), dt, kind="ExternalInput").ap()

    def dout(name, shape, dt=F32):
        return nc.dram_tensor(name, list(shape), dt, kind="ExternalOutput").ap()

    xT = din("xT", [D, T])
    cache = din("cache", [NPOOL * 16, 8 * ROW])
    ptab = din("ptab", [128, 4], I32)
    st_in = din("st_in", [4, NH, 128, 128])
    cv_in = din("cv_in", [3072, 12])
    normw = din("normw", [128, 64])
    consts_d = din("consts", [128, 1536])
    gmask_d = din("gmask", [128, 1408])
    rope_d = din("rope", [64, 2, T])
    m_win = din("m_win", [D, 704])
    m_gq = din("m_gq", [128, 3])
    m_gkv = din("m_gkv", [128, 2])
    m_wuq = din("m_wuq", [QLORA, NH * 192])
    m_wuk = din("m_wuk", [NH, KVLORA, 128])
    m_wukT = din("m_wukT", [NH, 128, KVLORA])
    m_wuv = din("m_wuv", [NH, KVLORA, 128])
    m_wo = din("m_wo", [D, D])
    d_win = din("d_win", [D, 4112])
    d_cw = din("d_cw", [128, 24, 4])
    d_alog = din("d_alog", [1, NH])
    d_dtb = din("d_dtb", [1, NH])
    d_gout = din("d_gout", [128, 1])
    d_wo = din("d_wo", [D, D])
    f_win = din("f_win", [2, D, 2 * DFF])
    f_wout = din("f_wout", [2, DFF, D])

    yT = dout("yT", [D, T])
    rowsT = dout("rowsT", [ROW, T])
    st_p = dout("st_p", [NH, 128, 128])
    st_s = dout("st_s", [4, NH, 128, 128])
    cv_p = dout("cv_p", [3072, 3])
    cv_s = dout("cv_s", [3072, 12])

    uid = [0]

    def sb(stack, name, shape, dt):
        uid[0] += 1
        return stack.enter_context(nc.sbuf_tensor("%s_%d" % (name, uid[0]), list(shape), dt))

    h = sb(es, "h", [128, KC, T], F32)
    h_r = [[Reg("h%d_%d" % (k, j)) for j in range(8)] for k in range(KC)]
    cf = sb(es, "cf", [128, 1536], F32)
    cb = sb(es, "cb", [128, 1536], BF16)
    nw = sb(es, "nw", [128, 64], F32)
    epst = sb(es, "epst", [128, 2], F32)
    c_r = Reg("consts")
    I_f, U_f, MS_f, NEGL_f, NEGQ_f, ROT_f, ONE_f = (cf[:, 0:128], cf[:, 128:256], cf[:, 256:384],
                                                    cf[:, 384:512], cf[:, 512:640], cf[:, 640:704],
                                                    cf[:, 704:832])
    I_b, U_b, ONE_b, DM_b = cb[:, 0:128], cb[:, 128:256], cb[:, 704:832], cb[:, 832:864]
    I4_b = cb[:, 1024:1536]
    psb = [es.enter_context(nc.psum_tensor("ps%d" % i, [128, 512], F32)) for i in range(8)]
    ps_r = [Reg("ps%d" % i, excl=True) for i in range(8)]
    bank_i = [0]

    reserved = set()

    def bank():
        for _ in range(16):
            i = bank_i[0]
            bank_i[0] = (i + 1) % 8
            if i not in reserved:
                return psb[i], ps_r[i]
        raise RuntimeError("no free psum bank")

    def reserve(n):
        out = []
        for _ in range(n):
            for _ in range(16):
                i = bank_i[0]
                bank_i[0] = (i + 1) % 8
                if i not in reserved:
                    break
            reserved.add(i)
            out.append((psb[i], ps_r[i], i))
        return out

    def release(lst):
        for (_, _, i) in lst:
            reserved.discard(i)

    ch_in = tk.chan()
    tk.dma("sp", ch_in, lambda e: e.dma_start(out=cf[:], in_=consts_d), writes=[c_r])
    ch_cb = tk.chan()
    cb_r = Reg("cb")
    tk.dma("pool", ch_cb, lambda e: e.dma_start(out=cb[:], in_=consts_d), writes=[cb_r])
    tk.dma("sp", ch_in, lambda e: e.dma_start(out=nw[:], in_=normw), writes=[c_r])
    tk.op("dve", lambda e: e.memset(epst[:, 0:1], 1e-6), writes=[c_r])
    tk.op("dve", lambda e: e.memset(epst[:, 1:2], 1.0), reads=[cb_r], writes=[c_r])
    ch_x = tk.chan()
    for k in range(KC):
        tk.dma("sp", ch_x, lambda e, k=k: e.dma_start(out=h[:, k, :], in_=xT[k * 128:(k + 1) * 128, :]),
               writes=h_r[k], cont=(k > 0))
    for k in range(KC):
        for r_ in h_r[k]:
            r_.w = (ch_x, tk.cnt[ch_x])

    def hregs(ts, n):
        j0, j1 = ts // 512, (ts + n - 1) // 512
        return [h_r[k][j] for k in range(KC) for j in range(j0, j1 + 1)]

    def hreg_k(k, ts, n):
        j0, j1 = ts // 512, (ts + n - 1) // 512
        return [h_r[k][j] for j in range(j0, j1 + 1)]

    def tiles(t0, t1):
        out = []
        t = t0
        while t < min(t1, SEQ):
            n = min(512, min(t1, SEQ) - t)
            out.append((t, n))
            t += n
        if t1 > SEQ:
            out.append((SEQ, t1 - SEQ))
        return out

    def ftiles(t0, t1):
        th = t1 - t0
        nt = (th + 511) // 512
        base, rem = divmod(th, nt)
        out, t = [], t0
        for i in range(nt):
            n = base + (1 if i < rem else 0)
            out.append((t, n))
            t += n
        return out

    def nwcol(layer, i, k):
        j = (layer * 4 + i) * 8 + k
        return nw[:, j:j + 1]

    def rstd_from_ss(ps_ap, ps_reg, npart, n, scale, tmp, tmp_r, out, out_r):
        tk.op("act", lambda e: e.activation(out=tmp[:npart, :n], in_=ps_ap, func=AF.Ln,
                                            bias=epst[:npart, 0:1], scale=scale),
              reads=[ps_reg, c_r], writes=[tmp_r])
        tk.op("act", lambda e: e.activation(out=out[:npart, :n], in_=tmp[:npart, :n], func=AF.Exp, scale=-0.5),
              reads=[tmp_r], writes=[out_r])

    class Scratch:
        def __init__(self, stack, name, shape, dt, nbuf, nreg=None, chan=False):
            self.t = [sb(stack, "%s%d" % (name, i), shape, dt) for i in range(nbuf)]
            if nreg is None:
                self.r = [Reg("%s%d" % (name, i)) for i in range(nbuf)]
            else:
                self.r = [[Reg("%s%d_%d" % (name, i, j)) for j in range(nreg)] for i in range(nbuf)]
            self.ch = [tk.chan() for _ in range(nbuf)] if chan else None
            self.i = 0
            self.last = 0

        def get(self):
            i = self.i
            self.last = i
            self.i = (i + 1) % len(self.t)
            return self.t[i], self.r[i]

        def chan(self):
            return self.ch[self.last]

    def rmsnorm_pre(layer, wi, ts, n, dst, dst_r, dst_off, sq_s, t1_s, t2_s):
        sq, sq_r = sq_s.get()
        tk.op("act", lambda e: e.activation(out=sq[:, :, :n], in_=h[:, :, ts:ts + n], func=AF.Square),
              reads=hregs(ts, n), writes=[sq_r])
        pb, pr = bank()
        for k in range(KC):
            tk.op("pe", lambda e, k=k: e.matmul(pb[:, :n], ONE_b, sq[:, k, :n], start=(k == 0), stop=(k == KC - 1)),
                  reads=[sq_r, c_r], writes=[pr])
        t1, t1r = t1_s.get()
        t2, t2r = t2_s.get()
        rstd_from_ss(pb[:, :n], pr, 128, n, 1.0 / D, t1, t1r, t2, t2r)
        for k in range(KC):
            tk.op("dve", lambda e, k=k: e.scalar_tensor_tensor(
                out=dst[:, k, dst_off:dst_off + n], in0=h[:, k, ts:ts + n], scalar=nwcol(layer, wi, k),
                in1=t2[:, :n], op0=ALU.mult, op1=ALU.mult),
                reads=hreg_k(k, ts, n) + [t2r, c_r], writes=[dst_r])

    def postnorm_add(layer, wi, y, y_r, yoff, ts, n, sq_s, t1_s, t2_s):
        sq, sq_r = sq_s.get()
        tk.op("act", lambda e: e.activation(out=sq[:, :, :n], in_=y[:, :, yoff:yoff + n], func=AF.Square),
              reads=y_r, writes=[sq_r])
        pb, pr = bank()
        for k in range(KC):
            tk.op("pe", lambda e, k=k: e.matmul(pb[:, :n], ONE_b, sq[:, k, :n], start=(k == 0), stop=(k == KC - 1)),
                  reads=[sq_r, c_r], writes=[pr])
        t1, t1r = t1_s.get()
        t2, t2r = t2_s.get()
        rstd_from_ss(pb[:, :n], pr, 128, n, 1.0 / D, t1, t1r, t2, t2r)
        for k in range(KC):
            tk.op("dve", lambda e, k=k: e.scalar_tensor_tensor(
                out=y[:, k, yoff:yoff + n], in0=y[:, k, yoff:yoff + n], scalar=nwcol(layer, wi, k),
                in1=t2[:, :n], op0=ALU.mult, op1=ALU.mult),
                reads=[t2r, c_r, y_r[k]], writes=[y_r[k]])
            tk.op("dve", lambda e, k=k: e.tensor_tensor(out=h[:, k, ts:ts + n], in0=h[:, k, ts:ts + n],
                                                        in1=y[:, k, yoff:yoff + n], op=ALU.add),
                  reads=[y_r[k]], writes=hreg_k(k, ts, n))

    ev_tog = [0]

    def evac(out_ap, in_ap, reads, writes):
        ev_tog[0] ^= 1
        if ev_tog[0]:
            tk.op("act", lambda e: e.copy(out=out_ap, in_=in_ap), reads=reads, writes=writes)
        else:
            tk.op("dve", lambda e: e.tensor_copy(out=out_ap, in_=in_ap), reads=reads, writes=writes)

    wch = [tk.chan() for _ in range(4)]

    def ffn_phase(layer, t0, t1):
        TH = t1 - t0
        tl = ftiles(t0, t1)
        ph = contextlib.ExitStack()
        act = sb(ph, "f_act", [128, FC, TH], BF16)
        act_r = [Reg("act%d" % c) for c in range(FC)]
        w_in_v = f_win[layer].rearrange("(k p) o -> p k o", p=128)
        w_out_v = f_wout[layer].rearrange("(k p) o -> p k o", p=128)
        pa = contextlib.ExitStack()
        u = sb(pa, "f_u", [128, KC, TH], BF16)
        u_r = Reg("f_u")
        sq_s = Scratch(pa, "f_sq", [128, KC, 512], BF16, 1)
        t1_s = Scratch(pa, "f_t1", [128, 512], F32, 2)
        t2_s = Scratch(pa, "f_t2", [128, 512], F32, 2)
        sg_s = Scratch(pa, "f_sg", [128, 512], BF16, 3)
        slab = [sb(pa, "f_ws%d" % i, [128, KC, 2, 512], BF16) for i in range(2)]
        slab_r = [Reg("f_ws%d" % i) for i in range(2)]
        groups = [(g * 4, min(4, FC - g * 4)) for g in range((FC + 3) // 4)]

        def load_slab(gi):
            c0, ncg = groups[gi]
            s = gi % 2
            for two in range(2):
                col = two * DFF + c0 * 128
                tk.dma("pool", wch[s], lambda e, two=two, col=col: e.dma_start(
                    out=slab[s][:, :, two, :ncg * 128], in_=w_in_v[:, :, col:col + ncg * 128]),
                    writes=[slab_r[s]], cont=(two > 0))

        load_slab(0)
        for (ts, n) in tl:
            rmsnorm_pre(layer, 2, ts, n, u, u_r, ts - t0, sq_s, t1_s, t2_s)
        for gi, (c0, ncg) in enumerate(groups):
            if gi + 1 < len(groups):
                load_slab(gi + 1)
            s = gi % 2
            for cc in range(ncg):
                c = c0 + cc
                for (ts, n) in tl:
                    o = ts - t0
                    pg, pgr = bank()
                    for k in range(KC):
                        tk.op("pe", lambda e, k=k: e.matmul(pg[:, :n], slab[s][:, k, 0, cc * 128:(cc + 1) * 128],
                                                            u[:, k, o:o + n], start=(k == 0), stop=(k == KC - 1)),
                              reads=[slab_r[s], u_r], writes=[pgr])
                    pu, pur = bank()
                    for k in range(KC):
                        tk.op("pe", lambda e, k=k: e.matmul(pu[:, :n], slab[s][:, k, 1, cc * 128:(cc + 1) * 128],
                                                            u[:, k, o:o + n], start=(k == 0), stop=(k == KC - 1)),
                              reads=[slab_r[s], u_r], writes=[pur])
                    sg, sgr = sg_s.get()
                    tk.op("act", lambda e: e.activation(out=sg[:, :n], in_=pg[:, :n], func=AF.Silu),
                          reads=[pgr], writes=[sgr])
                    tk.op("dve", lambda e: e.tensor_tensor(out=act[:, c, o:o + n], in0=sg[:, :n], in1=pu[:, :n],
                                                           op=ALU.mult),
                          reads=[sgr, pur], writes=[act_r[c]])
        tk.barrier()
        pa.close()
        pbk = contextlib.ExitStack()
        y = sb(pbk, "f_y", [128, KC, TH], F32)
        y_r = [[Reg("f_y%d_%d" % (i, k)) for k in range(KC)] for i in range(len(tl))]
        sq_s = Scratch(pbk, "f_sq2", [128, KC, 512], BF16, 1)
        t1_s = Scratch(pbk, "f_t1b", [128, 512], F32, 2)
        t2_s = Scratch(pbk, "f_t2b", [128, 512], F32, 2)
        oslab = [sb(pbk, "f_wo%d" % i, [128, FC, 256], BF16) for i in range(2)]
        oslab_r = [Reg("f_wo%d" % i) for i in range(2)]

        def load_oslab(gi):
            s = gi % 2
            for kk in range(0, FC, 11):
                tk.dma("pool", wch[2 + s], lambda e, kk=kk: e.dma_start(
                    out=oslab[s][:, kk:kk + 11, :], in_=w_out_v[:, kk:kk + 11, gi * 256:(gi + 1) * 256]),
                    writes=[oslab_r[s]], cont=(kk > 0))

        load_oslab(0)
        for gi in range(4):
            if gi + 1 < 4:
                load_oslab(gi + 1)
            s = gi % 2
            for oc in range(2):
                ochunk = gi * 2 + oc
                for ti, (ts, n) in enumerate(tl):
                    o = ts - t0
                    pb_, pbr = bank()
                    for c in range(FC):
                        tk.op("pe", lambda e, c=c: e.matmul(pb_[:, :n], oslab[s][:, c, oc * 128:(oc + 1) * 128],
                                                            act[:, c, o:o + n], start=(c == 0), stop=(c == FC - 1)),
                              reads=[oslab_r[s], act_r[c]], writes=[pbr])
                    evac(y[:, ochunk, o:o + n], pb_[:, :n], [pbr], [y_r[ti][ochunk]])
        for ti, (ts, n) in enumerate(tl):
            postnorm_add(layer, 3, y, y_r[ti], ts - t0, ts, n, sq_s, t1_s, t2_s)
        tk.barrier()
        pbk.close()
        ph.close()

    l0s = contextlib.ExitStack()
    ckv_r = [Reg("ckv%d" % j) for j in range(8)]
    ckv_r_init = ckv_r
    ckv_b = sb(l0s, "ckv_b", [128, 2, T], BF16)
    kr_b = sb(l0s, "kr_b", [128, T], BF16)
    tk.op("pool", lambda e: e.memset(kr_b[64:128, :], 0.0), writes=ckv_r_init)
    och = [tk.chan() for _ in range(4)]
    och_i = [0]

    def next_och():
        och_i[0] = (och_i[0] + 1) % len(och)
        return och[och_i[0]]

    def kvregs(ts, n):
        return [ckv_r[j] for j in range(ts // 512, (ts + n - 1) // 512 + 1)]

    def mla_phase(t0, t1):
        TH = t1 - t0
        tl = tiles(t0, t1)
        has_s = t1 > SEQ
        ph = contextlib.ExitStack()
        cq = sb(ph, "m_cq", [128, 3, TH], BF16)
        cq_r = Reg("m_cq")
        ropet = sb(ph, "m_rope", [64, 2, TH], F32)
        rope_r = Reg("m_rope")
        tk.dma("sp", ch_in, lambda e: e.dma_start(out=ropet[:], in_=rope_d[:, :, t0:t1]), writes=[rope_r])
        oT = sb(ph, "m_oT", [128, NH, TH], BF16)
        oT_r = [Reg("m_oT%d" % hh) for hh in range(NH)]
        p1 = contextlib.ExitStack()
        win = sb(p1, "m_win", [128, KC, 704], BF16)
        win_r = Reg("m_win")
        gq = sb(p1, "m_gq", [128, 3], F32)
        gkv = sb(p1, "m_gkv", [128, 2], F32)
        tk.dma("pool", wch[0], lambda e: e.dma_start(out=win[:], in_=m_win.rearrange("(k p) o -> p k o", p=128)),
               writes=[win_r])
        gv_r = Reg("m_gv")
        tk.dma("sp", ch_in, lambda e: e.dma_start(out=gq[:], in_=m_gq), writes=[gv_r])
        tk.dma("sp", ch_in, lambda e: e.dma_start(out=gkv[:], in_=m_gkv), writes=[gv_r])
        u_s = Scratch(p1, "m_u", [128, KC, 512], BF16, 2)
        sq_s = Scratch(p1, "m_sq", [128, KC, 512], BF16, 1)
        t1_s = Scratch(p1, "m_t1", [128, 512], F32, 2)
        t2_s = Scratch(p1, "m_t2", [128, 512], F32, 2)
        a_s = Scratch(p1, "m_a", [128, 6, 512], F32, 2)
        rowo_s = Scratch(p1, "m_rowo", [128, 3, 512], F32, 2, chan=True)
        for (ts, n) in tl:
            o = ts - t0
            u, u_r = u_s.get()
            rmsnorm_pre(0, 0, ts, n, u, u_r, 0, sq_s, t1_s, t2_s)
            a, a_r = a_s.get()
            for oc in range(6):
                m = 128 if oc < 5 else 64
                pb_, pbr = bank()
                for k in range(KC):
                    tk.op("pe", lambda e, k=k: e.matmul(pb_[:m, :n], win[:, k, oc * 128:oc * 128 + m], u[:, k, :n],
                                                        start=(k == 0), stop=(k == KC - 1)),
                          reads=[win_r, u_r], writes=[pbr])
                evac(a[:m, oc, :n], pb_[:m, :n], [pbr], [a_r])
            sq, sq_r = sq_s.get()
            tk.op("act", lambda e: e.activation(out=sq[:, 0:5, :n], in_=a[:, 0:5, :n], func=AF.Square),
                  reads=[a_r], writes=[sq_r])
            pq, pqr = bank()
            for k in range(3):
                tk.op("pe", lambda e, k=k: e.matmul(pq[:, :n], ONE_b, sq[:, k, :n], start=(k == 0), stop=(k == 2)),
                      reads=[sq_r, c_r], writes=[pqr])
            pk, pkr = bank()
            for k in range(2):
                tk.op("pe", lambda e, k=k: e.matmul(pk[:, :n], ONE_b, sq[:, 3 + k, :n], start=(k == 0), stop=(k == 1)),
                      reads=[sq_r, c_r], writes=[pkr])
            ta, tar = t1_s.get()
            rq, rqr = t2_s.get()
            rstd_from_ss(pq[:, :n], pqr, 128, n, 1.0 / QLORA, ta, tar, rq, rqr)
            tb, tbr = t1_s.get()
            rk, rkr = t2_s.get()
            rstd_from_ss(pk[:, :n], pkr, 128, n, 1.0 / KVLORA, tb, tbr, rk, rkr)
            for k in range(3):
                tk.op("dve", lambda e, k=k: e.scalar_tensor_tensor(
                    out=cq[:, k, o:o + n], in0=a[:, k, :n], scalar=gq[:, k:k + 1], in1=rq[:, :n],
                    op0=ALU.mult, op1=ALU.mult), reads=[a_r, rqr, gv_r], writes=[cq_r])
            rowo, rowo_r = rowo_s.get()
            for k in range(2):
                tk.op("dve", lambda e, k=k: e.scalar_tensor_tensor(
                    out=rowo[:, k, :n], in0=a[:, 3 + k, :n], scalar=gkv[:, k:k + 1], in1=rk[:, :n],
                    op0=ALU.mult, op1=ALU.mult), reads=[a_r, rkr, gv_r], writes=[rowo_r])
            tk.op("act", lambda e: e.copy(out=ckv_b[:, :, ts:ts + n], in_=rowo[:, 0:2, :n]),
                  reads=[rowo_r], writes=kvregs(ts, n))
            pr_, prr = bank()
            tk.op("pe", lambda e: e.matmul(pr_[:64, :n], ROT_f[:64, :], a[:64, 5, :n], start=True, stop=True),
                  reads=[a_r, c_r], writes=[prr])
            tk.op("dve", lambda e: e.tensor_tensor(out=rowo[:64, 2, :n], in0=a[:64, 5, :n],
                                                   in1=ropet[:, 0, o:o + n], op=ALU.mult),
                  reads=[a_r, rope_r], writes=[rowo_r])
            tk.op("dve", lambda e: e.tensor_tensor(out=a[:64, 5, :n], in0=pr_[:64, :n],
                                                   in1=ropet[:, 1, o:o + n], op=ALU.mult),
                  reads=[prr, rope_r], writes=[a_r])
            tk.op("dve", lambda e: e.tensor_tensor(out=rowo[:64, 2, :n], in0=rowo[:64, 2, :n],
                                                   in1=a[:64, 5, :n], op=ALU.add),
                  reads=[a_r], writes=[rowo_r])
            tk.op("act", lambda e: e.copy(out=kr_b[:64, ts:ts + n], in_=rowo[:64, 2, :n]),
                  reads=[rowo_r], writes=kvregs(ts, n))
            oc_ = rowo_s.chan()
            tk.dma("sp", oc_, lambda e: e.dma_start(
                out=rowsT[0:256, ts:ts + n].rearrange("(k p) t -> p k t", p=128), in_=rowo[:, 0:2, :n]),
                reads=[rowo_r])
            tk.dma("sp", oc_, lambda e: e.dma_start(out=rowsT[256:320, ts:ts + n], in_=rowo[:64, 2, :n]),
                   reads=[rowo_r], cont=True)
        tk.barrier()
        p1.close()
        p2 = contextlib.ExitStack()
        wuq = sb(p2, "m_wuq", [128, 3, NH * 192], BF16)
        wuk = sb(p2, "m_wuk", [128, NH, 2, 128], BF16)
        wuv = sb(p2, "m_wuv", [128, NH, 2, 128], BF16)
        wukT = sb(p2, "m_wukT", [128, NH, 256], BF16)
        w2_r = Reg("m_w2")
        tk.dma("pool", wch[1], lambda e: e.dma_start(out=wuq[:], in_=m_wuq.rearrange("(k p) o -> p k o", p=128)),
               writes=[w2_r])
        tk.dma("pool", wch[1], lambda e: e.dma_start(out=wuk[:], in_=m_wuk.rearrange("h (k p) n -> p h k n", p=128)),
               writes=[w2_r], cont=True)
        tk.dma("pool", wch[1], lambda e: e.dma_start(out=wuv[:], in_=m_wuv.rearrange("h (k p) n -> p h k n", p=128)),
               writes=[w2_r], cont=True)
        tk.dma("pool", wch[1], lambda e: e.dma_start(out=wukT[:], in_=m_wukT.rearrange("h p r -> p h r")),
               writes=[w2_r], cont=True)
        NKB = t1 // 128 if not has_s else SEQ // 128
        if has_s:
            Qs = sb(p2, "m_Qs", [128, 3, 4, 32], BF16)
            Qs_r = Reg("m_Qs")
            tk.op("pool", lambda e: e.memset(Qs[64:128, 2, :, :], 0.0), writes=[Qs_r])
        p2a = contextlib.ExitStack()
        qn_s = Scratch(p2a, "m_qn", [128, TH], BF16, 2)
        qr_s = Scratch(p2a, "m_qr", [128, TH], BF16, 2)
        for t_, r_ in zip(qr_s.t, qr_s.r):
            tk.op("pool", lambda e, t_=t_: e.memset(t_[64:128, :], 0.0), writes=[r_])
        qx_s = Scratch(p2a, "m_qx", [64, 2, 512], F32, 2)
        kn_s = Scratch(p2a, "m_kn", [128, SEQ], BF16, 2)
        vp_s = Scratch(p2a, "m_vp", [128, SEQ // 128, 132], BF16, 2)
        pT_s = Scratch(p2a, "m_pT", [128, 512], BF16, 3)
        on_s = Scratch(p2a, "m_on", [128, 4, 128], BF16, 2)
        rs_s = Scratch(p2a, "m_rs", [128, 4], F32, 2)
        ptl = [x for x in tl if x[0] < SEQ]
        for hh in range(NH):
            qn, qn_r = qn_s.get()
            qr, qr_r = qr_s.get()
            for (ts, n) in tl:
                o = ts - t0
                pb_, pbr = bank()
                for k in range(3):
                    tk.op("pe", lambda e, k=k: e.matmul(pb_[:, :n], wuq[:, k, hh * 192:hh * 192 + 128], cq[:, k, o:o + n],
                                                        start=(k == 0), stop=(k == 2)),
                          reads=[w2_r, cq_r], writes=[pbr])
                evac(qn[:, o:o + n], pb_[:, :n], [pbr], [qn_r])
                p2_, p2r = bank()
                for k in range(3):
                    tk.op("pe", lambda e, k=k: e.matmul(p2_[:64, :n], wuq[:, k, hh * 192 + 128:hh * 192 + 192],
                                                        cq[:, k, o:o + n], start=(k == 0), stop=(k == 2)),
                          reads=[w2_r, cq_r], writes=[p2r])
                qx, qx_r = qx_s.get()
                tk.op("act", lambda e: e.copy(out=qx[:, 0, :n], in_=p2_[:64, :n]), reads=[p2r], writes=[qx_r])
                p3_, p3r = bank()
                tk.op("pe", lambda e: e.matmul(p3_[:64, :n], ROT_f[:64, :], qx[:, 0, :n], start=True, stop=True),
                      reads=[qx_r, c_r], writes=[p3r])
                tk.op("dve", lambda e: e.tensor_tensor(out=qx[:, 1, :n], in0=p3_[:64, :n], in1=ropet[:, 1, o:o + n],
                                                       op=ALU.mult), reads=[p3r, rope_r], writes=[qx_r])
                tk.op("dve", lambda e: e.tensor_tensor(out=qx[:, 0, :n], in0=qx[:, 0, :n], in1=ropet[:, 0, o:o + n],
                                                       op=ALU.mult), reads=[rope_r], writes=[qx_r])
                tk.op("dve", lambda e: e.tensor_tensor(out=qr[:64, o:o + n], in0=qx[:, 0, :n], in1=qx[:, 1, :n],
                                                       op=ALU.add), reads=[qx_r], writes=[qr_r])
            kn, kn_r = kn_s.get()
            vp, vp_r = vp_s.get()
            nk = NKB * 128
            for ks in range(0, nk, 512):
                n = min(512, nk - ks)
                pb_, pbr = bank()
                for k in range(2):
                    tk.op("pe", lambda e, k=k: e.matmul(pb_[:, :n], wuk[:, hh, k, :], ckv_b[:, k, ks:ks + n],
                                                        start=(k == 0), stop=(k == 1)),
                          reads=[w2_r] + kvregs(ks, n), writes=[pbr])
                evac(kn[:, ks:ks + n], pb_[:, :n], [pbr], [kn_r])
            for kb4 in range(0, NKB, 4):
                nb = min(4, NKB - kb4)
                pb_, pbr = bank()
                for j in range(nb):
                    kb = kb4 + j
                    for k in range(2):
                        tk.op("pe", lambda e, k=k, j=j, kb=kb: e.matmul(
                            pb_[:, j * 128:(j + 1) * 128], ckv_b[:, k, kb * 128:(kb + 1) * 128], wuv[:, hh, k, :],
                            start=(k == 0), stop=(k == 1)),
                            reads=[w2_r] + kvregs(kb * 128, 128), writes=[pbr])
                evac(vp[:, kb4:kb4 + nb, 0:128], pb_[:, :nb * 128].rearrange("p (j v) -> p j v", v=128), [pbr], [vp_r])
            tk.op("dve", lambda e: e.memset(vp[:, :, 128:129], 1.0), writes=[vp_r])
            for (ts, n) in ptl:
                o = ts - t0
                nsub = n // 128
                kb_hi = (ts + n) // 128
                accs_l = reserve(nsub)
                accs = [(a_, r_) for (a_, r_, _) in accs_l]
                pend = None

                def emit_pv(pv):
                    kb, d, j0, pT, pT_r = pv
                    for si in range(max(d, 0), nsub):
                        ab, abr = accs[si]
                        last_kb = ts // 128 + si
                        c0 = si * 128 - j0
                        tk.op("pe", lambda e, ab=ab, c0=c0, kb=kb, last_kb=last_kb: e.matmul(
                            ab[:, 0:129], pT[:, c0:c0 + 128], vp[:, kb, 0:129], start=(kb == 0), stop=(kb == last_kb)),
                            reads=[pT_r, vp_r], writes=[abr])

                for kb in range(kb_hi):
                    d = kb - ts // 128
                    j0 = max(d, 0) * 128
                    ncol = n - j0
                    psc, pscr = bank()
                    tk.op("pe", lambda e: e.matmul(psc[:, :ncol], kn[:, kb * 128:(kb + 1) * 128],
                                                   qn[:, o + j0:o + n], start=True, stop=False),
                          reads=[kn_r, qn_r], writes=[pscr])
                    tk.op("pe", lambda e: e.matmul(psc[:, :ncol], kr_b[:, kb * 128:(kb + 1) * 128],
                                                   qr[:, o + j0:o + n], start=False, stop=True),
                          reads=kvregs(kb * 128, 128) + [qr_r], writes=[pscr])
                    if pend is not None:
                        emit_pv(pend)
                    pT, pT_r = pT_s.get()
                    tk.op("act", lambda e: e.activation(out=pT[:, :ncol], in_=psc[:, :ncol], func=AF.Exp,
                                                        scale=MLA_SCALE), reads=[pscr], writes=[pT_r])
                    if d >= 0:
                        tk.op("dve", lambda e: e.tensor_tensor(out=pT[:, 0:128], in0=pT[:, 0:128], in1=U_b,
                                                               op=ALU.mult), reads=[c_r], writes=[pT_r])
                    pend = (kb, d, j0, pT, pT_r)
                emit_pv(pend)
                rs, rs_r = rs_s.get()
                on, on_r = on_s.get()
                for si in range(nsub):
                    ab, abr = accs[si]
                    tk.op("dve", lambda e, ab=ab, si=si: e.reciprocal(out=rs[:, si:si + 1], in_=ab[:, 128:129]),
                          reads=[abr], writes=[rs_r])
                    tk.op("act", lambda e, ab=ab, si=si: e.activation(out=on[:, si, :], in_=ab[:, 0:128], func=AF.Copy,
                                                                      scale=rs[:, si:si + 1]),
                          reads=[abr, rs_r], writes=[on_r])
                ptb, ptr = bank()
                for si in range(nsub):
                    tk.op("pe", lambda e, si=si: e.matmul(ptb[:, si * 128:(si + 1) * 128], on[:, si, :], I_b,
                                                          start=True, stop=True),
                          reads=[on_r, c_r], writes=[ptr])
                evac(oT[:, hh, o:o + n], ptb[:, :n], [ptr], [oT_r[hh]])
                release(accs_l)
            if has_s:
                so = SEQ - t0
                pb_, pbr = bank()
                for k in range(2):
                    tk.op("pe", lambda e, k=k: e.matmul(pb_[:, k * 16:(k + 1) * 16], wukT[:, hh, k * 128:(k + 1) * 128],
                                                        qn[:, so:so + 16], start=True, stop=True),
                          reads=[w2_r, qn_r], writes=[pbr])
                evac(Qs[:, 0:2, :, hh * 4:(hh + 1) * 4],
                     pb_[:, 0:32].rearrange("p (k s t) -> p k s t", k=2, s=4), [pbr], [Qs_r])
                tk.op("act", lambda e: e.copy(out=Qs[:64, 2, :, hh * 4:(hh + 1) * 4],
                                              in_=qr[:64, so:so + 16].rearrange("p (s t) -> p s t", s=4)),
                      reads=[qr_r], writes=[Qs_r])
        tk.barrier()
        p2a.close()
        if has_s:
            R = 8
            pt_sb = sb(p2, "m_pt", [128, 4], I32)
            idx_sb = sb(p2, "m_idx", [128, 16, 4], I32)
            pt_r = Reg("m_pt")
            tk.dma("sp", ch_in, lambda e: e.dma_start(out=pt_sb[:], in_=ptab), writes=[pt_r])
            for gi in range(16):
                tk.op("dve", lambda e, gi=gi: e.tensor_scalar(out=idx_sb[:, gi, :], in0=pt_sb[:, :], scalar1=16.0,
                                                               scalar2=float(gi), op0=ALU.mult, op1=ALU.add),
                      reads=[pt_r], writes=[pt_r])
            NCB = 3
            cbuf = [sb(p2, "m_cb%d" % i, [128, R, ROW], F32) for i in range(NCB)]
            cbuf_r = [Reg("m_cb%d" % i) for i in range(NCB)]
            cch = [tk.chan() for _ in range(NCB)]
            ctok_s = Scratch(p2, "m_ctok", [128, R, 388], BF16, 3)
            for t_ in ctok_s.t:
                tk.op("dve", lambda e, t_=t_: e.memset(t_[:, :, 64:128], 0.0), writes=ctok_s.r)
                tk.op("dve", lambda e, t_=t_: e.memset(t_[:, :, 384:385], 1.0), writes=ctok_s.r)
            pd_s = Scratch(p2, "m_pd", [128, 32], BF16, 2)
            cnew = sb(p2, "m_cnew", [4, 260], BF16)
            cnew_r = Reg("m_cnew")
            oln = sb(p2, "m_oln", [32, 256], BF16)
            oln_r = Reg("m_oln")
            olT = sb(p2, "m_olT", [128, 2, 32], BF16)
            olT_r = Reg("m_olT")
            rsd = sb(p2, "m_rsd", [32, 1], F32)
            ngath = 128 // R
            so = SEQ - t0

            def gather(s, gi):
                i = (s * ngath + gi) % NCB
                tk.dma("pool", cch[i], lambda e: e.indirect_dma_start(
                    out=cbuf[i][:].rearrange("p r d -> p (r d)"), out_offset=None,
                    in_=cache,
                    in_offset=bass.IndirectOffsetOnAxis(ap=idx_sb[:, gi, s:s + 1], axis=0)),
                    reads=[pt_r], writes=[cbuf_r[i]])

            gather(0, 0)
            gather(0, 1)
            cT4_s = Scratch(p2, "m_cT4", [128, 3, 512], BF16, 3)
            pd4_s = Scratch(p2, "m_pd4", [128, 128], BF16, 3)
            batches = [(s_, gi, r0) for s_ in range(4) for gi in range(ngath) for r0 in range(0, R, 4)]
            nbt = len(batches)
            bst = [dict() for _ in range(nbt)]
            grp = {}
            accs_d = {}

            def stA(b):
                s, gi, r0 = batches[b]
                g = s * ngath + gi
                if r0 == 0:
                    nxt = g + 2
                    if nxt < 4 * ngath:
                        gather(nxt // ngath, nxt % ngath)
                    i = g % NCB
                    ctok, ctok_r = ctok_s.get()
                    tk.op("dve", lambda e: e.tensor_copy(out=ctok[:, :, 128:384], in_=cbuf[i][:, :, 0:256]),
                          reads=[cbuf_r[i]], writes=[ctok_r])
                    tk.op("dve", lambda e: e.tensor_copy(out=ctok[:, :, 0:64], in_=cbuf[i][:, :, 256:320]),
                          reads=[cbuf_r[i]], writes=[ctok_r])
                    grp[g] = (ctok, ctok_r)
                ctok, ctok_r = grp[g]
                cT4, cT4_r = cT4_s.get()
                for k in range(3):
                    m = 128
                    c0_ = (128, 256, 0)[k]
                    ptp, ptpr = bank()
                    for j in range(4):
                        tk.op("pe", lambda e, k=k, m=m, j=j, c0_=c0_: e.matmul(
                            ptp[:m, j * 128:(j + 1) * 128], ctok[:, r0 + j, c0_:c0_ + 128], I_b,
                            start=True, stop=True), reads=[ctok_r, c_r], writes=[ptpr])
                    if k == 1:
                        tk.op("dve", lambda e, k=k, m=m, ptp=ptp: e.tensor_copy(out=cT4[:m, k, :], in_=ptp[:m, :]),
                              reads=[ptpr], writes=[cT4_r])
                    else:
                        tk.op("act", lambda e, k=k, m=m, ptp=ptp: e.copy(out=cT4[:m, k, :], in_=ptp[:m, :]),
                              reads=[ptpr], writes=[cT4_r])
                bst[b]["cT4"] = (cT4, cT4_r)

            def stB(b):
                s, gi, r0 = batches[b]
                cT4, cT4_r = bst[b]["cT4"]
                psc, pscr = bank()
                for j in range(4):
                    for k in range(3):
                        m = 128
                        tk.op("pe", lambda e, k=k, m=m, j=j: e.matmul(
                            psc[:, j * 32:(j + 1) * 32], cT4[:m, k, j * 128:(j + 1) * 128], Qs[:m, k, s, :],
                            start=(k == 0), stop=(k == 2)), reads=[cT4_r, Qs_r], writes=[pscr])
                pd4, pd4_r = pd4_s.get()
                tk.op("act", lambda e: e.activation(out=pd4[:, :], in_=psc[:, 0:128], func=AF.Exp, scale=MLA_SCALE),
                      reads=[pscr], writes=[pd4_r])
                bst[b]["pd4"] = (pd4, pd4_r)

            def stC(b):
                s, gi, r0 = batches[b]
                g = s * ngath + gi
                ctok, ctok_r = grp[g]
                pd4, pd4_r = bst[b]["pd4"]
                if s not in accs_d:
                    accs_d[s] = reserve(1)
                (acc, acc_r, _), = acc_l = accs_d[s]
                for j in range(4):
                    first = (gi == 0 and r0 == 0 and j == 0)
                    tk.op("pe", lambda e, j=j, first=first: e.matmul(
                        acc[:32, 0:257], pd4[:, j * 32:(j + 1) * 32], ctok[:, r0 + j, 128:385],
                        start=first, stop=False), reads=[pd4_r, ctok_r], writes=[acc_r])
                if gi == ngath - 1 and r0 == R - 4:
                    tail(s, acc, acc_r, acc_l)

            def tail(s, acc, acc_r, acc_l):
                    tcol = SEQ + s * 4
                    ptp, ptpr = bank()
                    for k in range(2):
                        tk.op("pe", lambda e, k=k: e.matmul(ptp[:4, k * 128:(k + 1) * 128], ckv_b[:, k, tcol:tcol + 4], I_b,
                                                            start=True, stop=True),
                              reads=kvregs(tcol, 4) + [c_r], writes=[ptpr])
                    tk.op("dve", lambda e: e.memset(cnew[:, :], 0.0), writes=[cnew_r])
                    tk.op("act", lambda e: e.copy(out=cnew[:4, 0:256], in_=ptp[:4, 0:256]), reads=[ptpr], writes=[cnew_r])
                    tk.op("dve", lambda e: e.memset(cnew[:4, 256:257], 1.0), writes=[cnew_r])
                    psc, pscr = bank()
                    for k in range(3):
                        m = 128
                        src = ckv_b[:, k, tcol:tcol + 4] if k < 2 else kr_b[:, tcol:tcol + 4]
                        tk.op("pe", lambda e, k=k, m=m, src=src: e.matmul(psc[:4, 0:32], src, Qs[:m, k, s, :],
                                                                          start=(k == 0), stop=(k == 2)),
                              reads=kvregs(tcol, 4) + [Qs_r], writes=[pscr])
                    pd, pd_r = pd_s.get()
                    tk.op("act", lambda e: e.activation(out=pd[:4, :], in_=psc[:4, 0:32], func=AF.Exp, scale=MLA_SCALE),
                          reads=[pscr], writes=[pd_r])
                    tk.op("dve", lambda e: e.tensor_tensor(out=pd[:4, :], in0=pd[:4, :], in1=DM_b[:4, :], op=ALU.mult),
                          reads=[c_r], writes=[pd_r])
                    tk.op("pe", lambda e: e.matmul(acc[:32, 0:257], pd[:4, :], cnew[:4, 0:257], start=False, stop=True),
                          reads=[pd_r, cnew_r], writes=[acc_r])
                    tk.op("dve", lambda e: e.reciprocal(out=rsd[:, :], in_=acc[:32, 256:257]), reads=[acc_r], writes=[oln_r])
                    tk.op("act", lambda e: e.activation(out=oln[:, :], in_=acc[:32, 0:256], func=AF.Copy, scale=rsd[:, 0:1]),
                          reads=[acc_r, oln_r], writes=[oln_r])
                    release(acc_l)
                    ptp, ptpr = bank()
                    for k in range(2):
                        tk.op("pe", lambda e, k=k: e.matmul(ptp[:, k * 32:(k + 1) * 32], oln[:, k * 128:(k + 1) * 128],
                                                            I_b[:32, :32], start=True, stop=True),
                              reads=[oln_r, c_r], writes=[ptpr])
                    evac(olT[:, :, :], ptp[:, 0:64].rearrange("p (k q) -> p k q", k=2), [ptpr], [olT_r])
                    pov, povr = bank()
                    for hh in range(NH):
                        for k in range(2):
                            tk.op("pe", lambda e, k=k, hh=hh: e.matmul(pov[:, hh * 4:(hh + 1) * 4], wuv[:, hh, k, :],
                                                                       olT[:, k, hh * 4:(hh + 1) * 4],
                                                                       start=(k == 0), stop=(k == 1)),
                                  reads=[w2_r, olT_r], writes=[povr])
                    tk.op("act", lambda e: e.copy(out=oT[:, :, so + s * 4:so + s * 4 + 4],
                                                  in_=pov[:, 0:32].rearrange("p (h t) -> p h t", h=NH)),
                          reads=[povr], writes=oT_r)
            for b in range(nbt + 2):
                if b < nbt:
                    stA(b)
                if 0 <= b - 1 < nbt:
                    stB(b - 1)
                if 0 <= b - 2 < nbt:
                    stC(b - 2)
        tk.barrier()
        p2.close()
        p4 = contextlib.ExitStack()
        wo = sb(p4, "m_wo", [128, KC, D], BF16)
        wo_r = Reg("m_wo")
        for k0 in range(0, KC, 4):
            tk.dma("pool", wch[2], lambda e, k0=k0: e.dma_start(
                out=wo[:, k0:k0 + 4, :], in_=m_wo.rearrange("(k p) o -> p k o", p=128)[:, k0:k0 + 4, :]),
                writes=[wo_r], cont=(k0 > 0))
        y_s = Scratch(p4, "m_y", [128, KC, 512], F32, 2, nreg=KC)
        sq_s = Scratch(p4, "m_sq4", [128, KC, 512], BF16, 1)
        t1_s = Scratch(p4, "m_t14", [128, 512], F32, 2)
        t2_s = Scratch(p4, "m_t24", [128, 512], F32, 2)
        for (ts, n) in ftiles(t0, t1):
            o = ts - t0
            y, y_r = y_s.get()
            for oc in range(KC):
                pb_, pbr = bank()
                for k in range(NH):
                    tk.op("pe", lambda e, k=k: e.matmul(pb_[:, :n], wo[:, k, oc * 128:(oc + 1) * 128], oT[:, k, o:o + n],
                                                        start=(k == 0), stop=(k == NH - 1)),
                          reads=[wo_r, oT_r[k]], writes=[pbr])
                evac(y[:, oc, :n], pb_[:, :n], [pbr], [y_r[oc]])
            postnorm_add(0, 1, y, y_r, 0, ts, n, sq_s, t1_s, t2_s)
        tk.barrier()
        p4.close()
        ph.close()


    Sst_r = [Reg("Sst%d" % i) for i in range(NH)]
    craw_r = [Reg("craw%d" % i) for i in range(24)]
    gst = {}

    def gdn_init():
        gst["Sst"] = sb(es, "Sst", [128, NH, 128], F32)
        gst["craw"] = sb(es, "craw", [128, 24, 3], F32)
        tk.op("dve", lambda e: e.memset(gst["Sst"][:], 0.0), writes=Sst_r)
        tk.op("dve", lambda e: e.memset(gst["craw"][:], 0.0), writes=craw_r)
        XD_ = F32 if INV_F32 else BF16
        gst["gm"] = sb(es, "gm", [128, 1408], XD_)
        gst["gm_r"] = Reg("gm")
        gst["i4x"] = sb(es, "i4x", [128, 512], XD_)
        if INV_F32:
            tk.dma("sp", ch_in, lambda e: e.dma_start(out=gst["gm"][:], in_=gmask_d), writes=[gst["gm_r"]])
            tk.dma("sp", ch_in, lambda e: e.dma_start(out=gst["i4x"][:], in_=consts_d[:, 1024:1536]), writes=[gst["gm_r"]])
        else:
            tk.dma("pool", ch_cb, lambda e: e.dma_start(out=gst["gm"][:], in_=gmask_d), writes=[gst["gm_r"]])
            tk.dma("pool", ch_cb, lambda e: e.dma_start(out=gst["i4x"][:], in_=consts_d[:, 1024:1536]),
                   writes=[gst["gm_r"]], cont=True)

    def v3(ap2, C, nb):
        return ap2.rearrange("p (j c) -> p j c", c=128)[:, :nb, :C]

    def gdn_phase(t0, t1):
        Sst, craw = gst["Sst"], gst["craw"]
        gm, gm_r, i4x = gst["gm"], gst["gm_r"], gst["i4x"]
        XD = F32 if INV_F32 else BF16
        I_x = I_f if INV_F32 else I_b
        TH = t1 - t0
        tl = tiles(t0, t1)
        has_s = t1 > SEQ
        THp = min(t1, SEQ) - t0
        NB = THp // 128
        ph = contextlib.ExitStack()
        u = sb(ph, "g_u", [128, KC, TH], BF16)
        u_r = Reg("g_u")
        oT = sb(ph, "g_oT", [128, NH, TH], BF16)
        oT_r = [Reg("g_oT%d" % i) for i in range(NH)]
        p0 = contextlib.ExitStack()
        sq_s = Scratch(p0, "g_sq", [128, KC, 512], BF16, 1)
        t1_s = Scratch(p0, "g_t1", [128, 512], F32, 2)
        t2_s = Scratch(p0, "g_t2", [128, 512], F32, 2)
        for (ts, n) in ftiles(t0, t1):
            rmsnorm_pre(1, 0, ts, n, u, u_r, ts - t0, sq_s, t1_s, t2_s)
        tk.barrier()
        p0.close()
        p1 = contextlib.ExitStack()
        t1_s = Scratch(p1, "g_t1b", [128, 512], F32, 1)
        t2_s = Scratch(p1, "g_t2b", [128, 512], F32, 1)
        cw = sb(p1, "g_cw", [128, 24, 4], F32)
        gout = sb(p1, "g_gout", [128, 1], F32)
        abc = sb(p1, "g_abc", [128, 2, NH], F32)
        wba = sb(p1, "g_wba", [128, KC, 16], BF16)
        gp_r = Reg("g_par")
        tk.dma("sp", ch_in, lambda e: e.dma_start(out=cw[:], in_=d_cw), writes=[gp_r])
        tk.dma("sp", ch_in, lambda e: e.dma_start(out=gout[:], in_=d_gout), writes=[gp_r])
        tk.dma("sp", ch_in, lambda e: e.dma_start(out=abc[:, 0, :], in_=d_alog[0].partition_broadcast(128)), writes=[gp_r])
        tk.dma("sp", ch_in, lambda e: e.dma_start(out=abc[:, 1, :], in_=d_dtb[0].partition_broadcast(128)), writes=[gp_r])
        tk.op("act", lambda e: e.activation(out=abc[:, 0, :], in_=abc[:, 0, :], func=AF.Exp), reads=[gp_r], writes=[gp_r])
        wba_r = Reg("g_wba")
        tk.dma("pool", wch[2], lambda e: e.dma_start(
            out=wba[:], in_=d_win.rearrange("(k p) o -> p k o", p=128)[:, :, 4096:4112]), writes=[wba_r])
        if has_s:
            cvin = sb(p1, "g_cvin", [128, 24, 4, 3], F32)
            cvs = sb(p1, "g_cvs", [128, 24, 4, 3], F32)
            cvin_r = Reg("g_cvin")
            cvs_r = Reg("g_cvs")
            tk.dma("sp", ch_in, lambda e: e.dma_start(out=cvin[:], in_=cv_in.rearrange("(c p) (s j) -> p c s j", p=128, j=3)),
                   writes=[cvin_r])
        NBS = NB + (1 if has_s else 0)
        gtok = sb(p1, "g_gtok", [128, NBS, 8], F32)
        btok = sb(p1, "g_btok", [128, NBS, 8], F32)
        nbtok = sb(p1, "g_nbtok", [128, NBS, 8], F32)
        gt_r = Reg("g_gt")
        xt_ = sb(p1, "g_xt", [128, NBS, 8], F32)
        pba, pbar = bank()
        for b in range(NB):
            for k in range(KC):
                tk.op("pe", lambda e, k=k, b=b: e.matmul(pba[:, b * 16:(b + 1) * 16], u[:, k, b * 128:(b + 1) * 128],
                                                         wba[:, k, :], start=(k == 0), stop=(k == KC - 1)),
                      reads=[u_r, wba_r], writes=[pbar])
        if has_s:
            pbs, pbsr = bank()
            for s_ in range(4):
                for k in range(KC):
                    tk.op("pe", lambda e, k=k, s_=s_: e.matmul(pbs[:4, s_ * 16:(s_ + 1) * 16],
                                                               u[:, k, THp + 4 * s_:THp + 4 * s_ + 4], wba[:, k, :],
                                                               start=(k == 0), stop=(k == KC - 1)),
                          reads=[u_r, wba_r], writes=[pbsr])
        def ba_post(P_, src3, dstsl, preg):
            gt, bt, nbt, xt = dstsl
            nblk = src3.shape[1]
            tk.op("act", lambda e: e.activation(out=bt, in_=src3[:, :, 0:8], func=AF.Sigmoid), reads=[preg], writes=[gt_r])
            tk.op("dve", lambda e: e.tensor_scalar(out=nbt, in0=bt, scalar1=-1.0, scalar2=None, op0=ALU.mult),
                  reads=[gt_r], writes=[gt_r])
            for b in range(nblk):
                tk.op("dve", lambda e, b=b: e.tensor_tensor(out=xt[:, b, :], in0=src3[:, b, 8:16], in1=abc[:P_, 1, :],
                                                            op=ALU.add), reads=[preg, gp_r], writes=[gt_r])
            tk.op("act", lambda e: e.activation(out=xt, in_=xt, func=AF.Exp), reads=[gt_r], writes=[gt_r])
            tk.op("act", lambda e: e.activation(out=xt, in_=xt, func=AF.Ln, bias=epst[:P_, 1:2]), reads=[gt_r, c_r],
                  writes=[gt_r])
            for b in range(nblk):
                tk.op("dve", lambda e, b=b: e.scalar_tensor_tensor(out=gt[:, b, :], in0=xt[:, b, :], scalar=-1.0,
                                                                   in1=abc[:P_, 0, :], op0=ALU.mult, op1=ALU.mult),
                      reads=[gt_r, gp_r], writes=[gt_r])

        ba_post(128, pba[:, 0:NB * 16].rearrange("p (b x) -> p b x", x=16),
                (gtok[:, 0:NB, :], btok[:, 0:NB, :], nbtok[:, 0:NB, :], xt_[:, 0:NB, :]), pbar)
        if has_s:
            gts = sb(p1, "g_gts", [4, 4, 8], F32)
            bts = sb(p1, "g_bts", [4, 4, 8], F32)
            nbts = sb(p1, "g_nbts", [4, 4, 8], F32)
            xts = sb(p1, "g_xts", [4, 4, 8], F32)
            ba_post(4, pbs[:4, 0:64].rearrange("p (b x) -> p b x", x=16), (gts[:], bts[:], nbts[:], xts[:]), pbsr)
        wsl = [sb(p1, "g_wsl%d" % i, [128, KC, 4, 128], BF16) for i in range(2)]
        wsl_r = [Reg("g_wsl%d" % i) for i in range(2)]
        d_win_v = d_win.rearrange("(k p) o -> p k o", p=128)

        def load_wsl(hh):
            s_ = hh % 2
            for w_ in range(4):
                col = w_ * 1024 + hh * 128
                tk.dma("pool", wch[s_], lambda e, w_=w_, col=col: e.dma_start(
                    out=wsl[s_][:, :, w_, :], in_=d_win_v[:, :, col:col + 128]), writes=[wsl_r[s_]], cont=(w_ > 0))

        raw = sb(p1, "g_raw", [128, 3 + THp], F32)
        raw_r = Reg("g_raw")
        raws = sb(p1, "g_raws", [128, 4, 7], F32)
        raws_r = Reg("g_raws")
        acc = sb(p1, "g_acc", [128, TH], F32)
        acc_r = Reg("g_acc")
        sqb_s = Scratch(p1, "g_sqb", [128, 512], BF16, 2)
        qn_s = Scratch(p1, "g_qn", [128, TH], BF16, 2)
        kn_s = Scratch(p1, "g_kn", [128, TH], BF16, 2)
        vv_s = Scratch(p1, "g_vv", [128, TH], BF16, 2)
        zs_s = Scratch(p1, "g_zs", [128, TH], BF16, 2)
        if has_s:
            Ssm_s = Scratch(p1, "g_Ssm", [128, 4, 128], F32, 2, nreg=4, chan=True)
            ssl_ch = [tk.chan() for _ in range(2)]

        class BB:
            pass

        def make_bb(tag, wide):
            b_ = BB()
            CW = 128 if wide else 4
            F1 = sb(p1, "b_F1" + tag, [128, 4, CW], F32)
            F2 = sb(p1, "b_F2" + tag, [128, 4, CW], F32)
            b_.R, b_.gB = F1, F2
            F1b, F2b = F1[:].bitcast(BF16), F2[:].bitcast(BF16)
            b_.LT, b_.X = F1b[:, :, 0:CW], F1b[:, :, CW:2 * CW]
            b_.W1, b_.W2 = F2b[:, :, 0:CW], F2b[:, :, CW:2 * CW]
            b_.sc = sb(p1, "b_sc" + tag, [128, 4, 4], F32)
            b_.gl = sb(p1, "b_gl" + tag, [128, 4, 4], F32)
            b_.E1 = sb(p1, "b_E1" + tag, [128, 4, CW], BF16)
            b_.E2 = sb(p1, "b_E2" + tag, [128, 4, CW], BF16)
            b_.Ao, b_.AoT = b_.E1, b_.E2
            b_.Lp = sb(p1, "b_Lp" + tag, [128, 4, CW], BF16)
            b_.XT = sb(p1, "b_XT" + tag, [128, 4, CW], BF16)
            b_.qkt = sb(p1, "b_qkt" + tag, [128, 4, CW], BF16)
            b_.qd = sb(p1, "b_qd" + tag, [128, 4, CW], BF16)
            AOFF = _os.environ.get("AOFF", "")
            if wide:
                b_.egbc = sb(p1, "b_eg" + tag, [128, 4, 128], BF16)
                b_.utok = b_.egbc if "u" not in AOFF else sb(p1, "b_ut" + tag, [128, 4, 128], BF16)
                b_.wtok = b_.Lp if "w" not in AOFF else sb(p1, "b_wt" + tag, [128, 4, 128], BF16)
                b_.nWk = F1b[:, :, 0:128] if "n" not in AOFF else sb(p1, "b_nw" + tag, [128, 4, 128], BF16)
                if "a" in AOFF:
                    b_.Ao = sb(p1, "b_Ao" + tag, [128, 4, 128], BF16)
                    b_.AoT = sb(p1, "b_AoT" + tag, [128, 4, 128], BF16)
                if "x" in AOFF:
                    b_.LT = sb(p1, "b_LT" + tag, [128, 4, 128], BF16)
                    b_.X = sb(p1, "b_X" + tag, [128, 4, 128], BF16)
                    b_.W1 = sb(p1, "b_W1" + tag, [128, 4, 128], BF16)
                    b_.W2 = sb(p1, "b_W2" + tag, [128, 4, 128], BF16)
            else:
                b_.egbc = sb(p1, "b_eg" + tag, [128, 4, 4], BF16)
                b_.utok = sb(p1, "b_ut" + tag, [128, 4, 128], BF16)
                b_.wtok = sb(p1, "b_wt" + tag, [128, 4, 128], BF16)
                b_.nWk = sb(p1, "b_nw" + tag, [128, 4, 128], BF16)
            for nm in ("kdec", "kbg", "vb"):
                setattr(b_, nm, sb(p1, "b_%s%s" % (nm, tag), [128, 4, 128], BF16))
            rg = {nm: Reg("b_%s%s" % (nm, tag)) for nm in ("F1", "F2", "E1", "E2", "eg", "Lp", "XT", "qkt", "qd", "sc",
                                                          "kdec", "kbg", "vb", "ut", "wt", "nw")}
            b_.r = {"R": rg["F1"], "LT": rg["F1"], "X": rg["F1"], "gB": rg["F2"], "W1": rg["F2"], "W2": rg["F2"],
                    "E1": rg["E1"], "Ao": rg["E1"], "E2": rg["E2"], "AoT": rg["E2"], "Lp": rg["Lp"], "XT": rg["XT"],
                    "qkt": rg["qkt"], "qd": rg["qd"], "sc": rg["sc"], "kdec": rg["kdec"], "kbg": rg["kbg"],
                    "vb": rg["vb"]}
            if wide:
                b_.r.update({"eg": rg["eg"], "utok": rg["eg"], "wtok": rg["Lp"], "nWk": rg["F1"]})
            else:
                b_.r.update({"eg": rg["eg"], "utok": rg["ut"], "wtok": rg["wt"], "nWk": rg["nw"]})
            return b_

        bbs = {"p0": make_bb("p0", True), "p1": make_bb("p1", True)}
        if has_s:
            bbs["s"] = make_bb("s", False)
        bb_i = [0]
        Sb_s = Scratch(p1, "g_Sb", [128, 128], BF16, 2)
        vn_s = Scratch(p1, "g_vn", [128, 128], BF16, 2)
        on_s = Scratch(p1, "g_on", [128, 128], BF16, 2)
        jk_s = Scratch(p1, "g_jk", [128, 128], BF16, 2)
        ss_s = Scratch(p1, "g_ss", [128, 4], F32, 2)
        og_s = Scratch(p1, "g_og", [128, 128], BF16, 2)

        NWARM = int(_os.environ.get("NWARM", "0"))
        if NWARM:
            (wps, wps_r, _), = warm_l = reserve(1)

        def warm():
            for _ in range(NWARM):
                tk.op("pe", lambda e: e.matmul(wps[:, 0:512], ONE_b, cb[:, 0:512], start=True, stop=True),
                      reads=[c_r], writes=[wps_r])

        def gdn_pre(b_, hh, C, nb, o0, gmat, bcols, qn, qn_r, kn, kn_r, vv, vv_r):
            r = b_.r
            NC_ = nb * C
            gcols = [gmat[:, j:j + 1] for j in range(nb)]

            def flat(t, P_=C):
                return t[:].rearrange("p j c -> p (j c)")[:P_, 0:NC_]

            def cv(t, P_=C):
                return flat(t, P_).rearrange("p (j c) -> p j c", c=C)

            def cvp(ps, P_=C):
                return ps[:P_, 0:NC_].rearrange("p (j c) -> p j c", c=C)

            for j in range(nb):
                tk.op("dve", lambda e, j=j: e.tensor_scalar(out=cv(b_.R)[:, j, :], in0=MS_f[:C, :C], scalar1=gcols[j],
                                                            scalar2=None, op0=ALU.mult), reads=[c_r, gt_r], writes=[r["R"]])
                tk.op("dve", lambda e, j=j: e.tensor_scalar(out=cv(b_.gB)[:, j, :], in0=U_f[:C, :C], scalar1=gcols[j],
                                                            scalar2=None, op0=ALU.mult), reads=[c_r, gt_r], writes=[r["gB"]])
            k1, k1r = bank()
            k2, k2r = bank()
            k3, k3r = bank()
            k4, k4r = bank()
            tk.op("pe", lambda e: e.matmul(k1[:C, 0:NC_], U_f[:C, :C], flat(b_.R), start=True, stop=True),
                  reads=[c_r, r["R"]], writes=[k1r])
            tk.op("pe", lambda e: e.matmul(k2[:C, 0:NC_], MS_f[:C, :C], flat(b_.gB), start=True, stop=True),
                  reads=[c_r, r["gB"]], writes=[k2r])
            tk.op("pe", lambda e: e.matmul(k3[:, 0:NC_], ONE_f[:C, :], flat(b_.gB), start=True, stop=True),
                  reads=[c_r, r["gB"]], writes=[k3r])
            tk.op("pe", lambda e: e.matmul(k4[:C, 0:nb], U_f[:C, :C], gmat, start=True, stop=True),
                  reads=[c_r, gt_r], writes=[k4r])
            tk.op("pe", lambda e: e.matmul(k4[:, 16:16 + nb], ONE_f[:C, :], gmat, start=True, stop=True),
                  reads=[c_r, gt_r], writes=[k4r])
            tk.op("act", lambda e: e.activation(out=cv(b_.E1), in_=cvp(k1), func=AF.Exp), reads=[k1r], writes=[r["E1"]])
            tk.op("act", lambda e: e.activation(out=cv(b_.E2), in_=cvp(k2), func=AF.Exp), reads=[k2r], writes=[r["E2"]])
            tk.op("act", lambda e: e.activation(out=cv(b_.egbc, 128), in_=cvp(k3, 128), func=AF.Exp),
                  reads=[k3r], writes=[r["eg"]])
            tk.op("act", lambda e: e.activation(out=b_.sc[:C, :nb, 0:1], in_=k4[:C, 0:nb].unsqueeze(2), func=AF.Exp),
                  reads=[k4r], writes=[r["sc"]])
            tk.op("act", lambda e: e.activation(out=b_.gl[:, :nb, 0:1], in_=k4[:, 16:16 + nb].unsqueeze(2), func=AF.Exp),
                  reads=[k4r], writes=[r["sc"]])
            tk.op("dve", lambda e: e.tensor_copy(out=b_.sc[:C, :nb, 1:2], in_=cv(b_.E2)[:, :, C - 1:C]),
                  reads=[r["E2"]], writes=[r["sc"]])
            tk.op("pool", lambda e: e.tensor_tensor(out=cv(b_.E1), in0=cv(b_.E1),
                                                    in1=cb[:C, 256:256 + C].unsqueeze(1).to_broadcast([C, nb, C]),
                                                    op=ALU.mult), reads=[c_r], writes=[r["E1"]])
            tk.op("pool", lambda e: e.tensor_tensor(out=cv(b_.E2), in0=cv(b_.E2),
                                                    in1=cb[:C, 128:128 + C].unsqueeze(1).to_broadcast([C, nb, C]),
                                                    op=ALU.mult), reads=[c_r, r["sc"]], writes=[r["E2"]])
            for j in range(nb):
                tk.op("dve", lambda e, j=j: e.tensor_tensor(out=b_.sc[:C, j, 2:3], in0=b_.sc[:C, j, 0:1], in1=bcols[j],
                                                            op=ALU.mult), reads=[gt_r], writes=[r["sc"]])
            yield
            k5, k5r = bank()
            k6, k6r = bank()
            k7, k7r = bank()
            k8, k8r = bank()
            for j in range(nb):
                cs = slice(j * 128, j * 128 + C)
                ts_ = slice(o0 + j * C, o0 + (j + 1) * C)
                tk.op("pe", lambda e, cs=cs, ts_=ts_: e.matmul(k5[:C, cs], kn[:, ts_], kn[:, ts_], start=True, stop=True),
                      reads=[kn_r], writes=[k5r])
                tk.op("pe", lambda e, cs=cs, ts_=ts_: e.matmul(k6[:C, cs], kn[:, ts_], qn[:, ts_], start=True, stop=True),
                      reads=[kn_r, qn_r], writes=[k6r])
                tk.op("pe", lambda e, j=j, ts_=ts_: e.matmul(k7[:C, j * 128:(j + 1) * 128], kn[:, ts_], I_b,
                                                             start=True, stop=True), reads=[kn_r, c_r], writes=[k7r])
                tk.op("pe", lambda e, j=j, ts_=ts_: e.matmul(k8[:C, j * 128:(j + 1) * 128], vv[:, ts_], I_b,
                                                             start=True, stop=True), reads=[vv_r, c_r], writes=[k8r])
            for j in range(nb):
                cs = slice(j * 128, j * 128 + C)
                tk.op("dve", lambda e, j=j, cs=cs: e.scalar_tensor_tensor(
                    out=b_.Lp[:C, j, :C], in0=k5[:C, cs], scalar=bcols[j], in1=cv(b_.E1)[:, j, :],
                    op0=ALU.mult, op1=ALU.mult), reads=[k5r, r["E1"], gt_r], writes=[r["Lp"]])
                tk.op("act", lambda e, j=j: e.activation(out=b_.kdec[:C, j, :], in_=k7[:C, j * 128:(j + 1) * 128],
                                                         func=AF.Copy, scale=b_.sc[:C, j, 1:2]),
                      reads=[k7r, r["sc"]], writes=[r["kdec"]])
                tk.op("dve", lambda e, j=j: e.tensor_scalar(out=b_.kbg[:C, j, :], in0=k7[:C, j * 128:(j + 1) * 128],
                                                            scalar1=b_.sc[:C, j, 2:3], scalar2=None, op0=ALU.mult),
                      reads=[k7r, r["sc"]], writes=[r["kbg"]])
                tk.op("act", lambda e, j=j: e.activation(out=b_.vb[:C, j, :], in_=k8[:C, j * 128:(j + 1) * 128],
                                                         func=AF.Copy, scale=bcols[j]),
                      reads=[k8r, gt_r], writes=[r["vb"]])
            tk.op("dve", lambda e: e.tensor_tensor(out=b_.qkt[:C, :nb, :C], in0=v3(k6[:C, :], C, nb),
                                                   in1=cv(b_.E2), op=ALU.mult),
                  reads=[k6r, r["E2"]], writes=[r["qkt"]])
            tk.op("pool", lambda e: e.tensor_tensor(
                out=b_.qd[:, :nb, :C], in0=qn[:, o0:o0 + nb * C].rearrange("p (j c) -> p j c", c=C),
                in1=cv(b_.egbc, 128), op=ALU.mult), reads=[qn_r, r["eg"]], writes=[r["qd"]])
            yield
            def bc(off):
                return gm[:C, off:off + C].unsqueeze(1).to_broadcast([C, nb, C])

            def mm4(lhs, rhs, lr, rr, PO=C, wl=C, wr=C):
                warm()
                kx, kxr = bank()
                for j in range(nb):
                    tk.op("pe", lambda e, j=j: e.matmul(kx[:PO, j * 128:j * 128 + wr], lhs[:C, j, :wl],
                                                        rhs[:C, j, :wr], start=True, stop=True),
                          reads=[lr, rr], writes=[kxr])
                return kx, kxr

            kt, ktr = bank()
            for j in range(nb):
                cs = slice(j * 128, j * 128 + C)
                tk.op("pe", lambda e, j=j, cs=cs: e.matmul(kt[:C, cs], b_.Lp[:C, j, :C], I_x[:C, :C], start=True, stop=True),
                      reads=[r["Lp"], c_r], writes=[ktr])
            tk.op("act", lambda e: e.copy(out=b_.LT[:C, :nb, :C], in_=v3(kt[:C, :], C, nb)), reads=[ktr], writes=[r["LT"]])
            tk.op("dve", lambda e: e.scalar_tensor_tensor(out=b_.Ao[:C, :nb, :C], in0=b_.Lp[:C, :nb, :C], scalar=-1.0,
                                                          in1=bc(0), op0=ALU.mult, op1=ALU.mult),
                  reads=[r["Lp"], gm_r], writes=[r["Ao"]])
            tk.op("dve", lambda e: e.scalar_tensor_tensor(out=b_.AoT[:C, :nb, :C], in0=b_.LT[:C, :nb, :C], scalar=-1.0,
                                                          in1=bc(768), op0=ALU.mult, op1=ALU.mult),
                  reads=[r["LT"], gm_r], writes=[r["AoT"]])
            tk.op("dve", lambda e: e.tensor_tensor(out=b_.X[:C, :nb, :C], in0=b_.Ao[:C, :nb, :C],
                                                   in1=v3(i4x[:C, :], C, nb), op=ALU.add),
                  reads=[r["Ao"], gm_r], writes=[r["X"]])
            tk.op("dve", lambda e: e.tensor_tensor(out=b_.XT[:C, :nb, :C], in0=b_.AoT[:C, :nb, :C],
                                                   in1=v3(i4x[:C, :], C, nb), op=ALU.add),
                  reads=[r["AoT"], gm_r], writes=[r["XT"]])
            kx, kxr = mm4(b_.AoT, b_.Ao, r["AoT"], r["Ao"])
            tk.op("act", lambda e: e.copy(out=b_.W1[:C, :nb, :C], in_=v3(kx[:C, :], C, nb)), reads=[kxr], writes=[r["W1"]])
            yield
            kxa, kxar = mm4(b_.XT, b_.W1, r["XT"], r["W1"])
            kxb, kxbr = mm4(b_.W1, b_.XT, r["W1"], r["XT"])
            tk.op("dve", lambda e: e.tensor_tensor(out=b_.X[:C, :nb, :C], in0=v3(kxa[:C, :], C, nb),
                                                   in1=b_.X[:C, :nb, :C], op=ALU.add),
                  reads=[kxar, r["X"]], writes=[r["X"]])
            tk.op("dve", lambda e: e.tensor_tensor(out=b_.XT[:C, :nb, :C], in0=v3(kxb[:C, :], C, nb),
                                                   in1=b_.XT[:C, :nb, :C], op=ALU.add),
                  reads=[kxbr, r["XT"]], writes=[r["XT"]])
            yield
            levels = [b for b in (4, 8, 16, 32, 64) if 2 * b <= C]
            for li, b in enumerate(levels):
                last = (li == len(levels) - 1)
                off = 128 + li * 128
                kx, kxr = bank()
                for j in range(nb):
                    cs = slice(j * 128, j * 128 + C)
                    tk.op("pe", lambda e, j=j, cs=cs: e.matmul(kx[:C, cs], b_.LT[:C, j, :C], b_.X[:C, j, :C],
                                                               start=True, stop=False), reads=[r["LT"], r["X"]], writes=[kxr])
                    tk.op("pe", lambda e, j=j, cs=cs: e.matmul(kx[:C, cs], I_b[:C, :C], I_b[:C, :C],
                                                               start=False, stop=True), reads=[c_r], writes=[kxr])
                tk.op("dve", lambda e, kx=kx, off=off: e.tensor_tensor(out=b_.W1[:C, :nb, :C], in0=v3(kx[:C, :], C, nb),
                                                                       in1=bc(off), op=ALU.mult),
                      reads=[kxr, gm_r], writes=[r["W1"]])
                yield
                kb_, kbr = mm4(b_.W1, b_.XT, r["W1"], r["XT"])
                if not last:
                    ka_, kar = mm4(b_.XT, b_.W1, r["XT"], r["W1"])
                    tk.op("act", lambda e, ka_=ka_: e.copy(out=b_.X[:C, :nb, :C], in_=v3(ka_[:C, :], C, nb)),
                          reads=[kar], writes=[r["X"]])
                tk.op("dve", lambda e, kb_=kb_: e.tensor_copy(out=b_.XT[:C, :nb, :C], in_=v3(kb_[:C, :], C, nb)),
                      reads=[kbr], writes=[r["XT"]])
                yield
            TTc, rTT = b_.XT, r["XT"]
            ku, kur = bank()
            kw, kwr = bank()
            for j in range(nb):
                tk.op("pe", lambda e, j=j: e.matmul(ku[:C, j * 128:(j + 1) * 128], TTc[:C, j, :C], b_.vb[:C, j, :],
                                                    start=True, stop=True), reads=[rTT, r["vb"]], writes=[kur])
                tk.op("pe", lambda e, j=j: e.matmul(kw[:C, j * 128:(j + 1) * 128], TTc[:C, j, :C], b_.kbg[:C, j, :],
                                                    start=True, stop=True), reads=[rTT, r["kbg"]], writes=[kwr])
            tk.op("act", lambda e: e.copy(out=b_.utok[:C, :nb, :], in_=ku[:C, 0:nb * 128].rearrange("p (j c) -> p j c", c=128)),
                  reads=[kur], writes=[r["utok"]])
            tk.op("dve", lambda e: e.tensor_copy(out=b_.wtok[:C, :nb, :], in_=kw[:C, 0:nb * 128].rearrange("p (j c) -> p j c", c=128)),
                  reads=[kwr], writes=[r["wtok"]])
            yield
            kk, kkr = bank()
            kq, kqr = bank()
            for j in range(nb):
                tk.op("pe", lambda e, j=j: e.matmul(kk[:, j * 128:(j + 1) * 128], b_.wtok[:C, j, :], b_.kdec[:C, j, :],
                                                    start=True, stop=True), reads=[r["wtok"], r["kdec"]], writes=[kkr])
                tk.op("pe", lambda e, j=j: e.matmul(kq[:, j * 128:j * 128 + C], b_.wtok[:C, j, :], b_.qkt[:C, j, :C],
                                                    start=True, stop=True), reads=[r["wtok"], r["qkt"]], writes=[kqr])
            tk.op("act", lambda e: e.activation(out=b_.nWk[:, :nb, :], in_=kk[:, 0:nb * 128].rearrange("p (j c) -> p j c", c=128),
                                                func=AF.Copy, scale=-1.0), reads=[kkr], writes=[r["nWk"]])
            tk.op("dve", lambda e: e.tensor_tensor(out=b_.qd[:, :nb, :C], in0=b_.qd[:, :nb, :C], in1=v3(kq[:, :], C, nb),
                                                   op=ALU.subtract), reads=[kqr, r["qd"]], writes=[r["qd"]])
            yield
        def gdn_scan(b_, hh, C, nb, o0, S_f, S_regs, zs, zs_r, carry):
            r = b_.r
            Sb, Sb_r = None, None
            for j in range(nb):
                Sf, Sreg = S_f[j], S_regs[j]
                if Sb is None or not carry:
                    Sb, Sb_r = Sb_s.get()
                    tk.op("dve", lambda e, Sb=Sb, Sf=Sf: e.tensor_copy(out=Sb[:, :], in_=Sf), reads=[Sreg], writes=[Sb_r])
                cs = slice(o0 + j * C, o0 + (j + 1) * C)
                ks_, ksr = bank()
                tk.op("pe", lambda e, j=j: e.matmul(ks_[:, 0:128], b_.kdec[:C, j, :], b_.utok[:C, j, :], start=True, stop=False),
                      reads=[r["kdec"], r["utok"]], writes=[ksr])
                tk.op("pe", lambda e, j=j, Sb=Sb: e.matmul(ks_[:, 0:128], b_.nWk[:, j, :], Sb[:, :], start=False, stop=True),
                      reads=[r["nWk"], Sb_r], writes=[ksr])
                ko, kor = bank()
                tk.op("pe", lambda e, j=j: e.matmul(ko[:C, 0:128], b_.qkt[:C, j, :C], b_.utok[:C, j, :], start=True, stop=False),
                      reads=[r["qkt"], r["utok"]], writes=[kor])
                tk.op("pe", lambda e, j=j, Sb=Sb: e.matmul(ko[:C, 0:128], b_.qd[:, j, :C], Sb[:, :], start=False, stop=True),
                      reads=[r["qd"], Sb_r], writes=[kor])
                tk.op("dve", lambda e, j=j, Sf=Sf: e.scalar_tensor_tensor(out=Sf, in0=Sf, scalar=b_.gl[:, j, 0:1],
                                                                          in1=ks_[:, 0:128], op0=ALU.mult, op1=ALU.add),
                      reads=[ksr, r["sc"], Sreg], writes=[Sreg])
                if carry and j + 1 < nb:
                    Sb, Sb_r = Sb_s.get()
                    tk.op("dve", lambda e, Sb=Sb, Sf=Sf: e.tensor_copy(out=Sb[:, :], in_=Sf), reads=[Sreg], writes=[Sb_r])
                jk, jk_r = jk_s.get()
                ss, ss_r = ss_s.get()
                tk.op("act", lambda e, jk=jk, ss=ss: e.activation(out=jk[:C, :], in_=ko[:C, 0:128], func=AF.Square,
                                                                  accum_out=ss[:C, 0:1]), reads=[kor], writes=[jk_r, ss_r])
                tk.op("act", lambda e, ss=ss: e.activation(out=ss[:C, 1:2], in_=ss[:C, 0:1], func=AF.Ln,
                                                           bias=epst[:C, 0:1], scale=1.0 / 128), reads=[c_r], writes=[ss_r])
                tk.op("act", lambda e, ss=ss: e.activation(out=ss[:C, 2:3], in_=ss[:C, 1:2], func=AF.Exp, scale=-0.5),
                      writes=[ss_r])
                on, on_r = on_s.get()
                tk.op("act", lambda e, on=on, ss=ss: e.activation(out=on[:C, :], in_=ko[:C, 0:128], func=AF.Copy,
                                                                  scale=ss[:C, 2:3]), reads=[kor, ss_r], writes=[on_r])
                kp, kpr = bank()
                tk.op("pe", lambda e, on=on: e.matmul(kp[:, 0:C], on[:C, :], I_b[:C, :C], start=True, stop=True),
                      reads=[on_r, c_r], writes=[kpr])
                og, og_r = og_s.get()
                tk.op("act", lambda e, og=og: e.activation(out=og[:, 0:C], in_=kp[:, 0:C], func=AF.Copy, scale=gout[:, 0:1]),
                      reads=[kpr, gp_r], writes=[og_r])
                tk.op("pool", lambda e, og=og, cs=cs: e.tensor_tensor(out=oT[:, hh, cs], in0=og[:, 0:C], in1=zs[:, cs],
                                                                     op=ALU.mult), reads=[og_r, zs_r], writes=[oT_r[hh]])
                yield

        heads = {}

        def bulk(hh):
            if hh + 1 < NH:
                load_wsl(hh + 1)
            ws_, ws_r = wsl[hh % 2], wsl_r[hh % 2]
            outs = {}
            for w_ in range(3):
                cidx = w_ * 8 + hh
                tk.op("dve", lambda e: e.tensor_copy(out=raw[:, 0:3], in_=craw[:, cidx, :]),
                      reads=[craw_r[cidx]], writes=[raw_r])
                for (ts, n) in tl:
                    o = ts - t0
                    pb_, pbr = bank()
                    for k in range(KC):
                        tk.op("pe", lambda e, k=k: e.matmul(pb_[:, :n], ws_[:, k, w_, :], u[:, k, o:o + n],
                                                            start=(k == 0), stop=(k == KC - 1)),
                              reads=[ws_r, u_r], writes=[pbr])
                    if ts < SEQ:
                        evac(raw[:, 3 + o:3 + o + n], pb_[:, :n], [pbr], [raw_r])
                    else:
                        tk.op("act", lambda e: e.copy(out=raws[:, :, 3:7], in_=pb_[:, 0:16].rearrange("p (s t) -> p s t", s=4)),
                              reads=[pbr], writes=[raws_r])
                        tk.op("dve", lambda e: e.tensor_copy(out=raws[:, :, 0:3], in_=cvin[:, cidx, :, :]),
                              reads=[cvin_r], writes=[raws_r])
                        tk.op("dve", lambda e: e.tensor_copy(out=cvs[:, cidx, :, :], in_=raws[:, :, 4:7]),
                              reads=[raws_r], writes=[cvs_r])
                    yield
                tk.op("dve", lambda e: e.tensor_copy(out=craw[:, cidx, :], in_=raw[:, THp:THp + 3]),
                      reads=[raw_r], writes=[craw_r[cidx]])
                tk.op("dve", lambda e: e.tensor_scalar(out=acc[:, 0:THp], in0=raw[:, 0:THp], scalar1=cw[:, cidx, 0:1],
                                                       scalar2=None, op0=ALU.mult), reads=[raw_r, gp_r], writes=[acc_r])
                for j in range(1, 4):
                    tk.op("dve", lambda e, j=j: e.scalar_tensor_tensor(
                        out=acc[:, 0:THp], in0=raw[:, j:j + THp], scalar=cw[:, cidx, j:j + 1], in1=acc[:, 0:THp],
                        op0=ALU.mult, op1=ALU.add), reads=[raw_r, gp_r], writes=[acc_r])
                if has_s:
                    accs = acc[:, THp:THp + 16].rearrange("p (s t) -> p s t", s=4)
                    tk.op("dve", lambda e: e.tensor_scalar(out=accs, in0=raws[:, :, 0:4], scalar1=cw[:, cidx, 0:1],
                                                           scalar2=None, op0=ALU.mult), reads=[raws_r, gp_r], writes=[acc_r])
                    for j in range(1, 4):
                        tk.op("dve", lambda e, j=j: e.scalar_tensor_tensor(
                            out=accs, in0=raws[:, :, j:j + 4], scalar=cw[:, cidx, j:j + 1], in1=accs,
                            op0=ALU.mult, op1=ALU.add), reads=[raws_r, gp_r], writes=[acc_r])
                yield
                if w_ == 2:
                    vv, vv_r = vv_s.get()
                    tk.op("act", lambda e: e.activation(out=vv[:, :], in_=acc[:, :], func=AF.Silu), reads=[acc_r], writes=[vv_r])
                    outs[2] = (vv, vv_r)
                else:
                    tk.op("act", lambda e: e.activation(out=acc[:, :], in_=acc[:, :], func=AF.Silu), reads=[acc_r], writes=[acc_r])
                    dst, dst_r = (qn_s if w_ == 0 else kn_s).get()
                    outs[w_] = (dst, dst_r)
                    for (ts, n) in tl:
                        o = ts - t0
                        sqb, sqb_r = sqb_s.get()
                        tk.op("act", lambda e: e.activation(out=sqb[:, :n], in_=acc[:, o:o + n], func=AF.Square),
                              reads=[acc_r], writes=[sqb_r])
                        pb_, pbr = bank()
                        tk.op("pe", lambda e: e.matmul(pb_[:, :n], ONE_b, sqb[:, :n], start=True, stop=True),
                              reads=[sqb_r, c_r], writes=[pbr])
                        ta, tar = t1_s.get()
                        tb, tbr = t2_s.get()
                        rstd_from_ss(pb_[:, :n], pbr, 128, n, 1.0, ta, tar, tb, tbr)
                        if w_ == 0:
                            tk.op("dve", lambda e: e.scalar_tensor_tensor(
                                out=dst[:, o:o + n], in0=acc[:, o:o + n], scalar=128.0 ** -0.5, in1=tb[:, :n],
                                op0=ALU.mult, op1=ALU.mult), reads=[acc_r, tbr], writes=[dst_r])
                        else:
                            tk.op("dve", lambda e: e.tensor_tensor(out=dst[:, o:o + n], in0=acc[:, o:o + n], in1=tb[:, :n],
                                                                   op=ALU.mult), reads=[acc_r, tbr], writes=[dst_r])
                        yield
            zs, zs_r = zs_s.get()
            for (ts, n) in tl:
                o = ts - t0
                pb_, pbr = bank()
                for k in range(KC):
                    tk.op("pe", lambda e, k=k: e.matmul(pb_[:, :n], ws_[:, k, 3, :], u[:, k, o:o + n],
                                                        start=(k == 0), stop=(k == KC - 1)),
                          reads=[ws_r, u_r], writes=[pbr])
                tk.op("act", lambda e: e.activation(out=zs[:, o:o + n], in_=pb_[:, :n], func=AF.Silu),
                      reads=[pbr], writes=[zs_r])
                yield
            heads[hh] = (outs[0], outs[1], outs[2], (zs, zs_r))
            yield

        def seq_gens(*gs):
            for g_ in gs:
                yield from g_

        def head_pres(hh):
            (qn, qn_r), (kn, kn_r), (vv, vv_r), (zs, zs_r) = heads[hh]
            pres, scans_p, scan_s = [], [], None
            for bi, b0 in enumerate(range(0, NB, 4)):
                nb = min(4, NB - b0)
                b_ = bbs["p%d" % (bi % 2)]
                pres.append(gdn_pre(b_, hh, 128, nb, b0 * 128, gtok[:, b0:b0 + nb, hh],
                                    [btok[:, b0 + j, hh:hh + 1] for j in range(nb)], qn, qn_r, kn, kn_r, vv, vv_r))
                scans_p.append(gdn_scan(b_, hh, 128, nb, b0 * 128, [Sst[:, hh, :]] * nb, [Sst_r[hh]] * nb, zs, zs_r, True))
            if has_s:
                Ssm, Ssm_r = Ssm_s.get()
                ch_ = Ssm_s.chan()
                tk.dma("sp", ch_, lambda e: e.dma_start(out=Ssm[:], in_=st_in[:, hh, :, :].rearrange("s d e -> d s e")),
                       writes=Ssm_r)
                pres.append(gdn_pre(bbs["s"], hh, 4, 4, THp, gts[:4, 0:4, hh],
                                    [bts[:4, j, hh:hh + 1] for j in range(4)], qn, qn_r, kn, kn_r, vv, vv_r))

                def sample_scan(Ssm=Ssm, Ssm_r=Ssm_r, ch_=ch_):
                    yield from gdn_scan(bbs["s"], hh, 4, 4, THp, [Ssm[:, j, :] for j in range(4)],
                                        [Ssm_r[j] for j in range(4)], zs, zs_r, False)
                    tk.dma("sp", ch_, lambda e: e.dma_start(out=st_s[:, hh, :, :].rearrange("s d e -> d s e"), in_=Ssm[:]),
                           reads=Ssm_r)
                    yield
                scan_s = sample_scan()
            assert len(scans_p) <= 2
            return pres, scans_p, scan_s

        STAG = int(_os.environ.get("STAG", "0"))

        def run_il(gens, stagger=0):
            gens = list(gens)
            for gi_, g_ in enumerate(list(gens)):
                for _ in range(gi_ * stagger):
                    try:
                        next(g_)
                    except StopIteration:
                        if g_ in gens:
                            gens.remove(g_)
            while gens:
                for g_ in list(gens):
                    try:
                        next(g_)
                    except StopIteration:
                        gens.remove(g_)

        load_wsl(0)
        run_il([bulk(0)])
        for hh in range(NH):
            pres, scans_p, scan_s = head_pres(hh)
            gl_ = list(pres)
            if hh + 1 < NH:
                gl_.append(bulk(hh + 1))
            run_il(gl_, stagger=STAG)
            sl_ = [seq_gens(*scans_p)]
            if scan_s is not None:
                sl_.append(scan_s)
            run_il(sl_)
        if has_s:
            och_f = tk.chan()
            for hh in range(NH):
                tk.dma("sp", och_f, lambda e, hh=hh: e.dma_start(out=st_p[hh], in_=Sst[:, hh, :]), reads=[Sst_r[hh]],
                       cont=(hh > 0))
            tk.dma("sp", och_f, lambda e: e.dma_start(out=cv_p.rearrange("(c p) j -> p c j", p=128), in_=craw[:]),
                   reads=craw_r, cont=True)
            tk.dma("sp", och_f, lambda e: e.dma_start(out=cv_s.rearrange("(c p) (s j) -> p c s j", p=128, j=3), in_=cvs[:]),
                   reads=[cvs_r], cont=True)
        if NWARM:
            release(warm_l)
        tk.barrier()
        p1.close()
        p4 = contextlib.ExitStack()
        wo = sb(p4, "g_wo", [128, KC, D], BF16)
        wo_r = Reg("g_wo")
        for k0 in range(0, KC, 4):
            tk.dma("pool", wch[2], lambda e, k0=k0: e.dma_start(
                out=wo[:, k0:k0 + 4, :], in_=d_wo.rearrange("(k p) o -> p k o", p=128)[:, k0:k0 + 4, :]),
                writes=[wo_r], cont=(k0 > 0))
        y_s = Scratch(p4, "g_y", [128, KC, 512], F32, 2, nreg=KC)
        sq_s = Scratch(p4, "g_sq4", [128, KC, 512], BF16, 1)
        t1_s = Scratch(p4, "g_t14", [128, 512], F32, 2)
        t2_s = Scratch(p4, "g_t24", [128, 512], F32, 2)
        for (ts, n) in ftiles(t0, t1):
            o = ts - t0
            y, y_r = y_s.get()
            for oc in range(KC):
                pb_, pbr = bank()
                for k in range(NH):
                    tk.op("pe", lambda e, k=k: e.matmul(pb_[:, :n], wo[:, k, oc * 128:(oc + 1) * 128], oT[:, k, o:o + n],
                                                        start=(k == 0), stop=(k == NH - 1)),
                          reads=[wo_r, oT_r[k]], writes=[pbr])
                evac(y[:, oc, :n], pb_[:, :n], [pbr], [y_r[oc]])
            postnorm_add(1, 1, y, y_r, 0, ts, n, sq_s, t1_s, t2_s)
        tk.barrier()
        p4.close()
        ph.close()

    halves = [(0, HALF), (HALF, T)]
    if "mla" in stages:
        for (t0, t1) in halves:
            mla_phase(t0, t1)
            if "ffn0" in stages:
                ffn_phase(0, t0, t1)
    elif "ffn0" in stages:
        for (t0, t1) in halves:
            ffn_phase(0, t0, t1)
    tk.barrier()
    l0s.close()
    if "gdn" in stages:
        gdn_init()
        for (t0, t1) in halves:
            gdn_phase(t0, t1)
            if "ffn1" in stages:
                ffn_phase(1, t0, t1)
    elif "ffn1" in stages:
        for (t0, t1) in halves:
            ffn_phase(1, t0, t1)

    for k in range(KC):
        tk.dma("sp", next_och(), lambda e, k=k: e.dma_start(out=yT[k * 128:(k + 1) * 128, :], in_=h[:, k, :]),
               reads=h_r[k])
    tk.final()
    es.close()
    return nc


_NC_CACHE = {}


def prep_core_inputs(c, inp, SEQ, NPG):
    PAST = NPG * 128
    T = SEQ + 16
    x_p = inp["x_prompt"][c]
    x_s = inp["x_sample"][4 * c:4 * c + 4].reshape(16, D)
    xT = np.ascontiguousarray(np.concatenate([x_p, x_s], axis=0).T)
    consts, rope = host_consts(SEQ, PAST)
    cm = inp["cache_mla"][0]
    d = {
        "xT": xT,
        "cache": cm.reshape(cm.shape[0] * 16, 8 * ROW),
        "ptab": np.ascontiguousarray(inp["page_table"][4 * c:4 * c + 4].T.astype(np.int32)),
        "st_in": np.ascontiguousarray(inp["state_dn"][0, 4 * c:4 * c + 4]),
        "cv_in": np.ascontiguousarray(inp["state_dn_conv"][0, 4 * c:4 * c + 4].transpose(2, 0, 1)).reshape(3072, 12),
        "normw": np.ascontiguousarray(inp["norm_w"].reshape(2, 4, 8, 128).transpose(3, 0, 1, 2).reshape(128, 64)),
        "consts": consts,
        "gmask": host_gmask(),
        "rope": rope,
        "m_win": inp["mla_w_in"][0],
        "m_gq": np.ascontiguousarray(inp["mla_g_q"][0].reshape(3, 128).T),
        "m_gkv": np.ascontiguousarray(inp["mla_g_kv"][0].reshape(2, 128).T),
        "m_wuq": inp["mla_w_uq"][0],
        "m_wuk": inp["mla_w_uk"][0],
        "m_wukT": np.ascontiguousarray(inp["mla_w_uk"][0].transpose(0, 2, 1)),
        "m_wuv": inp["mla_w_uv"][0],
        "m_wo": inp["mla_w_o"][0],
        "d_win": inp["dn_w_in"][0],
        "d_cw": np.ascontiguousarray(inp["dn_conv_w"][0].T.reshape(24, 128, 4).transpose(1, 0, 2)),
        "d_alog": inp["dn_a_log"],
        "d_dtb": inp["dn_dt_bias"],
        "d_gout": np.ascontiguousarray(inp["dn_g_out"][0].reshape(128, 1)),
        "d_wo": inp["dn_w_o"][0],
        "f_win": inp["ffn_w_in"],
        "f_wout": inp["ffn_w_out"],
    }
    return {k: np.ascontiguousarray(np.asarray(v)) for k, v in d.items()}


def kernel(**inputs):
    inp = {k: np.asarray(v) for k, v in inputs.items()}
    B, SEQ, _ = inp["x_prompt"].shape
    NPG = inp["page_table"].shape[1]
    NPOOL = inp["cache_mla"].shape[1]
    key = (SEQ, NPG, NPOOL)
    nc = build(SEQ, NPG, NPOOL)
    in_maps = [prep_core_inputs(c, inp, SEQ, NPG) for c in range(8)]
    res = run_bass_kernel_spmd(nc, in_maps, core_ids=list(range(8))).results
    y_p = np.stack([res[c]["yT"][:, :SEQ].T for c in range(8)])
    y_s = np.concatenate([res[c]["yT"][:, SEQ:].T.reshape(4, 4, D) for c in range(8)])
    r_p = np.stack([res[c]["rowsT"][:, :SEQ].T for c in range(8)])[None]
    r_s = np.concatenate([res[c]["rowsT"][:, SEQ:].T.reshape(4, 4, ROW) for c in range(8)])[None]
    s_p = np.stack([res[c]["st_p"] for c in range(8)])[None]
    s_s = np.concatenate([res[c]["st_s"] for c in range(8)])[None]
    c_p = np.stack([res[c]["cv_p"].T for c in range(8)])[None]
    c_s = np.concatenate([res[c]["cv_s"].reshape(3072, 4, 3).transpose(1, 2, 0) for c in range(8)])[None]
    f = lambda a: np.ascontiguousarray(a.astype(np.float32))
    return (f(y_p), f(y_s), f(r_p), f(r_s), f(s_p), f(s_s), f(c_p), f(c_s))
```

```python
import contextlib
import math
import numpy as np
import concourse.bass as bass
import concourse.mybir as mybir
from concourse.bass_utils import run_bass_kernel_spmd

F32 = mybir.dt.float32
BF16 = mybir.dt.bfloat16
I32 = mybir.dt.int32
AF = mybir.ActivationFunctionType
ALU = mybir.AluOpType

D = 1024
KC = 8
NH = 8
QLORA, KVLORA, ROPE = 384, 256, 64
ROW = KVLORA + ROPE
DFF = 2816
FC = DFF // 128
NEG = -30000.0
MLA_SCALE = (128 + 64) ** -0.5


class Reg:
    __slots__ = ("name", "w", "rd", "excl")

    def __init__(self, name="", excl=False):
        self.name = name
        self.w = None
        self.rd = {}
        self.excl = excl


class Trk:
    def __init__(self, nc, es):
        self.nc = nc
        self.es = es
        self.eng = {"pe": nc.tensor, "act": nc.scalar, "dve": nc.vector, "pool": nc.gpsimd, "sp": nc.sync}
        self.semh = {}
        self.cnt = {}
        self.seen = {k: {} for k in self.eng}
        for k in self.eng:
            self.semh[k] = es.enter_context(nc.semaphore("s_" + k))
            self.cnt[k] = 0
        self.same_sync = {"pool": True, "act": True, "dve": True}
        self.nchan = 0

    def chan(self, name=None):
        self.nchan += 1
        k = "c%d" % self.nchan
        self.semh[k] = self.es.enter_context(self.nc.semaphore("d_%d" % self.nchan))
        self.cnt[k] = 0
        return k

    def _waits(self, e, reads, writes, skip=None):
        need = {}
        for r in reads:
            if r.w is not None and need.get(r.w[0], 0) < r.w[1]:
                need[r.w[0]] = r.w[1]
            if r.excl:
                for k, c in r.rd.items():
                    if k != e and need.get(k, 0) < c:
                        need[k] = c
        for w in writes:
            if w.w is not None and need.get(w.w[0], 0) < w.w[1]:
                need[w.w[0]] = w.w[1]
            for k, c in w.rd.items():
                if need.get(k, 0) < c:
                    need[k] = c
        for k, c in need.items():
            if k == skip:
                continue
            if k == e and not self.same_sync.get(e):
                continue
            if self.seen[e].get(k, 0) >= c:
                continue
            self.eng[e].wait_ge(self.semh[k], c)
            self.seen[e][k] = c

    def op(self, e, fn, reads=(), writes=()):
        self._waits(e, reads, writes)
        ins = fn(self.eng[e])
        self.cnt[e] += 1
        ins.then_inc(self.semh[e], 1)
        c = self.cnt[e]
        for r in reads:
            r.rd[e] = c
        for w in writes:
            w.w = (e, c)
            w.rd = {}
        return ins

    def dma(self, q, ch, fn, reads=(), writes=(), cont=False):
        if not cont and self.cnt[ch] > self.seen[q].get(ch, 0):
            self.eng[q].wait_ge(self.semh[ch], self.cnt[ch])
            self.seen[q][ch] = self.cnt[ch]
        self._waits(q, reads, writes, skip=ch)
        ins = fn(self.eng[q])
        self.cnt[ch] += 16
        ins.then_inc(self.semh[ch], 16)
        c = self.cnt[ch]
        for r in reads:
            r.rd[ch] = c
        for w in writes:
            w.w = (ch, c)
            w.rd = {}
        return ins

    def barrier(self):
        for e in self.eng:
            for k, c in self.cnt.items():
                if c == 0:
                    continue
                if self.seen[e].get(k, 0) >= c:
                    continue
                self.eng[e].wait_ge(self.semh[k], c)
                self.seen[e][k] = c

    def final(self):
        e = "sp"
        for k, c in self.cnt.items():
            if k == e or c == 0:
                continue
            if self.seen[e].get(k, 0) >= c:
                continue
            self.eng[e].wait_ge(self.semh[k], c)
            self.seen[e][k] = c


def host_consts(SEQ, PAST):
    T = SEQ + 16
    c = np.zeros((128, 1536), np.float32)
    idx = np.arange(128)
    c[:, 0:128] = np.eye(128, dtype=np.float32)
    c[:, 128:256] = (idx[:, None] <= idx[None, :]).astype(np.float32)
    c[:, 256:384] = (idx[:, None] > idx[None, :]).astype(np.float32)
    c[:, 384:512] = np.where(idx[None, :] >= idx[:, None], NEG, 0.0)
    c[:, 512:640] = np.where(idx[None, :] < idx[:, None], NEG, 0.0)
    rot = np.zeros((128, 64), np.float32)
    for m in range(32):
        rot[m + 32, m] = -1.0
        rot[m, m + 32] = 1.0
    c[:, 640:704] = rot
    c[:, 704:832] = 1.0
    dm = np.zeros((128, 32), np.float32)
    for j in range(4):
        for q in range(32):
            dm[j, q] = 1.0 if j <= (q % 4) else 0.0
    c[:, 832:864] = dm
    for j in range(4):
        c[:, 1024 + j * 128:1024 + (j + 1) * 128] = np.eye(128, dtype=np.float32)
    half = 32
    freq = (10000.0 ** (-np.arange(half, dtype=np.float32) / half)).astype(np.float32)
    pos = np.concatenate([np.arange(SEQ), np.tile(PAST + np.arange(4), 4)]).astype(np.float32)
    ang = pos[None, :] * freq[:, None]
    rope = np.zeros((64, 2, T), np.float32)
    rope[0:32, 0] = np.cos(ang)
    rope[32:64, 0] = np.cos(ang)
    rope[0:32, 1] = np.sin(ang)
    rope[32:64, 1] = np.sin(ang)
    return c, rope


def host_gmask():
    idx = np.arange(128)
    c_, s_ = idx[:, None], idx[None, :]
    g = np.zeros((128, 1408), np.float32)
    g[:, 0:128] = (c_ // 4 == s_ // 4) & (c_ > s_)
    for li, b in enumerate([4, 8, 16, 32, 64]):
        m = ((c_ // (2 * b)) == (s_ // (2 * b))) & ((c_ // b) > (s_ // b))
        g[:, 128 + li * 128:128 + (li + 1) * 128] = np.eye(128) - m
    g[:, 768:896] = (c_ // 4 == s_ // 4) & (c_ < s_)
    return g


INV_F32 = False
import os as _os
GCUT = int(_os.environ.get('GCUT', '99'))
GSK = int(_os.environ.get('GSK', '0'))
GOP = int(_os.environ.get('GOP', '0'))


def build(SEQ=2048, NPG=128, NPOOL=5120, stages=("mla", "ffn0", "gdn", "ffn1")):
    T = SEQ + 16
    HALF = SEQ // 2
    PAST = NPG * 128
    nc = bass.Bass("TRN2", target_bir_lowering=False)
    es = contextlib.ExitStack()
    tk = Trk(nc, es)

    def din(name, shape, dt=F32):
        return nc.dram_tensor(name, list(shape), dt, kind="ExternalInput").ap()

    def dout(name, shape, dt=F32):
        return nc.dram_tensor(name, list(shape), dt, kind="ExternalOutput").ap()

    xT = din("xT", [D, T])
    cache = din("cache", [NPOOL * 16, 8 * ROW])
    ptab = din("ptab", [128, 4], I32)
    st_in = din("st_in", [4, NH, 128, 128])
    cv_in = din("cv_in", [3072, 12])
    normw = din("normw", [128, 64])
    consts_d = din("consts", [128, 1536])
    gmask_d = din("gmask", [128, 1408])
    rope_d = din("rope", [64, 2, T])
    m_win = din("m_win", [D, 704])
    m_gq = din("m_gq", [128, 3])
    m_gkv = din("m_gkv", [128, 2])
    m_wuq = din("m_wuq", [QLORA, NH * 192])
    m_wuk = din("m_wuk", [NH, KVLORA, 128])
    m_wukT = din("m_wukT", [NH, 128, KVLORA])
    m_wuv = din("m_wuv", [NH, KVLORA, 128])
    m_wo = din("m_wo", [D, D])
    d_win = din("d_win", [D, 4112])
    d_cw = din("d_cw", [128, 24, 4])
    d_alog = din("d_alog", [1, NH])
    d_dtb = din("d_dtb", [1, NH])
    d_gout = din("d_gout", [128, 1])
    d_wo = din("d_wo", [D, D])
    f_win = din("f_win", [2, D, 2 * DFF])
    f_wout = din("f_wout", [2, DFF, D])

    yT = dout("yT", [D, T])
    rowsT = dout("rowsT", [ROW, T])
    st_p = dout("st_p", [NH, 128, 128])
    st_s = dout("st_s", [4, NH, 128, 128])
    cv_p = dout("cv_p", [3072, 3])
    cv_s = dout("cv_s", [3072, 12])

    uid = [0]

    def sb(stack, name, shape, dt):
        uid[0] += 1
        return stack.enter_context(nc.sbuf_tensor("%s_%d" % (name, uid[0]), list(shape), dt))

    h = sb(es, "h", [128, KC, T], F32)
    h_r = [[Reg("h%d_%d" % (k, j)) for j in range(8)] for k in range(KC)]
    cf = sb(es, "cf", [128, 1536], F32)
    cb = sb(es, "cb", [128, 1536], BF16)
    nw = sb(es, "nw", [128, 64], F32)
    epst = sb(es, "epst", [128, 2], F32)
    c_r = Reg("consts")
    I_f, U_f, MS_f, NEGL_f, NEGQ_f, ROT_f, ONE_f = (cf[:, 0:128], cf[:, 128:256], cf[:, 256:384],
                                                    cf[:, 384:512], cf[:, 512:640], cf[:, 640:704],
                                                    cf[:, 704:832])
    I_b, U_b, ONE_b, DM_b = cb[:, 0:128], cb[:, 128:256], cb[:, 704:832], cb[:, 832:864]
    I4_b = cb[:, 1024:1536]
    psb = [es.enter_context(nc.psum_tensor("ps%d" % i, [128, 512], F32)) for i in range(8)]
    ps_r = [Reg("ps%d" % i, excl=True) for i in range(8)]
    bank_i = [0]

    reserved = set()

    def bank():
        for _ in range(16):
            i = bank_i[0]
            bank_i[0] = (i + 1) % 8
            if i not in reserved:
                return psb[i], ps_r[i]
        raise RuntimeError("no free psum bank")

    def reserve(n):
        out = []
        for _ in range(n):
            for _ in range(16):
                i = bank_i[0]
                bank_i[0] = (i + 1) % 8
                if i not in reserved:
                    break
            reserved.add(i)
            out.append((psb[i], ps_r[i], i))
        return out

    def release(lst):
        for (_, _, i) in lst:
            reserved.discard(i)

    ch_in = tk.chan()
    tk.dma("sp", ch_in, lambda e: e.dma_start(out=cf[:], in_=consts_d), writes=[c_r])
    ch_cb = tk.chan()
    cb_r = Reg("cb")
    tk.dma("pool", ch_cb, lambda e: e.dma_start(out=cb[:], in_=consts_d), writes=[cb_r])
    tk.dma("sp", ch_in, lambda e: e.dma_start(out=nw[:], in_=normw), writes=[c_r])
    tk.op("dve", lambda e: e.memset(epst[:, 0:1], 1e-6), writes=[c_r])
    tk.op("dve", lambda e: e.memset(epst[:, 1:2], 1.0), reads=[cb_r], writes=[c_r])
    ch_x = tk.chan()
    ch_x2 = tk.chan()
    if (SEQ // 2) % 512 == 0:
        for k in range(KC):
            tk.dma("sp", ch_x, lambda e, k=k: e.dma_start(out=h[:, k, 0:SEQ // 2], in_=xT[k * 128:(k + 1) * 128, 0:SEQ // 2]),
                   cont=(k > 0))
        for k in range(KC):
            tk.dma("sp", ch_x2, lambda e, k=k: e.dma_start(out=h[:, k, SEQ // 2:T], in_=xT[k * 128:(k + 1) * 128, SEQ // 2:T]),
                   cont=(k > 0))
        for k in range(KC):
            for j_, r_ in enumerate(h_r[k]):
                if (j_ + 1) * 512 <= SEQ // 2:
                    r_.w = (ch_x, tk.cnt[ch_x])
                else:
                    r_.w = (ch_x2, tk.cnt[ch_x2])
    else:
        for k in range(KC):
            tk.dma("sp", ch_x, lambda e, k=k: e.dma_start(out=h[:, k, :], in_=xT[k * 128:(k + 1) * 128, :]), cont=(k > 0))
        for k in range(KC):
            for r_ in h_r[k]:
                r_.w = (ch_x, tk.cnt[ch_x])

    def hregs(ts, n):
        j0, j1 = ts // 512, (ts + n - 1) // 512
        return [h_r[k][j] for k in range(KC) for j in range(j0, j1 + 1)]

    def hreg_k(k, ts, n):
        j0, j1 = ts // 512, (ts + n - 1) // 512
        return [h_r[k][j] for j in range(j0, j1 + 1)]

    def tiles(t0, t1):
        out = []
        t = t0
        while t < min(t1, SEQ):
            n = min(512, min(t1, SEQ) - t)
            out.append((t, n))
            t += n
        if t1 > SEQ:
            out.append((SEQ, t1 - SEQ))
        return out

    def ftiles(t0, t1):
        th = t1 - t0
        nt = (th + 511) // 512
        base, rem = divmod(th, nt)
        out, t = [], t0
        for i in range(nt):
            n = base + (1 if i < rem else 0)
            out.append((t, n))
            t += n
        return out

    def nwcol(layer, i, k):
        j = (layer * 4 + i) * 8 + k
        return nw[:, j:j + 1]

    def rstd_from_ss(ps_ap, ps_reg, npart, n, scale, tmp, tmp_r, out, out_r):
        tk.op("act", lambda e: e.activation(out=tmp[:npart, :n], in_=ps_ap, func=AF.Ln,
                                            bias=epst[:npart, 0:1], scale=scale),
              reads=[ps_reg, c_r], writes=[tmp_r])
        tk.op("act", lambda e: e.activation(out=out[:npart, :n], in_=tmp[:npart, :n], func=AF.Exp, scale=-0.5),
              reads=[tmp_r], writes=[out_r])

    class Scratch:
        def __init__(self, stack, name, shape, dt, nbuf, nreg=None, chan=False):
            self.t = [sb(stack, "%s%d" % (name, i), shape, dt) for i in range(nbuf)]
            if nreg is None:
                self.r = [Reg("%s%d" % (name, i)) for i in range(nbuf)]
            else:
                self.r = [[Reg("%s%d_%d" % (name, i, j)) for j in range(nreg)] for i in range(nbuf)]
            self.ch = [tk.chan() for _ in range(nbuf)] if chan else None
            self.i = 0
            self.last = 0

        def get(self):
            i = self.i
            self.last = i
            self.i = (i + 1) % len(self.t)
            return self.t[i], self.r[i]

        def chan(self):
            return self.ch[self.last]

    def rmsnorm_pre(layer, wi, ts, n, dst, dst_r, dst_off, sq_s, t1_s, t2_s):
        sq, sq_r = sq_s.get()
        tk.op("act", lambda e: e.activation(out=sq[:, :, :n], in_=h[:, :, ts:ts + n], func=AF.Square),
              reads=hregs(ts, n), writes=[sq_r])
        pb, pr = bank()
        for k in range(KC):
            tk.op("pe", lambda e, k=k: e.matmul(pb[:, :n], ONE_b, sq[:, k, :n], start=(k == 0), stop=(k == KC - 1)),
                  reads=[sq_r, c_r], writes=[pr])
        t1, t1r = t1_s.get()
        t2, t2r = t2_s.get()
        rstd_from_ss(pb[:, :n], pr, 128, n, 1.0 / D, t1, t1r, t2, t2r)
        for k in range(KC):
            tk.op("dve", lambda e, k=k: e.scalar_tensor_tensor(
                out=dst[:, k, dst_off:dst_off + n], in0=h[:, k, ts:ts + n], scalar=nwcol(layer, wi, k),
                in1=t2[:, :n], op0=ALU.mult, op1=ALU.mult),
                reads=hreg_k(k, ts, n) + [t2r, c_r], writes=[dst_r])

    def postnorm_add(layer, wi, y, y_r, yoff, ts, n, sq_s, t1_s, t2_s):
        sq, sq_r = sq_s.get()
        tk.op("act", lambda e: e.activation(out=sq[:, :, :n], in_=y[:, :, yoff:yoff + n], func=AF.Square),
              reads=y_r, writes=[sq_r])
        pb, pr = bank()
        for k in range(KC):
            tk.op("pe", lambda e, k=k: e.matmul(pb[:, :n], ONE_b, sq[:, k, :n], start=(k == 0), stop=(k == KC - 1)),
                  reads=[sq_r, c_r], writes=[pr])
        t1, t1r = t1_s.get()
        t2, t2r = t2_s.get()
        rstd_from_ss(pb[:, :n], pr, 128, n, 1.0 / D, t1, t1r, t2, t2r)
        for k in range(KC):
            tk.op("dve", lambda e, k=k: e.scalar_tensor_tensor(
                out=y[:, k, yoff:yoff + n], in0=y[:, k, yoff:yoff + n], scalar=nwcol(layer, wi, k),
                in1=t2[:, :n], op0=ALU.mult, op1=ALU.mult),
                reads=[t2r, c_r, y_r[k]], writes=[y_r[k]])
            tk.op("dve", lambda e, k=k: e.tensor_tensor(out=h[:, k, ts:ts + n], in0=h[:, k, ts:ts + n],
                                                        in1=y[:, k, yoff:yoff + n], op=ALU.add),
                  reads=[y_r[k]], writes=hreg_k(k, ts, n))

    ev_tog = [0]

    def evac(out_ap, in_ap, reads, writes):
        ev_tog[0] ^= 1
        if ev_tog[0]:
            tk.op("act", lambda e: e.copy(out=out_ap, in_=in_ap), reads=reads, writes=writes)
        else:
            tk.op("dve", lambda e: e.tensor_copy(out=out_ap, in_=in_ap), reads=reads, writes=writes)

    wch = [tk.chan() for _ in range(4)]

    def ffn_phase(layer, t0, t1):
        TH = t1 - t0
        tl = ftiles(t0, t1)
        ph = contextlib.ExitStack()
        act = sb(ph, "f_act", [128, FC, TH], BF16)
        act_r = [Reg("act%d" % c) for c in range(FC)]
        w_in_v = f_win[layer].rearrange("(k p) o -> p k o", p=128)
        w_out_v = f_wout[layer].rearrange("(k p) o -> p k o", p=128)
        pa = contextlib.ExitStack()
        u = sb(pa, "f_u", [128, KC, TH], BF16)
        u_r = Reg("f_u")
        sq_s = Scratch(pa, "f_sq", [128, KC, 512], BF16, 1)
        t1_s = Scratch(pa, "f_t1", [128, 512], F32, 2)
        t2_s = Scratch(pa, "f_t2", [128, 512], F32, 2)
        sg_s = Scratch(pa, "f_sg", [128, 512], BF16, 3)
        slab = [sb(pa, "f_ws%d" % i, [128, KC, 2, 512], BF16) for i in range(2)]
        slab_r = [Reg("f_ws%d" % i) for i in range(2)]
        groups = [(g * 4, min(4, FC - g * 4)) for g in range((FC + 3) // 4)]

        def load_slab(gi):
            c0, ncg = groups[gi]
            s = gi % 2
            for two in range(2):
                col = two * DFF + c0 * 128
                tk.dma("pool", wch[s], lambda e, two=two, col=col: e.dma_start(
                    out=slab[s][:, :, two, :ncg * 128], in_=w_in_v[:, :, col:col + ncg * 128]),
                    writes=[slab_r[s]], cont=(two > 0))

        load_slab(0)
        for (ts, n) in tl:
            rmsnorm_pre(layer, 2, ts, n, u, u_r, ts - t0, sq_s, t1_s, t2_s)
        for gi, (c0, ncg) in enumerate(groups):
            if gi + 1 < len(groups):
                load_slab(gi + 1)
            s = gi % 2
            for cc in range(ncg):
                c = c0 + cc
                for (ts, n) in tl:
                    o = ts - t0
                    pg, pgr = bank()
                    for k in range(KC):
                        tk.op("pe", lambda e, k=k: e.matmul(pg[:, :n], slab[s][:, k, 0, cc * 128:(cc + 1) * 128],
                                                            u[:, k, o:o + n], start=(k == 0), stop=(k == KC - 1)),
                              reads=[slab_r[s], u_r], writes=[pgr])
                    pu, pur = bank()
                    for k in range(KC):
                        tk.op("pe", lambda e, k=k: e.matmul(pu[:, :n], slab[s][:, k, 1, cc * 128:(cc + 1) * 128],
                                                            u[:, k, o:o + n], start=(k == 0), stop=(k == KC - 1)),
                              reads=[slab_r[s], u_r], writes=[pur])
                    sg, sgr = sg_s.get()
                    tk.op("act", lambda e: e.activation(out=sg[:, :n], in_=pg[:, :n], func=AF.Silu),
                          reads=[pgr], writes=[sgr])
                    tk.op("dve", lambda e: e.tensor_tensor(out=act[:, c, o:o + n], in0=sg[:, :n], in1=pu[:, :n],
                                                           op=ALU.mult),
                          reads=[sgr, pur], writes=[act_r[c]])
        tk.barrier()
        pa.close()
        pbk = contextlib.ExitStack()
        y = sb(pbk, "f_y", [128, KC, TH], F32)
        y_r = [[Reg("f_y%d_%d" % (i, k)) for k in range(KC)] for i in range(len(tl))]
        sq_s = Scratch(pbk, "f_sq2", [128, KC, 512], BF16, 1)
        t1_s = Scratch(pbk, "f_t1b", [128, 512], F32, 2)
        t2_s = Scratch(pbk, "f_t2b", [128, 512], F32, 2)
        oslab = [sb(pbk, "f_wo%d" % i, [128, FC, 256], BF16) for i in range(2)]
        oslab_r = [Reg("f_wo%d" % i) for i in range(2)]

        def load_oslab(gi):
            s = gi % 2
            for kk in range(0, FC, 11):
                tk.dma("pool", wch[2 + s], lambda e, kk=kk: e.dma_start(
                    out=oslab[s][:, kk:kk + 11, :], in_=w_out_v[:, kk:kk + 11, gi * 256:(gi + 1) * 256]),
                    writes=[oslab_r[s]], cont=(kk > 0))

        load_oslab(0)
        for gi in range(4):
            if gi + 1 < 4:
                load_oslab(gi + 1)
            s = gi % 2
            for oc in range(2):
                ochunk = gi * 2 + oc
                for ti, (ts, n) in enumerate(tl):
                    o = ts - t0
                    pb_, pbr = bank()
                    for c in range(FC):
                        tk.op("pe", lambda e, c=c: e.matmul(pb_[:, :n], oslab[s][:, c, oc * 128:(oc + 1) * 128],
                                                            act[:, c, o:o + n], start=(c == 0), stop=(c == FC - 1)),
                              reads=[oslab_r[s], act_r[c]], writes=[pbr])
                    evac(y[:, ochunk, o:o + n], pb_[:, :n], [pbr], [y_r[ti][ochunk]])
        for ti, (ts, n) in enumerate(tl):
            postnorm_add(layer, 3, y, y_r[ti], ts - t0, ts, n, sq_s, t1_s, t2_s)
        tk.barrier()
        pbk.close()
        ph.close()

    l0s = contextlib.ExitStack()
    ckv_r = [Reg("ckv%d" % j) for j in range(8)]
    ckv_r_init = ckv_r
    ckv_b = sb(l0s, "ckv_b", [128, 2, T], BF16)
    kr_b = sb(l0s, "kr_b", [128, T], BF16)
    tk.op("pool", lambda e: e.memset(kr_b[64:128, :], 0.0), writes=ckv_r_init)
    och = [tk.chan() for _ in range(4)]
    och_i = [0]

    def next_och():
        och_i[0] = (och_i[0] + 1) % len(och)
        return och[och_i[0]]

    def kvregs(ts, n):
        return [ckv_r[j] for j in range(ts // 512, (ts + n - 1) // 512 + 1)]

    def mla_phase(t0, t1):
        TH = t1 - t0
        tl = tiles(t0, t1)
        has_s = t1 > SEQ
        ph = contextlib.ExitStack()
        cq = sb(ph, "m_cq", [128, 3, TH], BF16)
        cq_r = Reg("m_cq")
        ropet = sb(ph, "m_rope", [64, 2, TH], F32)
        rope_r = Reg("m_rope")
        tk.dma("sp", ch_in, lambda e: e.dma_start(out=ropet[:], in_=rope_d[:, :, t0:t1]), writes=[rope_r])
        oT = sb(ph, "m_oT", [128, NH, TH], BF16)
        oT_r = [Reg("m_oT%d" % hh) for hh in range(NH)]
        p1 = contextlib.ExitStack()
        win = sb(p1, "m_win", [128, KC, 704], BF16)
        win_r = Reg("m_win")
        gq = sb(p1, "m_gq", [128, 3], F32)
        gkv = sb(p1, "m_gkv", [128, 2], F32)
        tk.dma("pool", wch[0], lambda e: e.dma_start(out=win[:], in_=m_win.rearrange("(k p) o -> p k o", p=128)),
               writes=[win_r])
        gv_r = Reg("m_gv")
        tk.dma("sp", ch_in, lambda e: e.dma_start(out=gq[:], in_=m_gq), writes=[gv_r])
        tk.dma("sp", ch_in, lambda e: e.dma_start(out=gkv[:], in_=m_gkv), writes=[gv_r])
        u_s = Scratch(p1, "m_u", [128, KC, 512], BF16, 2)
        sq_s = Scratch(p1, "m_sq", [128, KC, 512], BF16, 1)
        t1_s = Scratch(p1, "m_t1", [128, 512], F32, 3)
        t2_s = Scratch(p1, "m_t2", [128, 512], F32, 3)
        a_s = Scratch(p1, "m_a", [128, 6, 512], F32, 2)
        rowo_s = Scratch(p1, "m_rowo", [128, 3, 512], F32, 2, chan=True)
        def m1_tile(ts, n):
            o = ts - t0
            u, u_r = u_s.get()
            rmsnorm_pre(0, 0, ts, n, u, u_r, 0, sq_s, t1_s, t2_s)
            yield
            a, a_r = a_s.get()
            for oc in range(6):
                m = 128 if oc < 5 else 64
                pb_, pbr = bank()
                for k in range(KC):
                    tk.op("pe", lambda e, k=k: e.matmul(pb_[:m, :n], win[:, k, oc * 128:oc * 128 + m], u[:, k, :n],
                                                        start=(k == 0), stop=(k == KC - 1)),
                          reads=[win_r, u_r], writes=[pbr])
                evac(a[:m, oc, :n], pb_[:m, :n], [pbr], [a_r])
                if oc % 2 == 1:
                    yield
            sq, sq_r = sq_s.get()
            tk.op("act", lambda e: e.activation(out=sq[:, 0:5, :n], in_=a[:, 0:5, :n], func=AF.Square),
                  reads=[a_r], writes=[sq_r])
            pq, pqr = bank()
            for k in range(3):
                tk.op("pe", lambda e, k=k: e.matmul(pq[:, :n], ONE_b, sq[:, k, :n], start=(k == 0), stop=(k == 2)),
                      reads=[sq_r, c_r], writes=[pqr])
            pk, pkr = bank()
            for k in range(2):
                tk.op("pe", lambda e, k=k: e.matmul(pk[:, :n], ONE_b, sq[:, 3 + k, :n], start=(k == 0), stop=(k == 1)),
                      reads=[sq_r, c_r], writes=[pkr])
            ta, tar = t1_s.get()
            rq, rqr = t2_s.get()
            rstd_from_ss(pq[:, :n], pqr, 128, n, 1.0 / QLORA, ta, tar, rq, rqr)
            tb, tbr = t1_s.get()
            rk, rkr = t2_s.get()
            rstd_from_ss(pk[:, :n], pkr, 128, n, 1.0 / KVLORA, tb, tbr, rk, rkr)
            for k in range(3):
                tk.op("dve", lambda e, k=k: e.scalar_tensor_tensor(
                    out=cq[:, k, o:o + n], in0=a[:, k, :n], scalar=gq[:, k:k + 1], in1=rq[:, :n],
                    op0=ALU.mult, op1=ALU.mult), reads=[a_r, rqr, gv_r], writes=[cq_r])
            rowo, rowo_r = rowo_s.get()
            for k in range(2):
                tk.op("dve", lambda e, k=k: e.scalar_tensor_tensor(
                    out=rowo[:, k, :n], in0=a[:, 3 + k, :n], scalar=gkv[:, k:k + 1], in1=rk[:, :n],
                    op0=ALU.mult, op1=ALU.mult), reads=[a_r, rkr, gv_r], writes=[rowo_r])
            tk.op("act", lambda e: e.copy(out=ckv_b[:, :, ts:ts + n], in_=rowo[:, 0:2, :n]),
                  reads=[rowo_r], writes=kvregs(ts, n))
            yield
            pr_, prr = bank()
            tk.op("pe", lambda e: e.matmul(pr_[:64, :n], ROT_f[:64, :], a[:64, 5, :n], start=True, stop=True),
                  reads=[a_r, c_r], writes=[prr])
            tk.op("dve", lambda e: e.tensor_tensor(out=rowo[:64, 2, :n], in0=a[:64, 5, :n],
                                                   in1=ropet[:, 0, o:o + n], op=ALU.mult),
                  reads=[a_r, rope_r], writes=[rowo_r])
            tk.op("dve", lambda e: e.tensor_tensor(out=a[:64, 5, :n], in0=pr_[:64, :n],
                                                   in1=ropet[:, 1, o:o + n], op=ALU.mult),
                  reads=[prr, rope_r], writes=[a_r])
            tk.op("dve", lambda e: e.tensor_tensor(out=rowo[:64, 2, :n], in0=rowo[:64, 2, :n],
                                                   in1=a[:64, 5, :n], op=ALU.add),
                  reads=[a_r], writes=[rowo_r])
            tk.op("act", lambda e: e.copy(out=kr_b[:64, ts:ts + n], in_=rowo[:64, 2, :n]),
                  reads=[rowo_r], writes=kvregs(ts, n))
            oc_ = rowo_s.chan()
            tk.dma("sp", oc_, lambda e: e.dma_start(
                out=rowsT[0:256, ts:ts + n].rearrange("(k p) t -> p k t", p=128), in_=rowo[:, 0:2, :n]),
                reads=[rowo_r])
            tk.dma("sp", oc_, lambda e: e.dma_start(out=rowsT[256:320, ts:ts + n], in_=rowo[:64, 2, :n]),
                   reads=[rowo_r], cont=True)
            yield

        for i_ in range(0, len(tl), 2):
            gens_ = [m1_tile(ts, n) for (ts, n) in tl[i_:i_ + 2]]
            while gens_:
                for g_ in list(gens_):
                    try:
                        next(g_)
                    except StopIteration:
                        gens_.remove(g_)
        tk.barrier()
        p1.close()
        p2 = contextlib.ExitStack()
        wuq = sb(p2, "m_wuq", [128, 3, NH * 192], BF16)
        wuk = sb(p2, "m_wuk", [128, NH, 2, 128], BF16)
        wuv = sb(p2, "m_wuv", [128, NH, 2, 128], BF16)
        wukT = sb(p2, "m_wukT", [128, NH, 256], BF16)
        w2_r = Reg("m_w2")
        tk.dma("pool", wch[1], lambda e: e.dma_start(out=wuq[:], in_=m_wuq.rearrange("(k p) o -> p k o", p=128)),
               writes=[w2_r])
        tk.dma("pool", wch[1], lambda e: e.dma_start(out=wuk[:], in_=m_wuk.rearrange("h (k p) n -> p h k n", p=128)),
               writes=[w2_r], cont=True)
        tk.dma("pool", wch[1], lambda e: e.dma_start(out=wuv[:], in_=m_wuv.rearrange("h (k p) n -> p h k n", p=128)),
               writes=[w2_r], cont=True)
        tk.dma("pool", wch[1], lambda e: e.dma_start(out=wukT[:], in_=m_wukT.rearrange("h p r -> p h r")),
               writes=[w2_r], cont=True)
        NKB = t1 // 128 if not has_s else SEQ // 128
        if has_s:
            Qs = sb(p2, "m_Qs", [128, 3, 4, 32], BF16)
            Qs_r = Reg("m_Qs")
            tk.op("pool", lambda e: e.memset(Qs[64:128, 2, :, :], 0.0), writes=[Qs_r])
        p2a = contextlib.ExitStack()
        qn_s = Scratch(p2a, "m_qn", [128, TH], BF16, 2)
        qr_s = Scratch(p2a, "m_qr", [128, TH], BF16, 2)
        for t_, r_ in zip(qr_s.t, qr_s.r):
            tk.op("pool", lambda e, t_=t_: e.memset(t_[64:128, :], 0.0), writes=[r_])
        qx_s = Scratch(p2a, "m_qx", [64, 2, 512], F32, 2)
        kn_s = Scratch(p2a, "m_kn", [128, SEQ], BF16, 2)
        vp_s = Scratch(p2a, "m_vp", [128, SEQ // 128, 132], BF16, 2)
        pT_s = Scratch(p2a, "m_pT", [128, 512], BF16, 4)
        on_s = Scratch(p2a, "m_on", [128, 4, 128], BF16, 2)
        rs_s = Scratch(p2a, "m_rs", [128, 4], F32, 2)
        ptl = [x for x in tl if x[0] < SEQ]
        mh = {}

        def m2_pro(hh):
            qn, qn_r = qn_s.get()
            qr, qr_r = qr_s.get()
            for (ts, n) in tl:
                o = ts - t0
                pb_, pbr = bank()
                for k in range(3):
                    tk.op("pe", lambda e, k=k: e.matmul(pb_[:, :n], wuq[:, k, hh * 192:hh * 192 + 128], cq[:, k, o:o + n],
                                                        start=(k == 0), stop=(k == 2)),
                          reads=[w2_r, cq_r], writes=[pbr])
                evac(qn[:, o:o + n], pb_[:, :n], [pbr], [qn_r])
                p2_, p2r = bank()
                for k in range(3):
                    tk.op("pe", lambda e, k=k: e.matmul(p2_[:64, :n], wuq[:, k, hh * 192 + 128:hh * 192 + 192],
                                                        cq[:, k, o:o + n], start=(k == 0), stop=(k == 2)),
                          reads=[w2_r, cq_r], writes=[p2r])
                qx, qx_r = qx_s.get()
                tk.op("act", lambda e: e.copy(out=qx[:, 0, :n], in_=p2_[:64, :n]), reads=[p2r], writes=[qx_r])
                p3_, p3r = bank()
                tk.op("pe", lambda e: e.matmul(p3_[:64, :n], ROT_f[:64, :], qx[:, 0, :n], start=True, stop=True),
                      reads=[qx_r, c_r], writes=[p3r])
                tk.op("dve", lambda e: e.tensor_tensor(out=qx[:, 1, :n], in0=p3_[:64, :n], in1=ropet[:, 1, o:o + n],
                                                       op=ALU.mult), reads=[p3r, rope_r], writes=[qx_r])
                tk.op("dve", lambda e: e.tensor_tensor(out=qx[:, 0, :n], in0=qx[:, 0, :n], in1=ropet[:, 0, o:o + n],
                                                       op=ALU.mult), reads=[rope_r], writes=[qx_r])
                tk.op("dve", lambda e: e.tensor_tensor(out=qr[:64, o:o + n], in0=qx[:, 0, :n], in1=qx[:, 1, :n],
                                                       op=ALU.add), reads=[qx_r], writes=[qr_r])
                yield
            kn, kn_r = kn_s.get()
            vp, vp_r = vp_s.get()
            nk = NKB * 128
            for ks in range(0, nk, 512):
                n = min(512, nk - ks)
                pb_, pbr = bank()
                for k in range(2):
                    tk.op("pe", lambda e, k=k: e.matmul(pb_[:, :n], wuk[:, hh, k, :], ckv_b[:, k, ks:ks + n],
                                                        start=(k == 0), stop=(k == 1)),
                          reads=[w2_r] + kvregs(ks, n), writes=[pbr])
                evac(kn[:, ks:ks + n], pb_[:, :n], [pbr], [kn_r])
                yield
            for kb4 in range(0, NKB, 4):
                nb = min(4, NKB - kb4)
                pb_, pbr = bank()
                for j in range(nb):
                    kb = kb4 + j
                    for k in range(2):
                        tk.op("pe", lambda e, k=k, j=j, kb=kb: e.matmul(
                            pb_[:, j * 128:(j + 1) * 128], ckv_b[:, k, kb * 128:(kb + 1) * 128], wuv[:, hh, k, :],
                            start=(k == 0), stop=(k == 1)),
                            reads=[w2_r] + kvregs(kb * 128, 128), writes=[pbr])
                evac(vp[:, kb4:kb4 + nb, 0:128], pb_[:, :nb * 128].rearrange("p (j v) -> p j v", v=128), [pbr], [vp_r])
                yield
            tk.op("dve", lambda e: e.memset(vp[:, :, 128:129], 1.0), writes=[vp_r])
            mh[hh] = (qn, qn_r, qr, qr_r, kn, kn_r, vp, vp_r)
            yield

        def m2_att(hh):
            qn, qn_r, qr, qr_r, kn, kn_r, vp, vp_r = mh[hh]
            for (ts, n) in ptl:
                o = ts - t0
                nsub = n // 128
                kb_hi = (ts + n) // 128
                accs_l = reserve(nsub)
                accs = [(a_, r_) for (a_, r_, _) in accs_l]
                pend = []

                def emit_pv(pv):
                    kb, d, j0, pT, pT_r = pv
                    for si in range(max(d, 0), nsub):
                        ab, abr = accs[si]
                        last_kb = ts // 128 + si
                        c0 = si * 128 - j0
                        tk.op("pe", lambda e, ab=ab, c0=c0, kb=kb, last_kb=last_kb: e.matmul(
                            ab[:, 0:129], pT[:, c0:c0 + 128], vp[:, kb, 0:129], start=(kb == 0), stop=(kb == last_kb)),
                            reads=[pT_r, vp_r], writes=[abr])

                for kb in range(kb_hi):
                    d = kb - ts // 128
                    j0 = max(d, 0) * 128
                    ncol = n - j0
                    psc, pscr = bank()
                    tk.op("pe", lambda e: e.matmul(psc[:, :ncol], kn[:, kb * 128:(kb + 1) * 128],
                                                   qn[:, o + j0:o + n], start=True, stop=False),
                          reads=[kn_r, qn_r], writes=[pscr])
                    tk.op("pe", lambda e: e.matmul(psc[:, :ncol], kr_b[:, kb * 128:(kb + 1) * 128],
                                                   qr[:, o + j0:o + n], start=False, stop=True),
                          reads=kvregs(kb * 128, 128) + [qr_r], writes=[pscr])
                    if len(pend) >= 2:
                        emit_pv(pend.pop(0))
                    pT, pT_r = pT_s.get()
                    tk.op("act", lambda e: e.activation(out=pT[:, :ncol], in_=psc[:, :ncol], func=AF.Exp,
                                                        scale=MLA_SCALE), reads=[pscr], writes=[pT_r])
                    if d >= 0:
                        tk.op("dve", lambda e: e.tensor_tensor(out=pT[:, 0:128], in0=pT[:, 0:128], in1=U_b,
                                                               op=ALU.mult), reads=[c_r], writes=[pT_r])
                    pend.append((kb, d, j0, pT, pT_r))
                    yield
                for pv_ in pend:
                    emit_pv(pv_)
                rs, rs_r = rs_s.get()
                on, on_r = on_s.get()
                for si in range(nsub):
                    ab, abr = accs[si]
                    tk.op("dve", lambda e, ab=ab, si=si: e.reciprocal(out=rs[:, si:si + 1], in_=ab[:, 128:129]),
                          reads=[abr], writes=[rs_r])
                    tk.op("act", lambda e, ab=ab, si=si: e.activation(out=on[:, si, :], in_=ab[:, 0:128], func=AF.Copy,
                                                                      scale=rs[:, si:si + 1]),
                          reads=[abr, rs_r], writes=[on_r])
                ptb, ptr = bank()
                for si in range(nsub):
                    tk.op("pe", lambda e, si=si: e.matmul(ptb[:, si * 128:(si + 1) * 128], on[:, si, :], I_b,
                                                          start=True, stop=True),
                          reads=[on_r, c_r], writes=[ptr])
                evac(oT[:, hh, o:o + n], ptb[:, :n], [ptr], [oT_r[hh]])
                release(accs_l)
                yield
            if has_s:
                so = SEQ - t0
                pb_, pbr = bank()
                for k in range(2):
                    tk.op("pe", lambda e, k=k: e.matmul(pb_[:, k * 16:(k + 1) * 16], wukT[:, hh, k * 128:(k + 1) * 128],
                                                        qn[:, so:so + 16], start=True, stop=True),
                          reads=[w2_r, qn_r], writes=[pbr])
                evac(Qs[:, 0:2, :, hh * 4:(hh + 1) * 4],
                     pb_[:, 0:32].rearrange("p (k s t) -> p k s t", k=2, s=4), [pbr], [Qs_r])
                tk.op("act", lambda e: e.copy(out=Qs[:64, 2, :, hh * 4:(hh + 1) * 4],
                                              in_=qr[:64, so:so + 16].rearrange("p (s t) -> p s t", s=4)),
                      reads=[qr_r], writes=[Qs_r])
            yield

        def run2(gens_):
            gens_ = list(gens_)
            while gens_:
                for g_ in list(gens_):
                    try:
                        next(g_)
                    except StopIteration:
                        gens_.remove(g_)

        run2([m2_pro(0)])
        for hh in range(NH):
            gl_ = [m2_att(hh)]
            if hh + 1 < NH:
                gl_.append(m2_pro(hh + 1))
            run2(gl_)
        tk.barrier()
        p2a.close()
        if has_s:
            R = 8
            pt_sb = sb(p2, "m_pt", [128, 4], I32)
            idx_sb = sb(p2, "m_idx", [128, 16, 4], I32)
            pt_r = Reg("m_pt")
            tk.dma("sp", ch_in, lambda e: e.dma_start(out=pt_sb[:], in_=ptab), writes=[pt_r])
            for gi in range(16):
                tk.op("dve", lambda e, gi=gi: e.tensor_scalar(out=idx_sb[:, gi, :], in0=pt_sb[:, :], scalar1=16.0,
                                                               scalar2=float(gi), op0=ALU.mult, op1=ALU.add),
                      reads=[pt_r], writes=[pt_r])
            NCB = 3
            cbuf = [sb(p2, "m_cb%d" % i, [128, R, ROW], F32) for i in range(NCB)]
            cbuf_r = [Reg("m_cb%d" % i) for i in range(NCB)]
            cch = [tk.chan() for _ in range(NCB)]
            ctok_s = Scratch(p2, "m_ctok", [128, R, 388], BF16, 3)
            for t_ in ctok_s.t:
                tk.op("dve", lambda e, t_=t_: e.memset(t_[:, :, 64:128], 0.0), writes=ctok_s.r)
                tk.op("dve", lambda e, t_=t_: e.memset(t_[:, :, 384:385], 1.0), writes=ctok_s.r)
            pd_s = Scratch(p2, "m_pd", [128, 32], BF16, 2)
            cnew = sb(p2, "m_cnew", [4, 260], BF16)
            cnew_r = Reg("m_cnew")
            oln = sb(p2, "m_oln", [32, 256], BF16)
            oln_r = Reg("m_oln")
            olT = sb(p2, "m_olT", [128, 2, 32], BF16)
            olT_r = Reg("m_olT")
            rsd = sb(p2, "m_rsd", [32, 1], F32)
            ngath = 128 // R
            so = SEQ - t0

            def gather(s, gi):
                i = (s * ngath + gi) % NCB
                tk.dma("pool", cch[i], lambda e: e.indirect_dma_start(
                    out=cbuf[i][:].rearrange("p r d -> p (r d)"), out_offset=None,
                    in_=cache,
                    in_offset=bass.IndirectOffsetOnAxis(ap=idx_sb[:, gi, s:s + 1], axis=0)),
                    reads=[pt_r], writes=[cbuf_r[i]])

            gather(0, 0)
            gather(0, 1)
            cT4_s = Scratch(p2, "m_cT4", [128, 3, 512], BF16, 3)
            pd4_s = Scratch(p2, "m_pd4", [128, 128], BF16, 3)
            batches = [(s_, gi, r0) for s_ in range(4) for gi in range(ngath) for r0 in range(0, R, 4)]
            nbt = len(batches)
            bst = [dict() for _ in range(nbt)]
            grp = {}
            accs_d = {}

            def stA(b):
                s, gi, r0 = batches[b]
                g = s * ngath + gi
                if r0 == 0:
                    nxt = g + 2
                    if nxt < 4 * ngath:
                        gather(nxt // ngath, nxt % ngath)
                    i = g % NCB
                    ctok, ctok_r = ctok_s.get()
                    tk.op("dve", lambda e: e.tensor_copy(out=ctok[:, :, 128:384], in_=cbuf[i][:, :, 0:256]),
                          reads=[cbuf_r[i]], writes=[ctok_r])
                    tk.op("dve", lambda e: e.tensor_copy(out=ctok[:, :, 0:64], in_=cbuf[i][:, :, 256:320]),
                          reads=[cbuf_r[i]], writes=[ctok_r])
                    grp[g] = (ctok, ctok_r)
                ctok, ctok_r = grp[g]
                cT4, cT4_r = cT4_s.get()
                for k in range(3):
                    m = 128
                    c0_ = (128, 256, 0)[k]
                    ptp, ptpr = bank()
                    for j in range(4):
                        tk.op("pe", lambda e, k=k, m=m, j=j, c0_=c0_: e.matmul(
                            ptp[:m, j * 128:(j + 1) * 128], ctok[:, r0 + j, c0_:c0_ + 128], I_b,
                            start=True, stop=True), reads=[ctok_r, c_r], writes=[ptpr])
                    if k == 1:
                        tk.op("dve", lambda e, k=k, m=m, ptp=ptp: e.tensor_copy(out=cT4[:m, k, :], in_=ptp[:m, :]),
                              reads=[ptpr], writes=[cT4_r])
                    else:
                        tk.op("act", lambda e, k=k, m=m, ptp=ptp: e.copy(out=cT4[:m, k, :], in_=ptp[:m, :]),
                              reads=[ptpr], writes=[cT4_r])
                bst[b]["cT4"] = (cT4, cT4_r)

            def stB(b):
                s, gi, r0 = batches[b]
                cT4, cT4_r = bst[b]["cT4"]
                psc, pscr = bank()
                for j in range(4):
                    for k in range(3):
                        m = 128
                        tk.op("pe", lambda e, k=k, m=m, j=j: e.matmul(
                            psc[:, j * 32:(j + 1) * 32], cT4[:m, k, j * 128:(j + 1) * 128], Qs[:m, k, s, :],
                            start=(k == 0), stop=(k == 2)), reads=[cT4_r, Qs_r], writes=[pscr])
                pd4, pd4_r = pd4_s.get()
                tk.op("act", lambda e: e.activation(out=pd4[:, :], in_=psc[:, 0:128], func=AF.Exp, scale=MLA_SCALE),
                      reads=[pscr], writes=[pd4_r])
                bst[b]["pd4"] = (pd4, pd4_r)

            def stC(b):
                s, gi, r0 = batches[b]
                g = s * ngath + gi
                ctok, ctok_r = grp[g]
                pd4, pd4_r = bst[b]["pd4"]
                if s not in accs_d:
                    accs_d[s] = reserve(1)
                (acc, acc_r, _), = acc_l = accs_d[s]
                for j in range(4):
                    first = (gi == 0 and r0 == 0 and j == 0)
                    tk.op("pe", lambda e, j=j, first=first: e.matmul(
                        acc[:32, 0:257], pd4[:, j * 32:(j + 1) * 32], ctok[:, r0 + j, 128:385],
                        start=first, stop=False), reads=[pd4_r, ctok_r], writes=[acc_r])
                if gi == ngath - 1 and r0 == R - 4:
                    tail(s, acc, acc_r, acc_l)

            def tail(s, acc, acc_r, acc_l):
                    tcol = SEQ + s * 4
                    ptp, ptpr = bank()
                    for k in range(2):
                        tk.op("pe", lambda e, k=k: e.matmul(ptp[:4, k * 128:(k + 1) * 128], ckv_b[:, k, tcol:tcol + 4], I_b,
                                                            start=True, stop=True),
                              reads=kvregs(tcol, 4) + [c_r], writes=[ptpr])
                    tk.op("dve", lambda e: e.memset(cnew[:, :], 0.0), writes=[cnew_r])
                    tk.op("act", lambda e: e.copy(out=cnew[:4, 0:256], in_=ptp[:4, 0:256]), reads=[ptpr], writes=[cnew_r])
                    tk.op("dve", lambda e: e.memset(cnew[:4, 256:257], 1.0), writes=[cnew_r])
                    psc, pscr = bank()
                    for k in range(3):
                        m = 128
                        src = ckv_b[:, k, tcol:tcol + 4] if k < 2 else kr_b[:, tcol:tcol + 4]
                        tk.op("pe", lambda e, k=k, m=m, src=src: e.matmul(psc[:4, 0:32], src, Qs[:m, k, s, :],
                                                                          start=(k == 0), stop=(k == 2)),
                              reads=kvregs(tcol, 4) + [Qs_r], writes=[pscr])
                    pd, pd_r = pd_s.get()
                    tk.op("act", lambda e: e.activation(out=pd[:4, :], in_=psc[:4, 0:32], func=AF.Exp, scale=MLA_SCALE),
                          reads=[pscr], writes=[pd_r])
                    tk.op("dve", lambda e: e.tensor_tensor(out=pd[:4, :], in0=pd[:4, :], in1=DM_b[:4, :], op=ALU.mult),
                          reads=[c_r], writes=[pd_r])
                    tk.op("pe", lambda e: e.matmul(acc[:32, 0:257], pd[:4, :], cnew[:4, 0:257], start=False, stop=True),
                          reads=[pd_r, cnew_r], writes=[acc_r])
                    tk.op("dve", lambda e: e.reciprocal(out=rsd[:, :], in_=acc[:32, 256:257]), reads=[acc_r], writes=[oln_r])
                    tk.op("act", lambda e: e.activation(out=oln[:, :], in_=acc[:32, 0:256], func=AF.Copy, scale=rsd[:, 0:1]),
                          reads=[acc_r, oln_r], writes=[oln_r])
                    release(acc_l)
                    ptp, ptpr = bank()
                    for k in range(2):
                        tk.op("pe", lambda e, k=k: e.matmul(ptp[:, k * 32:(k + 1) * 32], oln[:, k * 128:(k + 1) * 128],
                                                            I_b[:32, :32], start=True, stop=True),
                              reads=[oln_r, c_r], writes=[ptpr])
                    evac(olT[:, :, :], ptp[:, 0:64].rearrange("p (k q) -> p k q", k=2), [ptpr], [olT_r])
                    pov, povr = bank()
                    for hh in range(NH):
                        for k in range(2):
                            tk.op("pe", lambda e, k=k, hh=hh: e.matmul(pov[:, hh * 4:(hh + 1) * 4], wuv[:, hh, k, :],
                                                                       olT[:, k, hh * 4:(hh + 1) * 4],
                                                                       start=(k == 0), stop=(k == 1)),
                                  reads=[w2_r, olT_r], writes=[povr])
                    tk.op("act", lambda e: e.copy(out=oT[:, :, so + s * 4:so + s * 4 + 4],
                                                  in_=pov[:, 0:32].rearrange("p (h t) -> p h t", h=NH)),
                          reads=[povr], writes=oT_r)
            for b in range(nbt + 2):
                if b < nbt:
                    stA(b)
                if 0 <= b - 1 < nbt:
                    stB(b - 1)
                if 0 <= b - 2 < nbt:
                    stC(b - 2)
        tk.barrier()
        p2.close()
        p4 = contextlib.ExitStack()
        wo = sb(p4, "m_wo", [128, KC, D], BF16)
        wo_r = Reg("m_wo")
        for k0 in range(0, KC, 4):
            tk.dma("pool", wch[2], lambda e, k0=k0: e.dma_start(
                out=wo[:, k0:k0 + 4, :], in_=m_wo.rearrange("(k p) o -> p k o", p=128)[:, k0:k0 + 4, :]),
                writes=[wo_r], cont=(k0 > 0))
        y_s = Scratch(p4, "m_y", [128, KC, 512], F32, 2, nreg=KC)
        sq_s = Scratch(p4, "m_sq4", [128, KC, 512], BF16, 1)
        t1_s = Scratch(p4, "m_t14", [128, 512], F32, 2)
        t2_s = Scratch(p4, "m_t24", [128, 512], F32, 2)
        for (ts, n) in ftiles(t0, t1):
            o = ts - t0
            y, y_r = y_s.get()
            for oc in range(KC):
                pb_, pbr = bank()
                for k in range(NH):
                    tk.op("pe", lambda e, k=k: e.matmul(pb_[:, :n], wo[:, k, oc * 128:(oc + 1) * 128], oT[:, k, o:o + n],
                                                        start=(k == 0), stop=(k == NH - 1)),
                          reads=[wo_r, oT_r[k]], writes=[pbr])
                evac(y[:, oc, :n], pb_[:, :n], [pbr], [y_r[oc]])
            postnorm_add(0, 1, y, y_r, 0, ts, n, sq_s, t1_s, t2_s)
        tk.barrier()
        p4.close()
        ph.close()


    Sst_r = [Reg("Sst%d" % i) for i in range(NH)]
    craw_r = [Reg("craw%d" % i) for i in range(24)]
    gst = {}

    def gdn_init():
        gst["Sst"] = sb(es, "Sst", [128, NH, 128], F32)
        gst["craw"] = sb(es, "craw", [128, 24, 3], F32)
        tk.op("dve", lambda e: e.memset(gst["Sst"][:], 0.0), writes=Sst_r)
        tk.op("dve", lambda e: e.memset(gst["craw"][:], 0.0), writes=craw_r)
        XD_ = F32 if INV_F32 else BF16
        gst["gm"] = sb(es, "gm", [128, 1408], XD_)
        gst["gm_r"] = Reg("gm")
        gst["i4x"] = sb(es, "i4x", [128, 512], XD_)
        if INV_F32:
            tk.dma("sp", ch_in, lambda e: e.dma_start(out=gst["gm"][:], in_=gmask_d), writes=[gst["gm_r"]])
            tk.dma("sp", ch_in, lambda e: e.dma_start(out=gst["i4x"][:], in_=consts_d[:, 1024:1536]), writes=[gst["gm_r"]])
        else:
            tk.dma("pool", ch_cb, lambda e: e.dma_start(out=gst["gm"][:], in_=gmask_d), writes=[gst["gm_r"]])
            tk.dma("pool", ch_cb, lambda e: e.dma_start(out=gst["i4x"][:], in_=consts_d[:, 1024:1536]),
                   writes=[gst["gm_r"]], cont=True)

    def v3(ap2, C, nb):
        return ap2.rearrange("p (j c) -> p j c", c=128)[:, :nb, :C]

    def gdn_phase(t0, t1):
        Sst, craw = gst["Sst"], gst["craw"]
        gm, gm_r, i4x = gst["gm"], gst["gm_r"], gst["i4x"]
        XD = F32 if INV_F32 else BF16
        I_x = I_f if INV_F32 else I_b
        TH = t1 - t0
        tl = tiles(t0, t1)
        has_s = t1 > SEQ
        THp = min(t1, SEQ) - t0
        NB = THp // 128
        ph = contextlib.ExitStack()
        u = sb(ph, "g_u", [128, KC, TH], BF16)
        u_r = Reg("g_u")
        oT = sb(ph, "g_oT", [128, NH, TH], BF16)
        oT_r = [Reg("g_oT%d" % i) for i in range(NH)]
        p0 = contextlib.ExitStack()
        sq_s = Scratch(p0, "g_sq", [128, KC, 512], BF16, 1)
        t1_s = Scratch(p0, "g_t1", [128, 512], F32, 2)
        t2_s = Scratch(p0, "g_t2", [128, 512], F32, 2)
        for (ts, n) in ftiles(t0, t1):
            rmsnorm_pre(1, 0, ts, n, u, u_r, ts - t0, sq_s, t1_s, t2_s)
        tk.barrier()
        p0.close()
        p1 = contextlib.ExitStack()
        t1_s = Scratch(p1, "g_t1b", [128, 512], F32, 1)
        t2_s = Scratch(p1, "g_t2b", [128, 512], F32, 1)
        cw = sb(p1, "g_cw", [128, 24, 4], F32)
        gout = sb(p1, "g_gout", [128, 1], F32)
        abc = sb(p1, "g_abc", [128, 2, NH], F32)
        wba = sb(p1, "g_wba", [128, KC, 16], BF16)
        gp_r = Reg("g_par")
        tk.dma("sp", ch_in, lambda e: e.dma_start(out=cw[:], in_=d_cw), writes=[gp_r])
        tk.dma("sp", ch_in, lambda e: e.dma_start(out=gout[:], in_=d_gout), writes=[gp_r])
        tk.dma("sp", ch_in, lambda e: e.dma_start(out=abc[:, 0, :], in_=d_alog[0].partition_broadcast(128)), writes=[gp_r])
        tk.dma("sp", ch_in, lambda e: e.dma_start(out=abc[:, 1, :], in_=d_dtb[0].partition_broadcast(128)), writes=[gp_r])
        tk.op("act", lambda e: e.activation(out=abc[:, 0, :], in_=abc[:, 0, :], func=AF.Exp), reads=[gp_r], writes=[gp_r])
        wba_r = Reg("g_wba")
        tk.dma("pool", wch[2], lambda e: e.dma_start(
            out=wba[:], in_=d_win.rearrange("(k p) o -> p k o", p=128)[:, :, 4096:4112]), writes=[wba_r])
        if has_s:
            cvin = sb(p1, "g_cvin", [128, 24, 4, 3], F32)
            cvs = sb(p1, "g_cvs", [128, 24, 4, 3], F32)
            cvin_r = Reg("g_cvin")
            cvs_r = Reg("g_cvs")
            tk.dma("sp", ch_in, lambda e: e.dma_start(out=cvin[:], in_=cv_in.rearrange("(c p) (s j) -> p c s j", p=128, j=3)),
                   writes=[cvin_r])
        NBS = NB + (1 if has_s else 0)
        gtok = sb(p1, "g_gtok", [128, NBS, 8], F32)
        btok = sb(p1, "g_btok", [128, NBS, 8], F32)
        nbtok = sb(p1, "g_nbtok", [128, NBS, 8], F32)
        gt_r = Reg("g_gt")
        xt_ = sb(p1, "g_xt", [128, NBS, 8], F32)
        pba, pbar = bank()
        for b in range(NB):
            for k in range(KC):
                tk.op("pe", lambda e, k=k, b=b: e.matmul(pba[:, b * 16:(b + 1) * 16], u[:, k, b * 128:(b + 1) * 128],
                                                         wba[:, k, :], start=(k == 0), stop=(k == KC - 1)),
                      reads=[u_r, wba_r], writes=[pbar])
        if has_s:
            pbs, pbsr = bank()
            for s_ in range(4):
                for k in range(KC):
                    tk.op("pe", lambda e, k=k, s_=s_: e.matmul(pbs[:4, s_ * 16:(s_ + 1) * 16],
                                                               u[:, k, THp + 4 * s_:THp + 4 * s_ + 4], wba[:, k, :],
                                                               start=(k == 0), stop=(k == KC - 1)),
                          reads=[u_r, wba_r], writes=[pbsr])
        def ba_post(P_, src3, dstsl, preg):
            gt, bt, nbt, xt = dstsl
            nblk = src3.shape[1]
            tk.op("act", lambda e: e.activation(out=bt, in_=src3[:, :, 0:8], func=AF.Sigmoid), reads=[preg], writes=[gt_r])
            tk.op("dve", lambda e: e.tensor_scalar(out=nbt, in0=bt, scalar1=-1.0, scalar2=None, op0=ALU.mult),
                  reads=[gt_r], writes=[gt_r])
            for b in range(nblk):
                tk.op("dve", lambda e, b=b: e.tensor_tensor(out=xt[:, b, :], in0=src3[:, b, 8:16], in1=abc[:P_, 1, :],
                                                            op=ALU.add), reads=[preg, gp_r], writes=[gt_r])
            tk.op("act", lambda e: e.activation(out=xt, in_=xt, func=AF.Exp), reads=[gt_r], writes=[gt_r])
            tk.op("act", lambda e: e.activation(out=xt, in_=xt, func=AF.Ln, bias=epst[:P_, 1:2]), reads=[gt_r, c_r],
                  writes=[gt_r])
            for b in range(nblk):
                tk.op("dve", lambda e, b=b: e.scalar_tensor_tensor(out=gt[:, b, :], in0=xt[:, b, :], scalar=-1.0,
                                                                   in1=abc[:P_, 0, :], op0=ALU.mult, op1=ALU.mult),
                      reads=[gt_r, gp_r], writes=[gt_r])

        ba_post(128, pba[:, 0:NB * 16].rearrange("p (b x) -> p b x", x=16),
                (gtok[:, 0:NB, :], btok[:, 0:NB, :], nbtok[:, 0:NB, :], xt_[:, 0:NB, :]), pbar)
        if has_s:
            gts = sb(p1, "g_gts", [4, 4, 8], F32)
            bts = sb(p1, "g_bts", [4, 4, 8], F32)
            nbts = sb(p1, "g_nbts", [4, 4, 8], F32)
            xts = sb(p1, "g_xts", [4, 4, 8], F32)
            ba_post(4, pbs[:4, 0:64].rearrange("p (b x) -> p b x", x=16), (gts[:], bts[:], nbts[:], xts[:]), pbsr)
        wsl = [sb(p1, "g_wsl%d" % i, [128, KC, 4, 128], BF16) for i in range(2)]
        wsl_r = [Reg("g_wsl%d" % i) for i in range(2)]
        d_win_v = d_win.rearrange("(k p) o -> p k o", p=128)

        def load_wsl(hh):
            s_ = hh % 2
            for w_ in range(4):
                col = w_ * 1024 + hh * 128
                tk.dma("pool", wch[s_], lambda e, w_=w_, col=col: e.dma_start(
                    out=wsl[s_][:, :, w_, :], in_=d_win_v[:, :, col:col + 128]), writes=[wsl_r[s_]], cont=(w_ > 0))

        raw = sb(p1, "g_raw", [128, 3 + THp], F32)
        raw_r = Reg("g_raw")
        raws = sb(p1, "g_raws", [128, 4, 7], F32)
        raws_r = Reg("g_raws")
        acc = sb(p1, "g_acc", [128, TH], F32)
        acc_r = Reg("g_acc")
        sqb_s = Scratch(p1, "g_sqb", [128, 512], BF16, 2)
        qn_s = Scratch(p1, "g_qn", [128, TH], BF16, 2)
        kn_s = Scratch(p1, "g_kn", [128, TH], BF16, 2)
        vv_s = Scratch(p1, "g_vv", [128, TH], BF16, 2)
        zs_s = Scratch(p1, "g_zs", [128, TH], BF16, 2)
        if has_s:
            Ssm_s = Scratch(p1, "g_Ssm", [128, 4, 128], F32, 2, nreg=4, chan=True)
            ssl_ch = [tk.chan() for _ in range(2)]

        class BB:
            pass

        def make_bb(tag, wide):
            b_ = BB()
            CW = 128 if wide else 4
            F1 = sb(p1, "b_F1" + tag, [128, 4, CW], F32)
            F2 = sb(p1, "b_F2" + tag, [128, 4, CW], F32)
            b_.R, b_.gB = F1, F2
            F1b, F2b = F1[:].bitcast(BF16), F2[:].bitcast(BF16)
            b_.LT, b_.X = F1b[:, :, 0:CW], F1b[:, :, CW:2 * CW]
            b_.W1, b_.W2 = F2b[:, :, 0:CW], F2b[:, :, CW:2 * CW]
            b_.sc = sb(p1, "b_sc" + tag, [128, 4, 4], F32)
            b_.gl = sb(p1, "b_gl" + tag, [128, 4, 4], F32)
            b_.E1 = sb(p1, "b_E1" + tag, [128, 4, CW], BF16)
            b_.E2 = sb(p1, "b_E2" + tag, [128, 4, CW], BF16)
            b_.Ao, b_.AoT = b_.E1, b_.E2
            b_.Lp = sb(p1, "b_Lp" + tag, [128, 4, CW], BF16)
            b_.XT = sb(p1, "b_XT" + tag, [128, 4, CW], BF16)
            b_.qkt = sb(p1, "b_qkt" + tag, [128, 4, CW], BF16)
            b_.qd = sb(p1, "b_qd" + tag, [128, 4, CW], BF16)
            AOFF = _os.environ.get("AOFF", "")
            if wide:
                b_.egbc = sb(p1, "b_eg" + tag, [128, 4, 128], BF16)
                b_.utok = b_.egbc if "u" not in AOFF else sb(p1, "b_ut" + tag, [128, 4, 128], BF16)
                b_.wtok = b_.Lp if "w" not in AOFF else sb(p1, "b_wt" + tag, [128, 4, 128], BF16)
                b_.nWk = F1b[:, :, 0:128] if "n" not in AOFF else sb(p1, "b_nw" + tag, [128, 4, 128], BF16)
                if "a" in AOFF:
                    b_.Ao = sb(p1, "b_Ao" + tag, [128, 4, 128], BF16)
                    b_.AoT = sb(p1, "b_AoT" + tag, [128, 4, 128], BF16)
                if "x" in AOFF:
                    b_.LT = sb(p1, "b_LT" + tag, [128, 4, 128], BF16)
                    b_.X = sb(p1, "b_X" + tag, [128, 4, 128], BF16)
                    b_.W1 = sb(p1, "b_W1" + tag, [128, 4, 128], BF16)
                    b_.W2 = sb(p1, "b_W2" + tag, [128, 4, 128], BF16)
            else:
                b_.egbc = sb(p1, "b_eg" + tag, [128, 4, 4], BF16)
                b_.utok = sb(p1, "b_ut" + tag, [128, 4, 128], BF16)
                b_.wtok = sb(p1, "b_wt" + tag, [128, 4, 128], BF16)
                b_.nWk = sb(p1, "b_nw" + tag, [128, 4, 128], BF16)
            for nm in ("kdec", "kbg", "vb"):
                setattr(b_, nm, sb(p1, "b_%s%s" % (nm, tag), [128, 4, 128], BF16))
            rg = {nm: Reg("b_%s%s" % (nm, tag)) for nm in ("F1", "F2", "E1", "E2", "eg", "Lp", "XT", "qkt", "qd", "sc",
                                                          "kdec", "kbg", "vb", "ut", "wt", "nw")}
            b_.r = {"R": rg["F1"], "LT": rg["F1"], "X": rg["F1"], "gB": rg["F2"], "W1": rg["F2"], "W2": rg["F2"],
                    "E1": rg["E1"], "Ao": rg["E1"], "E2": rg["E2"], "AoT": rg["E2"], "Lp": rg["Lp"], "XT": rg["XT"],
                    "qkt": rg["qkt"], "qd": rg["qd"], "sc": rg["sc"], "kdec": rg["kdec"], "kbg": rg["kbg"],
                    "vb": rg["vb"]}
            if wide:
                b_.r.update({"eg": rg["eg"], "utok": rg["eg"], "wtok": rg["Lp"], "nWk": rg["F1"]})
            else:
                b_.r.update({"eg": rg["eg"], "utok": rg["ut"], "wtok": rg["wt"], "nWk": rg["nw"]})
            return b_

        bbs = {"p0": make_bb("p0", True), "p1": make_bb("p1", True)}
        if has_s:
            bbs["s"] = make_bb("s", False)
        bb_i = [0]
        Sb_s = Scratch(p1, "g_Sb", [128, 128], BF16, 2)
        vn_s = Scratch(p1, "g_vn", [128, 128], BF16, 2)
        on_s = Scratch(p1, "g_on", [128, 128], BF16, 2)
        jk_s = Scratch(p1, "g_jk", [128, 128], BF16, 2)
        ss_s = Scratch(p1, "g_ss", [128, 4], F32, 2)
        og_s = Scratch(p1, "g_og", [128, 128], BF16, 2)

        NWARM = int(_os.environ.get("NWARM", "0"))
        if NWARM:
            (wps, wps_r, _), = warm_l = reserve(1)

        def warm():
            for _ in range(NWARM):
                tk.op("pe", lambda e: e.matmul(wps[:, 0:512], ONE_b, cb[:, 0:512], start=True, stop=True),
                      reads=[c_r], writes=[wps_r])

        def gdn_pre(b_, hh, C, nb, o0, gmat, bcols, qn, qn_r, kn, kn_r, vv, vv_r):
            r = b_.r
            NC_ = nb * C
            gcols = [gmat[:, j:j + 1] for j in range(nb)]

            def flat(t, P_=C):
                return t[:].rearrange("p j c -> p (j c)")[:P_, 0:NC_]

            def cv(t, P_=C):
                return flat(t, P_).rearrange("p (j c) -> p j c", c=C)

            def cvp(ps, P_=C):
                return ps[:P_, 0:NC_].rearrange("p (j c) -> p j c", c=C)

            for j in range(nb):
                tk.op("dve", lambda e, j=j: e.tensor_scalar(out=cv(b_.R)[:, j, :], in0=MS_f[:C, :C], scalar1=gcols[j],
                                                            scalar2=None, op0=ALU.mult), reads=[c_r, gt_r], writes=[r["R"]])
                tk.op("dve", lambda e, j=j: e.tensor_scalar(out=cv(b_.gB)[:, j, :], in0=U_f[:C, :C], scalar1=gcols[j],
                                                            scalar2=None, op0=ALU.mult), reads=[c_r, gt_r], writes=[r["gB"]])
            k1, k1r = bank()
            k2, k2r = bank()
            k3, k3r = bank()
            k4, k4r = bank()
            tk.op("pe", lambda e: e.matmul(k1[:C, 0:NC_], U_f[:C, :C], flat(b_.R), start=True, stop=True),
                  reads=[c_r, r["R"]], writes=[k1r])
            tk.op("pe", lambda e: e.matmul(k2[:C, 0:NC_], MS_f[:C, :C], flat(b_.gB), start=True, stop=True),
                  reads=[c_r, r["gB"]], writes=[k2r])
            tk.op("pe", lambda e: e.matmul(k3[:, 0:NC_], ONE_f[:C, :], flat(b_.gB), start=True, stop=True),
                  reads=[c_r, r["gB"]], writes=[k3r])
            tk.op("pe", lambda e: e.matmul(k4[:C, 0:nb], U_f[:C, :C], gmat, start=True, stop=True),
                  reads=[c_r, gt_r], writes=[k4r])
            tk.op("pe", lambda e: e.matmul(k4[:, 16:16 + nb], ONE_f[:C, :], gmat, start=True, stop=True),
                  reads=[c_r, gt_r], writes=[k4r])
            tk.op("act", lambda e: e.activation(out=cv(b_.E1), in_=cvp(k1), func=AF.Exp), reads=[k1r], writes=[r["E1"]])
            tk.op("act", lambda e: e.activation(out=cv(b_.E2), in_=cvp(k2), func=AF.Exp), reads=[k2r], writes=[r["E2"]])
            tk.op("act", lambda e: e.activation(out=cv(b_.egbc, 128), in_=cvp(k3, 128), func=AF.Exp),
                  reads=[k3r], writes=[r["eg"]])
            tk.op("act", lambda e: e.activation(out=b_.sc[:C, :nb, 0:1], in_=k4[:C, 0:nb].unsqueeze(2), func=AF.Exp),
                  reads=[k4r], writes=[r["sc"]])
            tk.op("act", lambda e: e.activation(out=b_.gl[:, :nb, 0:1], in_=k4[:, 16:16 + nb].unsqueeze(2), func=AF.Exp),
                  reads=[k4r], writes=[r["sc"]])
            tk.op("dve", lambda e: e.tensor_copy(out=b_.sc[:C, :nb, 1:2], in_=cv(b_.E2)[:, :, C - 1:C]),
                  reads=[r["E2"]], writes=[r["sc"]])
            tk.op("pool", lambda e: e.tensor_tensor(out=cv(b_.E1), in0=cv(b_.E1),
                                                    in1=cb[:C, 256:256 + C].unsqueeze(1).to_broadcast([C, nb, C]),
                                                    op=ALU.mult), reads=[c_r], writes=[r["E1"]])
            tk.op("pool", lambda e: e.tensor_tensor(out=cv(b_.E2), in0=cv(b_.E2),
                                                    in1=cb[:C, 128:128 + C].unsqueeze(1).to_broadcast([C, nb, C]),
                                                    op=ALU.mult), reads=[c_r, r["sc"]], writes=[r["E2"]])
            for j in range(nb):
                tk.op("dve", lambda e, j=j: e.tensor_tensor(out=b_.sc[:C, j, 2:3], in0=b_.sc[:C, j, 0:1], in1=bcols[j],
                                                            op=ALU.mult), reads=[gt_r], writes=[r["sc"]])
            yield
            k5, k5r = bank()
            k6, k6r = bank()
            k7, k7r = bank()
            k8, k8r = bank()
            for j in range(nb):
                cs = slice(j * 128, j * 128 + C)
                ts_ = slice(o0 + j * C, o0 + (j + 1) * C)
                tk.op("pe", lambda e, cs=cs, ts_=ts_: e.matmul(k5[:C, cs], kn[:, ts_], kn[:, ts_], start=True, stop=True),
                      reads=[kn_r], writes=[k5r])
                tk.op("pe", lambda e, cs=cs, ts_=ts_: e.matmul(k6[:C, cs], kn[:, ts_], qn[:, ts_], start=True, stop=True),
                      reads=[kn_r, qn_r], writes=[k6r])
                tk.op("pe", lambda e, j=j, ts_=ts_: e.matmul(k7[:C, j * 128:(j + 1) * 128], kn[:, ts_], I_b,
                                                             start=True, stop=True), reads=[kn_r, c_r], writes=[k7r])
                tk.op("pe", lambda e, j=j, ts_=ts_: e.matmul(k8[:C, j * 128:(j + 1) * 128], vv[:, ts_], I_b,
                                                             start=True, stop=True), reads=[vv_r, c_r], writes=[k8r])
            for j in range(nb):
                cs = slice(j * 128, j * 128 + C)
                tk.op("dve", lambda e, j=j, cs=cs: e.scalar_tensor_tensor(
                    out=b_.Lp[:C, j, :C], in0=k5[:C, cs], scalar=bcols[j], in1=cv(b_.E1)[:, j, :],
                    op0=ALU.mult, op1=ALU.mult), reads=[k5r, r["E1"], gt_r], writes=[r["Lp"]])
                tk.op("act", lambda e, j=j: e.activation(out=b_.kdec[:C, j, :], in_=k7[:C, j * 128:(j + 1) * 128],
                                                         func=AF.Copy, scale=b_.sc[:C, j, 1:2]),
                      reads=[k7r, r["sc"]], writes=[r["kdec"]])
                tk.op("dve", lambda e, j=j: e.tensor_scalar(out=b_.kbg[:C, j, :], in0=k7[:C, j * 128:(j + 1) * 128],
                                                            scalar1=b_.sc[:C, j, 2:3], scalar2=None, op0=ALU.mult),
                      reads=[k7r, r["sc"]], writes=[r["kbg"]])
                tk.op("act", lambda e, j=j: e.activation(out=b_.vb[:C, j, :], in_=k8[:C, j * 128:(j + 1) * 128],
                                                         func=AF.Copy, scale=bcols[j]),
                      reads=[k8r, gt_r], writes=[r["vb"]])
            tk.op("dve", lambda e: e.tensor_tensor(out=b_.qkt[:C, :nb, :C], in0=v3(k6[:C, :], C, nb),
                                                   in1=cv(b_.E2), op=ALU.mult),
                  reads=[k6r, r["E2"]], writes=[r["qkt"]])
            tk.op("pool", lambda e: e.tensor_tensor(
                out=b_.qd[:, :nb, :C], in0=qn[:, o0:o0 + nb * C].rearrange("p (j c) -> p j c", c=C),
                in1=cv(b_.egbc, 128), op=ALU.mult), reads=[qn_r, r["eg"]], writes=[r["qd"]])
            yield
            def bc(off):
                return gm[:C, off:off + C].unsqueeze(1).to_broadcast([C, nb, C])

            def mm4(lhs, rhs, lr, rr, PO=C, wl=C, wr=C):
                warm()
                kx, kxr = bank()
                for j in range(nb):
                    tk.op("pe", lambda e, j=j: e.matmul(kx[:PO, j * 128:j * 128 + wr], lhs[:C, j, :wl],
                                                        rhs[:C, j, :wr], start=True, stop=True),
                          reads=[lr, rr], writes=[kxr])
                return kx, kxr

            kt, ktr = bank()
            for j in range(nb):
                cs = slice(j * 128, j * 128 + C)
                tk.op("pe", lambda e, j=j, cs=cs: e.matmul(kt[:C, cs], b_.Lp[:C, j, :C], I_x[:C, :C], start=True, stop=True),
                      reads=[r["Lp"], c_r], writes=[ktr])
            tk.op("act", lambda e: e.copy(out=b_.LT[:C, :nb, :C], in_=v3(kt[:C, :], C, nb)), reads=[ktr], writes=[r["LT"]])
            tk.op("dve", lambda e: e.scalar_tensor_tensor(out=b_.Ao[:C, :nb, :C], in0=b_.Lp[:C, :nb, :C], scalar=-1.0,
                                                          in1=bc(0), op0=ALU.mult, op1=ALU.mult),
                  reads=[r["Lp"], gm_r], writes=[r["Ao"]])
            tk.op("dve", lambda e: e.scalar_tensor_tensor(out=b_.AoT[:C, :nb, :C], in0=b_.LT[:C, :nb, :C], scalar=-1.0,
                                                          in1=bc(768), op0=ALU.mult, op1=ALU.mult),
                  reads=[r["LT"], gm_r], writes=[r["AoT"]])
            tk.op("dve", lambda e: e.tensor_tensor(out=b_.X[:C, :nb, :C], in0=b_.Ao[:C, :nb, :C],
                                                   in1=v3(i4x[:C, :], C, nb), op=ALU.add),
                  reads=[r["Ao"], gm_r], writes=[r["X"]])
            tk.op("dve", lambda e: e.tensor_tensor(out=b_.XT[:C, :nb, :C], in0=b_.AoT[:C, :nb, :C],
                                                   in1=v3(i4x[:C, :], C, nb), op=ALU.add),
                  reads=[r["AoT"], gm_r], writes=[r["XT"]])
            kx, kxr = mm4(b_.AoT, b_.Ao, r["AoT"], r["Ao"])
            tk.op("act", lambda e: e.copy(out=b_.W1[:C, :nb, :C], in_=v3(kx[:C, :], C, nb)), reads=[kxr], writes=[r["W1"]])
            yield
            kxa, kxar = mm4(b_.XT, b_.W1, r["XT"], r["W1"])
            kxb, kxbr = mm4(b_.W1, b_.XT, r["W1"], r["XT"])
            tk.op("dve", lambda e: e.tensor_tensor(out=b_.X[:C, :nb, :C], in0=v3(kxa[:C, :], C, nb),
                                                   in1=b_.X[:C, :nb, :C], op=ALU.add),
                  reads=[kxar, r["X"]], writes=[r["X"]])
            tk.op("dve", lambda e: e.tensor_tensor(out=b_.XT[:C, :nb, :C], in0=v3(kxb[:C, :], C, nb),
                                                   in1=b_.XT[:C, :nb, :C], op=ALU.add),
                  reads=[kxbr, r["XT"]], writes=[r["XT"]])
            yield
            levels = [b for b in (4, 8, 16, 32, 64) if 2 * b <= C]
            for li, b in enumerate(levels):
                last = (li == len(levels) - 1)
                off = 128 + li * 128
                kx, kxr = bank()
                for j in range(nb):
                    cs = slice(j * 128, j * 128 + C)
                    tk.op("pe", lambda e, j=j, cs=cs: e.matmul(kx[:C, cs], b_.LT[:C, j, :C], b_.X[:C, j, :C],
                                                               start=True, stop=False), reads=[r["LT"], r["X"]], writes=[kxr])
                    tk.op("pe", lambda e, j=j, cs=cs: e.matmul(kx[:C, cs], I_b[:C, :C], I_b[:C, :C],
                                                               start=False, stop=True), reads=[c_r], writes=[kxr])
                tk.op("dve", lambda e, kx=kx, off=off: e.tensor_tensor(out=b_.W1[:C, :nb, :C], in0=v3(kx[:C, :], C, nb),
                                                                       in1=bc(off), op=ALU.mult),
                      reads=[kxr, gm_r], writes=[r["W1"]])
                yield
                kb_, kbr = mm4(b_.W1, b_.XT, r["W1"], r["XT"])
                if not last:
                    ka_, kar = mm4(b_.XT, b_.W1, r["XT"], r["W1"])
                    tk.op("act", lambda e, ka_=ka_: e.copy(out=b_.X[:C, :nb, :C], in_=v3(ka_[:C, :], C, nb)),
                          reads=[kar], writes=[r["X"]])
                tk.op("dve", lambda e, kb_=kb_: e.tensor_copy(out=b_.XT[:C, :nb, :C], in_=v3(kb_[:C, :], C, nb)),
                      reads=[kbr], writes=[r["XT"]])
                yield
            TTc, rTT = b_.XT, r["XT"]
            ku, kur = bank()
            kw, kwr = bank()
            for j in range(nb):
                tk.op("pe", lambda e, j=j: e.matmul(ku[:C, j * 128:(j + 1) * 128], TTc[:C, j, :C], b_.vb[:C, j, :],
                                                    start=True, stop=True), reads=[rTT, r["vb"]], writes=[kur])
                tk.op("pe", lambda e, j=j: e.matmul(kw[:C, j * 128:(j + 1) * 128], TTc[:C, j, :C], b_.kbg[:C, j, :],
                                                    start=True, stop=True), reads=[rTT, r["kbg"]], writes=[kwr])
            tk.op("act", lambda e: e.copy(out=b_.utok[:C, :nb, :], in_=ku[:C, 0:nb * 128].rearrange("p (j c) -> p j c", c=128)),
                  reads=[kur], writes=[r["utok"]])
            tk.op("dve", lambda e: e.tensor_copy(out=b_.wtok[:C, :nb, :], in_=kw[:C, 0:nb * 128].rearrange("p (j c) -> p j c", c=128)),
                  reads=[kwr], writes=[r["wtok"]])
            yield
            kk, kkr = bank()
            kq, kqr = bank()
            for j in range(nb):
                tk.op("pe", lambda e, j=j: e.matmul(kk[:, j * 128:(j + 1) * 128], b_.wtok[:C, j, :], b_.kdec[:C, j, :],
                                                    start=True, stop=True), reads=[r["wtok"], r["kdec"]], writes=[kkr])
                tk.op("pe", lambda e, j=j: e.matmul(kq[:, j * 128:j * 128 + C], b_.wtok[:C, j, :], b_.qkt[:C, j, :C],
                                                    start=True, stop=True), reads=[r["wtok"], r["qkt"]], writes=[kqr])
            tk.op("act", lambda e: e.activation(out=b_.nWk[:, :nb, :], in_=kk[:, 0:nb * 128].rearrange("p (j c) -> p j c", c=128),
                                                func=AF.Copy, scale=-1.0), reads=[kkr], writes=[r["nWk"]])
            tk.op("dve", lambda e: e.tensor_tensor(out=b_.qd[:, :nb, :C], in0=b_.qd[:, :nb, :C], in1=v3(kq[:, :], C, nb),
                                                   op=ALU.subtract), reads=[kqr, r["qd"]], writes=[r["qd"]])
            yield
        def gdn_scan(b_, hh, C, nb, o0, S_f, S_regs, zs, zs_r, carry):
            r = b_.r
            Sb, Sb_r = None, None
            for j in range(nb):
                Sf, Sreg = S_f[j], S_regs[j]
                if Sb is None or not carry:
                    Sb, Sb_r = Sb_s.get()
                    tk.op("dve", lambda e, Sb=Sb, Sf=Sf: e.tensor_copy(out=Sb[:, :], in_=Sf), reads=[Sreg], writes=[Sb_r])
                cs = slice(o0 + j * C, o0 + (j + 1) * C)
                ks_, ksr = bank()
                tk.op("pe", lambda e, j=j: e.matmul(ks_[:, 0:128], b_.kdec[:C, j, :], b_.utok[:C, j, :], start=True, stop=False),
                      reads=[r["kdec"], r["utok"]], writes=[ksr])
                tk.op("pe", lambda e, j=j, Sb=Sb: e.matmul(ks_[:, 0:128], b_.nWk[:, j, :], Sb[:, :], start=False, stop=True),
                      reads=[r["nWk"], Sb_r], writes=[ksr])
                ko, kor = bank()
                tk.op("pe", lambda e, j=j: e.matmul(ko[:C, 0:128], b_.qkt[:C, j, :C], b_.utok[:C, j, :], start=True, stop=False),
                      reads=[r["qkt"], r["utok"]], writes=[kor])
                tk.op("pe", lambda e, j=j, Sb=Sb: e.matmul(ko[:C, 0:128], b_.qd[:, j, :C], Sb[:, :], start=False, stop=True),
                      reads=[r["qd"], Sb_r], writes=[kor])
                tk.op("dve", lambda e, j=j, Sf=Sf: e.scalar_tensor_tensor(out=Sf, in0=Sf, scalar=b_.gl[:, j, 0:1],
                                                                          in1=ks_[:, 0:128], op0=ALU.mult, op1=ALU.add),
                      reads=[ksr, r["sc"], Sreg], writes=[Sreg])
                if carry and j + 1 < nb:
                    Sb, Sb_r = Sb_s.get()
                    tk.op("dve", lambda e, Sb=Sb, Sf=Sf: e.tensor_copy(out=Sb[:, :], in_=Sf), reads=[Sreg], writes=[Sb_r])
                jk, jk_r = jk_s.get()
                ss, ss_r = ss_s.get()
                tk.op("act", lambda e, jk=jk, ss=ss: e.activation(out=jk[:C, :], in_=ko[:C, 0:128], func=AF.Square,
                                                                  accum_out=ss[:C, 0:1]), reads=[kor], writes=[jk_r, ss_r])
                tk.op("act", lambda e, ss=ss: e.activation(out=ss[:C, 1:2], in_=ss[:C, 0:1], func=AF.Ln,
                                                           bias=epst[:C, 0:1], scale=1.0 / 128), reads=[c_r], writes=[ss_r])
                tk.op("act", lambda e, ss=ss: e.activation(out=ss[:C, 2:3], in_=ss[:C, 1:2], func=AF.Exp, scale=-0.5),
                      writes=[ss_r])
                on, on_r = on_s.get()
                tk.op("act", lambda e, on=on, ss=ss: e.activation(out=on[:C, :], in_=ko[:C, 0:128], func=AF.Copy,
                                                                  scale=ss[:C, 2:3]), reads=[kor, ss_r], writes=[on_r])
                kp, kpr = bank()
                tk.op("pe", lambda e, on=on: e.matmul(kp[:, 0:C], on[:C, :], I_b[:C, :C], start=True, stop=True),
                      reads=[on_r, c_r], writes=[kpr])
                og, og_r = og_s.get()
                tk.op("act", lambda e, og=og: e.activation(out=og[:, 0:C], in_=kp[:, 0:C], func=AF.Copy, scale=gout[:, 0:1]),
                      reads=[kpr, gp_r], writes=[og_r])
                tk.op("pool", lambda e, og=og, cs=cs: e.tensor_tensor(out=oT[:, hh, cs], in0=og[:, 0:C], in1=zs[:, cs],
                                                                     op=ALU.mult), reads=[og_r, zs_r], writes=[oT_r[hh]])
                yield

        heads = {}

        def bulk(hh):
            if hh + 1 < NH:
                load_wsl(hh + 1)
            ws_, ws_r = wsl[hh % 2], wsl_r[hh % 2]
            outs = {}
            for w_ in range(3):
                cidx = w_ * 8 + hh
                tk.op("dve", lambda e: e.tensor_copy(out=raw[:, 0:3], in_=craw[:, cidx, :]),
                      reads=[craw_r[cidx]], writes=[raw_r])
                for (ts, n) in tl:
                    o = ts - t0
                    pb_, pbr = bank()
                    for k in range(KC):
                        tk.op("pe", lambda e, k=k: e.matmul(pb_[:, :n], ws_[:, k, w_, :], u[:, k, o:o + n],
                                                            start=(k == 0), stop=(k == KC - 1)),
                              reads=[ws_r, u_r], writes=[pbr])
                    if ts < SEQ:
                        evac(raw[:, 3 + o:3 + o + n], pb_[:, :n], [pbr], [raw_r])
                    else:
                        tk.op("act", lambda e: e.copy(out=raws[:, :, 3:7], in_=pb_[:, 0:16].rearrange("p (s t) -> p s t", s=4)),
                              reads=[pbr], writes=[raws_r])
                        tk.op("dve", lambda e: e.tensor_copy(out=raws[:, :, 0:3], in_=cvin[:, cidx, :, :]),
                              reads=[cvin_r], writes=[raws_r])
                        tk.op("dve", lambda e: e.tensor_copy(out=cvs[:, cidx, :, :], in_=raws[:, :, 4:7]),
                              reads=[raws_r], writes=[cvs_r])
                    yield
                tk.op("dve", lambda e: e.tensor_copy(out=craw[:, cidx, :], in_=raw[:, THp:THp + 3]),
                      reads=[raw_r], writes=[craw_r[cidx]])
                tk.op("dve", lambda e: e.tensor_scalar(out=acc[:, 0:THp], in0=raw[:, 0:THp], scalar1=cw[:, cidx, 0:1],
                                                       scalar2=None, op0=ALU.mult), reads=[raw_r, gp_r], writes=[acc_r])
                for j in range(1, 4):
                    tk.op("dve", lambda e, j=j: e.scalar_tensor_tensor(
                        out=acc[:, 0:THp], in0=raw[:, j:j + THp], scalar=cw[:, cidx, j:j + 1], in1=acc[:, 0:THp],
                        op0=ALU.mult, op1=ALU.add), reads=[raw_r, gp_r], writes=[acc_r])
                if has_s:
                    accs = acc[:, THp:THp + 16].rearrange("p (s t) -> p s t", s=4)
                    tk.op("dve", lambda e: e.tensor_scalar(out=accs, in0=raws[:, :, 0:4], scalar1=cw[:, cidx, 0:1],
                                                           scalar2=None, op0=ALU.mult), reads=[raws_r, gp_r], writes=[acc_r])
                    for j in range(1, 4):
                        tk.op("dve", lambda e, j=j: e.scalar_tensor_tensor(
                            out=accs, in0=raws[:, :, j:j + 4], scalar=cw[:, cidx, j:j + 1], in1=accs,
                            op0=ALU.mult, op1=ALU.add), reads=[raws_r, gp_r], writes=[acc_r])
                yield
                if w_ == 2:
                    vv, vv_r = vv_s.get()
                    tk.op("act", lambda e: e.activation(out=vv[:, :], in_=acc[:, :], func=AF.Silu), reads=[acc_r], writes=[vv_r])
                    outs[2] = (vv, vv_r)
                else:
                    tk.op("act", lambda e: e.activation(out=acc[:, :], in_=acc[:, :], func=AF.Silu), reads=[acc_r], writes=[acc_r])
                    dst, dst_r = (qn_s if w_ == 0 else kn_s).get()
                    outs[w_] = (dst, dst_r)
                    for (ts, n) in tl:
                        o = ts - t0
                        sqb, sqb_r = sqb_s.get()
                        tk.op("act", lambda e: e.activation(out=sqb[:, :n], in_=acc[:, o:o + n], func=AF.Square),
                              reads=[acc_r], writes=[sqb_r])
                        pb_, pbr = bank()
                        tk.op("pe", lambda e: e.matmul(pb_[:, :n], ONE_b, sqb[:, :n], start=True, stop=True),
                              reads=[sqb_r, c_r], writes=[pbr])
                        ta, tar = t1_s.get()
                        tb, tbr = t2_s.get()
                        rstd_from_ss(pb_[:, :n], pbr, 128, n, 1.0, ta, tar, tb, tbr)
                        if w_ == 0:
                            tk.op("dve", lambda e: e.scalar_tensor_tensor(
                                out=dst[:, o:o + n], in0=acc[:, o:o + n], scalar=128.0 ** -0.5, in1=tb[:, :n],
                                op0=ALU.mult, op1=ALU.mult), reads=[acc_r, tbr], writes=[dst_r])
                        else:
                            tk.op("dve", lambda e: e.tensor_tensor(out=dst[:, o:o + n], in0=acc[:, o:o + n], in1=tb[:, :n],
                                                                   op=ALU.mult), reads=[acc_r, tbr], writes=[dst_r])
                        yield
            zs, zs_r = zs_s.get()
            for (ts, n) in tl:
                o = ts - t0
                pb_, pbr = bank()
                for k in range(KC):
                    tk.op("pe", lambda e, k=k: e.matmul(pb_[:, :n], ws_[:, k, 3, :], u[:, k, o:o + n],
                                                        start=(k == 0), stop=(k == KC - 1)),
                          reads=[ws_r, u_r], writes=[pbr])
                tk.op("act", lambda e: e.activation(out=zs[:, o:o + n], in_=pb_[:, :n], func=AF.Silu),
                      reads=[pbr], writes=[zs_r])
                yield
            heads[hh] = (outs[0], outs[1], outs[2], (zs, zs_r))
            yield

        def seq_gens(*gs):
            for g_ in gs:
                yield from g_

        def head_pres(hh):
            (qn, qn_r), (kn, kn_r), (vv, vv_r), (zs, zs_r) = heads[hh]
            pres, scans_p, scan_s = [], [], None
            for bi, b0 in enumerate(range(0, NB, 4)):
                nb = min(4, NB - b0)
                b_ = bbs["p%d" % (bi % 2)]
                pres.append(gdn_pre(b_, hh, 128, nb, b0 * 128, gtok[:, b0:b0 + nb, hh],
                                    [btok[:, b0 + j, hh:hh + 1] for j in range(nb)], qn, qn_r, kn, kn_r, vv, vv_r))
                scans_p.append(gdn_scan(b_, hh, 128, nb, b0 * 128, [Sst[:, hh, :]] * nb, [Sst_r[hh]] * nb, zs, zs_r, True))
            if has_s:
                Ssm, Ssm_r = Ssm_s.get()
                ch_ = Ssm_s.chan()
                tk.dma("sp", ch_, lambda e: e.dma_start(out=Ssm[:], in_=st_in[:, hh, :, :].rearrange("s d e -> d s e")),
                       writes=Ssm_r)
                pres.append(gdn_pre(bbs["s"], hh, 4, 4, THp, gts[:4, 0:4, hh],
                                    [bts[:4, j, hh:hh + 1] for j in range(4)], qn, qn_r, kn, kn_r, vv, vv_r))

                def sample_scan(Ssm=Ssm, Ssm_r=Ssm_r, ch_=ch_):
                    yield from gdn_scan(bbs["s"], hh, 4, 4, THp, [Ssm[:, j, :] for j in range(4)],
                                        [Ssm_r[j] for j in range(4)], zs, zs_r, False)
                    tk.dma("sp", ch_, lambda e: e.dma_start(out=st_s[:, hh, :, :].rearrange("s d e -> d s e"), in_=Ssm[:]),
                           reads=Ssm_r)
                    yield
                scan_s = sample_scan()
            assert len(scans_p) <= 2
            return pres, scans_p, scan_s

        STAG = int(_os.environ.get("STAG", "0"))

        def run_il(gens, stagger=0):
            gens = list(gens)
            for gi_, g_ in enumerate(list(gens)):
                for _ in range(gi_ * stagger):
                    try:
                        next(g_)
                    except StopIteration:
                        if g_ in gens:
                            gens.remove(g_)
            while gens:
                for g_ in list(gens):
                    try:
                        next(g_)
                    except StopIteration:
                        gens.remove(g_)

        load_wsl(0)
        run_il([bulk(0)])
        for hh in range(NH):
            pres, scans_p, scan_s = head_pres(hh)
            gl_ = list(pres)
            if hh + 1 < NH:
                gl_.append(bulk(hh + 1))
            run_il(gl_, stagger=STAG)
            sl_ = [seq_gens(*scans_p)]
            if scan_s is not None:
                sl_.append(scan_s)
            run_il(sl_)
        if has_s:
            och_f = tk.chan()
            for hh in range(NH):
                tk.dma("sp", och_f, lambda e, hh=hh: e.dma_start(out=st_p[hh], in_=Sst[:, hh, :]), reads=[Sst_r[hh]],
                       cont=(hh > 0))
            tk.dma("sp", och_f, lambda e: e.dma_start(out=cv_p.rearrange("(c p) j -> p c j", p=128), in_=craw[:]),
                   reads=craw_r, cont=True)
            tk.dma("sp", och_f, lambda e: e.dma_start(out=cv_s.rearrange("(c p) (s j) -> p c s j", p=128, j=3), in_=cvs[:]),
                   reads=[cvs_r], cont=True)
        if NWARM:
            release(warm_l)
        tk.barrier()
        p1.close()
        p4 = contextlib.ExitStack()
        wo = sb(p4, "g_wo", [128, KC, D], BF16)
        wo_r = Reg("g_wo")
        for k0 in range(0, KC, 4):
            tk.dma("pool", wch[2], lambda e, k0=k0: e.dma_start(
                out=wo[:, k0:k0 + 4, :], in_=d_wo.rearrange("(k p) o -> p k o", p=128)[:, k0:k0 + 4, :]),
                writes=[wo_r], cont=(k0 > 0))
        y_s = Scratch(p4, "g_y", [128, KC, 512], F32, 2, nreg=KC)
        sq_s = Scratch(p4, "g_sq4", [128, KC, 512], BF16, 1)
        t1_s = Scratch(p4, "g_t14", [128, 512], F32, 2)
        t2_s = Scratch(p4, "g_t24", [128, 512], F32, 2)
        for (ts, n) in ftiles(t0, t1):
            o = ts - t0
            y, y_r = y_s.get()
            for oc in range(KC):
                pb_, pbr = bank()
                for k in range(NH):
                    tk.op("pe", lambda e, k=k: e.matmul(pb_[:, :n], wo[:, k, oc * 128:(oc + 1) * 128], oT[:, k, o:o + n],
                                                        start=(k == 0), stop=(k == NH - 1)),
                          reads=[wo_r, oT_r[k]], writes=[pbr])
                evac(y[:, oc, :n], pb_[:, :n], [pbr], [y_r[oc]])
            postnorm_add(1, 1, y, y_r, 0, ts, n, sq_s, t1_s, t2_s)
        tk.barrier()
        p4.close()
        ph.close()

    halves = [(0, HALF), (HALF, T)]
    if "mla" in stages:
        for (t0, t1) in halves:
            mla_phase(t0, t1)
            if "ffn0" in stages:
                ffn_phase(0, t0, t1)
    elif "ffn0" in stages:
        for (t0, t1) in halves:
            ffn_phase(0, t0, t1)
    tk.barrier()
    l0s.close()
    if "gdn" in stages:
        gdn_init()
        for (t0, t1) in halves:
            gdn_phase(t0, t1)
            if "ffn1" in stages:
                ffn_phase(1, t0, t1)
    elif "ffn1" in stages:
        for (t0, t1) in halves:
            ffn_phase(1, t0, t1)

    for k in range(KC):
        tk.dma("sp", next_och(), lambda e, k=k: e.dma_start(out=yT[k * 128:(k + 1) * 128, :], in_=h[:, k, :]),
               reads=h_r[k])
    tk.final()
    es.close()
    return nc


_NC_CACHE = {}


def prep_core_inputs(c, inp, SEQ, NPG):
    PAST = NPG * 128
    T = SEQ + 16
    x_p = inp["x_prompt"][c]
    x_s = inp["x_sample"][4 * c:4 * c + 4].reshape(16, D)
    xT = np.ascontiguousarray(np.concatenate([x_p, x_s], axis=0).T)
    consts, rope = host_consts(SEQ, PAST)
    cm = inp["cache_mla"][0]
    d = {
        "xT": xT,
        "cache": cm.reshape(cm.shape[0] * 16, 8 * ROW),
        "ptab": np.ascontiguousarray(inp["page_table"][4 * c:4 * c + 4].T.astype(np.int32)),
        "st_in": np.ascontiguousarray(inp["state_dn"][0, 4 * c:4 * c + 4]),
        "cv_in": np.ascontiguousarray(inp["state_dn_conv"][0, 4 * c:4 * c + 4].transpose(2, 0, 1)).reshape(3072, 12),
        "normw": np.ascontiguousarray(inp["norm_w"].reshape(2, 4, 8, 128).transpose(3, 0, 1, 2).reshape(128, 64)),
        "consts": consts,
        "gmask": host_gmask(),
        "rope": rope,
        "m_win": inp["mla_w_in"][0],
        "m_gq": np.ascontiguousarray(inp["mla_g_q"][0].reshape(3, 128).T),
        "m_gkv": np.ascontiguousarray(inp["mla_g_kv"][0].reshape(2, 128).T),
        "m_wuq": inp["mla_w_uq"][0],
        "m_wuk": inp["mla_w_uk"][0],
        "m_wukT": np.ascontiguousarray(inp["mla_w_uk"][0].transpose(0, 2, 1)),
        "m_wuv": inp["mla_w_uv"][0],
        "m_wo": inp["mla_w_o"][0],
        "d_win": inp["dn_w_in"][0],
        "d_cw": np.ascontiguousarray(inp["dn_conv_w"][0].T.reshape(24, 128, 4).transpose(1, 0, 2)),
        "d_alog": inp["dn_a_log"],
        "d_dtb": inp["dn_dt_bias"],
        "d_gout": np.ascontiguousarray(inp["dn_g_out"][0].reshape(128, 1)),
        "d_wo": inp["dn_w_o"][0],
        "f_win": inp["ffn_w_in"],
        "f_wout": inp["ffn_w_out"],
    }
    return {k: np.ascontiguousarray(np.asarray(v)) for k, v in d.items()}


def kernel(**inputs):
    inp = {k: np.asarray(v) for k, v in inputs.items()}
    B, SEQ, _ = inp["x_prompt"].shape
    NPG = inp["page_table"].shape[1]
    NPOOL = inp["cache_mla"].shape[1]
    key = (SEQ, NPG, NPOOL)
    nc = build(SEQ, NPG, NPOOL)
    in_maps = [prep_core_inputs(c, inp, SEQ, NPG) for c in range(8)]
    res = run_bass_kernel_spmd(nc, in_maps, core_ids=list(range(8))).results
    y_p = np.stack([res[c]["yT"][:, :SEQ].T for c in range(8)])
    y_s = np.concatenate([res[c]["yT"][:, SEQ:].T.reshape(4, 4, D) for c in range(8)])
    r_p = np.stack([res[c]["rowsT"][:, :SEQ].T for c in range(8)])[None]
    r_s = np.concatenate([res[c]["rowsT"][:, SEQ:].T.reshape(4, 4, ROW) for c in range(8)])[None]
    s_p = np.stack([res[c]["st_p"] for c in range(8)])[None]
    s_s = np.concatenate([res[c]["st_s"] for c in range(8)])[None]
    c_p = np.stack([res[c]["cv_p"].T for c in range(8)])[None]
    c_s = np.concatenate([res[c]["cv_s"].reshape(3072, 4, 3).transpose(1, 2, 0) for c in range(8)])[None]
    f = lambda a: np.ascontiguousarray(a.astype(np.float32))
    return (f(y_p), f(y_s), f(r_p), f(r_s), f(s_p), f(s_s), f(c_p), f(c_s))
```

```python
import contextlib
import math
import numpy as np
import concourse.bass as bass
import concourse.mybir as mybir
from concourse.bass_utils import run_bass_kernel_spmd

F32 = mybir.dt.float32
BF16 = mybir.dt.bfloat16
I32 = mybir.dt.int32
AF = mybir.ActivationFunctionType
ALU = mybir.AluOpType

D = 1024
KC = 8
NH = 8
QLORA, KVLORA, ROPE = 384, 256, 64
ROW = KVLORA + ROPE
DFF = 2816
FC = DFF // 128
NEG = -30000.0
MLA_SCALE = (128 + 64) ** -0.5


class Reg:
    __slots__ = ("name", "w", "rd", "excl")

    def __init__(self, name="", excl=False):
        self.name = name
        self.w = None
        self.rd = {}
        self.excl = excl


class Trk:
    def __init__(self, nc, es):
        self.nc = nc
        self.es = es
        self.eng = {"pe": nc.tensor, "act": nc.scalar, "dve": nc.vector, "pool": nc.gpsimd, "sp": nc.sync}
        self.semh = {}
        self.cnt = {}
        self.seen = {k: {} for k in self.eng}
        for k in self.eng:
            self.semh[k] = es.enter_context(nc.semaphore("s_" + k))
            self.cnt[k] = 0
        self.same_sync = {"pool": True, "act": True, "dve": True}
        self.nchan = 0

    def chan(self, name=None):
        self.nchan += 1
        k = "c%d" % self.nchan
        self.semh[k] = self.es.enter_context(self.nc.semaphore("d_%d" % self.nchan))
        self.cnt[k] = 0
        return k

    def _waits(self, e, reads, writes, skip=None):
        need = {}
        for r in reads:
            if r.w is not None and need.get(r.w[0], 0) < r.w[1]:
                need[r.w[0]] = r.w[1]
            if r.excl:
                for k, c in r.rd.items():
                    if k != e and need.get(k, 0) < c:
                        need[k] = c
        for w in writes:
            if w.w is not None and need.get(w.w[0], 0) < w.w[1]:
                need[w.w[0]] = w.w[1]
            for k, c in w.rd.items():
                if need.get(k, 0) < c:
                    need[k] = c
        for k, c in need.items():
            if k == skip:
                continue
            if k == e and not self.same_sync.get(e):
                continue
            if self.seen[e].get(k, 0) >= c:
                continue
            self.eng[e].wait_ge(self.semh[k], c)
            self.seen[e][k] = c

    def op(self, e, fn, reads=(), writes=()):
        self._waits(e, reads, writes)
        ins = fn(self.eng[e])
        self.cnt[e] += 1
        ins.then_inc(self.semh[e], 1)
        c = self.cnt[e]
        for r in reads:
            r.rd[e] = c
        for w in writes:
            w.w = (e, c)
            w.rd = {}
        return ins

    def dma(self, q, ch, fn, reads=(), writes=(), cont=False):
        if not cont and self.cnt[ch] > self.seen[q].get(ch, 0):
            self.eng[q].wait_ge(self.semh[ch], self.cnt[ch])
            self.seen[q][ch] = self.cnt[ch]
        self._waits(q, reads, writes, skip=ch)
        ins = fn(self.eng[q])
        self.cnt[ch] += 16
        ins.then_inc(self.semh[ch], 16)
        c = self.cnt[ch]
        for r in reads:
            r.rd[ch] = c
        for w in writes:
            w.w = (ch, c)
            w.rd = {}
        return ins

    def barrier(self):
        for e in self.eng:
            for k, c in self.cnt.items():
                if c == 0:
                    continue
                if self.seen[e].get(k, 0) >= c:
                    continue
                self.eng[e].wait_ge(self.semh[k], c)
                self.seen[e][k] = c

    def final(self):
        e = "sp"
        for k, c in self.cnt.items():
            if k == e or c == 0:
                continue
            if self.seen[e].get(k, 0) >= c:
                continue
            self.eng[e].wait_ge(self.semh[k], c)
            self.seen[e][k] = c


def host_consts(SEQ, PAST):
    T = SEQ + 16
    c = np.zeros((128, 1536), np.float32)
    idx = np.arange(128)
    c[:, 0:128] = np.eye(128, dtype=np.float32)
    c[:, 128:256] = (idx[:, None] <= idx[None, :]).astype(np.float32)
    c[:, 256:384] = (idx[:, None] > idx[None, :]).astype(np.float32)
    c[:, 384:512] = np.where(idx[None, :] >= idx[:, None], NEG, 0.0)
    c[:, 512:640] = np.where(idx[None, :] < idx[:, None], NEG, 0.0)
    rot = np.zeros((128, 64), np.float32)
    for m in range(32):
        rot[m + 32, m] = -1.0
        rot[m, m + 32] = 1.0
    c[:, 640:704] = rot
    c[:, 704:832] = 1.0
    dm = np.zeros((128, 32), np.float32)
    for j in range(4):
        for q in range(32):
            dm[j, q] = 1.0 if j <= (q % 4) else 0.0
    c[:, 832:864] = dm
    for j in range(4):
        c[:, 1024 + j * 128:1024 + (j + 1) * 128] = np.eye(128, dtype=np.float32)
    half = 32
    freq = (10000.0 ** (-np.arange(half, dtype=np.float32) / half)).astype(np.float32)
    pos = np.concatenate([np.arange(SEQ), np.tile(PAST + np.arange(4), 4)]).astype(np.float32)
    ang = pos[None, :] * freq[:, None]
    rope = np.zeros((64, 2, T), np.float32)
    rope[0:32, 0] = np.cos(ang)
    rope[32:64, 0] = np.cos(ang)
    rope[0:32, 1] = np.sin(ang)
    rope[32:64, 1] = np.sin(ang)
    return c, rope


def host_gmask():
    idx = np.arange(128)
    c_, s_ = idx[:, None], idx[None, :]
    g = np.zeros((128, 1408), np.float32)
    g[:, 0:128] = (c_ // 4 == s_ // 4) & (c_ > s_)
    for li, b in enumerate([4, 8, 16, 32, 64]):
        m = ((c_ // (2 * b)) == (s_ // (2 * b))) & ((c_ // b) > (s_ // b))
        g[:, 128 + li * 128:128 + (li + 1) * 128] = np.eye(128) - m
    g[:, 768:896] = (c_ // 4 == s_ // 4) & (c_ < s_)
    return g


INV_F32 = False
import os as _os
GCUT = int(_os.environ.get('GCUT', '99'))
GSK = int(_os.environ.get('GSK', '0'))
GOP = int(_os.environ.get('GOP', '0'))


def build(SEQ=2048, NPG=128, NPOOL=5120, stages=("mla", "ffn0", "gdn", "ffn1")):
    T = SEQ + 16
    HALF = SEQ // 2
    PAST = NPG * 128
    nc = bass.Bass("TRN2", target_bir_lowering=False)
    es = contextlib.ExitStack()
    tk = Trk(nc, es)

    def din(name, shape, dt=F32):
        return nc.dram_tensor(name, list(shape), dt, kind="ExternalInput").ap()

    def dout(name, shape, dt=F32):
        return nc.dram_tensor(name, list(shape), dt, kind="ExternalOutput").ap()

    xT = din("xT", [D, T])
    cache = din("cache", [NPOOL * 16, 8 * ROW])
    ptab = din("ptab", [128, 4], I32)
    st_in = din("st_in", [4, NH, 128, 128])
    cv_in = din("cv_in", [3072, 12])
    normw = din("normw", [128, 64])
    consts_d = din("consts", [128, 1536])
    gmask_d = din("gmask", [128, 1408])
    rope_d = din("rope", [64, 2, T])
    m_win = din("m_win", [D, 704])
    m_gq = din("m_gq", [128, 3])
    m_gkv = din("m_gkv", [128, 2])
    m_wuq = din("m_wuq", [QLORA, NH * 192])
    m_wuk = din("m_wuk", [NH, KVLORA, 128])
    m_wukT = din("m_wukT", [NH, 128, KVLORA])
    m_wuv = din("m_wuv", [NH, KVLORA, 128])
    m_wo = din("m_wo", [D, D])
    d_win = din("d_win", [D, 4112])
    d_cw = din("d_cw", [128, 24, 4])
    d_alog = din("d_alog", [1, NH])
    d_dtb = din("d_dtb", [1, NH])
    d_gout = din("d_gout", [128, 1])
    d_wo = din("d_wo", [D, D])
    f_win = din("f_win", [2, D, 2 * DFF])
    f_wout = din("f_wout", [2, DFF, D])

    yT = dout("yT", [D, T])
    rowsT = dout("rowsT", [ROW, T])
    st_p = dout("st_p", [NH, 128, 128])
    st_s = dout("st_s", [4, NH, 128, 128])
    cv_p = dout("cv_p", [3072, 3])
    cv_s = dout("cv_s", [3072, 12])

    uid = [0]

    def sb(stack, name, shape, dt):
        uid[0] += 1
        return stack.enter_context(nc.sbuf_tensor("%s_%d" % (name, uid[0]), list(shape), dt))

    h = sb(es, "h", [128, KC, T], F32)
    h_r = [[Reg("h%d_%d" % (k, j)) for j in range(8)] for k in range(KC)]
    cf = sb(es, "cf", [128, 1536], F32)
    cb = sb(es, "cb", [128, 1536], BF16)
    nw = sb(es, "nw", [128, 64], F32)
    epst = sb(es, "epst", [128, 2], F32)
    c_r = Reg("consts")
    I_f, U_f, MS_f, NEGL_f, NEGQ_f, ROT_f, ONE_f = (cf[:, 0:128], cf[:, 128:256], cf[:, 256:384],
                                                    cf[:, 384:512], cf[:, 512:640], cf[:, 640:704],
                                                    cf[:, 704:832])
    I_b, U_b, ONE_b, DM_b = cb[:, 0:128], cb[:, 128:256], cb[:, 704:832], cb[:, 832:864]
    I4_b = cb[:, 1024:1536]
    psb = [es.enter_context(nc.psum_tensor("ps%d" % i, [128, 512], F32)) for i in range(8)]
    ps_r = [Reg("ps%d" % i, excl=True) for i in range(8)]
    bank_i = [0]

    reserved = set()

    def bank():
        for _ in range(16):
            i = bank_i[0]
            bank_i[0] = (i + 1) % 8
            if i not in reserved:
                return psb[i], ps_r[i]
        raise RuntimeError("no free psum bank")

    def reserve(n):
        out = []
        for _ in range(n):
            for _ in range(16):
                i = bank_i[0]
                bank_i[0] = (i + 1) % 8
                if i not in reserved:
                    break
            reserved.add(i)
            out.append((psb[i], ps_r[i], i))
        return out

    def release(lst):
        for (_, _, i) in lst:
            reserved.discard(i)

    ch_in = tk.chan()
    tk.dma("sp", ch_in, lambda e: e.dma_start(out=cf[:], in_=consts_d), writes=[c_r])
    ch_cb = tk.chan()
    cb_r = Reg("cb")
    tk.dma("pool", ch_cb, lambda e: e.dma_start(out=cb[:], in_=consts_d), writes=[cb_r])
    tk.dma("sp", ch_in, lambda e: e.dma_start(out=nw[:], in_=normw), writes=[c_r])
    tk.op("dve", lambda e: e.memset(epst[:, 0:1], 1e-6), writes=[c_r])
    tk.op("dve", lambda e: e.memset(epst[:, 1:2], 1.0), reads=[cb_r], writes=[c_r])
    ch_x = tk.chan()
    ch_x2 = tk.chan()
    if (SEQ // 2) % 512 == 0:
        for k in range(KC):
            tk.dma("sp", ch_x, lambda e, k=k: e.dma_start(out=h[:, k, 0:SEQ // 2], in_=xT[k * 128:(k + 1) * 128, 0:SEQ // 2]),
                   cont=(k > 0))
        for k in range(KC):
            tk.dma("sp", ch_x2, lambda e, k=k: e.dma_start(out=h[:, k, SEQ // 2:T], in_=xT[k * 128:(k + 1) * 128, SEQ // 2:T]),
                   cont=(k > 0))
        for k in range(KC):
            for j_, r_ in enumerate(h_r[k]):
                if (j_ + 1) * 512 <= SEQ // 2:
                    r_.w = (ch_x, tk.cnt[ch_x])
                else:
                    r_.w = (ch_x2, tk.cnt[ch_x2])
    else:
        for k in range(KC):
            tk.dma("sp", ch_x, lambda e, k=k: e.dma_start(out=h[:, k, :], in_=xT[k * 128:(k + 1) * 128, :]), cont=(k > 0))
        for k in range(KC):
            for r_ in h_r[k]:
                r_.w = (ch_x, tk.cnt[ch_x])

    def hregs(ts, n):
        j0, j1 = ts // 512, (ts + n - 1) // 512
        return [h_r[k][j] for k in range(KC) for j in range(j0, j1 + 1)]

    def hreg_k(k, ts, n):
        j0, j1 = ts // 512, (ts + n - 1) // 512
        return [h_r[k][j] for j in range(j0, j1 + 1)]

    def tiles(t0, t1):
        out = []
        t = t0
        while t < min(t1, SEQ):
            n = min(512, min(t1, SEQ) - t)
            out.append((t, n))
            t += n
        if t1 > SEQ:
            out.append((SEQ, t1 - SEQ))
        return out

    def ftiles(t0, t1):
        th = t1 - t0
        nt = (th + 511) // 512
        base, rem = divmod(th, nt)
        out, t = [], t0
        for i in range(nt):
            n = base + (1 if i < rem else 0)
            out.append((t, n))
            t += n
        return out

    def nwcol(layer, i, k):
        j = (layer * 4 + i) * 8 + k
        return nw[:, j:j + 1]

    def rstd_from_ss(ps_ap, ps_reg, npart, n, scale, tmp, tmp_r, out, out_r):
        tk.op("act", lambda e: e.activation(out=tmp[:npart, :n], in_=ps_ap, func=AF.Ln,
                                            bias=epst[:npart, 0:1], scale=scale),
              reads=[ps_reg, c_r], writes=[tmp_r])
        tk.op("act", lambda e: e.activation(out=out[:npart, :n], in_=tmp[:npart, :n], func=AF.Exp, scale=-0.5),
              reads=[tmp_r], writes=[out_r])

    class Scratch:
        def __init__(self, stack, name, shape, dt, nbuf, nreg=None, chan=False):
            self.t = [sb(stack, "%s%d" % (name, i), shape, dt) for i in range(nbuf)]
            if nreg is None:
                self.r = [Reg("%s%d" % (name, i)) for i in range(nbuf)]
            else:
                self.r = [[Reg("%s%d_%d" % (name, i, j)) for j in range(nreg)] for i in range(nbuf)]
            self.ch = [tk.chan() for _ in range(nbuf)] if chan else None
            self.i = 0
            self.last = 0

        def get(self):
            i = self.i
            self.last = i
            self.i = (i + 1) % len(self.t)
            return self.t[i], self.r[i]

        def chan(self):
            return self.ch[self.last]

    def rmsnorm_pre(layer, wi, ts, n, dst, dst_r, dst_off, sq_s, t1_s, t2_s):
        sq, sq_r = sq_s.get()
        tk.op("act", lambda e: e.activation(out=sq[:, :, :n], in_=h[:, :, ts:ts + n], func=AF.Square),
              reads=hregs(ts, n), writes=[sq_r])
        pb, pr = bank()
        for k in range(KC):
            tk.op("pe", lambda e, k=k: e.matmul(pb[:, :n], ONE_b, sq[:, k, :n], start=(k == 0), stop=(k == KC - 1)),
                  reads=[sq_r, c_r], writes=[pr])
        t1, t1r = t1_s.get()
        t2, t2r = t2_s.get()
        rstd_from_ss(pb[:, :n], pr, 128, n, 1.0 / D, t1, t1r, t2, t2r)
        for k in range(KC):
            tk.op("dve", lambda e, k=k: e.scalar_tensor_tensor(
                out=dst[:, k, dst_off:dst_off + n], in0=h[:, k, ts:ts + n], scalar=nwcol(layer, wi, k),
                in1=t2[:, :n], op0=ALU.mult, op1=ALU.mult),
                reads=hreg_k(k, ts, n) + [t2r, c_r], writes=[dst_r])

    def postnorm_add(layer, wi, y, y_r, yoff, ts, n, sq_s, t1_s, t2_s):
        sq, sq_r = sq_s.get()
        tk.op("act", lambda e: e.activation(out=sq[:, :, :n], in_=y[:, :, yoff:yoff + n], func=AF.Square),
              reads=y_r, writes=[sq_r])
        pb, pr = bank()
        for k in range(KC):
            tk.op("pe", lambda e, k=k: e.matmul(pb[:, :n], ONE_b, sq[:, k, :n], start=(k == 0), stop=(k == KC - 1)),
                  reads=[sq_r, c_r], writes=[pr])
        t1, t1r = t1_s.get()
        t2, t2r = t2_s.get()
        rstd_from_ss(pb[:, :n], pr, 128, n, 1.0 / D, t1, t1r, t2, t2r)
        for k in range(KC):
            tk.op("dve", lambda e, k=k: e.scalar_tensor_tensor(
                out=y[:, k, yoff:yoff + n], in0=y[:, k, yoff:yoff + n], scalar=nwcol(layer, wi, k),
                in1=t2[:, :n], op0=ALU.mult, op1=ALU.mult),
                reads=[t2r, c_r, y_r[k]], writes=[y_r[k]])
            tk.op("dve", lambda e, k=k: e.tensor_tensor(out=h[:, k, ts:ts + n], in0=h[:, k, ts:ts + n],
                                                        in1=y[:, k, yoff:yoff + n], op=ALU.add),
                  reads=[y_r[k]], writes=hreg_k(k, ts, n))

    ev_tog = [0]

    def evac(out_ap, in_ap, reads, writes):
        ev_tog[0] ^= 1
        if ev_tog[0]:
            tk.op("act", lambda e: e.copy(out=out_ap, in_=in_ap), reads=reads, writes=writes)
        else:
            tk.op("dve", lambda e: e.tensor_copy(out=out_ap, in_=in_ap), reads=reads, writes=writes)

    wch = [tk.chan() for _ in range(4)]

    def ffn_phase(layer, t0, t1):
        TH = t1 - t0
        tl = ftiles(t0, t1)
        ph = contextlib.ExitStack()
        act = sb(ph, "f_act", [128, FC, TH], BF16)
        act_r = [Reg("act%d" % c) for c in range(FC)]
        w_in_v = f_win[layer].rearrange("(k p) o -> p k o", p=128)
        w_out_v = f_wout[layer].rearrange("(k p) o -> p k o", p=128)
        pa = contextlib.ExitStack()
        u = sb(pa, "f_u", [128, KC, TH], BF16)
        u_r = Reg("f_u")
        sq_s = Scratch(pa, "f_sq", [128, KC, 512], BF16, 1)
        t1_s = Scratch(pa, "f_t1", [128, 512], F32, 2)
        t2_s = Scratch(pa, "f_t2", [128, 512], F32, 2)
        sg_s = Scratch(pa, "f_sg", [128, 512], BF16, 3)
        slab = [sb(pa, "f_ws%d" % i, [128, KC, 2, 512], BF16) for i in range(2)]
        slab_r = [Reg("f_ws%d" % i) for i in range(2)]
        groups = [(g * 4, min(4, FC - g * 4)) for g in range((FC + 3) // 4)]

        def load_slab(gi):
            c0, ncg = groups[gi]
            s = gi % 2
            for two in range(2):
                col = two * DFF + c0 * 128
                tk.dma("pool", wch[s], lambda e, two=two, col=col: e.dma_start(
                    out=slab[s][:, :, two, :ncg * 128], in_=w_in_v[:, :, col:col + ncg * 128]),
                    writes=[slab_r[s]], cont=(two > 0))

        load_slab(0)
        for (ts, n) in tl:
            rmsnorm_pre(layer, 2, ts, n, u, u_r, ts - t0, sq_s, t1_s, t2_s)
        for gi, (c0, ncg) in enumerate(groups):
            if gi + 1 < len(groups):
                load_slab(gi + 1)
            s = gi % 2
            for cc in range(ncg):
                c = c0 + cc
                for (ts, n) in tl:
                    o = ts - t0
                    pg, pgr = bank()
                    for k in range(KC):
                        tk.op("pe", lambda e, k=k: e.matmul(pg[:, :n], slab[s][:, k, 0, cc * 128:(cc + 1) * 128],
                                                            u[:, k, o:o + n], start=(k == 0), stop=(k == KC - 1)),
                              reads=[slab_r[s], u_r], writes=[pgr])
                    pu, pur = bank()
                    for k in range(KC):
                        tk.op("pe", lambda e, k=k: e.matmul(pu[:, :n], slab[s][:, k, 1, cc * 128:(cc + 1) * 128],
                                                            u[:, k, o:o + n], start=(k == 0), stop=(k == KC - 1)),
                              reads=[slab_r[s], u_r], writes=[pur])
                    sg, sgr = sg_s.get()
                    tk.op("act", lambda e: e.activation(out=sg[:, :n], in_=pg[:, :n], func=AF.Silu),
                          reads=[pgr], writes=[sgr])
                    tk.op("dve", lambda e: e.tensor_tensor(out=act[:, c, o:o + n], in0=sg[:, :n], in1=pu[:, :n],
                                                           op=ALU.mult),
                          reads=[sgr, pur], writes=[act_r[c]])
        tk.barrier()
        pa.close()
        pbk = contextlib.ExitStack()
        y = sb(pbk, "f_y", [128, KC, TH], F32)
        y_r = [[Reg("f_y%d_%d" % (i, k)) for k in range(KC)] for i in range(len(tl))]
        sq_s = Scratch(pbk, "f_sq2", [128, KC, 512], BF16, 1)
        t1_s = Scratch(pbk, "f_t1b", [128, 512], F32, 2)
        t2_s = Scratch(pbk, "f_t2b", [128, 512], F32, 2)
        oslab = [sb(pbk, "f_wo%d" % i, [128, FC, 256], BF16) for i in range(2)]
        oslab_r = [Reg("f_wo%d" % i) for i in range(2)]

        def load_oslab(gi):
            s = gi % 2
            for kk in range(0, FC, 11):
                tk.dma("pool", wch[2 + s], lambda e, kk=kk: e.dma_start(
                    out=oslab[s][:, kk:kk + 11, :], in_=w_out_v[:, kk:kk + 11, gi * 256:(gi + 1) * 256]),
                    writes=[oslab_r[s]], cont=(kk > 0))

        load_oslab(0)
        for gi in range(4):
            if gi + 1 < 4:
                load_oslab(gi + 1)
            s = gi % 2
            for oc in range(2):
                ochunk = gi * 2 + oc
                for ti, (ts, n) in enumerate(tl):
                    o = ts - t0
                    pb_, pbr = bank()
                    for c in range(FC):
                        tk.op("pe", lambda e, c=c: e.matmul(pb_[:, :n], oslab[s][:, c, oc * 128:(oc + 1) * 128],
                                                            act[:, c, o:o + n], start=(c == 0), stop=(c == FC - 1)),
                              reads=[oslab_r[s], act_r[c]], writes=[pbr])
                    evac(y[:, ochunk, o:o + n], pb_[:, :n], [pbr], [y_r[ti][ochunk]])
        for ti, (ts, n) in enumerate(tl):
            postnorm_add(layer, 3, y, y_r[ti], ts - t0, ts, n, sq_s, t1_s, t2_s)
        tk.barrier()
        pbk.close()
        ph.close()

    l0s = contextlib.ExitStack()
    ckv_r = [Reg("ckv%d" % j) for j in range(8)]
    ckv_r_init = ckv_r
    ckv_b = sb(l0s, "ckv_b", [128, 2, T], BF16)
    kr_b = sb(l0s, "kr_b", [128, T], BF16)
    tk.op("pool", lambda e: e.memset(kr_b[64:128, :], 0.0), writes=ckv_r_init)
    och = [tk.chan() for _ in range(4)]
    och_i = [0]

    def next_och():
        och_i[0] = (och_i[0] + 1) % len(och)
        return och[och_i[0]]

    def kvregs(ts, n):
        return [ckv_r[j] for j in range(ts // 512, (ts + n - 1) // 512 + 1)]

    def mla_phase(t0, t1):
        TH = t1 - t0
        tl = tiles(t0, t1)
        has_s = t1 > SEQ
        ph = contextlib.ExitStack()
        cq = sb(ph, "m_cq", [128, 3, TH], BF16)
        cq_r = Reg("m_cq")
        ropet = sb(ph, "m_rope", [64, 2, TH], F32)
        rope_r = Reg("m_rope")
        tk.dma("sp", ch_in, lambda e: e.dma_start(out=ropet[:], in_=rope_d[:, :, t0:t1]), writes=[rope_r])
        oT = sb(ph, "m_oT", [128, NH, TH], BF16)
        oT_r = [Reg("m_oT%d" % hh) for hh in range(NH)]
        p1 = contextlib.ExitStack()
        win = sb(p1, "m_win", [128, KC, 704], BF16)
        win_r = Reg("m_win")
        gq = sb(p1, "m_gq", [128, 3], F32)
        gkv = sb(p1, "m_gkv", [128, 2], F32)
        tk.dma("pool", wch[0], lambda e: e.dma_start(out=win[:], in_=m_win.rearrange("(k p) o -> p k o", p=128)),
               writes=[win_r])
        gv_r = Reg("m_gv")
        tk.dma("sp", ch_in, lambda e: e.dma_start(out=gq[:], in_=m_gq), writes=[gv_r])
        tk.dma("sp", ch_in, lambda e: e.dma_start(out=gkv[:], in_=m_gkv), writes=[gv_r])
        u_s = Scratch(p1, "m_u", [128, KC, 512], BF16, 2)
        sq_s = Scratch(p1, "m_sq", [128, KC, 512], BF16, 1)
        t1_s = Scratch(p1, "m_t1", [128, 512], F32, 3)
        t2_s = Scratch(p1, "m_t2", [128, 512], F32, 3)
        a_s = Scratch(p1, "m_a", [128, 6, 512], F32, 2)
        rowo_s = Scratch(p1, "m_rowo", [128, 3, 512], F32, 2, chan=True)
        def m1_tile(ts, n):
            o = ts - t0
            u, u_r = u_s.get()
            rmsnorm_pre(0, 0, ts, n, u, u_r, 0, sq_s, t1_s, t2_s)
            yield
            a, a_r = a_s.get()
            for oc in range(6):
                m = 128 if oc < 5 else 64
                pb_, pbr = bank()
                for k in range(KC):
                    tk.op("pe", lambda e, k=k: e.matmul(pb_[:m, :n], win[:, k, oc * 128:oc * 128 + m], u[:, k, :n],
                                                        start=(k == 0), stop=(k == KC - 1)),
                          reads=[win_r, u_r], writes=[pbr])
                evac(a[:m, oc, :n], pb_[:m, :n], [pbr], [a_r])
                if oc % 2 == 1:
                    yield
            sq, sq_r = sq_s.get()
            tk.op("act", lambda e: e.activation(out=sq[:, 0:5, :n], in_=a[:, 0:5, :n], func=AF.Square),
                  reads=[a_r], writes=[sq_r])
            pq, pqr = bank()
            for k in range(3):
                tk.op("pe", lambda e, k=k: e.matmul(pq[:, :n], ONE_b, sq[:, k, :n], start=(k == 0), stop=(k == 2)),
                      reads=[sq_r, c_r], writes=[pqr])
            pk, pkr = bank()
            for k in range(2):
                tk.op("pe", lambda e, k=k: e.matmul(pk[:, :n], ONE_b, sq[:, 3 + k, :n], start=(k == 0), stop=(k == 1)),
                      reads=[sq_r, c_r], writes=[pkr])
            ta, tar = t1_s.get()
            rq, rqr = t2_s.get()
            rstd_from_ss(pq[:, :n], pqr, 128, n, 1.0 / QLORA, ta, tar, rq, rqr)
            tb, tbr = t1_s.get()
            rk, rkr = t2_s.get()
            rstd_from_ss(pk[:, :n], pkr, 128, n, 1.0 / KVLORA, tb, tbr, rk, rkr)
            for k in range(3):
                tk.op("dve", lambda e, k=k: e.scalar_tensor_tensor(
                    out=cq[:, k, o:o + n], in0=a[:, k, :n], scalar=gq[:, k:k + 1], in1=rq[:, :n],
                    op0=ALU.mult, op1=ALU.mult), reads=[a_r, rqr, gv_r], writes=[cq_r])
            rowo, rowo_r = rowo_s.get()
            for k in range(2):
                tk.op("dve", lambda e, k=k: e.scalar_tensor_tensor(
                    out=rowo[:, k, :n], in0=a[:, 3 + k, :n], scalar=gkv[:, k:k + 1], in1=rk[:, :n],
                    op0=ALU.mult, op1=ALU.mult), reads=[a_r, rkr, gv_r], writes=[rowo_r])
            tk.op("act", lambda e: e.copy(out=ckv_b[:, :, ts:ts + n], in_=rowo[:, 0:2, :n]),
                  reads=[rowo_r], writes=kvregs(ts, n))
            yield
            pr_, prr = bank()
            tk.op("pe", lambda e: e.matmul(pr_[:64, :n], ROT_f[:64, :], a[:64, 5, :n], start=True, stop=True),
                  reads=[a_r, c_r], writes=[prr])
            tk.op("dve", lambda e: e.tensor_tensor(out=rowo[:64, 2, :n], in0=a[:64, 5, :n],
                                                   in1=ropet[:, 0, o:o + n], op=ALU.mult),
                  reads=[a_r, rope_r], writes=[rowo_r])
            tk.op("dve", lambda e: e.tensor_tensor(out=a[:64, 5, :n], in0=pr_[:64, :n],
                                                   in1=ropet[:, 1, o:o + n], op=ALU.mult),
                  reads=[prr, rope_r], writes=[a_r])
            tk.op("dve", lambda e: e.tensor_tensor(out=rowo[:64, 2, :n], in0=rowo[:64, 2, :n],
                                                   in1=a[:64, 5, :n], op=ALU.add),
                  reads=[a_r], writes=[rowo_r])
            tk.op("act", lambda e: e.copy(out=kr_b[:64, ts:ts + n], in_=rowo[:64, 2, :n]),
                  reads=[rowo_r], writes=kvregs(ts, n))
            oc_ = rowo_s.chan()
            tk.dma("sp", oc_, lambda e: e.dma_start(
                out=rowsT[0:256, ts:ts + n].rearrange("(k p) t -> p k t", p=128), in_=rowo[:, 0:2, :n]),
                reads=[rowo_r])
            tk.dma("sp", oc_, lambda e: e.dma_start(out=rowsT[256:320, ts:ts + n], in_=rowo[:64, 2, :n]),
                   reads=[rowo_r], cont=True)
            yield

        for i_ in range(0, len(tl), 2):
            gens_ = [m1_tile(ts, n) for (ts, n) in tl[i_:i_ + 2]]
            while gens_:
                for g_ in list(gens_):
                    try:
                        next(g_)
                    except StopIteration:
                        gens_.remove(g_)
        tk.barrier()
        p1.close()
        p2 = contextlib.ExitStack()
        wuq = sb(p2, "m_wuq", [128, 3, NH * 192], BF16)
        wuk = sb(p2, "m_wuk", [128, NH, 2, 128], BF16)
        wuv = sb(p2, "m_wuv", [128, NH, 2, 128], BF16)
        wukT = sb(p2, "m_wukT", [128, NH, 256], BF16)
        w2_r = Reg("m_w2")
        tk.dma("pool", wch[1], lambda e: e.dma_start(out=wuq[:], in_=m_wuq.rearrange("(k p) o -> p k o", p=128)),
               writes=[w2_r])
        tk.dma("pool", wch[1], lambda e: e.dma_start(out=wuk[:], in_=m_wuk.rearrange("h (k p) n -> p h k n", p=128)),
               writes=[w2_r], cont=True)
        tk.dma("pool", wch[1], lambda e: e.dma_start(out=wuv[:], in_=m_wuv.rearrange("h (k p) n -> p h k n", p=128)),
               writes=[w2_r], cont=True)
        tk.dma("pool", wch[1], lambda e: e.dma_start(out=wukT[:], in_=m_wukT.rearrange("h p r -> p h r")),
               writes=[w2_r], cont=True)
        NKB = t1 // 128 if not has_s else SEQ // 128
        if has_s:
            Qs = sb(p2, "m_Qs", [128, 3, 4, 32], BF16)
            Qs_r = Reg("m_Qs")
            tk.op("pool", lambda e: e.memset(Qs[64:128, 2, :, :], 0.0), writes=[Qs_r])
        p2a = contextlib.ExitStack()
        qn_s = Scratch(p2a, "m_qn", [128, TH], BF16, 2)
        qr_s = Scratch(p2a, "m_qr", [128, TH], BF16, 2)
        for t_, r_ in zip(qr_s.t, qr_s.r):
            tk.op("pool", lambda e, t_=t_: e.memset(t_[64:128, :], 0.0), writes=[r_])
        qx_s = Scratch(p2a, "m_qx", [64, 2, 512], F32, 2)
        kn_s = Scratch(p2a, "m_kn", [128, SEQ], BF16, 2)
        vp_s = Scratch(p2a, "m_vp", [128, SEQ // 128, 132], BF16, 2)
        pT_s = Scratch(p2a, "m_pT", [128, 512], BF16, 4)
        on_s = Scratch(p2a, "m_on", [128, 4, 128], BF16, 2)
        rs_s = Scratch(p2a, "m_rs", [128, 4], F32, 2)
        ptl = [x for x in tl if x[0] < SEQ]
        mh = {}

        def m2_pro(hh):
            qn, qn_r = qn_s.get()
            qr, qr_r = qr_s.get()
            for (ts, n) in tl:
                o = ts - t0
                pb_, pbr = bank()
                for k in range(3):
                    tk.op("pe", lambda e, k=k: e.matmul(pb_[:, :n], wuq[:, k, hh * 192:hh * 192 + 128], cq[:, k, o:o + n],
                                                        start=(k == 0), stop=(k == 2)),
                          reads=[w2_r, cq_r], writes=[pbr])
                evac(qn[:, o:o + n], pb_[:, :n], [pbr], [qn_r])
                p2_, p2r = bank()
                for k in range(3):
                    tk.op("pe", lambda e, k=k: e.matmul(p2_[:64, :n], wuq[:, k, hh * 192 + 128:hh * 192 + 192],
                                                        cq[:, k, o:o + n], start=(k == 0), stop=(k == 2)),
                          reads=[w2_r, cq_r], writes=[p2r])
                qx, qx_r = qx_s.get()
                tk.op("act", lambda e: e.copy(out=qx[:, 0, :n], in_=p2_[:64, :n]), reads=[p2r], writes=[qx_r])
                p3_, p3r = bank()
                tk.op("pe", lambda e: e.matmul(p3_[:64, :n], ROT_f[:64, :], qx[:, 0, :n], start=True, stop=True),
                      reads=[qx_r, c_r], writes=[p3r])
                tk.op("dve", lambda e: e.tensor_tensor(out=qx[:, 1, :n], in0=p3_[:64, :n], in1=ropet[:, 1, o:o + n],
                                                       op=ALU.mult), reads=[p3r, rope_r], writes=[qx_r])
                tk.op("dve", lambda e: e.tensor_tensor(out=qx[:, 0, :n], in0=qx[:, 0, :n], in1=ropet[:, 0, o:o + n],
                                                       op=ALU.mult), reads=[rope_r], writes=[qx_r])
                tk.op("dve", lambda e: e.tensor_tensor(out=qr[:64, o:o + n], in0=qx[:, 0, :n], in1=qx[:, 1, :n],
                                                       op=ALU.add), reads=[qx_r], writes=[qr_r])
                yield
            kn, kn_r = kn_s.get()
            vp, vp_r = vp_s.get()
            nk = NKB * 128
            for ks in range(0, nk, 512):
                n = min(512, nk - ks)
                pb_, pbr = bank()
                for k in range(2):
                    tk.op("pe", lambda e, k=k: e.matmul(pb_[:, :n], wuk[:, hh, k, :], ckv_b[:, k, ks:ks + n],
                                                        start=(k == 0), stop=(k == 1)),
                          reads=[w2_r] + kvregs(ks, n), writes=[pbr])
                evac(kn[:, ks:ks + n], pb_[:, :n], [pbr], [kn_r])
                yield
            for kb4 in range(0, NKB, 4):
                nb = min(4, NKB - kb4)
                pb_, pbr = bank()
                for j in range(nb):
                    kb = kb4 + j
                    for k in range(2):
                        tk.op("pe", lambda e, k=k, j=j, kb=kb: e.matmul(
                            pb_[:, j * 128:(j + 1) * 128], ckv_b[:, k, kb * 128:(kb + 1) * 128], wuv[:, hh, k, :],
                            start=(k == 0), stop=(k == 1)),
                            reads=[w2_r] + kvregs(kb * 128, 128), writes=[pbr])
                evac(vp[:, kb4:kb4 + nb, 0:128], pb_[:, :nb * 128].rearrange("p (j v) -> p j v", v=128), [pbr], [vp_r])
                yield
            tk.op("dve", lambda e: e.memset(vp[:, :, 128:129], 1.0), writes=[vp_r])
            mh[hh] = (qn, qn_r, qr, qr_r, kn, kn_r, vp, vp_r)
            yield

        def m2_att(hh):
            qn, qn_r, qr, qr_r, kn, kn_r, vp, vp_r = mh[hh]
            for (ts, n) in ptl:
                o = ts - t0
                nsub = n // 128
                kb_hi = (ts + n) // 128
                accs_l = reserve(nsub)
                accs = [(a_, r_) for (a_, r_, _) in accs_l]
                pend = []

                def emit_pv(pv):
                    kb, d, j0, pT, pT_r = pv
                    for si in range(max(d, 0), nsub):
                        ab, abr = accs[si]
                        last_kb = ts // 128 + si
                        c0 = si * 128 - j0
                        tk.op("pe", lambda e, ab=ab, c0=c0, kb=kb, last_kb=last_kb: e.matmul(
                            ab[:, 0:129], pT[:, c0:c0 + 128], vp[:, kb, 0:129], start=(kb == 0), stop=(kb == last_kb)),
                            reads=[pT_r, vp_r], writes=[abr])

                for kb in range(kb_hi):
                    d = kb - ts // 128
                    j0 = max(d, 0) * 128
                    ncol = n - j0
                    psc, pscr = bank()
                    tk.op("pe", lambda e: e.matmul(psc[:, :ncol], kn[:, kb * 128:(kb + 1) * 128],
                                                   qn[:, o + j0:o + n], start=True, stop=False),
                          reads=[kn_r, qn_r], writes=[pscr])
                    tk.op("pe", lambda e: e.matmul(psc[:, :ncol], kr_b[:, kb * 128:(kb + 1) * 128],
                                                   qr[:, o + j0:o + n], start=False, stop=True),
                          reads=kvregs(kb * 128, 128) + [qr_r], writes=[pscr])
                    if len(pend) >= 2:
                        emit_pv(pend.pop(0))
                    pT, pT_r = pT_s.get()
                    tk.op("act", lambda e: e.activation(out=pT[:, :ncol], in_=psc[:, :ncol], func=AF.Exp,
                                                        scale=MLA_SCALE), reads=[pscr], writes=[pT_r])
                    if d >= 0:
                        tk.op("dve", lambda e: e.tensor_tensor(out=pT[:, 0:128], in0=pT[:, 0:128], in1=U_b,
                                                               op=ALU.mult), reads=[c_r], writes=[pT_r])
                    pend.append((kb, d, j0, pT, pT_r))
                    yield
                for pv_ in pend:
                    emit_pv(pv_)
                rs, rs_r = rs_s.get()
                on, on_r = on_s.get()
                for si in range(nsub):
                    ab, abr = accs[si]
                    tk.op("dve", lambda e, ab=ab, si=si: e.reciprocal(out=rs[:, si:si + 1], in_=ab[:, 128:129]),
                          reads=[abr], writes=[rs_r])
                    tk.op("act", lambda e, ab=ab, si=si: e.activation(out=on[:, si, :], in_=ab[:, 0:128], func=AF.Copy,
                                                                      scale=rs[:, si:si + 1]),
                          reads=[abr, rs_r], writes=[on_r])
                ptb, ptr = bank()
                for si in range(nsub):
                    tk.op("pe", lambda e, si=si: e.matmul(ptb[:, si * 128:(si + 1) * 128], on[:, si, :], I_b,
                                                          start=True, stop=True),
                          reads=[on_r, c_r], writes=[ptr])
                evac(oT[:, hh, o:o + n], ptb[:, :n], [ptr], [oT_r[hh]])
                release(accs_l)
                yield
            if has_s:
                so = SEQ - t0
                pb_, pbr = bank()
                for k in range(2):
                    tk.op("pe", lambda e, k=k: e.matmul(pb_[:, k * 16:(k + 1) * 16], wukT[:, hh, k * 128:(k + 1) * 128],
                                                        qn[:, so:so + 16], start=True, stop=True),
                          reads=[w2_r, qn_r], writes=[pbr])
                evac(Qs[:, 0:2, :, hh * 4:(hh + 1) * 4],
                     pb_[:, 0:32].rearrange("p (k s t) -> p k s t", k=2, s=4), [pbr], [Qs_r])
                tk.op("act", lambda e: e.copy(out=Qs[:64, 2, :, hh * 4:(hh + 1) * 4],
                                              in_=qr[:64, so:so + 16].rearrange("p (s t) -> p s t", s=4)),
                      reads=[qr_r], writes=[Qs_r])
            yield

        def run2(gens_):
            gens_ = list(gens_)
            while gens_:
                for g_ in list(gens_):
                    try:
                        next(g_)
                    except StopIteration:
                        gens_.remove(g_)

        run2([m2_pro(0)])
        for hh in range(NH):
            gl_ = [m2_att(hh)]
            if hh + 1 < NH:
                gl_.append(m2_pro(hh + 1))
            run2(gl_)
        tk.barrier()
        p2a.close()
        if has_s:
            R = 8
            pt_sb = sb(p2, "m_pt", [128, 4], I32)
            idx_sb = sb(p2, "m_idx", [128, 16, 4], I32)
            pt_r = Reg("m_pt")
            tk.dma("sp", ch_in, lambda e: e.dma_start(out=pt_sb[:], in_=ptab), writes=[pt_r])
            for gi in range(16):
                tk.op("dve", lambda e, gi=gi: e.tensor_scalar(out=idx_sb[:, gi, :], in0=pt_sb[:, :], scalar1=16.0,
                                                               scalar2=float(gi), op0=ALU.mult, op1=ALU.add),
                      reads=[pt_r], writes=[pt_r])
            NCB = 3
            cbuf = [sb(p2, "m_cb%d" % i, [128, R, ROW], F32) for i in range(NCB)]
            cbuf_r = [Reg("m_cb%d" % i) for i in range(NCB)]
            cch = [tk.chan() for _ in range(NCB)]
            ctok_s = Scratch(p2, "m_ctok", [128, R, 388], BF16, 3)
            for t_ in ctok_s.t:
                tk.op("dve", lambda e, t_=t_: e.memset(t_[:, :, 64:128], 0.0), writes=ctok_s.r)
                tk.op("dve", lambda e, t_=t_: e.memset(t_[:, :, 384:385], 1.0), writes=ctok_s.r)
            pd_s = Scratch(p2, "m_pd", [128, 32], BF16, 2)
            cnew = sb(p2, "m_cnew", [4, 260], BF16)
            cnew_r = Reg("m_cnew")
            oln = sb(p2, "m_oln", [32, 256], BF16)
            oln_r = Reg("m_oln")
            olT = sb(p2, "m_olT", [128, 2, 32], BF16)
            olT_r = Reg("m_olT")
            rsd = sb(p2, "m_rsd", [32, 1], F32)
            ngath = 128 // R
            so = SEQ - t0

            def gather(s, gi):
                i = (s * ngath + gi) % NCB
                tk.dma("pool", cch[i], lambda e: e.indirect_dma_start(
                    out=cbuf[i][:].rearrange("p r d -> p (r d)"), out_offset=None,
                    in_=cache,
                    in_offset=bass.IndirectOffsetOnAxis(ap=idx_sb[:, gi, s:s + 1], axis=0)),
                    reads=[pt_r], writes=[cbuf_r[i]])

            gather(0, 0)
            gather(0, 1)
            cT4_s = Scratch(p2, "m_cT4", [128, 3, 512], BF16, 3)
            pd4_s = Scratch(p2, "m_pd4", [128, 128], BF16, 3)
            batches = [(s_, gi, r0) for s_ in range(4) for gi in range(ngath) for r0 in range(0, R, 4)]
            nbt = len(batches)
            bst = [dict() for _ in range(nbt)]
            grp = {}
            accs_d = {}

            def stA(b):
                s, gi, r0 = batches[b]
                g = s * ngath + gi
                if r0 == 0:
                    nxt = g + 2
                    if nxt < 4 * ngath:
                        gather(nxt // ngath, nxt % ngath)
                    i = g % NCB
                    ctok, ctok_r = ctok_s.get()
                    tk.op("dve", lambda e: e.tensor_copy(out=ctok[:, :, 128:384], in_=cbuf[i][:, :, 0:256]),
                          reads=[cbuf_r[i]], writes=[ctok_r])
                    tk.op("dve", lambda e: e.tensor_copy(out=ctok[:, :, 0:64], in_=cbuf[i][:, :, 256:320]),
                          reads=[cbuf_r[i]], writes=[ctok_r])
                    grp[g] = (ctok, ctok_r)
                ctok, ctok_r = grp[g]
                cT4, cT4_r = cT4_s.get()
                for k in range(3):
                    m = 128
                    c0_ = (128, 256, 0)[k]
                    ptp, ptpr = bank()
                    for j in range(4):
                        tk.op("pe", lambda e, k=k, m=m, j=j, c0_=c0_: e.matmul(
                            ptp[:m, j * 128:(j + 1) * 128], ctok[:, r0 + j, c0_:c0_ + 128], I_b,
                            start=True, stop=True), reads=[ctok_r, c_r], writes=[ptpr])
                    if k == 1:
                        tk.op("dve", lambda e, k=k, m=m, ptp=ptp: e.tensor_copy(out=cT4[:m, k, :], in_=ptp[:m, :]),
                              reads=[ptpr], writes=[cT4_r])
                    else:
                        tk.op("act", lambda e, k=k, m=m, ptp=ptp: e.copy(out=cT4[:m, k, :], in_=ptp[:m, :]),
                              reads=[ptpr], writes=[cT4_r])
                bst[b]["cT4"] = (cT4, cT4_r)

            def stB(b):
                s, gi, r0 = batches[b]
                cT4, cT4_r = bst[b]["cT4"]
                psc, pscr = bank()
                for j in range(4):
                    for k in range(3):
                        m = 128
                        tk.op("pe", lambda e, k=k, m=m, j=j: e.matmul(
                            psc[:, j * 32:(j + 1) * 32], cT4[:m, k, j * 128:(j + 1) * 128], Qs[:m, k, s, :],
                            start=(k == 0), stop=(k == 2)), reads=[cT4_r, Qs_r], writes=[pscr])
                pd4, pd4_r = pd4_s.get()
                tk.op("act", lambda e: e.activation(out=pd4[:, :], in_=psc[:, 0:128], func=AF.Exp, scale=MLA_SCALE),
                      reads=[pscr], writes=[pd4_r])
                bst[b]["pd4"] = (pd4, pd4_r)

            def stC(b):
                s, gi, r0 = batches[b]
                g = s * ngath + gi
                ctok, ctok_r = grp[g]
                pd4, pd4_r = bst[b]["pd4"]
                if s not in accs_d:
                    accs_d[s] = reserve(1)
                (acc, acc_r, _), = acc_l = accs_d[s]
                for j in range(4):
                    first = (gi == 0 and r0 == 0 and j == 0)
                    tk.op("pe", lambda e, j=j, first=first: e.matmul(
                        acc[:32, 0:257], pd4[:, j * 32:(j + 1) * 32], ctok[:, r0 + j, 128:385],
                        start=first, stop=False), reads=[pd4_r, ctok_r], writes=[acc_r])
                if gi == ngath - 1 and r0 == R - 4:
                    tail(s, acc, acc_r, acc_l)

            def tail(s, acc, acc_r, acc_l):
                    tcol = SEQ + s * 4
                    ptp, ptpr = bank()
                    for k in range(2):
                        tk.op("pe", lambda e, k=k: e.matmul(ptp[:4, k * 128:(k + 1) * 128], ckv_b[:, k, tcol:tcol + 4], I_b,
                                                            start=True, stop=True),
                              reads=kvregs(tcol, 4) + [c_r], writes=[ptpr])
                    tk.op("dve", lambda e: e.memset(cnew[:, :], 0.0), writes=[cnew_r])
                    tk.op("act", lambda e: e.copy(out=cnew[:4, 0:256], in_=ptp[:4, 0:256]), reads=[ptpr], writes=[cnew_r])
                    tk.op("dve", lambda e: e.memset(cnew[:4, 256:257], 1.0), writes=[cnew_r])
                    psc, pscr = bank()
                    for k in range(3):
                        m = 128
                        src = ckv_b[:, k, tcol:tcol + 4] if k < 2 else kr_b[:, tcol:tcol + 4]
                        tk.op("pe", lambda e, k=k, m=m, src=src: e.matmul(psc[:4, 0:32], src, Qs[:m, k, s, :],
                                                                          start=(k == 0), stop=(k == 2)),
                              reads=kvregs(tcol, 4) + [Qs_r], writes=[pscr])
                    pd, pd_r = pd_s.get()
                    tk.op("act", lambda e: e.activation(out=pd[:4, :], in_=psc[:4, 0:32], func=AF.Exp, scale=MLA_SCALE),
                          reads=[pscr], writes=[pd_r])
                    tk.op("dve", lambda e: e.tensor_tensor(out=pd[:4, :], in0=pd[:4, :], in1=DM_b[:4, :], op=ALU.mult),
                          reads=[c_r], writes=[pd_r])
                    tk.op("pe", lambda e: e.matmul(acc[:32, 0:257], pd[:4, :], cnew[:4, 0:257], start=False, stop=True),
                          reads=[pd_r, cnew_r], writes=[acc_r])
                    tk.op("dve", lambda e: e.reciprocal(out=rsd[:, :], in_=acc[:32, 256:257]), reads=[acc_r], writes=[oln_r])
                    tk.op("act", lambda e: e.activation(out=oln[:, :], in_=acc[:32, 0:256], func=AF.Copy, scale=rsd[:, 0:1]),
                          reads=[acc_r, oln_r], writes=[oln_r])
                    release(acc_l)
                    ptp, ptpr = bank()
                    for k in range(2):
                        tk.op("pe", lambda e, k=k: e.matmul(ptp[:, k * 32:(k + 1) * 32], oln[:, k * 128:(k + 1) * 128],
                                                            I_b[:32, :32], start=True, stop=True),
                              reads=[oln_r, c_r], writes=[ptpr])
                    evac(olT[:, :, :], ptp[:, 0:64].rearrange("p (k q) -> p k q", k=2), [ptpr], [olT_r])
                    pov, povr = bank()
                    for hh in range(NH):
                        for k in range(2):
                            tk.op("pe", lambda e, k=k, hh=hh: e.matmul(pov[:, hh * 4:(hh + 1) * 4], wuv[:, hh, k, :],
                                                                       olT[:, k, hh * 4:(hh + 1) * 4],
                                                                       start=(k == 0), stop=(k == 1)),
                                  reads=[w2_r, olT_r], writes=[povr])
                    tk.op("act", lambda e: e.copy(out=oT[:, :, so + s * 4:so + s * 4 + 4],
                                                  in_=pov[:, 0:32].rearrange("p (h t) -> p h t", h=NH)),
                          reads=[povr], writes=oT_r)
            for b in range(nbt + 2):
                if b < nbt:
                    stA(b)
                if 0 <= b - 1 < nbt:
                    stB(b - 1)
                if 0 <= b - 2 < nbt:
                    stC(b - 2)
        tk.barrier()
        p2.close()
        p4 = contextlib.ExitStack()
        wo = sb(p4, "m_wo", [128, KC, D], BF16)
        wo_r = Reg("m_wo")
        for k0 in range(0, KC, 4):
            tk.dma("pool", wch[2], lambda e, k0=k0: e.dma_start(
                out=wo[:, k0:k0 + 4, :], in_=m_wo.rearrange("(k p) o -> p k o", p=128)[:, k0:k0 + 4, :]),
                writes=[wo_r], cont=(k0 > 0))
        y_s = Scratch(p4, "m_y", [128, KC, 512], F32, 2, nreg=KC)
        sq_s = Scratch(p4, "m_sq4", [128, KC, 512], BF16, 1)
        t1_s = Scratch(p4, "m_t14", [128, 512], F32, 2)
        t2_s = Scratch(p4, "m_t24", [128, 512], F32, 2)
        for (ts, n) in ftiles(t0, t1):
            o = ts - t0
            y, y_r = y_s.get()
            for oc in range(KC):
                pb_, pbr = bank()
                for k in range(NH):
                    tk.op("pe", lambda e, k=k: e.matmul(pb_[:, :n], wo[:, k, oc * 128:(oc + 1) * 128], oT[:, k, o:o + n],
                                                        start=(k == 0), stop=(k == NH - 1)),
                          reads=[wo_r, oT_r[k]], writes=[pbr])
                evac(y[:, oc, :n], pb_[:, :n], [pbr], [y_r[oc]])
            postnorm_add(0, 1, y, y_r, 0, ts, n, sq_s, t1_s, t2_s)
        tk.barrier()
        p4.close()
        ph.close()


    Sst_r = [Reg("Sst%d" % i) for i in range(NH)]
    craw_r = [Reg("craw%d" % i) for i in range(24)]
    gst = {}

    def gdn_init():
        gst["Sst"] = sb(es, "Sst", [128, NH, 128], F32)
        gst["craw"] = sb(es, "craw", [128, 24, 3], F32)
        tk.op("dve", lambda e: e.memset(gst["Sst"][:], 0.0), writes=Sst_r)
        tk.op("dve", lambda e: e.memset(gst["craw"][:], 0.0), writes=craw_r)
        XD_ = F32 if INV_F32 else BF16
        gst["gm"] = sb(es, "gm", [128, 1408], XD_)
        gst["gm_r"] = Reg("gm")
        gst["i4x"] = sb(es, "i4x", [128, 512], XD_)
        if INV_F32:
            tk.dma("sp", ch_in, lambda e: e.dma_start(out=gst["gm"][:], in_=gmask_d), writes=[gst["gm_r"]])
            tk.dma("sp", ch_in, lambda e: e.dma_start(out=gst["i4x"][:], in_=consts_d[:, 1024:1536]), writes=[gst["gm_r"]])
        else:
            tk.dma("pool", ch_cb, lambda e: e.dma_start(out=gst["gm"][:], in_=gmask_d), writes=[gst["gm_r"]])
            tk.dma("pool", ch_cb, lambda e: e.dma_start(out=gst["i4x"][:], in_=consts_d[:, 1024:1536]),
                   writes=[gst["gm_r"]], cont=True)

    def v3(ap2, C, nb):
        return ap2.rearrange("p (j c) -> p j c", c=128)[:, :nb, :C]

    def gdn_phase(t0, t1):
        Sst, craw = gst["Sst"], gst["craw"]
        gm, gm_r, i4x = gst["gm"], gst["gm_r"], gst["i4x"]
        XD = F32 if INV_F32 else BF16
        I_x = I_f if INV_F32 else I_b
        TH = t1 - t0
        tl = tiles(t0, t1)
        has_s = t1 > SEQ
        THp = min(t1, SEQ) - t0
        NB = THp // 128
        ph = contextlib.ExitStack()
        u = sb(ph, "g_u", [128, KC, TH], BF16)
        u_r = Reg("g_u")
        oT = sb(ph, "g_oT", [128, NH, TH], BF16)
        oT_r = [Reg("g_oT%d" % i) for i in range(NH)]
        p0 = contextlib.ExitStack()
        sq_s = Scratch(p0, "g_sq", [128, KC, 512], BF16, 1)
        t1_s = Scratch(p0, "g_t1", [128, 512], F32, 2)
        t2_s = Scratch(p0, "g_t2", [128, 512], F32, 2)
        for (ts, n) in ftiles(t0, t1):
            rmsnorm_pre(1, 0, ts, n, u, u_r, ts - t0, sq_s, t1_s, t2_s)
        tk.barrier()
        p0.close()
        p1 = contextlib.ExitStack()
        t1_s = Scratch(p1, "g_t1b", [128, 512], F32, 1)
        t2_s = Scratch(p1, "g_t2b", [128, 512], F32, 1)
        cw = sb(p1, "g_cw", [128, 24, 4], F32)
        gout = sb(p1, "g_gout", [128, 1], F32)
        abc = sb(p1, "g_abc", [128, 2, NH], F32)
        wba = sb(p1, "g_wba", [128, KC, 16], BF16)
        gp_r = Reg("g_par")
        tk.dma("sp", ch_in, lambda e: e.dma_start(out=cw[:], in_=d_cw), writes=[gp_r])
        tk.dma("sp", ch_in, lambda e: e.dma_start(out=gout[:], in_=d_gout), writes=[gp_r])
        tk.dma("sp", ch_in, lambda e: e.dma_start(out=abc[:, 0, :], in_=d_alog[0].partition_broadcast(128)), writes=[gp_r])
        tk.dma("sp", ch_in, lambda e: e.dma_start(out=abc[:, 1, :], in_=d_dtb[0].partition_broadcast(128)), writes=[gp_r])
        tk.op("act", lambda e: e.activation(out=abc[:, 0, :], in_=abc[:, 0, :], func=AF.Exp), reads=[gp_r], writes=[gp_r])
        wba_r = Reg("g_wba")
        tk.dma("pool", wch[2], lambda e: e.dma_start(
            out=wba[:], in_=d_win.rearrange("(k p) o -> p k o", p=128)[:, :, 4096:4112]), writes=[wba_r])
        if has_s:
            cvin = sb(p1, "g_cvin", [128, 24, 4, 3], F32)
            cvs = sb(p1, "g_cvs", [128, 24, 4, 3], F32)
            cvin_r = Reg("g_cvin")
            cvs_r = Reg("g_cvs")
            tk.dma("sp", ch_in, lambda e: e.dma_start(out=cvin[:], in_=cv_in.rearrange("(c p) (s j) -> p c s j", p=128, j=3)),
                   writes=[cvin_r])
        NBS = NB + (1 if has_s else 0)
        gtok = sb(p1, "g_gtok", [128, NBS, 8], F32)
        btok = sb(p1, "g_btok", [128, NBS, 8], F32)
        nbtok = sb(p1, "g_nbtok", [128, NBS, 8], F32)
        gt_r = Reg("g_gt")
        xt_ = sb(p1, "g_xt", [128, NBS, 8], F32)
        pba, pbar = bank()
        for b in range(NB):
            for k in range(KC):
                tk.op("pe", lambda e, k=k, b=b: e.matmul(pba[:, b * 16:(b + 1) * 16], u[:, k, b * 128:(b + 1) * 128],
                                                         wba[:, k, :], start=(k == 0), stop=(k == KC - 1)),
                      reads=[u_r, wba_r], writes=[pbar])
        if has_s:
            pbs, pbsr = bank()
            for s_ in range(4):
                for k in range(KC):
                    tk.op("pe", lambda e, k=k, s_=s_: e.matmul(pbs[:4, s_ * 16:(s_ + 1) * 16],
                                                               u[:, k, THp + 4 * s_:THp + 4 * s_ + 4], wba[:, k, :],
                                                               start=(k == 0), stop=(k == KC - 1)),
                          reads=[u_r, wba_r], writes=[pbsr])
        def ba_post(P_, src3, dstsl, preg):
            gt, bt, nbt, xt = dstsl
            nblk = src3.shape[1]
            tk.op("act", lambda e: e.activation(out=bt, in_=src3[:, :, 0:8], func=AF.Sigmoid), reads=[preg], writes=[gt_r])
            tk.op("dve", lambda e: e.tensor_scalar(out=nbt, in0=bt, scalar1=-1.0, scalar2=None, op0=ALU.mult),
                  reads=[gt_r], writes=[gt_r])
            for b in range(nblk):
                tk.op("dve", lambda e, b=b: e.tensor_tensor(out=xt[:, b, :], in0=src3[:, b, 8:16], in1=abc[:P_, 1, :],
                                                            op=ALU.add), reads=[preg, gp_r], writes=[gt_r])
            tk.op("act", lambda e: e.activation(out=xt, in_=xt, func=AF.Exp), reads=[gt_r], writes=[gt_r])
            tk.op("act", lambda e: e.activation(out=xt, in_=xt, func=AF.Ln, bias=epst[:P_, 1:2]), reads=[gt_r, c_r],
                  writes=[gt_r])
            for b in range(nblk):
                tk.op("dve", lambda e, b=b: e.scalar_tensor_tensor(out=gt[:, b, :], in0=xt[:, b, :], scalar=-1.0,
                                                                   in1=abc[:P_, 0, :], op0=ALU.mult, op1=ALU.mult),
                      reads=[gt_r, gp_r], writes=[gt_r])

        ba_post(128, pba[:, 0:NB * 16].rearrange("p (b x) -> p b x", x=16),
                (gtok[:, 0:NB, :], btok[:, 0:NB, :], nbtok[:, 0:NB, :], xt_[:, 0:NB, :]), pbar)
        if has_s:
            gts = sb(p1, "g_gts", [4, 4, 8], F32)
            bts = sb(p1, "g_bts", [4, 4, 8], F32)
            nbts = sb(p1, "g_nbts", [4, 4, 8], F32)
            xts = sb(p1, "g_xts", [4, 4, 8], F32)
            ba_post(4, pbs[:4, 0:64].rearrange("p (b x) -> p b x", x=16), (gts[:], bts[:], nbts[:], xts[:]), pbsr)
        wsl = [sb(p1, "g_wsl%d" % i, [128, KC, 4, 128], BF16) for i in range(2)]
        wsl_r = [Reg("g_wsl%d" % i) for i in range(2)]
        d_win_v = d_win.rearrange("(k p) o -> p k o", p=128)

        def load_wsl(hh):
            s_ = hh % 2
            for w_ in range(4):
                col = w_ * 1024 + hh * 128
                tk.dma("pool", wch[s_], lambda e, w_=w_, col=col: e.dma_start(
                    out=wsl[s_][:, :, w_, :], in_=d_win_v[:, :, col:col + 128]), writes=[wsl_r[s_]], cont=(w_ > 0))

        raw = sb(p1, "g_raw", [128, 3 + THp], F32)
        raw_r = Reg("g_raw")
        raws = sb(p1, "g_raws", [128, 4, 7], F32)
        raws_r = Reg("g_raws")
        acc = sb(p1, "g_acc", [128, TH], F32)
        acc_r = Reg("g_acc")
        sqb_s = Scratch(p1, "g_sqb", [128, 512], BF16, 2)
        qn_s = Scratch(p1, "g_qn", [128, TH], BF16, 2)
        kn_s = Scratch(p1, "g_kn", [128, TH], BF16, 2)
        vv_s = Scratch(p1, "g_vv", [128, TH], BF16, 2)
        zs_s = Scratch(p1, "g_zs", [128, TH], BF16, 2)
        if has_s:
            Ssm_s = Scratch(p1, "g_Ssm", [128, 4, 128], F32, 2, nreg=4, chan=True)
            ssl_ch = [tk.chan() for _ in range(2)]

        class BB:
            pass

        def make_bb(tag, wide):
            b_ = BB()
            CW = 128 if wide else 4
            F1 = sb(p1, "b_F1" + tag, [128, 4, CW], F32)
            F2 = sb(p1, "b_F2" + tag, [128, 4, CW], F32)
            b_.R, b_.gB = F1, F2
            F1b, F2b = F1[:].bitcast(BF16), F2[:].bitcast(BF16)
            b_.LT, b_.X = F1b[:, :, 0:CW], F1b[:, :, CW:2 * CW]
            b_.W1, b_.W2 = F2b[:, :, 0:CW], F2b[:, :, CW:2 * CW]
            b_.sc = sb(p1, "b_sc" + tag, [128, 4, 4], F32)
            b_.gl = sb(p1, "b_gl" + tag, [128, 4, 4], F32)
            b_.E1 = sb(p1, "b_E1" + tag, [128, 4, CW], BF16)
            b_.E2 = sb(p1, "b_E2" + tag, [128, 4, CW], BF16)
            b_.Ao, b_.AoT = b_.E1, b_.E2
            b_.Lp = sb(p1, "b_Lp" + tag, [128, 4, CW], BF16)
            b_.XT = sb(p1, "b_XT" + tag, [128, 4, CW], BF16)
            b_.qkt = sb(p1, "b_qkt" + tag, [128, 4, CW], BF16)
            b_.qd = sb(p1, "b_qd" + tag, [128, 4, CW], BF16)
            AOFF = _os.environ.get("AOFF", "")
            if wide:
                b_.egbc = sb(p1, "b_eg" + tag, [128, 4, 128], BF16)
                b_.utok = b_.egbc if "u" not in AOFF else sb(p1, "b_ut" + tag, [128, 4, 128], BF16)
                b_.wtok = b_.Lp if "w" not in AOFF else sb(p1, "b_wt" + tag, [128, 4, 128], BF16)
                b_.nWk = F1b[:, :, 0:128] if "n" not in AOFF else sb(p1, "b_nw" + tag, [128, 4, 128], BF16)
                if "a" in AOFF:
                    b_.Ao = sb(p1, "b_Ao" + tag, [128, 4, 128], BF16)
                    b_.AoT = sb(p1, "b_AoT" + tag, [128, 4, 128], BF16)
                if "x" in AOFF:
                    b_.LT = sb(p1, "b_LT" + tag, [128, 4, 128], BF16)
                    b_.X = sb(p1, "b_X" + tag, [128, 4, 128], BF16)
                    b_.W1 = sb(p1, "b_W1" + tag, [128, 4, 128], BF16)
                    b_.W2 = sb(p1, "b_W2" + tag, [128, 4, 128], BF16)
            else:
                b_.egbc = sb(p1, "b_eg" + tag, [128, 4, 4], BF16)
                b_.utok = sb(p1, "b_ut" + tag, [128, 4, 128], BF16)
                b_.wtok = sb(p1, "b_wt" + tag, [128, 4, 128], BF16)
                b_.nWk = sb(p1, "b_nw" + tag, [128, 4, 128], BF16)
            for nm in ("kdec", "kbg", "vb"):
                setattr(b_, nm, sb(p1, "b_%s%s" % (nm, tag), [128, 4, 128], BF16))
            rg = {nm: Reg("b_%s%s" % (nm, tag)) for nm in ("F1", "F2", "E1", "E2", "eg", "Lp", "XT", "qkt", "qd", "sc",
                                                          "kdec", "kbg", "vb", "ut", "wt", "nw")}
            b_.r = {"R": rg["F1"], "LT": rg["F1"], "X": rg["F1"], "gB": rg["F2"], "W1": rg["F2"], "W2": rg["F2"],
                    "E1": rg["E1"], "Ao": rg["E1"], "E2": rg["E2"], "AoT": rg["E2"], "Lp": rg["Lp"], "XT": rg["XT"],
                    "qkt": rg["qkt"], "qd": rg["qd"], "sc": rg["sc"], "kdec": rg["kdec"], "kbg": rg["kbg"],
                    "vb": rg["vb"]}
            if wide:
                b_.r.update({"eg": rg["eg"], "utok": rg["eg"], "wtok": rg["Lp"], "nWk": rg["F1"]})
            else:
                b_.r.update({"eg": rg["eg"], "utok": rg["ut"], "wtok": rg["wt"], "nWk": rg["nw"]})
            return b_

        bbs = {"p0": make_bb("p0", True), "p1": make_bb("p1", True)}
        if has_s:
            bbs["s"] = make_bb("s", False)
        bb_i = [0]
        Sb_s = Scratch(p1, "g_Sb", [128, 128], BF16, 2)
        vn_s = Scratch(p1, "g_vn", [128, 128], BF16, 2)
        on_s = Scratch(p1, "g_on", [128, 128], BF16, 2)
        jk_s = Scratch(p1, "g_jk", [128, 128], BF16, 2)
        ss_s = Scratch(p1, "g_ss", [128, 4], F32, 2)
        og_s = Scratch(p1, "g_og", [128, 128], BF16, 2)

        NWARM = int(_os.environ.get("NWARM", "0"))
        if NWARM:
            (wps, wps_r, _), = warm_l = reserve(1)

        def warm():
            for _ in range(NWARM):
                tk.op("pe", lambda e: e.matmul(wps[:, 0:512], ONE_b, cb[:, 0:512], start=True, stop=True),
                      reads=[c_r], writes=[wps_r])

        def gdn_pre(b_, hh, C, nb, o0, gmat, bcols, qn, qn_r, kn, kn_r, vv, vv_r, bmat=None):
            r = b_.r
            NC_ = nb * C
            gcols = [gmat[:, j:j + 1] for j in range(nb)]

            def flat(t, P_=C):
                return t[:].rearrange("p j c -> p (j c)")[:P_, 0:NC_]

            def cv(t, P_=C):
                return flat(t, P_).rearrange("p (j c) -> p j c", c=C)

            def cvp(ps, P_=C):
                return ps[:P_, 0:NC_].rearrange("p (j c) -> p j c", c=C)

            gbc = gmat.unsqueeze(2).to_broadcast([C, nb, C])
            tk.op("dve", lambda e: e.tensor_tensor(out=cv(b_.R), in0=MS_f[:C, :C].unsqueeze(1).to_broadcast([C, nb, C]),
                                                   in1=gbc, op=ALU.mult), reads=[c_r, gt_r], writes=[r["R"]])
            tk.op("dve", lambda e: e.tensor_tensor(out=cv(b_.gB), in0=U_f[:C, :C].unsqueeze(1).to_broadcast([C, nb, C]),
                                                   in1=gbc, op=ALU.mult), reads=[c_r, gt_r], writes=[r["gB"]])
            k1, k1r = bank()
            k2, k2r = bank()
            k3, k3r = bank()
            k4, k4r = bank()
            tk.op("pe", lambda e: e.matmul(k1[:C, 0:NC_], U_f[:C, :C], flat(b_.R), start=True, stop=True),
                  reads=[c_r, r["R"]], writes=[k1r])
            tk.op("pe", lambda e: e.matmul(k2[:C, 0:NC_], MS_f[:C, :C], flat(b_.gB), start=True, stop=True),
                  reads=[c_r, r["gB"]], writes=[k2r])
            tk.op("pe", lambda e: e.matmul(k3[:, 0:NC_], ONE_f[:C, :], flat(b_.gB), start=True, stop=True),
                  reads=[c_r, r["gB"]], writes=[k3r])
            tk.op("pe", lambda e: e.matmul(k4[:C, 0:nb], U_f[:C, :C], gmat, start=True, stop=True),
                  reads=[c_r, gt_r], writes=[k4r])
            tk.op("pe", lambda e: e.matmul(k4[:, 16:16 + nb], ONE_f[:C, :], gmat, start=True, stop=True),
                  reads=[c_r, gt_r], writes=[k4r])
            tk.op("act", lambda e: e.activation(out=cv(b_.E1), in_=cvp(k1), func=AF.Exp), reads=[k1r], writes=[r["E1"]])
            tk.op("act", lambda e: e.activation(out=cv(b_.E2), in_=cvp(k2), func=AF.Exp), reads=[k2r], writes=[r["E2"]])
            tk.op("act", lambda e: e.activation(out=cv(b_.egbc, 128), in_=cvp(k3, 128), func=AF.Exp),
                  reads=[k3r], writes=[r["eg"]])
            tk.op("act", lambda e: e.activation(out=b_.sc[:C, :nb, 0:1], in_=k4[:C, 0:nb].unsqueeze(2), func=AF.Exp),
                  reads=[k4r], writes=[r["sc"]])
            tk.op("act", lambda e: e.activation(out=b_.gl[:, :nb, 0:1], in_=k4[:, 16:16 + nb].unsqueeze(2), func=AF.Exp),
                  reads=[k4r], writes=[r["sc"]])
            tk.op("dve", lambda e: e.tensor_copy(out=b_.sc[:C, :nb, 1:2], in_=cv(b_.E2)[:, :, C - 1:C]),
                  reads=[r["E2"]], writes=[r["sc"]])
            tk.op("pool", lambda e: e.tensor_tensor(out=cv(b_.E1), in0=cv(b_.E1),
                                                    in1=cb[:C, 256:256 + C].unsqueeze(1).to_broadcast([C, nb, C]),
                                                    op=ALU.mult), reads=[c_r], writes=[r["E1"]])
            tk.op("pool", lambda e: e.tensor_tensor(out=cv(b_.E2), in0=cv(b_.E2),
                                                    in1=cb[:C, 128:128 + C].unsqueeze(1).to_broadcast([C, nb, C]),
                                                    op=ALU.mult), reads=[c_r, r["sc"]], writes=[r["E2"]])
            tk.op("dve", lambda e: e.tensor_tensor(out=b_.sc[:C, :nb, 2:3], in0=b_.sc[:C, :nb, 0:1], in1=bmat.unsqueeze(2),
                                                   op=ALU.mult), reads=[gt_r], writes=[r["sc"]])
            yield
            k5, k5r = bank()
            k6, k6r = bank()
            k7, k7r = bank()
            k8, k8r = bank()
            for j in range(nb):
                cs = slice(j * 128, j * 128 + C)
                ts_ = slice(o0 + j * C, o0 + (j + 1) * C)
                tk.op("pe", lambda e, cs=cs, ts_=ts_: e.matmul(k5[:C, cs], kn[:, ts_], kn[:, ts_], start=True, stop=True),
                      reads=[kn_r], writes=[k5r])
                tk.op("pe", lambda e, cs=cs, ts_=ts_: e.matmul(k6[:C, cs], kn[:, ts_], qn[:, ts_], start=True, stop=True),
                      reads=[kn_r, qn_r], writes=[k6r])
                tk.op("pe", lambda e, j=j, ts_=ts_: e.matmul(k7[:C, j * 128:(j + 1) * 128], kn[:, ts_], I_b,
                                                             start=True, stop=True), reads=[kn_r, c_r], writes=[k7r])
                tk.op("pe", lambda e, j=j, ts_=ts_: e.matmul(k8[:C, j * 128:(j + 1) * 128], vv[:, ts_], I_b,
                                                             start=True, stop=True), reads=[vv_r, c_r], writes=[k8r])
            tk.op("dve", lambda e: e.tensor_tensor(out=cv(b_.E1), in0=cv(b_.E1), in1=bmat.unsqueeze(2).to_broadcast([C, nb, C]),
                                                   op=ALU.mult), reads=[gt_r], writes=[r["E1"]])
            tk.op("dve", lambda e: e.tensor_tensor(out=b_.Lp[:C, :nb, :C], in0=v3(k5[:C, :], C, nb), in1=cv(b_.E1),
                                                   op=ALU.mult), reads=[k5r, r["E1"]], writes=[r["Lp"]])
            k7v = k7[:C, 0:nb * 128].rearrange("p (j c) -> p j c", c=128)
            k8v = k8[:C, 0:nb * 128].rearrange("p (j c) -> p j c", c=128)
            tk.op("dve", lambda e: e.tensor_tensor(out=b_.kdec[:C, :nb, :], in0=k7v,
                                                   in1=b_.sc[:C, :nb, 1:2].to_broadcast([C, nb, 128]), op=ALU.mult),
                  reads=[k7r, r["sc"]], writes=[r["kdec"]])
            tk.op("dve", lambda e: e.tensor_tensor(out=b_.kbg[:C, :nb, :], in0=k7v,
                                                   in1=b_.sc[:C, :nb, 2:3].to_broadcast([C, nb, 128]), op=ALU.mult),
                  reads=[k7r, r["sc"]], writes=[r["kbg"]])
            tk.op("dve", lambda e: e.tensor_tensor(out=b_.vb[:C, :nb, :], in0=k8v,
                                                   in1=bmat.unsqueeze(2).to_broadcast([C, nb, 128]), op=ALU.mult),
                  reads=[k8r, gt_r], writes=[r["vb"]])
            tk.op("dve", lambda e: e.tensor_tensor(out=b_.qkt[:C, :nb, :C], in0=v3(k6[:C, :], C, nb),
                                                   in1=cv(b_.E2), op=ALU.mult),
                  reads=[k6r, r["E2"]], writes=[r["qkt"]])
            tk.op("pool", lambda e: e.tensor_tensor(
                out=b_.qd[:, :nb, :C], in0=qn[:, o0:o0 + nb * C].rearrange("p (j c) -> p j c", c=C),
                in1=cv(b_.egbc, 128), op=ALU.mult), reads=[qn_r, r["eg"]], writes=[r["qd"]])
            yield
            def bc(off):
                return gm[:C, off:off + C].unsqueeze(1).to_broadcast([C, nb, C])

            def mm4(lhs, rhs, lr, rr, PO=C, wl=C, wr=C):
                warm()
                kx, kxr = bank()
                for j in range(nb):
                    tk.op("pe", lambda e, j=j: e.matmul(kx[:PO, j * 128:j * 128 + wr], lhs[:C, j, :wl],
                                                        rhs[:C, j, :wr], start=True, stop=True),
                          reads=[lr, rr], writes=[kxr])
                return kx, kxr

            kt, ktr = bank()
            for j in range(nb):
                cs = slice(j * 128, j * 128 + C)
                tk.op("pe", lambda e, j=j, cs=cs: e.matmul(kt[:C, cs], b_.Lp[:C, j, :C], I_x[:C, :C], start=True, stop=True),
                      reads=[r["Lp"], c_r], writes=[ktr])
            tk.op("act", lambda e: e.copy(out=b_.LT[:C, :nb, :C], in_=v3(kt[:C, :], C, nb)), reads=[ktr], writes=[r["LT"]])
            tk.op("dve", lambda e: e.scalar_tensor_tensor(out=b_.Ao[:C, :nb, :C], in0=b_.Lp[:C, :nb, :C], scalar=-1.0,
                                                          in1=bc(0), op0=ALU.mult, op1=ALU.mult),
                  reads=[r["Lp"], gm_r], writes=[r["Ao"]])
            tk.op("dve", lambda e: e.scalar_tensor_tensor(out=b_.AoT[:C, :nb, :C], in0=b_.LT[:C, :nb, :C], scalar=-1.0,
                                                          in1=bc(768), op0=ALU.mult, op1=ALU.mult),
                  reads=[r["LT"], gm_r], writes=[r["AoT"]])
            tk.op("dve", lambda e: e.tensor_tensor(out=b_.X[:C, :nb, :C], in0=b_.Ao[:C, :nb, :C],
                                                   in1=v3(i4x[:C, :], C, nb), op=ALU.add),
                  reads=[r["Ao"], gm_r], writes=[r["X"]])
            tk.op("dve", lambda e: e.tensor_tensor(out=b_.XT[:C, :nb, :C], in0=b_.AoT[:C, :nb, :C],
                                                   in1=v3(i4x[:C, :], C, nb), op=ALU.add),
                  reads=[r["AoT"], gm_r], writes=[r["XT"]])
            kx, kxr = mm4(b_.AoT, b_.Ao, r["AoT"], r["Ao"])
            tk.op("act", lambda e: e.copy(out=b_.W1[:C, :nb, :C], in_=v3(kx[:C, :], C, nb)), reads=[kxr], writes=[r["W1"]])
            yield
            kxa, kxar = mm4(b_.XT, b_.W1, r["XT"], r["W1"])
            kxb, kxbr = mm4(b_.W1, b_.XT, r["W1"], r["XT"])
            tk.op("dve", lambda e: e.tensor_tensor(out=b_.X[:C, :nb, :C], in0=v3(kxa[:C, :], C, nb),
                                                   in1=b_.X[:C, :nb, :C], op=ALU.add),
                  reads=[kxar, r["X"]], writes=[r["X"]])
            tk.op("dve", lambda e: e.tensor_tensor(out=b_.XT[:C, :nb, :C], in0=v3(kxb[:C, :], C, nb),
                                                   in1=b_.XT[:C, :nb, :C], op=ALU.add),
                  reads=[kxbr, r["XT"]], writes=[r["XT"]])
            yield
            levels = [b for b in (4, 8, 16, 32, 64) if 2 * b <= C]
            for li, b in enumerate(levels):
                last = (li == len(levels) - 1)
                off = 128 + li * 128
                kx, kxr = bank()
                for j in range(nb):
                    cs = slice(j * 128, j * 128 + C)
                    tk.op("pe", lambda e, j=j, cs=cs: e.matmul(kx[:C, cs], b_.LT[:C, j, :C], b_.X[:C, j, :C],
                                                               start=True, stop=False), reads=[r["LT"], r["X"]], writes=[kxr])
                    tk.op("pe", lambda e, j=j, cs=cs: e.matmul(kx[:C, cs], I_b[:C, :C], I_b[:C, :C],
                                                               start=False, stop=True), reads=[c_r], writes=[kxr])
                tk.op("dve", lambda e, kx=kx, off=off: e.tensor_tensor(out=b_.W1[:C, :nb, :C], in0=v3(kx[:C, :], C, nb),
                                                                       in1=bc(off), op=ALU.mult),
                      reads=[kxr, gm_r], writes=[r["W1"]])
                yield
                kb_, kbr = mm4(b_.W1, b_.XT, r["W1"], r["XT"])
                if not last:
                    ka_, kar = mm4(b_.XT, b_.W1, r["XT"], r["W1"])
                    tk.op("act", lambda e, ka_=ka_: e.copy(out=b_.X[:C, :nb, :C], in_=v3(ka_[:C, :], C, nb)),
                          reads=[kar], writes=[r["X"]])
                tk.op("dve", lambda e, kb_=kb_: e.tensor_copy(out=b_.XT[:C, :nb, :C], in_=v3(kb_[:C, :], C, nb)),
                      reads=[kbr], writes=[r["XT"]])
                yield
            TTc, rTT = b_.XT, r["XT"]
            ku, kur = bank()
            kw, kwr = bank()
            for j in range(nb):
                tk.op("pe", lambda e, j=j: e.matmul(ku[:C, j * 128:(j + 1) * 128], TTc[:C, j, :C], b_.vb[:C, j, :],
                                                    start=True, stop=True), reads=[rTT, r["vb"]], writes=[kur])
                tk.op("pe", lambda e, j=j: e.matmul(kw[:C, j * 128:(j + 1) * 128], TTc[:C, j, :C], b_.kbg[:C, j, :],
                                                    start=True, stop=True), reads=[rTT, r["kbg"]], writes=[kwr])
            tk.op("act", lambda e: e.copy(out=b_.utok[:C, :nb, :], in_=ku[:C, 0:nb * 128].rearrange("p (j c) -> p j c", c=128)),
                  reads=[kur], writes=[r["utok"]])
            tk.op("dve", lambda e: e.tensor_copy(out=b_.wtok[:C, :nb, :], in_=kw[:C, 0:nb * 128].rearrange("p (j c) -> p j c", c=128)),
                  reads=[kwr], writes=[r["wtok"]])
            yield
            kk, kkr = bank()
            kq, kqr = bank()
            for j in range(nb):
                tk.op("pe", lambda e, j=j: e.matmul(kk[:, j * 128:(j + 1) * 128], b_.wtok[:C, j, :], b_.kdec[:C, j, :],
                                                    start=True, stop=True), reads=[r["wtok"], r["kdec"]], writes=[kkr])
                tk.op("pe", lambda e, j=j: e.matmul(kq[:, j * 128:j * 128 + C], b_.wtok[:C, j, :], b_.qkt[:C, j, :C],
                                                    start=True, stop=True), reads=[r["wtok"], r["qkt"]], writes=[kqr])
            tk.op("act", lambda e: e.activation(out=b_.nWk[:, :nb, :], in_=kk[:, 0:nb * 128].rearrange("p (j c) -> p j c", c=128),
                                                func=AF.Copy, scale=-1.0), reads=[kkr], writes=[r["nWk"]])
            tk.op("dve", lambda e: e.tensor_tensor(out=b_.qd[:, :nb, :C], in0=b_.qd[:, :nb, :C], in1=v3(kq[:, :], C, nb),
                                                   op=ALU.subtract), reads=[kqr, r["qd"]], writes=[r["qd"]])
            yield
        def gdn_scan(b_, hh, C, nb, o0, S_f, S_regs, zs, zs_r, carry):
            r = b_.r
            Sb, Sb_r = None, None
            for j in range(nb):
                Sf, Sreg = S_f[j], S_regs[j]
                if Sb is None or not carry:
                    Sb, Sb_r = Sb_s.get()
                    tk.op("dve", lambda e, Sb=Sb, Sf=Sf: e.tensor_copy(out=Sb[:, :], in_=Sf), reads=[Sreg], writes=[Sb_r])
                cs = slice(o0 + j * C, o0 + (j + 1) * C)
                ks_, ksr = bank()
                tk.op("pe", lambda e, j=j: e.matmul(ks_[:, 0:128], b_.kdec[:C, j, :], b_.utok[:C, j, :], start=True, stop=False),
                      reads=[r["kdec"], r["utok"]], writes=[ksr])
                tk.op("pe", lambda e, j=j, Sb=Sb: e.matmul(ks_[:, 0:128], b_.nWk[:, j, :], Sb[:, :], start=False, stop=True),
                      reads=[r["nWk"], Sb_r], writes=[ksr])
                ko, kor = bank()
                tk.op("pe", lambda e, j=j: e.matmul(ko[:C, 0:128], b_.qkt[:C, j, :C], b_.utok[:C, j, :], start=True, stop=False),
                      reads=[r["qkt"], r["utok"]], writes=[kor])
                tk.op("pe", lambda e, j=j, Sb=Sb: e.matmul(ko[:C, 0:128], b_.qd[:, j, :C], Sb[:, :], start=False, stop=True),
                      reads=[r["qd"], Sb_r], writes=[kor])
                tk.op("dve", lambda e, j=j, Sf=Sf: e.scalar_tensor_tensor(out=Sf, in0=Sf, scalar=b_.gl[:, j, 0:1],
                                                                          in1=ks_[:, 0:128], op0=ALU.mult, op1=ALU.add),
                      reads=[ksr, r["sc"], Sreg], writes=[Sreg])
                if carry and j + 1 < nb:
                    Sb, Sb_r = Sb_s.get()
                    tk.op("dve", lambda e, Sb=Sb, Sf=Sf: e.tensor_copy(out=Sb[:, :], in_=Sf), reads=[Sreg], writes=[Sb_r])
                jk, jk_r = jk_s.get()
                ss, ss_r = ss_s.get()
                tk.op("act", lambda e, jk=jk, ss=ss: e.activation(out=jk[:C, :], in_=ko[:C, 0:128], func=AF.Square,
                                                                  accum_out=ss[:C, 0:1]), reads=[kor], writes=[jk_r, ss_r])
                tk.op("act", lambda e, ss=ss: e.activation(out=ss[:C, 1:2], in_=ss[:C, 0:1], func=AF.Ln,
                                                           bias=epst[:C, 0:1], scale=1.0 / 128), reads=[c_r], writes=[ss_r])
                tk.op("act", lambda e, ss=ss: e.activation(out=ss[:C, 2:3], in_=ss[:C, 1:2], func=AF.Exp, scale=-0.5),
                      writes=[ss_r])
                on, on_r = on_s.get()
                tk.op("act", lambda e, on=on, ss=ss: e.activation(out=on[:C, :], in_=ko[:C, 0:128], func=AF.Copy,
                                                                  scale=ss[:C, 2:3]), reads=[kor, ss_r], writes=[on_r])
                kp, kpr = bank()
                tk.op("pe", lambda e, on=on: e.matmul(kp[:, 0:C], on[:C, :], I_b[:C, :C], start=True, stop=True),
                      reads=[on_r, c_r], writes=[kpr])
                og, og_r = og_s.get()
                tk.op("act", lambda e, og=og: e.activation(out=og[:, 0:C], in_=kp[:, 0:C], func=AF.Copy, scale=gout[:, 0:1]),
                      reads=[kpr, gp_r], writes=[og_r])
                tk.op("pool", lambda e, og=og, cs=cs: e.tensor_tensor(out=oT[:, hh, cs], in0=og[:, 0:C], in1=zs[:, cs],
                                                                     op=ALU.mult), reads=[og_r, zs_r], writes=[oT_r[hh]])
                yield

        heads = {}

        def bulk(hh):
            if hh + 1 < NH:
                load_wsl(hh + 1)
            ws_, ws_r = wsl[hh % 2], wsl_r[hh % 2]
            outs = {}
            for w_ in range(3):
                cidx = w_ * 8 + hh
                tk.op("dve", lambda e: e.tensor_copy(out=raw[:, 0:3], in_=craw[:, cidx, :]),
                      reads=[craw_r[cidx]], writes=[raw_r])
                for (ts, n) in tl:
                    o = ts - t0
                    pb_, pbr = bank()
                    for k in range(KC):
                        tk.op("pe", lambda e, k=k: e.matmul(pb_[:, :n], ws_[:, k, w_, :], u[:, k, o:o + n],
                                                            start=(k == 0), stop=(k == KC - 1)),
                              reads=[ws_r, u_r], writes=[pbr])
                    if ts < SEQ:
                        evac(raw[:, 3 + o:3 + o + n], pb_[:, :n], [pbr], [raw_r])
                    else:
                        tk.op("act", lambda e: e.copy(out=raws[:, :, 3:7], in_=pb_[:, 0:16].rearrange("p (s t) -> p s t", s=4)),
                              reads=[pbr], writes=[raws_r])
                        tk.op("dve", lambda e: e.tensor_copy(out=raws[:, :, 0:3], in_=cvin[:, cidx, :, :]),
                              reads=[cvin_r], writes=[raws_r])
                        tk.op("dve", lambda e: e.tensor_copy(out=cvs[:, cidx, :, :], in_=raws[:, :, 4:7]),
                              reads=[raws_r], writes=[cvs_r])
                    yield
                tk.op("dve", lambda e: e.tensor_copy(out=craw[:, cidx, :], in_=raw[:, THp:THp + 3]),
                      reads=[raw_r], writes=[craw_r[cidx]])
                tk.op("dve", lambda e: e.tensor_scalar(out=acc[:, 0:THp], in0=raw[:, 0:THp], scalar1=cw[:, cidx, 0:1],
                                                       scalar2=None, op0=ALU.mult), reads=[raw_r, gp_r], writes=[acc_r])
                for j in range(1, 4):
                    tk.op("dve", lambda e, j=j: e.scalar_tensor_tensor(
                        out=acc[:, 0:THp], in0=raw[:, j:j + THp], scalar=cw[:, cidx, j:j + 1], in1=acc[:, 0:THp],
                        op0=ALU.mult, op1=ALU.add), reads=[raw_r, gp_r], writes=[acc_r])
                if has_s:
                    accs = acc[:, THp:THp + 16].rearrange("p (s t) -> p s t", s=4)
                    tk.op("dve", lambda e: e.tensor_scalar(out=accs, in0=raws[:, :, 0:4], scalar1=cw[:, cidx, 0:1],
                                                           scalar2=None, op0=ALU.mult), reads=[raws_r, gp_r], writes=[acc_r])
                    for j in range(1, 4):
                        tk.op("dve", lambda e, j=j: e.scalar_tensor_tensor(
                            out=accs, in0=raws[:, :, j:j + 4], scalar=cw[:, cidx, j:j + 1], in1=accs,
                            op0=ALU.mult, op1=ALU.add), reads=[raws_r, gp_r], writes=[acc_r])
                yield
                if w_ == 2:
                    vv, vv_r = vv_s.get()
                    tk.op("act", lambda e: e.activation(out=vv[:, :], in_=acc[:, :], func=AF.Silu), reads=[acc_r], writes=[vv_r])
                    outs[2] = (vv, vv_r)
                else:
                    tk.op("act", lambda e: e.activation(out=acc[:, :], in_=acc[:, :], func=AF.Silu), reads=[acc_r], writes=[acc_r])
                    dst, dst_r = (qn_s if w_ == 0 else kn_s).get()
                    outs[w_] = (dst, dst_r)
                    for (ts, n) in tl:
                        o = ts - t0
                        sqb, sqb_r = sqb_s.get()
                        tk.op("act", lambda e: e.activation(out=sqb[:, :n], in_=acc[:, o:o + n], func=AF.Square),
                              reads=[acc_r], writes=[sqb_r])
                        pb_, pbr = bank()
                        tk.op("pe", lambda e: e.matmul(pb_[:, :n], ONE_b, sqb[:, :n], start=True, stop=True),
                              reads=[sqb_r, c_r], writes=[pbr])
                        ta, tar = t1_s.get()
                        tb, tbr = t2_s.get()
                        rstd_from_ss(pb_[:, :n], pbr, 128, n, 1.0, ta, tar, tb, tbr)
                        if w_ == 0:
                            tk.op("dve", lambda e: e.scalar_tensor_tensor(
                                out=dst[:, o:o + n], in0=acc[:, o:o + n], scalar=128.0 ** -0.5, in1=tb[:, :n],
                                op0=ALU.mult, op1=ALU.mult), reads=[acc_r, tbr], writes=[dst_r])
                        else:
                            tk.op("dve", lambda e: e.tensor_tensor(out=dst[:, o:o + n], in0=acc[:, o:o + n], in1=tb[:, :n],
                                                                   op=ALU.mult), reads=[acc_r, tbr], writes=[dst_r])
                        yield
            zs, zs_r = zs_s.get()
            for (ts, n) in tl:
                o = ts - t0
                pb_, pbr = bank()
                for k in range(KC):
                    tk.op("pe", lambda e, k=k: e.matmul(pb_[:, :n], ws_[:, k, 3, :], u[:, k, o:o + n],
                                                        start=(k == 0), stop=(k == KC - 1)),
                          reads=[ws_r, u_r], writes=[pbr])
                tk.op("act", lambda e: e.activation(out=zs[:, o:o + n], in_=pb_[:, :n], func=AF.Silu),
                      reads=[pbr], writes=[zs_r])
                yield
            heads[hh] = (outs[0], outs[1], outs[2], (zs, zs_r))
            yield

        def seq_gens(*gs):
            for g_ in gs:
                yield from g_

        def head_pres(hh):
            (qn, qn_r), (kn, kn_r), (vv, vv_r), (zs, zs_r) = heads[hh]
            pres, scans_p, scan_s = [], [], None
            for bi, b0 in enumerate(range(0, NB, 4)):
                nb = min(4, NB - b0)
                b_ = bbs["p%d" % (bi % 2)]
                pres.append(gdn_pre(b_, hh, 128, nb, b0 * 128, gtok[:, b0:b0 + nb, hh],
                                    [btok[:, b0 + j, hh:hh + 1] for j in range(nb)], qn, qn_r, kn, kn_r, vv, vv_r,
                                    bmat=btok[:, b0:b0 + nb, hh]))
                scans_p.append(gdn_scan(b_, hh, 128, nb, b0 * 128, [Sst[:, hh, :]] * nb, [Sst_r[hh]] * nb, zs, zs_r, True))
            if has_s:
                Ssm, Ssm_r = Ssm_s.get()
                ch_ = Ssm_s.chan()
                tk.dma("sp", ch_, lambda e: e.dma_start(out=Ssm[:], in_=st_in[:, hh, :, :].rearrange("s d e -> d s e")),
                       writes=Ssm_r)
                pres.append(gdn_pre(bbs["s"], hh, 4, 4, THp, gts[:4, 0:4, hh],
                                    [bts[:4, j, hh:hh + 1] for j in range(4)], qn, qn_r, kn, kn_r, vv, vv_r,
                                    bmat=bts[:4, 0:4, hh]))

                def sample_scan(Ssm=Ssm, Ssm_r=Ssm_r, ch_=ch_):
                    yield from gdn_scan(bbs["s"], hh, 4, 4, THp, [Ssm[:, j, :] for j in range(4)],
                                        [Ssm_r[j] for j in range(4)], zs, zs_r, False)
                    tk.dma("sp", ch_, lambda e: e.dma_start(out=st_s[:, hh, :, :].rearrange("s d e -> d s e"), in_=Ssm[:]),
                           reads=Ssm_r)
                    yield
                scan_s = sample_scan()
            assert len(scans_p) <= 2
            return pres, scans_p, scan_s

        STAG = int(_os.environ.get("STAG", "0"))

        def run_il(gens, stagger=0):
            gens = list(gens)
            for gi_, g_ in enumerate(list(gens)):
                for _ in range(gi_ * stagger):
                    try:
                        next(g_)
                    except StopIteration:
                        if g_ in gens:
                            gens.remove(g_)
            while gens:
                for g_ in list(gens):
                    try:
                        next(g_)
                    except StopIteration:
                        gens.remove(g_)

        load_wsl(0)
        run_il([bulk(0)])
        for hh in range(NH):
            pres, scans_p, scan_s = head_pres(hh)
            gl_ = list(pres)
            if hh + 1 < NH:
                gl_.append(bulk(hh + 1))
            run_il(gl_, stagger=STAG)
            sl_ = [seq_gens(*scans_p)]
            if scan_s is not None:
                sl_.append(scan_s)
            run_il(sl_)
        if has_s:
            och_f = tk.chan()
            for hh in range(NH):
                tk.dma("sp", och_f, lambda e, hh=hh: e.dma_start(out=st_p[hh], in_=Sst[:, hh, :]), reads=[Sst_r[hh]],
                       cont=(hh > 0))
            tk.dma("sp", och_f, lambda e: e.dma_start(out=cv_p.rearrange("(c p) j -> p c j", p=128), in_=craw[:]),
                   reads=craw_r, cont=True)
            tk.dma("sp", och_f, lambda e: e.dma_start(out=cv_s.rearrange("(c p) (s j) -> p c s j", p=128, j=3), in_=cvs[:]),
                   reads=[cvs_r], cont=True)
        if NWARM:
            release(warm_l)
        tk.barrier()
        p1.close()
        p4 = contextlib.ExitStack()
        wo = sb(p4, "g_wo", [128, KC, D], BF16)
        wo_r = Reg("g_wo")
        for k0 in range(0, KC, 4):
            tk.dma("pool", wch[2], lambda e, k0=k0: e.dma_start(
                out=wo[:, k0:k0 + 4, :], in_=d_wo.rearrange("(k p) o -> p k o", p=128)[:, k0:k0 + 4, :]),
                writes=[wo_r], cont=(k0 > 0))
        y_s = Scratch(p4, "g_y", [128, KC, 512], F32, 2, nreg=KC)
        sq_s = Scratch(p4, "g_sq4", [128, KC, 512], BF16, 1)
        t1_s = Scratch(p4, "g_t14", [128, 512], F32, 2)
        t2_s = Scratch(p4, "g_t24", [128, 512], F32, 2)
        for (ts, n) in ftiles(t0, t1):
            o = ts - t0
            y, y_r = y_s.get()
            for oc in range(KC):
                pb_, pbr = bank()
                for k in range(NH):
                    tk.op("pe", lambda e, k=k: e.matmul(pb_[:, :n], wo[:, k, oc * 128:(oc + 1) * 128], oT[:, k, o:o + n],
                                                        start=(k == 0), stop=(k == NH - 1)),
                          reads=[wo_r, oT_r[k]], writes=[pbr])
                evac(y[:, oc, :n], pb_[:, :n], [pbr], [y_r[oc]])
            postnorm_add(1, 1, y, y_r, 0, ts, n, sq_s, t1_s, t2_s)
        tk.barrier()
        p4.close()
        ph.close()

    halves = [(0, HALF), (HALF, T)]
    if "mla" in stages:
        for (t0, t1) in halves:
            mla_phase(t0, t1)
            if "ffn0" in stages:
                ffn_phase(0, t0, t1)
    elif "ffn0" in stages:
        for (t0, t1) in halves:
            ffn_phase(0, t0, t1)
    tk.barrier()
    l0s.close()
    if "gdn" in stages:
        gdn_init()
        for (t0, t1) in halves:
            gdn_phase(t0, t1)
            if "ffn1" in stages:
                ffn_phase(1, t0, t1)
    elif "ffn1" in stages:
        for (t0, t1) in halves:
            ffn_phase(1, t0, t1)

    for k in range(KC):
        tk.dma("sp", next_och(), lambda e, k=k: e.dma_start(out=yT[k * 128:(k + 1) * 128, :], in_=h[:, k, :]),
               reads=h_r[k])
    tk.final()
    es.close()
    return nc


_NC_CACHE = {}


def prep_core_inputs(c, inp, SEQ, NPG):
    PAST = NPG * 128
    T = SEQ + 16
    x_p = inp["x_prompt"][c]
    x_s = inp["x_sample"][4 * c:4 * c + 4].reshape(16, D)
    xT = np.ascontiguousarray(np.concatenate([x_p, x_s], axis=0).T)
    consts, rope = host_consts(SEQ, PAST)
    cm = inp["cache_mla"][0]
    d = {
        "xT": xT,
        "cache": cm.reshape(cm.shape[0] * 16, 8 * ROW),
        "ptab": np.ascontiguousarray(inp["page_table"][4 * c:4 * c + 4].T.astype(np.int32)),
        "st_in": np.ascontiguousarray(inp["state_dn"][0, 4 * c:4 * c + 4]),
        "cv_in": np.ascontiguousarray(inp["state_dn_conv"][0, 4 * c:4 * c + 4].transpose(2, 0, 1)).reshape(3072, 12),
        "normw": np.ascontiguousarray(inp["norm_w"].reshape(2, 4, 8, 128).transpose(3, 0, 1, 2).reshape(128, 64)),
        "consts": consts,
        "gmask": host_gmask(),
        "rope": rope,
        "m_win": inp["mla_w_in"][0],
        "m_gq": np.ascontiguousarray(inp["mla_g_q"][0].reshape(3, 128).T),
        "m_gkv": np.ascontiguousarray(inp["mla_g_kv"][0].reshape(2, 128).T),
        "m_wuq": inp["mla_w_uq"][0],
        "m_wuk": inp["mla_w_uk"][0],
        "m_wukT": np.ascontiguousarray(inp["mla_w_uk"][0].transpose(0, 2, 1)),
        "m_wuv": inp["mla_w_uv"][0],
        "m_wo": inp["mla_w_o"][0],
        "d_win": inp["dn_w_in"][0],
        "d_cw": np.ascontiguousarray(inp["dn_conv_w"][0].T.reshape(24, 128, 4).transpose(1, 0, 2)),
        "d_alog": inp["dn_a_log"],
        "d_dtb": inp["dn_dt_bias"],
        "d_gout": np.ascontiguousarray(inp["dn_g_out"][0].reshape(128, 1)),
        "d_wo": inp["dn_w_o"][0],
        "f_win": inp["ffn_w_in"],
        "f_wout": inp["ffn_w_out"],
    }
    return {k: np.ascontiguousarray(np.asarray(v)) for k, v in d.items()}


def kernel(**inputs):
    inp = {k: np.asarray(v) for k, v in inputs.items()}
    B, SEQ, _ = inp["x_prompt"].shape
    NPG = inp["page_table"].shape[1]
    NPOOL = inp["cache_mla"].shape[1]
    key = (SEQ, NPG, NPOOL)
    nc = build(SEQ, NPG, NPOOL)
    in_maps = [prep_core_inputs(c, inp, SEQ, NPG) for c in range(8)]
    res = run_bass_kernel_spmd(nc, in_maps, core_ids=list(range(8))).results
    y_p = np.stack([res[c]["yT"][:, :SEQ].T for c in range(8)])
    y_s = np.concatenate([res[c]["yT"][:, SEQ:].T.reshape(4, 4, D) for c in range(8)])
    r_p = np.stack([res[c]["rowsT"][:, :SEQ].T for c in range(8)])[None]
    r_s = np.concatenate([res[c]["rowsT"][:, SEQ:].T.reshape(4, 4, ROW) for c in range(8)])[None]
    s_p = np.stack([res[c]["st_p"] for c in range(8)])[None]
    s_s = np.concatenate([res[c]["st_s"] for c in range(8)])[None]
    c_p = np.stack([res[c]["cv_p"].T for c in range(8)])[None]
    c_s = np.concatenate([res[c]["cv_s"].reshape(3072, 4, 3).transpose(1, 2, 0) for c in range(8)])[None]
    f = lambda a: np.ascontiguousarray(a.astype(np.float32))
    return (f(y_p), f(y_s), f(r_p), f(r_s), f(s_p), f(s_s), f(c_p), f(c_s))
```

```python
import contextlib
import math
import numpy as np
import concourse.bass as bass
import concourse.mybir as mybir
from concourse.bass_utils import run_bass_kernel_spmd

F32 = mybir.dt.float32
BF16 = mybir.dt.bfloat16
I32 = mybir.dt.int32
AF = mybir.ActivationFunctionType
ALU = mybir.AluOpType

D = 1024
KC = 8
NH = 8
QLORA, KVLORA, ROPE = 384, 256, 64
ROW = KVLORA + ROPE
DFF = 2816
FC = DFF // 128
NEG = -30000.0
MLA_SCALE = (128 + 64) ** -0.5


class Reg:
    __slots__ = ("name", "w", "rd", "excl")

    def __init__(self, name="", excl=False):
        self.name = name
        self.w = None
        self.rd = {}
        self.excl = excl


class Trk:
    def __init__(self, nc, es):
        self.nc = nc
        self.es = es
        self.eng = {"pe": nc.tensor, "act": nc.scalar, "dve": nc.vector, "pool": nc.gpsimd, "sp": nc.sync}
        self.semh = {}
        self.cnt = {}
        self.seen = {k: {} for k in self.eng}
        for k in self.eng:
            self.semh[k] = es.enter_context(nc.semaphore("s_" + k))
            self.cnt[k] = 0
        self.same_sync = {"pool": True, "act": True, "dve": True}
        self.nchan = 0

    def chan(self, name=None):
        self.nchan += 1
        k = "c%d" % self.nchan
        self.semh[k] = self.es.enter_context(self.nc.semaphore("d_%d" % self.nchan))
        self.cnt[k] = 0
        return k

    def _waits(self, e, reads, writes, skip=None):
        need = {}
        for r in reads:
            if r.w is not None and need.get(r.w[0], 0) < r.w[1]:
                need[r.w[0]] = r.w[1]
            if r.excl:
                for k, c in r.rd.items():
                    if k != e and need.get(k, 0) < c:
                        need[k] = c
        for w in writes:
            if w.w is not None and need.get(w.w[0], 0) < w.w[1]:
                need[w.w[0]] = w.w[1]
            for k, c in w.rd.items():
                if need.get(k, 0) < c:
                    need[k] = c
        for k, c in need.items():
            if k == skip:
                continue
            if k == e and not self.same_sync.get(e):
                continue
            if self.seen[e].get(k, 0) >= c:
                continue
            self.eng[e].wait_ge(self.semh[k], c)
            self.seen[e][k] = c

    def op(self, e, fn, reads=(), writes=()):
        self._waits(e, reads, writes)
        ins = fn(self.eng[e])
        self.cnt[e] += 1
        ins.then_inc(self.semh[e], 1)
        c = self.cnt[e]
        for r in reads:
            r.rd[e] = c
        for w in writes:
            w.w = (e, c)
            w.rd = {}
        return ins

    def dma(self, q, ch, fn, reads=(), writes=(), cont=False):
        if not cont and self.cnt[ch] > self.seen[q].get(ch, 0):
            self.eng[q].wait_ge(self.semh[ch], self.cnt[ch])
            self.seen[q][ch] = self.cnt[ch]
        self._waits(q, reads, writes, skip=ch)
        ins = fn(self.eng[q])
        self.cnt[ch] += 16
        ins.then_inc(self.semh[ch], 16)
        c = self.cnt[ch]
        for r in reads:
            r.rd[ch] = c
        for w in writes:
            w.w = (ch, c)
            w.rd = {}
        return ins

    def barrier(self):
        for e in self.eng:
            for k, c in self.cnt.items():
                if c == 0:
                    continue
                if self.seen[e].get(k, 0) >= c:
                    continue
                self.eng[e].wait_ge(self.semh[k], c)
                self.seen[e][k] = c

    def final(self):
        e = "sp"
        for k, c in self.cnt.items():
            if k == e or c == 0:
                continue
            if self.seen[e].get(k, 0) >= c:
                continue
            self.eng[e].wait_ge(self.semh[k], c)
            self.seen[e][k] = c


def host_consts(SEQ, PAST):
    T = SEQ + 16
    c = np.zeros((128, 1536), np.float32)
    idx = np.arange(128)
    c[:, 0:128] = np.eye(128, dtype=np.float32)
    c[:, 128:256] = (idx[:, None] <= idx[None, :]).astype(np.float32)
    c[:, 256:384] = (idx[:, None] > idx[None, :]).astype(np.float32)
    c[:, 384:512] = np.where(idx[None, :] >= idx[:, None], NEG, 0.0)
    c[:, 512:640] = np.where(idx[None, :] < idx[:, None], NEG, 0.0)
    rot = np.zeros((128, 64), np.float32)
    for m in range(32):
        rot[m + 32, m] = -1.0
        rot[m, m + 32] = 1.0
    c[:, 640:704] = rot
    c[:, 704:832] = 1.0
    dm = np.zeros((128, 32), np.float32)
    for j in range(4):
        for q in range(32):
            dm[j, q] = 1.0 if j <= (q % 4) else 0.0
    c[:, 832:864] = dm
    for j in range(4):
        c[:, 1024 + j * 128:1024 + (j + 1) * 128] = np.eye(128, dtype=np.float32)
    half = 32
    freq = (10000.0 ** (-np.arange(half, dtype=np.float32) / half)).astype(np.float32)
    pos = np.concatenate([np.arange(SEQ), np.tile(PAST + np.arange(4), 4)]).astype(np.float32)
    ang = pos[None, :] * freq[:, None]
    rope = np.zeros((64, 2, T), np.float32)
    rope[0:32, 0] = np.cos(ang)
    rope[32:64, 0] = np.cos(ang)
    rope[0:32, 1] = np.sin(ang)
    rope[32:64, 1] = np.sin(ang)
    return c, rope


def host_gmask():
    idx = np.arange(128)
    c_, s_ = idx[:, None], idx[None, :]
    g = np.zeros((128, 1408), np.float32)
    g[:, 0:128] = (c_ // 4 == s_ // 4) & (c_ > s_)
    for li, b in enumerate([4, 8, 16, 32, 64]):
        m = ((c_ // (2 * b)) == (s_ // (2 * b))) & ((c_ // b) > (s_ // b))
        g[:, 128 + li * 128:128 + (li + 1) * 128] = np.eye(128) - m
    g[:, 768:896] = (c_ // 4 == s_ // 4) & (c_ < s_)
    return g


INV_F32 = False
import os as _os
GCUT = int(_os.environ.get('GCUT', '99'))
GSK = int(_os.environ.get('GSK', '0'))
GOP = int(_os.environ.get('GOP', '0'))


def build(SEQ=2048, NPG=128, NPOOL=5120, stages=("mla", "ffn0", "gdn", "ffn1")):
    T = SEQ + 16
    HALF = SEQ // 2
    PAST = NPG * 128
    nc = bass.Bass("TRN2", target_bir_lowering=False)
    es = contextlib.ExitStack()
    tk = Trk(nc, es)

    def din(name, shape, dt=F32):
        return nc.dram_tensor(name, list(shape), dt, kind="ExternalInput").ap()

    def dout(name, shape, dt=F32):
        return nc.dram_tensor(name, list(shape), dt, kind="ExternalOutput").ap()

    xT = din("xT", [D, T])
    cache = din("cache", [NPOOL * 16, 8 * ROW])
    ptab = din("ptab", [128, 4], I32)
    st_in = din("st_in", [4, NH, 128, 128])
    cv_in = din("cv_in", [3072, 12])
    normw = din("normw", [128, 64])
    consts_d = din("consts", [128, 1536])
    gmask_d = din("gmask", [128, 1408])
    rope_d = din("rope", [64, 2, T])
    m_win = din("m_win", [D, 704])
    m_gq = din("m_gq", [128, 3])
    m_gkv = din("m_gkv", [128, 2])
    m_wuq = din("m_wuq", [QLORA, NH * 192])
    m_wuk = din("m_wuk", [NH, KVLORA, 128])
    m_wukT = din("m_wukT", [NH, 128, KVLORA])
    m_wuv = din("m_wuv", [NH, KVLORA, 128])
    m_wo = din("m_wo", [D, D])
    d_win = din("d_win", [D, 4112])
    d_cw = din("d_cw", [128, 24, 4])
    d_alog = din("d_alog", [1, NH])
    d_dtb = din("d_dtb", [1, NH])
    d_gout = din("d_gout", [128, 1])
    d_wo = din("d_wo", [D, D])
    f_win = din("f_win", [2, D, 2 * DFF])
    f_wout = din("f_wout", [2, DFF, D])

    yT = dout("yT", [D, T])
    rowsT = dout("rowsT", [ROW, T])
    st_p = dout("st_p", [NH, 128, 128])
    st_s = dout("st_s", [4, NH, 128, 128])
    cv_p = dout("cv_p", [3072, 3])
    cv_s = dout("cv_s", [3072, 12])

    uid = [0]

    def sb(stack, name, shape, dt):
        uid[0] += 1
        return stack.enter_context(nc.sbuf_tensor("%s_%d" % (name, uid[0]), list(shape), dt))

    h = sb(es, "h", [128, KC, T], F32)
    h_r = [[Reg("h%d_%d" % (k, j)) for j in range(8)] for k in range(KC)]
    cf = sb(es, "cf", [128, 1536], F32)
    cb = sb(es, "cb", [128, 1536], BF16)
    nw = sb(es, "nw", [128, 64], F32)
    epst = sb(es, "epst", [128, 2], F32)
    c_r = Reg("consts")
    I_f, U_f, MS_f, NEGL_f, NEGQ_f, ROT_f, ONE_f = (cf[:, 0:128], cf[:, 128:256], cf[:, 256:384],
                                                    cf[:, 384:512], cf[:, 512:640], cf[:, 640:704],
                                                    cf[:, 704:832])
    I_b, U_b, ONE_b, DM_b = cb[:, 0:128], cb[:, 128:256], cb[:, 704:832], cb[:, 832:864]
    I4_b = cb[:, 1024:1536]
    psb = [es.enter_context(nc.psum_tensor("ps%d" % i, [128, 512], F32)) for i in range(8)]
    ps_r = [Reg("ps%d" % i, excl=True) for i in range(8)]
    bank_i = [0]

    reserved = set()

    def bank():
        for _ in range(16):
            i = bank_i[0]
            bank_i[0] = (i + 1) % 8
            if i not in reserved:
                return psb[i], ps_r[i]
        raise RuntimeError("no free psum bank")

    def reserve(n):
        out = []
        for _ in range(n):
            for _ in range(16):
                i = bank_i[0]
                bank_i[0] = (i + 1) % 8
                if i not in reserved:
                    break
            reserved.add(i)
            out.append((psb[i], ps_r[i], i))
        return out

    def release(lst):
        for (_, _, i) in lst:
            reserved.discard(i)

    ch_in = tk.chan()
    tk.dma("sp", ch_in, lambda e: e.dma_start(out=cf[:], in_=consts_d), writes=[c_r])
    ch_cb = tk.chan()
    cb_r = Reg("cb")
    tk.dma("pool", ch_cb, lambda e: e.dma_start(out=cb[:], in_=consts_d), writes=[cb_r])
    tk.dma("sp", ch_in, lambda e: e.dma_start(out=nw[:], in_=normw), writes=[c_r])
    tk.op("dve", lambda e: e.memset(epst[:, 0:1], 1e-6), writes=[c_r])
    tk.op("dve", lambda e: e.memset(epst[:, 1:2], 1.0), reads=[cb_r], writes=[c_r])
    ch_x = tk.chan()
    ch_x2 = tk.chan()
    if (SEQ // 2) % 512 == 0:
        for k in range(KC):
            tk.dma("sp", ch_x, lambda e, k=k: e.dma_start(out=h[:, k, 0:SEQ // 2], in_=xT[k * 128:(k + 1) * 128, 0:SEQ // 2]),
                   cont=(k > 0))
        for k in range(KC):
            tk.dma("sp", ch_x2, lambda e, k=k: e.dma_start(out=h[:, k, SEQ // 2:T], in_=xT[k * 128:(k + 1) * 128, SEQ // 2:T]),
                   cont=(k > 0))
        for k in range(KC):
            for j_, r_ in enumerate(h_r[k]):
                if (j_ + 1) * 512 <= SEQ // 2:
                    r_.w = (ch_x, tk.cnt[ch_x])
                else:
                    r_.w = (ch_x2, tk.cnt[ch_x2])
    else:
        for k in range(KC):
            tk.dma("sp", ch_x, lambda e, k=k: e.dma_start(out=h[:, k, :], in_=xT[k * 128:(k + 1) * 128, :]), cont=(k > 0))
        for k in range(KC):
            for r_ in h_r[k]:
                r_.w = (ch_x, tk.cnt[ch_x])

    def hregs(ts, n):
        j0, j1 = ts // 512, (ts + n - 1) // 512
        return [h_r[k][j] for k in range(KC) for j in range(j0, j1 + 1)]

    def hreg_k(k, ts, n):
        j0, j1 = ts // 512, (ts + n - 1) // 512
        return [h_r[k][j] for j in range(j0, j1 + 1)]

    def tiles(t0, t1):
        out = []
        t = t0
        while t < min(t1, SEQ):
            n = min(512, min(t1, SEQ) - t)
            out.append((t, n))
            t += n
        if t1 > SEQ:
            out.append((SEQ, t1 - SEQ))
        return out

    def ftiles(t0, t1):
        th = t1 - t0
        nt = (th + 511) // 512
        base, rem = divmod(th, nt)
        out, t = [], t0
        for i in range(nt):
            n = base + (1 if i < rem else 0)
            out.append((t, n))
            t += n
        return out

    def nwcol(layer, i, k):
        j = (layer * 4 + i) * 8 + k
        return nw[:, j:j + 1]

    def rstd_from_ss(ps_ap, ps_reg, npart, n, scale, tmp, tmp_r, out, out_r):
        tk.op("act", lambda e: e.activation(out=tmp[:npart, :n], in_=ps_ap, func=AF.Ln,
                                            bias=epst[:npart, 0:1], scale=scale),
              reads=[ps_reg, c_r], writes=[tmp_r])
        tk.op("act", lambda e: e.activation(out=out[:npart, :n], in_=tmp[:npart, :n], func=AF.Exp, scale=-0.5),
              reads=[tmp_r], writes=[out_r])

    class Scratch:
        def __init__(self, stack, name, shape, dt, nbuf, nreg=None, chan=False):
            self.t = [sb(stack, "%s%d" % (name, i), shape, dt) for i in range(nbuf)]
            if nreg is None:
                self.r = [Reg("%s%d" % (name, i)) for i in range(nbuf)]
            else:
                self.r = [[Reg("%s%d_%d" % (name, i, j)) for j in range(nreg)] for i in range(nbuf)]
            self.ch = [tk.chan() for _ in range(nbuf)] if chan else None
            self.i = 0
            self.last = 0

        def get(self):
            i = self.i
            self.last = i
            self.i = (i + 1) % len(self.t)
            return self.t[i], self.r[i]

        def chan(self):
            return self.ch[self.last]

    def rmsnorm_pre(layer, wi, ts, n, dst, dst_r, dst_off, sq_s, t1_s, t2_s):
        sq, sq_r = sq_s.get()
        tk.op("act", lambda e: e.activation(out=sq[:, :, :n], in_=h[:, :, ts:ts + n], func=AF.Square),
              reads=hregs(ts, n), writes=[sq_r])
        pb, pr = bank()
        for k in range(KC):
            tk.op("pe", lambda e, k=k: e.matmul(pb[:, :n], ONE_b, sq[:, k, :n], start=(k == 0), stop=(k == KC - 1)),
                  reads=[sq_r, c_r], writes=[pr])
        t1, t1r = t1_s.get()
        t2, t2r = t2_s.get()
        rstd_from_ss(pb[:, :n], pr, 128, n, 1.0 / D, t1, t1r, t2, t2r)
        for k in range(KC):
            tk.op("dve", lambda e, k=k: e.scalar_tensor_tensor(
                out=dst[:, k, dst_off:dst_off + n], in0=h[:, k, ts:ts + n], scalar=nwcol(layer, wi, k),
                in1=t2[:, :n], op0=ALU.mult, op1=ALU.mult),
                reads=hreg_k(k, ts, n) + [t2r, c_r], writes=[dst_r])

    def postnorm_add(layer, wi, y, y_r, yoff, ts, n, sq_s, t1_s, t2_s):
        sq, sq_r = sq_s.get()
        tk.op("act", lambda e: e.activation(out=sq[:, :, :n], in_=y[:, :, yoff:yoff + n], func=AF.Square),
              reads=y_r, writes=[sq_r])
        pb, pr = bank()
        for k in range(KC):
            tk.op("pe", lambda e, k=k: e.matmul(pb[:, :n], ONE_b, sq[:, k, :n], start=(k == 0), stop=(k == KC - 1)),
                  reads=[sq_r, c_r], writes=[pr])
        t1, t1r = t1_s.get()
        t2, t2r = t2_s.get()
        rstd_from_ss(pb[:, :n], pr, 128, n, 1.0 / D, t1, t1r, t2, t2r)
        for k in range(KC):
            tk.op("dve", lambda e, k=k: e.scalar_tensor_tensor(
                out=y[:, k, yoff:yoff + n], in0=y[:, k, yoff:yoff + n], scalar=nwcol(layer, wi, k),
                in1=t2[:, :n], op0=ALU.mult, op1=ALU.mult),
                reads=[t2r, c_r, y_r[k]], writes=[y_r[k]])

        tk.op("dve", lambda e: e.tensor_tensor(out=h[:, :, ts:ts + n], in0=h[:, :, ts:ts + n],
                                               in1=y[:, :, yoff:yoff + n], op=ALU.add),
              reads=list(y_r), writes=hregs(ts, n))

    ev_tog = [0]

    def evac(out_ap, in_ap, reads, writes):
        ev_tog[0] ^= 1
        if ev_tog[0]:
            tk.op("act", lambda e: e.copy(out=out_ap, in_=in_ap), reads=reads, writes=writes)
        else:
            tk.op("dve", lambda e: e.tensor_copy(out=out_ap, in_=in_ap), reads=reads, writes=writes)

    wch = [tk.chan() for _ in range(4)]

    def ffn_phase(layer, t0, t1):
        TH = t1 - t0
        tl = ftiles(t0, t1)
        ph = contextlib.ExitStack()
        act = sb(ph, "f_act", [128, FC, TH], BF16)
        act_r = [Reg("act%d" % c) for c in range(FC)]
        w_in_v = f_win[layer].rearrange("(k p) o -> p k o", p=128)
        w_out_v = f_wout[layer].rearrange("(k p) o -> p k o", p=128)
        pa = contextlib.ExitStack()
        u = sb(pa, "f_u", [128, KC, TH], BF16)
        u_r = Reg("f_u")
        sq_s = Scratch(pa, "f_sq", [128, KC, 512], BF16, 1)
        t1_s = Scratch(pa, "f_t1", [128, 512], F32, 2)
        t2_s = Scratch(pa, "f_t2", [128, 512], F32, 2)
        sg_s = Scratch(pa, "f_sg", [128, 512], BF16, 3)
        slab = [sb(pa, "f_ws%d" % i, [128, KC, 2, 512], BF16) for i in range(2)]
        slab_r = [Reg("f_ws%d" % i) for i in range(2)]
        groups = [(g * 4, min(4, FC - g * 4)) for g in range((FC + 3) // 4)]

        def load_slab(gi):
            c0, ncg = groups[gi]
            s = gi % 2
            for two in range(2):
                col = two * DFF + c0 * 128
                tk.dma("pool", wch[s], lambda e, two=two, col=col: e.dma_start(
                    out=slab[s][:, :, two, :ncg * 128], in_=w_in_v[:, :, col:col + ncg * 128]),
                    writes=[slab_r[s]], cont=(two > 0))

        load_slab(0)
        for (ts, n) in tl:
            rmsnorm_pre(layer, 2, ts, n, u, u_r, ts - t0, sq_s, t1_s, t2_s)
        for gi, (c0, ncg) in enumerate(groups):
            if gi + 1 < len(groups):
                load_slab(gi + 1)
            s = gi % 2
            for cc in range(ncg):
                c = c0 + cc
                for (ts, n) in tl:
                    o = ts - t0
                    pg, pgr = bank()
                    for k in range(KC):
                        tk.op("pe", lambda e, k=k: e.matmul(pg[:, :n], slab[s][:, k, 0, cc * 128:(cc + 1) * 128],
                                                            u[:, k, o:o + n], start=(k == 0), stop=(k == KC - 1)),
                              reads=[slab_r[s], u_r], writes=[pgr])
                    pu, pur = bank()
                    for k in range(KC):
                        tk.op("pe", lambda e, k=k: e.matmul(pu[:, :n], slab[s][:, k, 1, cc * 128:(cc + 1) * 128],
                                                            u[:, k, o:o + n], start=(k == 0), stop=(k == KC - 1)),
                              reads=[slab_r[s], u_r], writes=[pur])
                    sg, sgr = sg_s.get()
                    tk.op("act", lambda e: e.activation(out=sg[:, :n], in_=pg[:, :n], func=AF.Silu),
                          reads=[pgr], writes=[sgr])
                    tk.op("dve", lambda e: e.tensor_tensor(out=act[:, c, o:o + n], in0=sg[:, :n], in1=pu[:, :n],
                                                           op=ALU.mult),
                          reads=[sgr, pur], writes=[act_r[c]])
        tk.barrier()
        pa.close()
        pbk = contextlib.ExitStack()
        y = sb(pbk, "f_y", [128, KC, TH], F32)
        y_r = [[Reg("f_y%d_%d" % (i, k)) for k in range(KC)] for i in range(len(tl))]
        sq_s = Scratch(pbk, "f_sq2", [128, KC, 512], BF16, 1)
        t1_s = Scratch(pbk, "f_t1b", [128, 512], F32, 2)
        t2_s = Scratch(pbk, "f_t2b", [128, 512], F32, 2)
        oslab = [sb(pbk, "f_wo%d" % i, [128, FC, 256], BF16) for i in range(2)]
        oslab_r = [Reg("f_wo%d" % i) for i in range(2)]

        def load_oslab(gi):
            s = gi % 2
            for kk in range(0, FC, 11):
                tk.dma("pool", wch[2 + s], lambda e, kk=kk: e.dma_start(
                    out=oslab[s][:, kk:kk + 11, :], in_=w_out_v[:, kk:kk + 11, gi * 256:(gi + 1) * 256]),
                    writes=[oslab_r[s]], cont=(kk > 0))

        load_oslab(0)
        for gi in range(4):
            if gi + 1 < 4:
                load_oslab(gi + 1)
            s = gi % 2
            for oc in range(2):
                ochunk = gi * 2 + oc
                for ti, (ts, n) in enumerate(tl):
                    o = ts - t0
                    pb_, pbr = bank()
                    for c in range(FC):
                        tk.op("pe", lambda e, c=c: e.matmul(pb_[:, :n], oslab[s][:, c, oc * 128:(oc + 1) * 128],
                                                            act[:, c, o:o + n], start=(c == 0), stop=(c == FC - 1)),
                              reads=[oslab_r[s], act_r[c]], writes=[pbr])
                    evac(y[:, ochunk, o:o + n], pb_[:, :n], [pbr], [y_r[ti][ochunk]])
        for ti, (ts, n) in enumerate(tl):
            postnorm_add(layer, 3, y, y_r[ti], ts - t0, ts, n, sq_s, t1_s, t2_s)
        tk.barrier()
        pbk.close()
        ph.close()

    l0s = contextlib.ExitStack()
    ckv_r = [Reg("ckv%d" % j) for j in range(8)]
    ckv_r_init = ckv_r
    ckv_b = sb(l0s, "ckv_b", [128, 2, T], BF16)
    kr_b = sb(l0s, "kr_b", [128, T], BF16)
    tk.op("pool", lambda e: e.memset(kr_b[64:128, :], 0.0), writes=ckv_r_init)
    och = [tk.chan() for _ in range(4)]
    och_i = [0]

    def next_och():
        och_i[0] = (och_i[0] + 1) % len(och)
        return och[och_i[0]]

    def kvregs(ts, n):
        return [ckv_r[j] for j in range(ts // 512, (ts + n - 1) // 512 + 1)]

    def mla_phase(t0, t1):
        TH = t1 - t0
        tl = tiles(t0, t1)
        has_s = t1 > SEQ
        ph = contextlib.ExitStack()
        cq = sb(ph, "m_cq", [128, 3, TH], BF16)
        cq_r = Reg("m_cq")
        ropet = sb(ph, "m_rope", [64, 2, TH], F32)
        rope_r = Reg("m_rope")
        tk.dma("sp", ch_in, lambda e: e.dma_start(out=ropet[:], in_=rope_d[:, :, t0:t1]), writes=[rope_r])
        oT = sb(ph, "m_oT", [128, NH, TH], BF16)
        oT_r = [Reg("m_oT%d" % hh) for hh in range(NH)]
        p1 = contextlib.ExitStack()
        win = sb(p1, "m_win", [128, KC, 704], BF16)
        win_r = Reg("m_win")
        gq = sb(p1, "m_gq", [128, 3], F32)
        gkv = sb(p1, "m_gkv", [128, 2], F32)
        tk.dma("pool", wch[0], lambda e: e.dma_start(out=win[:], in_=m_win.rearrange("(k p) o -> p k o", p=128)),
               writes=[win_r])
        gv_r = Reg("m_gv")
        tk.dma("sp", ch_in, lambda e: e.dma_start(out=gq[:], in_=m_gq), writes=[gv_r])
        tk.dma("sp", ch_in, lambda e: e.dma_start(out=gkv[:], in_=m_gkv), writes=[gv_r])
        u_s = Scratch(p1, "m_u", [128, KC, 512], BF16, 2)
        sq_s = Scratch(p1, "m_sq", [128, KC, 512], BF16, 1)
        t1_s = Scratch(p1, "m_t1", [128, 512], F32, 3)
        t2_s = Scratch(p1, "m_t2", [128, 512], F32, 3)
        a_s = Scratch(p1, "m_a", [128, 6, 512], F32, 2)
        rowo_s = Scratch(p1, "m_rowo", [128, 3, 512], F32, 2, chan=True)
        def m1_tile(ts, n):
            o = ts - t0
            u, u_r = u_s.get()
            rmsnorm_pre(0, 0, ts, n, u, u_r, 0, sq_s, t1_s, t2_s)
            yield
            a, a_r = a_s.get()
            for oc in range(6):
                m = 128 if oc < 5 else 64
                pb_, pbr = bank()
                for k in range(KC):
                    tk.op("pe", lambda e, k=k: e.matmul(pb_[:m, :n], win[:, k, oc * 128:oc * 128 + m], u[:, k, :n],
                                                        start=(k == 0), stop=(k == KC - 1)),
                          reads=[win_r, u_r], writes=[pbr])
                evac(a[:m, oc, :n], pb_[:m, :n], [pbr], [a_r])
                if oc % 2 == 1:
                    yield
            sq, sq_r = sq_s.get()
            tk.op("act", lambda e: e.activation(out=sq[:, 0:5, :n], in_=a[:, 0:5, :n], func=AF.Square),
                  reads=[a_r], writes=[sq_r])
            pq, pqr = bank()
            for k in range(3):
                tk.op("pe", lambda e, k=k: e.matmul(pq[:, :n], ONE_b, sq[:, k, :n], start=(k == 0), stop=(k == 2)),
                      reads=[sq_r, c_r], writes=[pqr])
            pk, pkr = bank()
            for k in range(2):
                tk.op("pe", lambda e, k=k: e.matmul(pk[:, :n], ONE_b, sq[:, 3 + k, :n], start=(k == 0), stop=(k == 1)),
                      reads=[sq_r, c_r], writes=[pkr])
            ta, tar = t1_s.get()
            rq, rqr = t2_s.get()
            rstd_from_ss(pq[:, :n], pqr, 128, n, 1.0 / QLORA, ta, tar, rq, rqr)
            tb, tbr = t1_s.get()
            rk, rkr = t2_s.get()
            rstd_from_ss(pk[:, :n], pkr, 128, n, 1.0 / KVLORA, tb, tbr, rk, rkr)
            for k in range(3):
                tk.op("dve", lambda e, k=k: e.scalar_tensor_tensor(
                    out=cq[:, k, o:o + n], in0=a[:, k, :n], scalar=gq[:, k:k + 1], in1=rq[:, :n],
                    op0=ALU.mult, op1=ALU.mult), reads=[a_r, rqr, gv_r], writes=[cq_r])
            rowo, rowo_r = rowo_s.get()
            for k in range(2):
                tk.op("dve", lambda e, k=k: e.scalar_tensor_tensor(
                    out=rowo[:, k, :n], in0=a[:, 3 + k, :n], scalar=gkv[:, k:k + 1], in1=rk[:, :n],
                    op0=ALU.mult, op1=ALU.mult), reads=[a_r, rkr, gv_r], writes=[rowo_r])
            tk.op("act", lambda e: e.copy(out=ckv_b[:, :, ts:ts + n], in_=rowo[:, 0:2, :n]),
                  reads=[rowo_r], writes=kvregs(ts, n))
            yield
            pr_, prr = bank()
            tk.op("pe", lambda e: e.matmul(pr_[:64, :n], ROT_f[:64, :], a[:64, 5, :n], start=True, stop=True),
                  reads=[a_r, c_r], writes=[prr])
            tk.op("dve", lambda e: e.tensor_tensor(out=rowo[:64, 2, :n], in0=a[:64, 5, :n],
                                                   in1=ropet[:, 0, o:o + n], op=ALU.mult),
                  reads=[a_r, rope_r], writes=[rowo_r])
            tk.op("dve", lambda e: e.tensor_tensor(out=a[:64, 5, :n], in0=pr_[:64, :n],
                                                   in1=ropet[:, 1, o:o + n], op=ALU.mult),
                  reads=[prr, rope_r], writes=[a_r])
            tk.op("dve", lambda e: e.tensor_tensor(out=rowo[:64, 2, :n], in0=rowo[:64, 2, :n],
                                                   in1=a[:64, 5, :n], op=ALU.add),
                  reads=[a_r], writes=[rowo_r])
            tk.op("act", lambda e: e.copy(out=kr_b[:64, ts:ts + n], in_=rowo[:64, 2, :n]),
                  reads=[rowo_r], writes=kvregs(ts, n))
            oc_ = rowo_s.chan()
            tk.dma("sp", oc_, lambda e: e.dma_start(
                out=rowsT[0:256, ts:ts + n].rearrange("(k p) t -> p k t", p=128), in_=rowo[:, 0:2, :n]),
                reads=[rowo_r])
            tk.dma("sp", oc_, lambda e: e.dma_start(out=rowsT[256:320, ts:ts + n], in_=rowo[:64, 2, :n]),
                   reads=[rowo_r], cont=True)
            yield

        for i_ in range(0, len(tl), 2):
            gens_ = [m1_tile(ts, n) for (ts, n) in tl[i_:i_ + 2]]
            while gens_:
                for g_ in list(gens_):
                    try:
                        next(g_)
                    except StopIteration:
                        gens_.remove(g_)
        tk.barrier()
        p1.close()
        p2 = contextlib.ExitStack()
        wuq = sb(p2, "m_wuq", [128, 3, NH * 192], BF16)
        wuk = sb(p2, "m_wuk", [128, NH, 2, 128], BF16)
        wuv = sb(p2, "m_wuv", [128, NH, 2, 128], BF16)
        wukT = sb(p2, "m_wukT", [128, NH, 256], BF16)
        w2_r = Reg("m_w2")
        tk.dma("pool", wch[1], lambda e: e.dma_start(out=wuq[:], in_=m_wuq.rearrange("(k p) o -> p k o", p=128)),
               writes=[w2_r])
        tk.dma("pool", wch[1], lambda e: e.dma_start(out=wuk[:], in_=m_wuk.rearrange("h (k p) n -> p h k n", p=128)),
               writes=[w2_r], cont=True)
        tk.dma("pool", wch[1], lambda e: e.dma_start(out=wuv[:], in_=m_wuv.rearrange("h (k p) n -> p h k n", p=128)),
               writes=[w2_r], cont=True)
        tk.dma("pool", wch[1], lambda e: e.dma_start(out=wukT[:], in_=m_wukT.rearrange("h p r -> p h r")),
               writes=[w2_r], cont=True)
        NKB = t1 // 128 if not has_s else SEQ // 128
        if has_s:
            Qs = sb(p2, "m_Qs", [128, 3, 4, 32], BF16)
            Qs_r = Reg("m_Qs")
            tk.op("pool", lambda e: e.memset(Qs[64:128, 2, :, :], 0.0), writes=[Qs_r])
        p2a = contextlib.ExitStack()
        qn_s = Scratch(p2a, "m_qn", [128, TH], BF16, 2)
        qr_s = Scratch(p2a, "m_qr", [128, TH], BF16, 2)
        for t_, r_ in zip(qr_s.t, qr_s.r):
            tk.op("pool", lambda e, t_=t_: e.memset(t_[64:128, :], 0.0), writes=[r_])
        qx_s = Scratch(p2a, "m_qx", [64, 2, 512], F32, 2)
        kn_s = Scratch(p2a, "m_kn", [128, SEQ], BF16, 2)
        vp_s = Scratch(p2a, "m_vp", [128, SEQ // 128, 132], BF16, 2)
        pT_s = Scratch(p2a, "m_pT", [128, 512], BF16, 4)
        on_s = Scratch(p2a, "m_on", [128, 4, 128], BF16, 2)
        rs_s = Scratch(p2a, "m_rs", [128, 4], F32, 2)
        ptl = [x for x in tl if x[0] < SEQ]
        mh = {}

        def m2_pro(hh):
            qn, qn_r = qn_s.get()
            qr, qr_r = qr_s.get()
            for (ts, n) in tl:
                o = ts - t0
                pb_, pbr = bank()
                for k in range(3):
                    tk.op("pe", lambda e, k=k: e.matmul(pb_[:, :n], wuq[:, k, hh * 192:hh * 192 + 128], cq[:, k, o:o + n],
                                                        start=(k == 0), stop=(k == 2)),
                          reads=[w2_r, cq_r], writes=[pbr])
                evac(qn[:, o:o + n], pb_[:, :n], [pbr], [qn_r])
                p2_, p2r = bank()
                for k in range(3):
                    tk.op("pe", lambda e, k=k: e.matmul(p2_[:64, :n], wuq[:, k, hh * 192 + 128:hh * 192 + 192],
                                                        cq[:, k, o:o + n], start=(k == 0), stop=(k == 2)),
                          reads=[w2_r, cq_r], writes=[p2r])
                qx, qx_r = qx_s.get()
                tk.op("act", lambda e: e.copy(out=qx[:, 0, :n], in_=p2_[:64, :n]), reads=[p2r], writes=[qx_r])
                p3_, p3r = bank()
                tk.op("pe", lambda e: e.matmul(p3_[:64, :n], ROT_f[:64, :], qx[:, 0, :n], start=True, stop=True),
                      reads=[qx_r, c_r], writes=[p3r])
                tk.op("dve", lambda e: e.tensor_tensor(out=qx[:, 1, :n], in0=p3_[:64, :n], in1=ropet[:, 1, o:o + n],
                                                       op=ALU.mult), reads=[p3r, rope_r], writes=[qx_r])
                tk.op("dve", lambda e: e.tensor_tensor(out=qx[:, 0, :n], in0=qx[:, 0, :n], in1=ropet[:, 0, o:o + n],
                                                       op=ALU.mult), reads=[rope_r], writes=[qx_r])
                tk.op("dve", lambda e: e.tensor_tensor(out=qr[:64, o:o + n], in0=qx[:, 0, :n], in1=qx[:, 1, :n],
                                                       op=ALU.add), reads=[qx_r], writes=[qr_r])
                yield
            kn, kn_r = kn_s.get()
            vp, vp_r = vp_s.get()
            nk = NKB * 128
            for ks in range(0, nk, 512):
                n = min(512, nk - ks)
                pb_, pbr = bank()
                for k in range(2):
                    tk.op("pe", lambda e, k=k: e.matmul(pb_[:, :n], wuk[:, hh, k, :], ckv_b[:, k, ks:ks + n],
                                                        start=(k == 0), stop=(k == 1)),
                          reads=[w2_r] + kvregs(ks, n), writes=[pbr])
                evac(kn[:, ks:ks + n], pb_[:, :n], [pbr], [kn_r])
                yield
            for kb4 in range(0, NKB, 4):
                nb = min(4, NKB - kb4)
                pb_, pbr = bank()
                for j in range(nb):
                    kb = kb4 + j
                    for k in range(2):
                        tk.op("pe", lambda e, k=k, j=j, kb=kb: e.matmul(
                            pb_[:, j * 128:(j + 1) * 128], ckv_b[:, k, kb * 128:(kb + 1) * 128], wuv[:, hh, k, :],
                            start=(k == 0), stop=(k == 1)),
                            reads=[w2_r] + kvregs(kb * 128, 128), writes=[pbr])
                evac(vp[:, kb4:kb4 + nb, 0:128], pb_[:, :nb * 128].rearrange("p (j v) -> p j v", v=128), [pbr], [vp_r])
                yield
            tk.op("dve", lambda e: e.memset(vp[:, :, 128:129], 1.0), writes=[vp_r])
            mh[hh] = (qn, qn_r, qr, qr_r, kn, kn_r, vp, vp_r)
            yield

        def m2_att(hh):
            qn, qn_r, qr, qr_r, kn, kn_r, vp, vp_r = mh[hh]
            for (ts, n) in ptl:
                o = ts - t0
                nsub = n // 128
                kb_hi = (ts + n) // 128
                accs_l = reserve(nsub)
                accs = [(a_, r_) for (a_, r_, _) in accs_l]
                pend = []

                def emit_pv(pv):
                    kb, d, j0, pT, pT_r = pv
                    for si in range(max(d, 0), nsub):
                        ab, abr = accs[si]
                        last_kb = ts // 128 + si
                        c0 = si * 128 - j0
                        tk.op("pe", lambda e, ab=ab, c0=c0, kb=kb, last_kb=last_kb: e.matmul(
                            ab[:, 0:129], pT[:, c0:c0 + 128], vp[:, kb, 0:129], start=(kb == 0), stop=(kb == last_kb)),
                            reads=[pT_r, vp_r], writes=[abr])

                for kb in range(kb_hi):
                    d = kb - ts // 128
                    j0 = max(d, 0) * 128
                    ncol = n - j0
                    psc, pscr = bank()
                    tk.op("pe", lambda e: e.matmul(psc[:, :ncol], kn[:, kb * 128:(kb + 1) * 128],
                                                   qn[:, o + j0:o + n], start=True, stop=False),
                          reads=[kn_r, qn_r], writes=[pscr])
                    tk.op("pe", lambda e: e.matmul(psc[:, :ncol], kr_b[:, kb * 128:(kb + 1) * 128],
                                                   qr[:, o + j0:o + n], start=False, stop=True),
                          reads=kvregs(kb * 128, 128) + [qr_r], writes=[pscr])
                    if len(pend) >= 2:
                        emit_pv(pend.pop(0))
                    pT, pT_r = pT_s.get()
                    tk.op("act", lambda e: e.activation(out=pT[:, :ncol], in_=psc[:, :ncol], func=AF.Exp,
                                                        scale=MLA_SCALE), reads=[pscr], writes=[pT_r])
                    if d >= 0:
                        tk.op("dve", lambda e: e.tensor_tensor(out=pT[:, 0:128], in0=pT[:, 0:128], in1=U_b,
                                                               op=ALU.mult), reads=[c_r], writes=[pT_r])
                    pend.append((kb, d, j0, pT, pT_r))
                    yield
                for pv_ in pend:
                    emit_pv(pv_)
                rs, rs_r = rs_s.get()
                on, on_r = on_s.get()
                for si in range(nsub):
                    ab, abr = accs[si]
                    tk.op("dve", lambda e, ab=ab, si=si: e.reciprocal(out=rs[:, si:si + 1], in_=ab[:, 128:129]),
                          reads=[abr], writes=[rs_r])
                    tk.op("act", lambda e, ab=ab, si=si: e.activation(out=on[:, si, :], in_=ab[:, 0:128], func=AF.Copy,
                                                                      scale=rs[:, si:si + 1]),
                          reads=[abr, rs_r], writes=[on_r])
                ptb, ptr = bank()
                for si in range(nsub):
                    tk.op("pe", lambda e, si=si: e.matmul(ptb[:, si * 128:(si + 1) * 128], on[:, si, :], I_b,
                                                          start=True, stop=True),
                          reads=[on_r, c_r], writes=[ptr])
                evac(oT[:, hh, o:o + n], ptb[:, :n], [ptr], [oT_r[hh]])
                release(accs_l)
                yield
            if has_s:
                so = SEQ - t0
                pb_, pbr = bank()
                for k in range(2):
                    tk.op("pe", lambda e, k=k: e.matmul(pb_[:, k * 16:(k + 1) * 16], wukT[:, hh, k * 128:(k + 1) * 128],
                                                        qn[:, so:so + 16], start=True, stop=True),
                          reads=[w2_r, qn_r], writes=[pbr])
                evac(Qs[:, 0:2, :, hh * 4:(hh + 1) * 4],
                     pb_[:, 0:32].rearrange("p (k s t) -> p k s t", k=2, s=4), [pbr], [Qs_r])
                tk.op("act", lambda e: e.copy(out=Qs[:64, 2, :, hh * 4:(hh + 1) * 4],
                                              in_=qr[:64, so:so + 16].rearrange("p (s t) -> p s t", s=4)),
                      reads=[qr_r], writes=[Qs_r])
            yield

        def run2(gens_):
            gens_ = list(gens_)
            while gens_:
                for g_ in list(gens_):
                    try:
                        next(g_)
                    except StopIteration:
                        gens_.remove(g_)

        run2([m2_pro(0)])
        for hh in range(NH):
            gl_ = [m2_att(hh)]
            if hh + 1 < NH:
                gl_.append(m2_pro(hh + 1))
            run2(gl_)
        tk.barrier()
        p2a.close()
        if has_s:
            R = 8
            pt_sb = sb(p2, "m_pt", [128, 4], I32)
            idx_sb = sb(p2, "m_idx", [128, 16, 4], I32)
            pt_r = Reg("m_pt")
            tk.dma("sp", ch_in, lambda e: e.dma_start(out=pt_sb[:], in_=ptab), writes=[pt_r])
            for gi in range(16):
                tk.op("dve", lambda e, gi=gi: e.tensor_scalar(out=idx_sb[:, gi, :], in0=pt_sb[:, :], scalar1=16.0,
                                                               scalar2=float(gi), op0=ALU.mult, op1=ALU.add),
                      reads=[pt_r], writes=[pt_r])
            NCB = 3
            cbuf = [sb(p2, "m_cb%d" % i, [128, R, ROW], F32) for i in range(NCB)]
            cbuf_r = [Reg("m_cb%d" % i) for i in range(NCB)]
            cch = [tk.chan() for _ in range(NCB)]
            ctok_s = Scratch(p2, "m_ctok", [128, R, 388], BF16, 3)
            for t_ in ctok_s.t:
                tk.op("dve", lambda e, t_=t_: e.memset(t_[:, :, 64:128], 0.0), writes=ctok_s.r)
                tk.op("dve", lambda e, t_=t_: e.memset(t_[:, :, 384:385], 1.0), writes=ctok_s.r)
            pd_s = Scratch(p2, "m_pd", [128, 32], BF16, 2)
            cnew = sb(p2, "m_cnew", [4, 260], BF16)
            cnew_r = Reg("m_cnew")
            oln = sb(p2, "m_oln", [32, 256], BF16)
            oln_r = Reg("m_oln")
            olT = sb(p2, "m_olT", [128, 2, 32], BF16)
            olT_r = Reg("m_olT")
            rsd = sb(p2, "m_rsd", [32, 1], F32)
            ngath = 128 // R
            so = SEQ - t0

            def gather(s, gi):
                i = (s * ngath + gi) % NCB
                tk.dma("pool", cch[i], lambda e: e.indirect_dma_start(
                    out=cbuf[i][:].rearrange("p r d -> p (r d)"), out_offset=None,
                    in_=cache,
                    in_offset=bass.IndirectOffsetOnAxis(ap=idx_sb[:, gi, s:s + 1], axis=0)),
                    reads=[pt_r], writes=[cbuf_r[i]])

            gather(0, 0)
            gather(0, 1)
            cT4_s = Scratch(p2, "m_cT4", [128, 3, 512], BF16, 3)
            pd4_s = Scratch(p2, "m_pd4", [128, 128], BF16, 3)
            batches = [(s_, gi, r0) for s_ in range(4) for gi in range(ngath) for r0 in range(0, R, 4)]
            nbt = len(batches)
            bst = [dict() for _ in range(nbt)]
            grp = {}
            accs_d = {}

            def stA(b):
                s, gi, r0 = batches[b]
                g = s * ngath + gi
                if r0 == 0:
                    nxt = g + 2
                    if nxt < 4 * ngath:
                        gather(nxt // ngath, nxt % ngath)
                    i = g % NCB
                    ctok, ctok_r = ctok_s.get()
                    tk.op("dve", lambda e: e.tensor_copy(out=ctok[:, :, 128:384], in_=cbuf[i][:, :, 0:256]),
                          reads=[cbuf_r[i]], writes=[ctok_r])
                    tk.op("dve", lambda e: e.tensor_copy(out=ctok[:, :, 0:64], in_=cbuf[i][:, :, 256:320]),
                          reads=[cbuf_r[i]], writes=[ctok_r])
                    grp[g] = (ctok, ctok_r)
                ctok, ctok_r = grp[g]
                cT4, cT4_r = cT4_s.get()
                for k in range(3):
                    m = 128
                    c0_ = (128, 256, 0)[k]
                    ptp, ptpr = bank()
                    for j in range(4):
                        tk.op("pe", lambda e, k=k, m=m, j=j, c0_=c0_: e.matmul(
                            ptp[:m, j * 128:(j + 1) * 128], ctok[:, r0 + j, c0_:c0_ + 128], I_b,
                            start=True, stop=True), reads=[ctok_r, c_r], writes=[ptpr])
                    if k == 1:
                        tk.op("dve", lambda e, k=k, m=m, ptp=ptp: e.tensor_copy(out=cT4[:m, k, :], in_=ptp[:m, :]),
                              reads=[ptpr], writes=[cT4_r])
                    else:
                        tk.op("act", lambda e, k=k, m=m, ptp=ptp: e.copy(out=cT4[:m, k, :], in_=ptp[:m, :]),
                              reads=[ptpr], writes=[cT4_r])
                bst[b]["cT4"] = (cT4, cT4_r)

            def stB(b):
                s, gi, r0 = batches[b]
                cT4, cT4_r = bst[b]["cT4"]
                psc, pscr = bank()
                for j in range(4):
                    for k in range(3):
                        m = 128
                        tk.op("pe", lambda e, k=k, m=m, j=j: e.matmul(
                            psc[:, j * 32:(j + 1) * 32], cT4[:m, k, j * 128:(j + 1) * 128], Qs[:m, k, s, :],
                            start=(k == 0), stop=(k == 2)), reads=[cT4_r, Qs_r], writes=[pscr])
                pd4, pd4_r = pd4_s.get()
                tk.op("act", lambda e: e.activation(out=pd4[:, :], in_=psc[:, 0:128], func=AF.Exp, scale=MLA_SCALE),
                      reads=[pscr], writes=[pd4_r])
                bst[b]["pd4"] = (pd4, pd4_r)

            def stC(b):
                s, gi, r0 = batches[b]
                g = s * ngath + gi
                ctok, ctok_r = grp[g]
                pd4, pd4_r = bst[b]["pd4"]
                if s not in accs_d:
                    accs_d[s] = reserve(1)
                (acc, acc_r, _), = acc_l = accs_d[s]
                for j in range(4):
                    first = (gi == 0 and r0 == 0 and j == 0)
                    tk.op("pe", lambda e, j=j, first=first: e.matmul(
                        acc[:32, 0:257], pd4[:, j * 32:(j + 1) * 32], ctok[:, r0 + j, 128:385],
                        start=first, stop=False), reads=[pd4_r, ctok_r], writes=[acc_r])
                if gi == ngath - 1 and r0 == R - 4:
                    tail(s, acc, acc_r, acc_l)

            def tail(s, acc, acc_r, acc_l):
                    tcol = SEQ + s * 4
                    ptp, ptpr = bank()
                    for k in range(2):
                        tk.op("pe", lambda e, k=k: e.matmul(ptp[:4, k * 128:(k + 1) * 128], ckv_b[:, k, tcol:tcol + 4], I_b,
                                                            start=True, stop=True),
                              reads=kvregs(tcol, 4) + [c_r], writes=[ptpr])
                    tk.op("dve", lambda e: e.memset(cnew[:, :], 0.0), writes=[cnew_r])
                    tk.op("act", lambda e: e.copy(out=cnew[:4, 0:256], in_=ptp[:4, 0:256]), reads=[ptpr], writes=[cnew_r])
                    tk.op("dve", lambda e: e.memset(cnew[:4, 256:257], 1.0), writes=[cnew_r])
                    psc, pscr = bank()
                    for k in range(3):
                        m = 128
                        src = ckv_b[:, k, tcol:tcol + 4] if k < 2 else kr_b[:, tcol:tcol + 4]
                        tk.op("pe", lambda e, k=k, m=m, src=src: e.matmul(psc[:4, 0:32], src, Qs[:m, k, s, :],
                                                                          start=(k == 0), stop=(k == 2)),
                              reads=kvregs(tcol, 4) + [Qs_r], writes=[pscr])
                    pd, pd_r = pd_s.get()
                    tk.op("act", lambda e: e.activation(out=pd[:4, :], in_=psc[:4, 0:32], func=AF.Exp, scale=MLA_SCALE),
                          reads=[pscr], writes=[pd_r])
                    tk.op("dve", lambda e: e.tensor_tensor(out=pd[:4, :], in0=pd[:4, :], in1=DM_b[:4, :], op=ALU.mult),
                          reads=[c_r], writes=[pd_r])
                    tk.op("pe", lambda e: e.matmul(acc[:32, 0:257], pd[:4, :], cnew[:4, 0:257], start=False, stop=True),
                          reads=[pd_r, cnew_r], writes=[acc_r])
                    tk.op("dve", lambda e: e.reciprocal(out=rsd[:, :], in_=acc[:32, 256:257]), reads=[acc_r], writes=[oln_r])
                    tk.op("act", lambda e: e.activation(out=oln[:, :], in_=acc[:32, 0:256], func=AF.Copy, scale=rsd[:, 0:1]),
                          reads=[acc_r, oln_r], writes=[oln_r])
                    release(acc_l)
                    ptp, ptpr = bank()
                    for k in range(2):
                        tk.op("pe", lambda e, k=k: e.matmul(ptp[:, k * 32:(k + 1) * 32], oln[:, k * 128:(k + 1) * 128],
                                                            I_b[:32, :32], start=True, stop=True),
                              reads=[oln_r, c_r], writes=[ptpr])
                    evac(olT[:, :, :], ptp[:, 0:64].rearrange("p (k q) -> p k q", k=2), [ptpr], [olT_r])
                    pov, povr = bank()
                    for hh in range(NH):
                        for k in range(2):
                            tk.op("pe", lambda e, k=k, hh=hh: e.matmul(pov[:, hh * 4:(hh + 1) * 4], wuv[:, hh, k, :],
                                                                       olT[:, k, hh * 4:(hh + 1) * 4],
                                                                       start=(k == 0), stop=(k == 1)),
                                  reads=[w2_r, olT_r], writes=[povr])
                    tk.op("act", lambda e: e.copy(out=oT[:, :, so + s * 4:so + s * 4 + 4],
                                                  in_=pov[:, 0:32].rearrange("p (h t) -> p h t", h=NH)),
                          reads=[povr], writes=oT_r)
            for b in range(nbt + 2):
                if b < nbt:
                    stA(b)
                if 0 <= b - 1 < nbt:
                    stB(b - 1)
                if 0 <= b - 2 < nbt:
                    stC(b - 2)
        tk.barrier()
        p2.close()
        p4 = contextlib.ExitStack()
        wo = sb(p4, "m_wo", [128, KC, D], BF16)
        wo_r = Reg("m_wo")
        for k0 in range(0, KC, 4):
            tk.dma("pool", wch[2], lambda e, k0=k0: e.dma_start(
                out=wo[:, k0:k0 + 4, :], in_=m_wo.rearrange("(k p) o -> p k o", p=128)[:, k0:k0 + 4, :]),
                writes=[wo_r], cont=(k0 > 0))
        y_s = Scratch(p4, "m_y", [128, KC, 512], F32, 2, nreg=KC)
        sq_s = Scratch(p4, "m_sq4", [128, KC, 512], BF16, 1)
        t1_s = Scratch(p4, "m_t14", [128, 512], F32, 2)
        t2_s = Scratch(p4, "m_t24", [128, 512], F32, 2)
        for (ts, n) in ftiles(t0, t1):
            o = ts - t0
            y, y_r = y_s.get()
            for oc in range(KC):
                pb_, pbr = bank()
                for k in range(NH):
                    tk.op("pe", lambda e, k=k: e.matmul(pb_[:, :n], wo[:, k, oc * 128:(oc + 1) * 128], oT[:, k, o:o + n],
                                                        start=(k == 0), stop=(k == NH - 1)),
                          reads=[wo_r, oT_r[k]], writes=[pbr])
                evac(y[:, oc, :n], pb_[:, :n], [pbr], [y_r[oc]])
            postnorm_add(0, 1, y, y_r, 0, ts, n, sq_s, t1_s, t2_s)
        tk.barrier()
        p4.close()
        ph.close()


    Sst_r = [Reg("Sst%d" % i) for i in range(NH)]
    craw_r = [Reg("craw%d" % i) for i in range(24)]
    gst = {}

    def gdn_init():
        gst["Sst"] = sb(es, "Sst", [128, NH, 128], F32)
        gst["craw"] = sb(es, "craw", [128, 24, 3], F32)
        tk.op("dve", lambda e: e.memset(gst["Sst"][:], 0.0), writes=Sst_r)
        tk.op("dve", lambda e: e.memset(gst["craw"][:], 0.0), writes=craw_r)
        XD_ = F32 if INV_F32 else BF16
        gst["gm"] = sb(es, "gm", [128, 1408], XD_)
        gst["gm_r"] = Reg("gm")
        gst["i4x"] = sb(es, "i4x", [128, 512], XD_)
        if INV_F32:
            tk.dma("sp", ch_in, lambda e: e.dma_start(out=gst["gm"][:], in_=gmask_d), writes=[gst["gm_r"]])
            tk.dma("sp", ch_in, lambda e: e.dma_start(out=gst["i4x"][:], in_=consts_d[:, 1024:1536]), writes=[gst["gm_r"]])
        else:
            tk.dma("pool", ch_cb, lambda e: e.dma_start(out=gst["gm"][:], in_=gmask_d), writes=[gst["gm_r"]])
            tk.dma("pool", ch_cb, lambda e: e.dma_start(out=gst["i4x"][:], in_=consts_d[:, 1024:1536]),
                   writes=[gst["gm_r"]], cont=True)

    def v3(ap2, C, nb):
        return ap2.rearrange("p (j c) -> p j c", c=128)[:, :nb, :C]

    def gdn_phase(t0, t1):
        Sst, craw = gst["Sst"], gst["craw"]
        gm, gm_r, i4x = gst["gm"], gst["gm_r"], gst["i4x"]
        XD = F32 if INV_F32 else BF16
        I_x = I_f if INV_F32 else I_b
        TH = t1 - t0
        tl = tiles(t0, t1)
        has_s = t1 > SEQ
        THp = min(t1, SEQ) - t0
        NB = THp // 128
        ph = contextlib.ExitStack()
        u = sb(ph, "g_u", [128, KC, TH], BF16)
        u_r = Reg("g_u")
        oT = sb(ph, "g_oT", [128, NH, TH], BF16)
        oT_r = [Reg("g_oT%d" % i) for i in range(NH)]
        p0 = contextlib.ExitStack()
        sq_s = Scratch(p0, "g_sq", [128, KC, 512], BF16, 1)
        t1_s = Scratch(p0, "g_t1", [128, 512], F32, 2)
        t2_s = Scratch(p0, "g_t2", [128, 512], F32, 2)
        for (ts, n) in ftiles(t0, t1):
            rmsnorm_pre(1, 0, ts, n, u, u_r, ts - t0, sq_s, t1_s, t2_s)
        tk.barrier()
        p0.close()
        p1 = contextlib.ExitStack()
        t1_s = Scratch(p1, "g_t1b", [128, 512], F32, 1)
        t2_s = Scratch(p1, "g_t2b", [128, 512], F32, 1)
        cw = sb(p1, "g_cw", [128, 24, 4], F32)
        gout = sb(p1, "g_gout", [128, 1], F32)
        abc = sb(p1, "g_abc", [128, 2, NH], F32)
        wba = sb(p1, "g_wba", [128, KC, 16], BF16)
        gp_r = Reg("g_par")
        tk.dma("sp", ch_in, lambda e: e.dma_start(out=cw[:], in_=d_cw), writes=[gp_r])
        tk.dma("sp", ch_in, lambda e: e.dma_start(out=gout[:], in_=d_gout), writes=[gp_r])
        tk.dma("sp", ch_in, lambda e: e.dma_start(out=abc[:, 0, :], in_=d_alog[0].partition_broadcast(128)), writes=[gp_r])
        tk.dma("sp", ch_in, lambda e: e.dma_start(out=abc[:, 1, :], in_=d_dtb[0].partition_broadcast(128)), writes=[gp_r])
        tk.op("act", lambda e: e.activation(out=abc[:, 0, :], in_=abc[:, 0, :], func=AF.Exp), reads=[gp_r], writes=[gp_r])
        wba_r = Reg("g_wba")
        tk.dma("pool", wch[2], lambda e: e.dma_start(
            out=wba[:], in_=d_win.rearrange("(k p) o -> p k o", p=128)[:, :, 4096:4112]), writes=[wba_r])
        if has_s:
            cvin = sb(p1, "g_cvin", [128, 24, 4, 3], F32)
            cvs = sb(p1, "g_cvs", [128, 24, 4, 3], F32)
            cvin_r = Reg("g_cvin")
            cvs_r = Reg("g_cvs")
            tk.dma("sp", ch_in, lambda e: e.dma_start(out=cvin[:], in_=cv_in.rearrange("(c p) (s j) -> p c s j", p=128, j=3)),
                   writes=[cvin_r])
        NBS = NB + (1 if has_s else 0)
        gtok = sb(p1, "g_gtok", [128, NBS, 8], F32)
        btok = sb(p1, "g_btok", [128, NBS, 8], F32)
        nbtok = sb(p1, "g_nbtok", [128, NBS, 8], F32)
        gt_r = Reg("g_gt")
        xt_ = sb(p1, "g_xt", [128, NBS, 8], F32)
        pba, pbar = bank()
        for b in range(NB):
            for k in range(KC):
                tk.op("pe", lambda e, k=k, b=b: e.matmul(pba[:, b * 16:(b + 1) * 16], u[:, k, b * 128:(b + 1) * 128],
                                                         wba[:, k, :], start=(k == 0), stop=(k == KC - 1)),
                      reads=[u_r, wba_r], writes=[pbar])
        if has_s:
            pbs, pbsr = bank()
            for s_ in range(4):
                for k in range(KC):
                    tk.op("pe", lambda e, k=k, s_=s_: e.matmul(pbs[:4, s_ * 16:(s_ + 1) * 16],
                                                               u[:, k, THp + 4 * s_:THp + 4 * s_ + 4], wba[:, k, :],
                                                               start=(k == 0), stop=(k == KC - 1)),
                          reads=[u_r, wba_r], writes=[pbsr])
        def ba_post(P_, src3, dstsl, preg):
            gt, bt, nbt, xt = dstsl
            nblk = src3.shape[1]
            tk.op("act", lambda e: e.activation(out=bt, in_=src3[:, :, 0:8], func=AF.Sigmoid), reads=[preg], writes=[gt_r])
            tk.op("dve", lambda e: e.tensor_scalar(out=nbt, in0=bt, scalar1=-1.0, scalar2=None, op0=ALU.mult),
                  reads=[gt_r], writes=[gt_r])
            for b in range(nblk):
                tk.op("dve", lambda e, b=b: e.tensor_tensor(out=xt[:, b, :], in0=src3[:, b, 8:16], in1=abc[:P_, 1, :],
                                                            op=ALU.add), reads=[preg, gp_r], writes=[gt_r])
            tk.op("act", lambda e: e.activation(out=xt, in_=xt, func=AF.Exp), reads=[gt_r], writes=[gt_r])
            tk.op("act", lambda e: e.activation(out=xt, in_=xt, func=AF.Ln, bias=epst[:P_, 1:2]), reads=[gt_r, c_r],
                  writes=[gt_r])
            for b in range(nblk):
                tk.op("dve", lambda e, b=b: e.scalar_tensor_tensor(out=gt[:, b, :], in0=xt[:, b, :], scalar=-1.0,
                                                                   in1=abc[:P_, 0, :], op0=ALU.mult, op1=ALU.mult),
                      reads=[gt_r, gp_r], writes=[gt_r])

        ba_post(128, pba[:, 0:NB * 16].rearrange("p (b x) -> p b x", x=16),
                (gtok[:, 0:NB, :], btok[:, 0:NB, :], nbtok[:, 0:NB, :], xt_[:, 0:NB, :]), pbar)
        if has_s:
            gts = sb(p1, "g_gts", [4, 4, 8], F32)
            bts = sb(p1, "g_bts", [4, 4, 8], F32)
            nbts = sb(p1, "g_nbts", [4, 4, 8], F32)
            xts = sb(p1, "g_xts", [4, 4, 8], F32)
            ba_post(4, pbs[:4, 0:64].rearrange("p (b x) -> p b x", x=16), (gts[:], bts[:], nbts[:], xts[:]), pbsr)
        wsl = [sb(p1, "g_wsl%d" % i, [128, KC, 4, 128], BF16) for i in range(2)]
        wsl_r = [Reg("g_wsl%d" % i) for i in range(2)]
        d_win_v = d_win.rearrange("(k p) o -> p k o", p=128)

        def load_wsl(hh):
            s_ = hh % 2
            for w_ in range(4):
                col = w_ * 1024 + hh * 128
                tk.dma("pool", wch[s_], lambda e, w_=w_, col=col: e.dma_start(
                    out=wsl[s_][:, :, w_, :], in_=d_win_v[:, :, col:col + 128]), writes=[wsl_r[s_]], cont=(w_ > 0))

        raw = sb(p1, "g_raw", [128, 3 + THp], F32)
        raw_r = Reg("g_raw")
        raws = sb(p1, "g_raws", [128, 4, 7], F32)
        raws_r = Reg("g_raws")
        acc = sb(p1, "g_acc", [128, TH], F32)
        acc_r = Reg("g_acc")
        sqb_s = Scratch(p1, "g_sqb", [128, 512], BF16, 2)
        qn_s = Scratch(p1, "g_qn", [128, TH], BF16, 2)
        kn_s = Scratch(p1, "g_kn", [128, TH], BF16, 2)
        vv_s = Scratch(p1, "g_vv", [128, TH], BF16, 2)
        zs_s = Scratch(p1, "g_zs", [128, TH], BF16, 2)
        if has_s:
            Ssm_s = Scratch(p1, "g_Ssm", [128, 4, 128], F32, 2, nreg=4, chan=True)
            ssl_ch = [tk.chan() for _ in range(2)]

        class BB:
            pass

        def make_bb(tag, wide):
            b_ = BB()
            CW = 128 if wide else 4
            F1 = sb(p1, "b_F1" + tag, [128, 4, CW], F32)
            F2 = sb(p1, "b_F2" + tag, [128, 4, CW], F32)
            b_.R, b_.gB = F1, F2
            F1b, F2b = F1[:].bitcast(BF16), F2[:].bitcast(BF16)
            b_.LT, b_.X = F1b[:, :, 0:CW], F1b[:, :, CW:2 * CW]
            b_.W1, b_.W2 = F2b[:, :, 0:CW], F2b[:, :, CW:2 * CW]
            b_.sc = sb(p1, "b_sc" + tag, [128, 4, 4], F32)
            b_.gl = sb(p1, "b_gl" + tag, [128, 4, 4], F32)
            b_.E1 = sb(p1, "b_E1" + tag, [128, 4, CW], BF16)
            b_.E2 = sb(p1, "b_E2" + tag, [128, 4, CW], BF16)
            b_.Ao, b_.AoT = b_.E1, b_.E2
            b_.Lp = sb(p1, "b_Lp" + tag, [128, 4, CW], BF16)
            b_.XT = sb(p1, "b_XT" + tag, [128, 4, CW], BF16)
            b_.qkt = sb(p1, "b_qkt" + tag, [128, 4, CW], BF16)
            b_.qd = sb(p1, "b_qd" + tag, [128, 4, CW], BF16)
            AOFF = _os.environ.get("AOFF", "")
            if wide:
                b_.egbc = sb(p1, "b_eg" + tag, [128, 4, 128], BF16)
                b_.utok = b_.egbc if "u" not in AOFF else sb(p1, "b_ut" + tag, [128, 4, 128], BF16)
                b_.wtok = b_.Lp if "w" not in AOFF else sb(p1, "b_wt" + tag, [128, 4, 128], BF16)
                b_.nWk = F1b[:, :, 0:128] if "n" not in AOFF else sb(p1, "b_nw" + tag, [128, 4, 128], BF16)
                if "a" in AOFF:
                    b_.Ao = sb(p1, "b_Ao" + tag, [128, 4, 128], BF16)
                    b_.AoT = sb(p1, "b_AoT" + tag, [128, 4, 128], BF16)
                if "x" in AOFF:
                    b_.LT = sb(p1, "b_LT" + tag, [128, 4, 128], BF16)
                    b_.X = sb(p1, "b_X" + tag, [128, 4, 128], BF16)
                    b_.W1 = sb(p1, "b_W1" + tag, [128, 4, 128], BF16)
                    b_.W2 = sb(p1, "b_W2" + tag, [128, 4, 128], BF16)
            else:
                b_.egbc = sb(p1, "b_eg" + tag, [128, 4, 4], BF16)
                b_.utok = sb(p1, "b_ut" + tag, [128, 4, 128], BF16)
                b_.wtok = sb(p1, "b_wt" + tag, [128, 4, 128], BF16)
                b_.nWk = sb(p1, "b_nw" + tag, [128, 4, 128], BF16)
            for nm in ("kdec", "kbg", "vb"):
                setattr(b_, nm, sb(p1, "b_%s%s" % (nm, tag), [128, 4, 128], BF16))
            rg = {nm: Reg("b_%s%s" % (nm, tag)) for nm in ("F1", "F2", "E1", "E2", "eg", "Lp", "XT", "qkt", "qd", "sc",
                                                          "kdec", "kbg", "vb", "ut", "wt", "nw")}
            b_.r = {"R": rg["F1"], "LT": rg["F1"], "X": rg["F1"], "gB": rg["F2"], "W1": rg["F2"], "W2": rg["F2"],
                    "E1": rg["E1"], "Ao": rg["E1"], "E2": rg["E2"], "AoT": rg["E2"], "Lp": rg["Lp"], "XT": rg["XT"],
                    "qkt": rg["qkt"], "qd": rg["qd"], "sc": rg["sc"], "kdec": rg["kdec"], "kbg": rg["kbg"],
                    "vb": rg["vb"]}
            if wide:
                b_.r.update({"eg": rg["eg"], "utok": rg["eg"], "wtok": rg["Lp"], "nWk": rg["F1"]})
            else:
                b_.r.update({"eg": rg["eg"], "utok": rg["ut"], "wtok": rg["wt"], "nWk": rg["nw"]})
            return b_

        bbs = {"p0": make_bb("p0", True), "p1": make_bb("p1", True)}
        if has_s:
            bbs["s"] = make_bb("s", False)
        bb_i = [0]
        Sb_s = Scratch(p1, "g_Sb", [128, 128], BF16, 2)
        vn_s = Scratch(p1, "g_vn", [128, 128], BF16, 2)
        on_s = Scratch(p1, "g_on", [128, 128], BF16, 2)
        jk_s = Scratch(p1, "g_jk", [128, 128], BF16, 2)
        ss_s = Scratch(p1, "g_ss", [128, 4], F32, 2)
        og_s = Scratch(p1, "g_og", [128, 128], BF16, 2)

        NWARM = int(_os.environ.get("NWARM", "0"))
        if NWARM:
            (wps, wps_r, _), = warm_l = reserve(1)

        def warm():
            for _ in range(NWARM):
                tk.op("pe", lambda e: e.matmul(wps[:, 0:512], ONE_b, cb[:, 0:512], start=True, stop=True),
                      reads=[c_r], writes=[wps_r])

        def gdn_pre(b_, hh, C, nb, o0, gmat, bcols, qn, qn_r, kn, kn_r, vv, vv_r, bmat=None):
            r = b_.r
            NC_ = nb * C
            gcols = [gmat[:, j:j + 1] for j in range(nb)]

            def flat(t, P_=C):
                return t[:].rearrange("p j c -> p (j c)")[:P_, 0:NC_]

            def cv(t, P_=C):
                return flat(t, P_).rearrange("p (j c) -> p j c", c=C)

            def cvp(ps, P_=C):
                return ps[:P_, 0:NC_].rearrange("p (j c) -> p j c", c=C)

            gbc = gmat.unsqueeze(2).to_broadcast([C, nb, C])
            tk.op("dve", lambda e: e.tensor_tensor(out=cv(b_.R), in0=MS_f[:C, :C].unsqueeze(1).to_broadcast([C, nb, C]),
                                                   in1=gbc, op=ALU.mult), reads=[c_r, gt_r], writes=[r["R"]])
            tk.op("dve", lambda e: e.tensor_tensor(out=cv(b_.gB), in0=U_f[:C, :C].unsqueeze(1).to_broadcast([C, nb, C]),
                                                   in1=gbc, op=ALU.mult), reads=[c_r, gt_r], writes=[r["gB"]])
            k1, k1r = bank()
            k2, k2r = bank()
            k3, k3r = bank()
            k4, k4r = bank()
            tk.op("pe", lambda e: e.matmul(k1[:C, 0:NC_], U_f[:C, :C], flat(b_.R), start=True, stop=True),
                  reads=[c_r, r["R"]], writes=[k1r])
            tk.op("pe", lambda e: e.matmul(k2[:C, 0:NC_], MS_f[:C, :C], flat(b_.gB), start=True, stop=True),
                  reads=[c_r, r["gB"]], writes=[k2r])
            tk.op("pe", lambda e: e.matmul(k3[:, 0:NC_], ONE_f[:C, :], flat(b_.gB), start=True, stop=True),
                  reads=[c_r, r["gB"]], writes=[k3r])
            tk.op("pe", lambda e: e.matmul(k4[:C, 0:nb], U_f[:C, :C], gmat, start=True, stop=True),
                  reads=[c_r, gt_r], writes=[k4r])
            tk.op("pe", lambda e: e.matmul(k4[:, 16:16 + nb], ONE_f[:C, :], gmat, start=True, stop=True),
                  reads=[c_r, gt_r], writes=[k4r])
            tk.op("act", lambda e: e.activation(out=cv(b_.E1), in_=cvp(k1), func=AF.Exp), reads=[k1r], writes=[r["E1"]])
            tk.op("act", lambda e: e.activation(out=cv(b_.E2), in_=cvp(k2), func=AF.Exp), reads=[k2r], writes=[r["E2"]])
            tk.op("act", lambda e: e.activation(out=cv(b_.egbc, 128), in_=cvp(k3, 128), func=AF.Exp),
                  reads=[k3r], writes=[r["eg"]])
            tk.op("act", lambda e: e.activation(out=b_.sc[:C, :nb, 0:1], in_=k4[:C, 0:nb].unsqueeze(2), func=AF.Exp),
                  reads=[k4r], writes=[r["sc"]])
            tk.op("act", lambda e: e.activation(out=b_.gl[:, :nb, 0:1], in_=k4[:, 16:16 + nb].unsqueeze(2), func=AF.Exp),
                  reads=[k4r], writes=[r["sc"]])
            tk.op("dve", lambda e: e.tensor_copy(out=b_.sc[:C, :nb, 1:2], in_=cv(b_.E2)[:, :, C - 1:C]),
                  reads=[r["E2"]], writes=[r["sc"]])
            tk.op("pool", lambda e: e.tensor_tensor(out=cv(b_.E1), in0=cv(b_.E1),
                                                    in1=cb[:C, 256:256 + C].unsqueeze(1).to_broadcast([C, nb, C]),
                                                    op=ALU.mult), reads=[c_r], writes=[r["E1"]])
            tk.op("pool", lambda e: e.tensor_tensor(out=cv(b_.E2), in0=cv(b_.E2),
                                                    in1=cb[:C, 128:128 + C].unsqueeze(1).to_broadcast([C, nb, C]),
                                                    op=ALU.mult), reads=[c_r, r["sc"]], writes=[r["E2"]])
            tk.op("dve", lambda e: e.tensor_tensor(out=b_.sc[:C, :nb, 2:3], in0=b_.sc[:C, :nb, 0:1], in1=bmat.unsqueeze(2),
                                                   op=ALU.mult), reads=[gt_r], writes=[r["sc"]])
            yield
            k5, k5r = bank()
            k6, k6r = bank()
            k7, k7r = bank()
            k8, k8r = bank()
            for j in range(nb):
                cs = slice(j * 128, j * 128 + C)
                ts_ = slice(o0 + j * C, o0 + (j + 1) * C)
                tk.op("pe", lambda e, cs=cs, ts_=ts_: e.matmul(k5[:C, cs], kn[:, ts_], kn[:, ts_], start=True, stop=True),
                      reads=[kn_r], writes=[k5r])
                tk.op("pe", lambda e, cs=cs, ts_=ts_: e.matmul(k6[:C, cs], kn[:, ts_], qn[:, ts_], start=True, stop=True),
                      reads=[kn_r, qn_r], writes=[k6r])
                tk.op("pe", lambda e, j=j, ts_=ts_: e.matmul(k7[:C, j * 128:(j + 1) * 128], kn[:, ts_], I_b,
                                                             start=True, stop=True), reads=[kn_r, c_r], writes=[k7r])
                tk.op("pe", lambda e, j=j, ts_=ts_: e.matmul(k8[:C, j * 128:(j + 1) * 128], vv[:, ts_], I_b,
                                                             start=True, stop=True), reads=[vv_r, c_r], writes=[k8r])
            tk.op("dve", lambda e: e.tensor_tensor(out=cv(b_.E1), in0=cv(b_.E1), in1=bmat.unsqueeze(2).to_broadcast([C, nb, C]),
                                                   op=ALU.mult), reads=[gt_r], writes=[r["E1"]])
            tk.op("dve", lambda e: e.tensor_tensor(out=b_.Lp[:C, :nb, :C], in0=v3(k5[:C, :], C, nb), in1=cv(b_.E1),
                                                   op=ALU.mult), reads=[k5r, r["E1"]], writes=[r["Lp"]])
            k7v = k7[:C, 0:nb * 128].rearrange("p (j c) -> p j c", c=128)
            k8v = k8[:C, 0:nb * 128].rearrange("p (j c) -> p j c", c=128)
            tk.op("dve", lambda e: e.tensor_tensor(out=b_.kdec[:C, :nb, :], in0=k7v,
                                                   in1=b_.sc[:C, :nb, 1:2].to_broadcast([C, nb, 128]), op=ALU.mult),
                  reads=[k7r, r["sc"]], writes=[r["kdec"]])
            tk.op("dve", lambda e: e.tensor_tensor(out=b_.kbg[:C, :nb, :], in0=k7v,
                                                   in1=b_.sc[:C, :nb, 2:3].to_broadcast([C, nb, 128]), op=ALU.mult),
                  reads=[k7r, r["sc"]], writes=[r["kbg"]])
            tk.op("dve", lambda e: e.tensor_tensor(out=b_.vb[:C, :nb, :], in0=k8v,
                                                   in1=bmat.unsqueeze(2).to_broadcast([C, nb, 128]), op=ALU.mult),
                  reads=[k8r, gt_r], writes=[r["vb"]])
            tk.op("dve", lambda e: e.tensor_tensor(out=b_.qkt[:C, :nb, :C], in0=v3(k6[:C, :], C, nb),
                                                   in1=cv(b_.E2), op=ALU.mult),
                  reads=[k6r, r["E2"]], writes=[r["qkt"]])
            tk.op("pool", lambda e: e.tensor_tensor(
                out=b_.qd[:, :nb, :C], in0=qn[:, o0:o0 + nb * C].rearrange("p (j c) -> p j c", c=C),
                in1=cv(b_.egbc, 128), op=ALU.mult), reads=[qn_r, r["eg"]], writes=[r["qd"]])
            yield
            def bc(off):
                return gm[:C, off:off + C].unsqueeze(1).to_broadcast([C, nb, C])

            def mm4(lhs, rhs, lr, rr, PO=C, wl=C, wr=C):
                warm()
                kx, kxr = bank()
                for j in range(nb):
                    tk.op("pe", lambda e, j=j: e.matmul(kx[:PO, j * 128:j * 128 + wr], lhs[:C, j, :wl],
                                                        rhs[:C, j, :wr], start=True, stop=True),
                          reads=[lr, rr], writes=[kxr])
                return kx, kxr

            kt, ktr = bank()
            for j in range(nb):
                cs = slice(j * 128, j * 128 + C)
                tk.op("pe", lambda e, j=j, cs=cs: e.matmul(kt[:C, cs], b_.Lp[:C, j, :C], I_x[:C, :C], start=True, stop=True),
                      reads=[r["Lp"], c_r], writes=[ktr])
            tk.op("act", lambda e: e.copy(out=b_.LT[:C, :nb, :C], in_=v3(kt[:C, :], C, nb)), reads=[ktr], writes=[r["LT"]])
            tk.op("dve", lambda e: e.scalar_tensor_tensor(out=b_.Ao[:C, :nb, :C], in0=b_.Lp[:C, :nb, :C], scalar=-1.0,
                                                          in1=bc(0), op0=ALU.mult, op1=ALU.mult),
                  reads=[r["Lp"], gm_r], writes=[r["Ao"]])
            tk.op("dve", lambda e: e.scalar_tensor_tensor(out=b_.AoT[:C, :nb, :C], in0=b_.LT[:C, :nb, :C], scalar=-1.0,
                                                          in1=bc(768), op0=ALU.mult, op1=ALU.mult),
                  reads=[r["LT"], gm_r], writes=[r["AoT"]])
            tk.op("dve", lambda e: e.tensor_tensor(out=b_.X[:C, :nb, :C], in0=b_.Ao[:C, :nb, :C],
                                                   in1=v3(i4x[:C, :], C, nb), op=ALU.add),
                  reads=[r["Ao"], gm_r], writes=[r["X"]])
            tk.op("dve", lambda e: e.tensor_tensor(out=b_.XT[:C, :nb, :C], in0=b_.AoT[:C, :nb, :C],
                                                   in1=v3(i4x[:C, :], C, nb), op=ALU.add),
                  reads=[r["AoT"], gm_r], writes=[r["XT"]])
            kx, kxr = mm4(b_.AoT, b_.Ao, r["AoT"], r["Ao"])
            tk.op("act", lambda e: e.copy(out=b_.W1[:C, :nb, :C], in_=v3(kx[:C, :], C, nb)), reads=[kxr], writes=[r["W1"]])
            yield
            kxa, kxar = mm4(b_.XT, b_.W1, r["XT"], r["W1"])
            kxb, kxbr = mm4(b_.W1, b_.XT, r["W1"], r["XT"])
            tk.op("dve", lambda e: e.tensor_tensor(out=b_.X[:C, :nb, :C], in0=v3(kxa[:C, :], C, nb),
                                                   in1=b_.X[:C, :nb, :C], op=ALU.add),
                  reads=[kxar, r["X"]], writes=[r["X"]])
            tk.op("dve", lambda e: e.tensor_tensor(out=b_.XT[:C, :nb, :C], in0=v3(kxb[:C, :], C, nb),
                                                   in1=b_.XT[:C, :nb, :C], op=ALU.add),
                  reads=[kxbr, r["XT"]], writes=[r["XT"]])
            yield
            levels = [b for b in (4, 8, 16, 32, 64) if 2 * b <= C]
            for li, b in enumerate(levels):
                last = (li == len(levels) - 1)
                off = 128 + li * 128
                kx, kxr = bank()
                for j in range(nb):
                    cs = slice(j * 128, j * 128 + C)
                    tk.op("pe", lambda e, j=j, cs=cs: e.matmul(kx[:C, cs], b_.LT[:C, j, :C], b_.X[:C, j, :C],
                                                               start=True, stop=False), reads=[r["LT"], r["X"]], writes=[kxr])
                    tk.op("pe", lambda e, j=j, cs=cs: e.matmul(kx[:C, cs], I_b[:C, :C], I_b[:C, :C],
                                                               start=False, stop=True), reads=[c_r], writes=[kxr])
                tk.op("dve", lambda e, kx=kx, off=off: e.tensor_tensor(out=b_.W1[:C, :nb, :C], in0=v3(kx[:C, :], C, nb),
                                                                       in1=bc(off), op=ALU.mult),
                      reads=[kxr, gm_r], writes=[r["W1"]])
                yield
                kb_, kbr = mm4(b_.W1, b_.XT, r["W1"], r["XT"])
                if not last:
                    ka_, kar = mm4(b_.XT, b_.W1, r["XT"], r["W1"])
                    tk.op("act", lambda e, ka_=ka_: e.copy(out=b_.X[:C, :nb, :C], in_=v3(ka_[:C, :], C, nb)),
                          reads=[kar], writes=[r["X"]])
                tk.op("dve", lambda e, kb_=kb_: e.tensor_copy(out=b_.XT[:C, :nb, :C], in_=v3(kb_[:C, :], C, nb)),
                      reads=[kbr], writes=[r["XT"]])
                yield
            TTc, rTT = b_.XT, r["XT"]
            ku, kur = bank()
            kw, kwr = bank()
            for j in range(nb):
                tk.op("pe", lambda e, j=j: e.matmul(ku[:C, j * 128:(j + 1) * 128], TTc[:C, j, :C], b_.vb[:C, j, :],
                                                    start=True, stop=True), reads=[rTT, r["vb"]], writes=[kur])
                tk.op("pe", lambda e, j=j: e.matmul(kw[:C, j * 128:(j + 1) * 128], TTc[:C, j, :C], b_.kbg[:C, j, :],
                                                    start=True, stop=True), reads=[rTT, r["kbg"]], writes=[kwr])
            tk.op("act", lambda e: e.copy(out=b_.utok[:C, :nb, :], in_=ku[:C, 0:nb * 128].rearrange("p (j c) -> p j c", c=128)),
                  reads=[kur], writes=[r["utok"]])
            tk.op("dve", lambda e: e.tensor_copy(out=b_.wtok[:C, :nb, :], in_=kw[:C, 0:nb * 128].rearrange("p (j c) -> p j c", c=128)),
                  reads=[kwr], writes=[r["wtok"]])
            yield
            kk, kkr = bank()
            kq, kqr = bank()
            for j in range(nb):
                tk.op("pe", lambda e, j=j: e.matmul(kk[:, j * 128:(j + 1) * 128], b_.wtok[:C, j, :], b_.kdec[:C, j, :],
                                                    start=True, stop=True), reads=[r["wtok"], r["kdec"]], writes=[kkr])
                tk.op("pe", lambda e, j=j: e.matmul(kq[:, j * 128:j * 128 + C], b_.wtok[:C, j, :], b_.qkt[:C, j, :C],
                                                    start=True, stop=True), reads=[r["wtok"], r["qkt"]], writes=[kqr])
            tk.op("act", lambda e: e.activation(out=b_.nWk[:, :nb, :], in_=kk[:, 0:nb * 128].rearrange("p (j c) -> p j c", c=128),
                                                func=AF.Copy, scale=-1.0), reads=[kkr], writes=[r["nWk"]])
            tk.op("dve", lambda e: e.tensor_tensor(out=b_.qd[:, :nb, :C], in0=b_.qd[:, :nb, :C], in1=v3(kq[:, :], C, nb),
                                                   op=ALU.subtract), reads=[kqr, r["qd"]], writes=[r["qd"]])
            yield
        def gdn_scan(b_, hh, C, nb, o0, S_f, S_regs, zs, zs_r, carry):
            r = b_.r
            Sb, Sb_r = None, None
            for j in range(nb):
                Sf, Sreg = S_f[j], S_regs[j]
                if Sb is None or not carry:
                    Sb, Sb_r = Sb_s.get()
                    tk.op("dve", lambda e, Sb=Sb, Sf=Sf: e.tensor_copy(out=Sb[:, :], in_=Sf), reads=[Sreg], writes=[Sb_r])
                cs = slice(o0 + j * C, o0 + (j + 1) * C)
                ks_, ksr = bank()
                tk.op("pe", lambda e, j=j: e.matmul(ks_[:, 0:128], b_.kdec[:C, j, :], b_.utok[:C, j, :], start=True, stop=False),
                      reads=[r["kdec"], r["utok"]], writes=[ksr])
                tk.op("pe", lambda e, j=j, Sb=Sb: e.matmul(ks_[:, 0:128], b_.nWk[:, j, :], Sb[:, :], start=False, stop=True),
                      reads=[r["nWk"], Sb_r], writes=[ksr])
                ko, kor = bank()
                tk.op("pe", lambda e, j=j: e.matmul(ko[:C, 0:128], b_.qkt[:C, j, :C], b_.utok[:C, j, :], start=True, stop=False),
                      reads=[r["qkt"], r["utok"]], writes=[kor])
                tk.op("pe", lambda e, j=j, Sb=Sb: e.matmul(ko[:C, 0:128], b_.qd[:, j, :C], Sb[:, :], start=False, stop=True),
                      reads=[r["qd"], Sb_r], writes=[kor])
                tk.op("dve", lambda e, j=j, Sf=Sf: e.scalar_tensor_tensor(out=Sf, in0=Sf, scalar=b_.gl[:, j, 0:1],
                                                                          in1=ks_[:, 0:128], op0=ALU.mult, op1=ALU.add),
                      reads=[ksr, r["sc"], Sreg], writes=[Sreg])
                if carry and j + 1 < nb:
                    Sb, Sb_r = Sb_s.get()
                    tk.op("dve", lambda e, Sb=Sb, Sf=Sf: e.tensor_copy(out=Sb[:, :], in_=Sf), reads=[Sreg], writes=[Sb_r])
                jk, jk_r = jk_s.get()
                ss, ss_r = ss_s.get()
                tk.op("act", lambda e, jk=jk, ss=ss: e.activation(out=jk[:C, :], in_=ko[:C, 0:128], func=AF.Square,
                                                                  accum_out=ss[:C, 0:1]), reads=[kor], writes=[jk_r, ss_r])
                tk.op("act", lambda e, ss=ss: e.activation(out=ss[:C, 1:2], in_=ss[:C, 0:1], func=AF.Ln,
                                                           bias=epst[:C, 0:1], scale=1.0 / 128), reads=[c_r], writes=[ss_r])
                tk.op("act", lambda e, ss=ss: e.activation(out=ss[:C, 2:3], in_=ss[:C, 1:2], func=AF.Exp, scale=-0.5),
                      writes=[ss_r])
                on, on_r = on_s.get()
                tk.op("act", lambda e, on=on, ss=ss: e.activation(out=on[:C, :], in_=ko[:C, 0:128], func=AF.Copy,
                                                                  scale=ss[:C, 2:3]), reads=[kor, ss_r], writes=[on_r])
                kp, kpr = bank()
                tk.op("pe", lambda e, on=on: e.matmul(kp[:, 0:C], on[:C, :], I_b[:C, :C], start=True, stop=True),
                      reads=[on_r, c_r], writes=[kpr])
                og, og_r = og_s.get()
                tk.op("act", lambda e, og=og: e.activation(out=og[:, 0:C], in_=kp[:, 0:C], func=AF.Copy, scale=gout[:, 0:1]),
                      reads=[kpr, gp_r], writes=[og_r])
                tk.op("pool", lambda e, og=og, cs=cs: e.tensor_tensor(out=oT[:, hh, cs], in0=og[:, 0:C], in1=zs[:, cs],
                                                                     op=ALU.mult), reads=[og_r, zs_r], writes=[oT_r[hh]])
                yield

        heads = {}

        def bulk(hh):
            if hh + 1 < NH:
                load_wsl(hh + 1)
            ws_, ws_r = wsl[hh % 2], wsl_r[hh % 2]
            outs = {}
            for w_ in range(3):
                cidx = w_ * 8 + hh
                tk.op("dve", lambda e: e.tensor_copy(out=raw[:, 0:3], in_=craw[:, cidx, :]),
                      reads=[craw_r[cidx]], writes=[raw_r])
                for (ts, n) in tl:
                    o = ts - t0
                    pb_, pbr = bank()
                    for k in range(KC):
                        tk.op("pe", lambda e, k=k: e.matmul(pb_[:, :n], ws_[:, k, w_, :], u[:, k, o:o + n],
                                                            start=(k == 0), stop=(k == KC - 1)),
                              reads=[ws_r, u_r], writes=[pbr])
                    if ts < SEQ:
                        evac(raw[:, 3 + o:3 + o + n], pb_[:, :n], [pbr], [raw_r])
                    else:
                        tk.op("act", lambda e: e.copy(out=raws[:, :, 3:7], in_=pb_[:, 0:16].rearrange("p (s t) -> p s t", s=4)),
                              reads=[pbr], writes=[raws_r])
                        tk.op("dve", lambda e: e.tensor_copy(out=raws[:, :, 0:3], in_=cvin[:, cidx, :, :]),
                              reads=[cvin_r], writes=[raws_r])
                        tk.op("dve", lambda e: e.tensor_copy(out=cvs[:, cidx, :, :], in_=raws[:, :, 4:7]),
                              reads=[raws_r], writes=[cvs_r])
                    yield
                tk.op("dve", lambda e: e.tensor_copy(out=craw[:, cidx, :], in_=raw[:, THp:THp + 3]),
                      reads=[raw_r], writes=[craw_r[cidx]])
                tk.op("dve", lambda e: e.tensor_scalar(out=acc[:, 0:THp], in0=raw[:, 0:THp], scalar1=cw[:, cidx, 0:1],
                                                       scalar2=None, op0=ALU.mult), reads=[raw_r, gp_r], writes=[acc_r])
                for j in range(1, 4):
                    tk.op("dve", lambda e, j=j: e.scalar_tensor_tensor(
                        out=acc[:, 0:THp], in0=raw[:, j:j + THp], scalar=cw[:, cidx, j:j + 1], in1=acc[:, 0:THp],
                        op0=ALU.mult, op1=ALU.add), reads=[raw_r, gp_r], writes=[acc_r])
                if has_s:
                    accs = acc[:, THp:THp + 16].rearrange("p (s t) -> p s t", s=4)
                    tk.op("dve", lambda e: e.tensor_scalar(out=accs, in0=raws[:, :, 0:4], scalar1=cw[:, cidx, 0:1],
                                                           scalar2=None, op0=ALU.mult), reads=[raws_r, gp_r], writes=[acc_r])
                    for j in range(1, 4):
                        tk.op("dve", lambda e, j=j: e.scalar_tensor_tensor(
                            out=accs, in0=raws[:, :, j:j + 4], scalar=cw[:, cidx, j:j + 1], in1=accs,
                            op0=ALU.mult, op1=ALU.add), reads=[raws_r, gp_r], writes=[acc_r])
                yield
                if w_ == 2:
                    vv, vv_r = vv_s.get()
                    tk.op("act", lambda e: e.activation(out=vv[:, :], in_=acc[:, :], func=AF.Silu), reads=[acc_r], writes=[vv_r])
                    outs[2] = (vv, vv_r)
                else:
                    tk.op("act", lambda e: e.activation(out=acc[:, :], in_=acc[:, :], func=AF.Silu), reads=[acc_r], writes=[acc_r])
                    dst, dst_r = (qn_s if w_ == 0 else kn_s).get()
                    outs[w_] = (dst, dst_r)
                    for (ts, n) in tl:
                        o = ts - t0
                        sqb, sqb_r = sqb_s.get()
                        tk.op("act", lambda e: e.activation(out=sqb[:, :n], in_=acc[:, o:o + n], func=AF.Square),
                              reads=[acc_r], writes=[sqb_r])
                        pb_, pbr = bank()
                        tk.op("pe", lambda e: e.matmul(pb_[:, :n], ONE_b, sqb[:, :n], start=True, stop=True),
                              reads=[sqb_r, c_r], writes=[pbr])
                        ta, tar = t1_s.get()
                        tb, tbr = t2_s.get()
                        rstd_from_ss(pb_[:, :n], pbr, 128, n, 1.0, ta, tar, tb, tbr)
                        if w_ == 0:
                            tk.op("dve", lambda e: e.scalar_tensor_tensor(
                                out=dst[:, o:o + n], in0=acc[:, o:o + n], scalar=128.0 ** -0.5, in1=tb[:, :n],
                                op0=ALU.mult, op1=ALU.mult), reads=[acc_r, tbr], writes=[dst_r])
                        else:
                            tk.op("dve", lambda e: e.tensor_tensor(out=dst[:, o:o + n], in0=acc[:, o:o + n], in1=tb[:, :n],
                                                                   op=ALU.mult), reads=[acc_r, tbr], writes=[dst_r])
                        yield
            zs, zs_r = zs_s.get()
            for (ts, n) in tl:
                o = ts - t0
                pb_, pbr = bank()
                for k in range(KC):
                    tk.op("pe", lambda e, k=k: e.matmul(pb_[:, :n], ws_[:, k, 3, :], u[:, k, o:o + n],
                                                        start=(k == 0), stop=(k == KC - 1)),
                          reads=[ws_r, u_r], writes=[pbr])
                tk.op("act", lambda e: e.activation(out=zs[:, o:o + n], in_=pb_[:, :n], func=AF.Silu),
                      reads=[pbr], writes=[zs_r])
                yield
            heads[hh] = (outs[0], outs[1], outs[2], (zs, zs_r))
            yield

        def seq_gens(*gs):
            for g_ in gs:
                yield from g_

        def head_pres(hh):
            (qn, qn_r), (kn, kn_r), (vv, vv_r), (zs, zs_r) = heads[hh]
            pres, scans_p, scan_s = [], [], None
            for bi, b0 in enumerate(range(0, NB, 4)):
                nb = min(4, NB - b0)
                b_ = bbs["p%d" % (bi % 2)]
                pres.append(gdn_pre(b_, hh, 128, nb, b0 * 128, gtok[:, b0:b0 + nb, hh],
                                    [btok[:, b0 + j, hh:hh + 1] for j in range(nb)], qn, qn_r, kn, kn_r, vv, vv_r,
                                    bmat=btok[:, b0:b0 + nb, hh]))
                scans_p.append(gdn_scan(b_, hh, 128, nb, b0 * 128, [Sst[:, hh, :]] * nb, [Sst_r[hh]] * nb, zs, zs_r, True))
            if has_s:
                Ssm, Ssm_r = Ssm_s.get()
                ch_ = Ssm_s.chan()
                tk.dma("sp", ch_, lambda e: e.dma_start(out=Ssm[:], in_=st_in[:, hh, :, :].rearrange("s d e -> d s e")),
                       writes=Ssm_r)
                pres.append(gdn_pre(bbs["s"], hh, 4, 4, THp, gts[:4, 0:4, hh],
                                    [bts[:4, j, hh:hh + 1] for j in range(4)], qn, qn_r, kn, kn_r, vv, vv_r,
                                    bmat=bts[:4, 0:4, hh]))

                def sample_scan(Ssm=Ssm, Ssm_r=Ssm_r, ch_=ch_):
                    yield from gdn_scan(bbs["s"], hh, 4, 4, THp, [Ssm[:, j, :] for j in range(4)],
                                        [Ssm_r[j] for j in range(4)], zs, zs_r, False)
                    tk.dma("sp", ch_, lambda e: e.dma_start(out=st_s[:, hh, :, :].rearrange("s d e -> d s e"), in_=Ssm[:]),
                           reads=Ssm_r)
                    yield
                scan_s = sample_scan()
            assert len(scans_p) <= 2
            return pres, scans_p, scan_s

        STAG = int(_os.environ.get("STAG", "0"))

        def run_il(gens, stagger=0):
            gens = list(gens)
            for gi_, g_ in enumerate(list(gens)):
                for _ in range(gi_ * stagger):
                    try:
                        next(g_)
                    except StopIteration:
                        if g_ in gens:
                            gens.remove(g_)
            while gens:
                for g_ in list(gens):
                    try:
                        next(g_)
                    except StopIteration:
                        gens.remove(g_)

        load_wsl(0)
        run_il([bulk(0)])
        for hh in range(NH):
            pres, scans_p, scan_s = head_pres(hh)
            gl_ = list(pres)
            if hh + 1 < NH:
                gl_.append(bulk(hh + 1))
            run_il(gl_, stagger=STAG)
            sl_ = [seq_gens(*scans_p)]
            if scan_s is not None:
                sl_.append(scan_s)
            run_il(sl_)
        if has_s:
            och_f = tk.chan()
            for hh in range(NH):
                tk.dma("sp", och_f, lambda e, hh=hh: e.dma_start(out=st_p[hh], in_=Sst[:, hh, :]), reads=[Sst_r[hh]],
                       cont=(hh > 0))
            tk.dma("sp", och_f, lambda e: e.dma_start(out=cv_p.rearrange("(c p) j -> p c j", p=128), in_=craw[:]),
                   reads=craw_r, cont=True)
            tk.dma("sp", och_f, lambda e: e.dma_start(out=cv_s.rearrange("(c p) (s j) -> p c s j", p=128, j=3), in_=cvs[:]),
                   reads=[cvs_r], cont=True)
        if NWARM:
            release(warm_l)
        tk.barrier()
        p1.close()
        p4 = contextlib.ExitStack()
        wo = sb(p4, "g_wo", [128, KC, D], BF16)
        wo_r = Reg("g_wo")
        for k0 in range(0, KC, 4):
            tk.dma("pool", wch[2], lambda e, k0=k0: e.dma_start(
                out=wo[:, k0:k0 + 4, :], in_=d_wo.rearrange("(k p) o -> p k o", p=128)[:, k0:k0 + 4, :]),
                writes=[wo_r], cont=(k0 > 0))
        y_s = Scratch(p4, "g_y", [128, KC, 512], F32, 2, nreg=KC)
        sq_s = Scratch(p4, "g_sq4", [128, KC, 512], BF16, 1)
        t1_s = Scratch(p4, "g_t14", [128, 512], F32, 2)
        t2_s = Scratch(p4, "g_t24", [128, 512], F32, 2)
        for (ts, n) in ftiles(t0, t1):
            o = ts - t0
            y, y_r = y_s.get()
            for oc in range(KC):
                pb_, pbr = bank()
                for k in range(NH):
                    tk.op("pe", lambda e, k=k: e.matmul(pb_[:, :n], wo[:, k, oc * 128:(oc + 1) * 128], oT[:, k, o:o + n],
                                                        start=(k == 0), stop=(k == NH - 1)),
                          reads=[wo_r, oT_r[k]], writes=[pbr])
                evac(y[:, oc, :n], pb_[:, :n], [pbr], [y_r[oc]])
            postnorm_add(1, 1, y, y_r, 0, ts, n, sq_s, t1_s, t2_s)
        tk.barrier()
        p4.close()
        ph.close()

    halves = [(0, HALF), (HALF, T)]
    if "mla" in stages:
        for (t0, t1) in halves:
            mla_phase(t0, t1)
            if "ffn0" in stages:
                ffn_phase(0, t0, t1)
    elif "ffn0" in stages:
        for (t0, t1) in halves:
            ffn_phase(0, t0, t1)
    tk.barrier()
    l0s.close()
    if "gdn" in stages:
        gdn_init()
        for (t0, t1) in halves:
            gdn_phase(t0, t1)
            if "ffn1" in stages:
                ffn_phase(1, t0, t1)
    elif "ffn1" in stages:
        for (t0, t1) in halves:
            ffn_phase(1, t0, t1)

    for k in range(KC):
        tk.dma("sp", next_och(), lambda e, k=k: e.dma_start(out=yT[k * 128:(k + 1) * 128, :], in_=h[:, k, :]),
               reads=h_r[k])
    tk.final()
    es.close()
    return nc


_NC_CACHE = {}


def prep_core_inputs(c, inp, SEQ, NPG):
    PAST = NPG * 128
    T = SEQ + 16
    x_p = inp["x_prompt"][c]
    x_s = inp["x_sample"][4 * c:4 * c + 4].reshape(16, D)
    xT = np.ascontiguousarray(np.concatenate([x_p, x_s], axis=0).T)
    consts, rope = host_consts(SEQ, PAST)
    cm = inp["cache_mla"][0]
    d = {
        "xT": xT,
        "cache": cm.reshape(cm.shape[0] * 16, 8 * ROW),
        "ptab": np.ascontiguousarray(inp["page_table"][4 * c:4 * c + 4].T.astype(np.int32)),
        "st_in": np.ascontiguousarray(inp["state_dn"][0, 4 * c:4 * c + 4]),
        "cv_in": np.ascontiguousarray(inp["state_dn_conv"][0, 4 * c:4 * c + 4].transpose(2, 0, 1)).reshape(3072, 12),
        "normw": np.ascontiguousarray(inp["norm_w"].reshape(2, 4, 8, 128).transpose(3, 0, 1, 2).reshape(128, 64)),
        "consts": consts,
        "gmask": host_gmask(),
        "rope": rope,
        "m_win": inp["mla_w_in"][0],
        "m_gq": np.ascontiguousarray(inp["mla_g_q"][0].reshape(3, 128).T),
        "m_gkv": np.ascontiguousarray(inp["mla_g_kv"][0].reshape(2, 128).T),
        "m_wuq": inp["mla_w_uq"][0],
        "m_wuk": inp["mla_w_uk"][0],
        "m_wukT": np.ascontiguousarray(inp["mla_w_uk"][0].transpose(0, 2, 1)),
        "m_wuv": inp["mla_w_uv"][0],
        "m_wo": inp["mla_w_o"][0],
        "d_win": inp["dn_w_in"][0],
        "d_cw": np.ascontiguousarray(inp["dn_conv_w"][0].T.reshape(24, 128, 4).transpose(1, 0, 2)),
        "d_alog": inp["dn_a_log"],
        "d_dtb": inp["dn_dt_bias"],
        "d_gout": np.ascontiguousarray(inp["dn_g_out"][0].reshape(128, 1)),
        "d_wo": inp["dn_w_o"][0],
        "f_win": inp["ffn_w_in"],
        "f_wout": inp["ffn_w_out"],
    }
    return {k: np.ascontiguousarray(np.asarray(v)) for k, v in d.items()}


def kernel(**inputs):
    inp = {k: np.asarray(v) for k, v in inputs.items()}
    B, SEQ, _ = inp["x_prompt"].shape
    NPG = inp["page_table"].shape[1]
    NPOOL = inp["cache_mla"].shape[1]
    key = (SEQ, NPG, NPOOL)
    nc = build(SEQ, NPG, NPOOL)
    in_maps = [prep_core_inputs(c, inp, SEQ, NPG) for c in range(8)]
    res = run_bass_kernel_spmd(nc, in_maps, core_ids=list(range(8))).results
    y_p = np.stack([res[c]["yT"][:, :SEQ].T for c in range(8)])
    y_s = np.concatenate([res[c]["yT"][:, SEQ:].T.reshape(4, 4, D) for c in range(8)])
    r_p = np.stack([res[c]["rowsT"][:, :SEQ].T for c in range(8)])[None]
    r_s = np.concatenate([res[c]["rowsT"][:, SEQ:].T.reshape(4, 4, ROW) for c in range(8)])[None]
    s_p = np.stack([res[c]["st_p"] for c in range(8)])[None]
    s_s = np.concatenate([res[c]["st_s"] for c in range(8)])[None]
    c_p = np.stack([res[c]["cv_p"].T for c in range(8)])[None]
    c_s = np.concatenate([res[c]["cv_s"].reshape(3072, 4, 3).transpose(1, 2, 0) for c in range(8)])[None]
    f = lambda a: np.ascontiguousarray(a.astype(np.float32))
    return (f(y_p), f(y_s), f(r_p), f(r_s), f(s_p), f(s_s), f(c_p), f(c_s))
```
